# Optimizing a Trainium2 kernel written in Bass

```python
import math
import jax, jax.numpy as jnp
from jax import lax
import numpy as np

D_MODEL = 1024
BATCH = 8
SEQ = 4096
DEPTH = 4

CHUNK = 64
Q_BLOCK = 128
N_EVEN = (DEPTH + 1) // 2
N_ODD = DEPTH // 2
EPS = 1e-6

RET_HEADS = 4
RET_DK = 64
RET_DV = 128
ROPE_BASE = 10000.0

SSD_HEADS = 8
SSD_HEAD_DIM = 64
SSD_INNER = SSD_HEADS * SSD_HEAD_DIM
SSD_GROUPS = 2
SSD_STATE = 64
SSD_CONV = 4
SSD_CONV_DIM = SSD_INNER + 2 * SSD_GROUPS * SSD_STATE
SSD_NORM_GROUP = SSD_INNER // SSD_GROUPS

HG_HEADS = 4
HG_DK = 128
HG_DV = 128

FOX_HEADS = 4
FOX_DIM = 128

FFN_HIDDEN = -(-8 * D_MODEL // (3 * 256)) * 256

AB_SIZES = (RET_HEADS * RET_DK, RET_HEADS * RET_DK, RET_HEADS * RET_DV, RET_HEADS * RET_DV,
            SSD_INNER, SSD_CONV_DIM, SSD_HEADS)
AB_IN = sum(AB_SIZES)
AB_OUT = RET_HEADS * RET_DV + SSD_INNER
CD_SIZES = (HG_HEADS * HG_DK, HG_HEADS * HG_DK, HG_HEADS * HG_DV, HG_HEADS * HG_DV,
            FOX_HEADS * FOX_DIM, FOX_HEADS * FOX_DIM, FOX_HEADS * FOX_DIM, FOX_HEADS)
CD_IN = sum(CD_SIZES)
CD_OUT = HG_HEADS * HG_DV + FOX_HEADS * FOX_DIM

kernel_name = "hybrid_retention_ssd_hgrn2_fox_trunk"


def rmsnorm(x, w):
    xf = x.astype(jnp.float32)
    y = xf * lax.rsqrt(jnp.mean(xf * xf, axis=-1, keepdims=True) + EPS)
    return (y * w.astype(jnp.float32)).astype(x.dtype)


def split_cols(h, sizes):
    offs, acc = [], 0
    for s in sizes[:-1]:
        acc += s
        offs.append(acc)
    return jnp.split(h, offs, axis=-1)


def to_heads(t, n):
    b, t_len, _ = t.shape
    return t.reshape(b, t_len, n, -1).transpose(0, 2, 1, 3)


def merge_heads(t):
    b, n, t_len, d = t.shape
    return t.transpose(0, 2, 1, 3).reshape(b, t_len, n * d)


def rotary_every_two(x):
    t_len, d = x.shape[2], x.shape[3]
    half = d // 2
    freqs = ROPE_BASE ** (-jnp.linspace(0.0, 1.0, half, dtype=jnp.float32))
    ang = jnp.arange(t_len, dtype=jnp.float32)[:, None] * freqs[None, :]
    cos, sin = jnp.cos(ang), jnp.sin(ang)
    xf = x.astype(jnp.float32).reshape(x.shape[:-1] + (half, 2))
    x1, x2 = xf[..., 0], xf[..., 1]
    out = jnp.stack([x1 * cos - x2 * sin, x1 * sin + x2 * cos], axis=-1)
    return out.reshape(x.shape).astype(x.dtype)


def chunk_recurrence(q, k, v, log_a):
    b_, h_, t_len, dk = q.shape
    dv = v.shape[-1]
    n_chunks = t_len // CHUNK
    per_channel = log_a.ndim == 4

    def to_chunks(a):
        return jnp.moveaxis(a.reshape(a.shape[:2] + (n_chunks, CHUNK) + a.shape[3:]), 2, 0)

    causal = jnp.tril(jnp.ones((CHUNK, CHUNK), dtype=bool))

    def step(state, inp):
        qi, ki, vi, li = inp
        cum = jnp.cumsum(li.astype(jnp.float32), axis=2)
        if per_channel:
            diff = cum[:, :, :, None, :] - cum[:, :, None, :, :]
            decay = jnp.exp(jnp.where(causal[:, :, None], diff, -jnp.inf))
            scores = jnp.einsum('bhid,bhjd,bhijd->bhij', qi, ki, decay)
            q_in = qi * jnp.exp(cum)
            k_out = ki * jnp.exp(cum[:, :, -1:] - cum)
            a_last = jnp.exp(cum[:, :, -1])[..., None]
        else:
            diff = cum[:, :, :, None] - cum[:, :, None, :]
            decay = jnp.exp(jnp.where(causal, diff, -jnp.inf))
            scores = jnp.einsum('bhid,bhjd->bhij', qi, ki) * decay
            q_in = qi * jnp.exp(cum)[..., None]
            k_out = ki * jnp.exp(cum[:, :, -1:] - cum)[..., None]
            a_last = jnp.exp(cum[:, :, -1])[..., None, None]
        out = (jnp.einsum('bhij,bhjv->bhiv', scores, vi)
               + jnp.einsum('bhid,bhdv->bhiv', q_in, state))
        state = a_last * state + jnp.einsum('bhjd,bhjv->bhdv', k_out, vi)
        return state, out

    state0 = jnp.zeros((b_, h_, dk, dv), jnp.float32)
    _, out = lax.scan(step, state0, (to_chunks(q), to_chunks(k), to_chunks(v), to_chunks(log_a)))
    return jnp.moveaxis(out, 0, 2).reshape(b_, h_, t_len, dv).astype(v.dtype)


def retention_mixer(rq, rk, rv, rg, gn_w):
    q = rotary_every_two(to_heads(rq, RET_HEADS))
    k = rotary_every_two(to_heads(rk, RET_HEADS)) * (RET_DK ** -0.5)
    v = to_heads(rv, RET_HEADS)
    log_gamma = jnp.log1p(-jnp.exp2(-5.0 - jnp.arange(RET_HEADS, dtype=jnp.float32)))
    log_a = jnp.broadcast_to(log_gamma[None, :, None], q.shape[:3])
    o = chunk_recurrence(q, k, v, log_a).astype(jnp.float32)
    mu = jnp.mean(o, axis=-1, keepdims=True)
    var = jnp.mean(jnp.square(o - mu), axis=-1, keepdims=True)
    o = (o - mu) * lax.rsqrt(var + EPS) * gn_w[:, None, :].astype(jnp.float32)
    return merge_heads(o).astype(rg.dtype) * jax.nn.silu(rg)


def ssd_mixer(z, xbc, dt_raw, conv_w, conv_b, dt_bias, a_log, d_skip, norm_w):
    xbc = lax.conv_general_dilated(xbc, conv_w[:, None, :], window_strides=(1,),
                                   padding=[(SSD_CONV - 1, 0)],
                                   dimension_numbers=('NWC', 'WIO', 'NWC'),
                                   feature_group_count=SSD_CONV_DIM)
    xbc = jax.nn.silu(xbc + conv_b)
    xs, bm, cm = split_cols(xbc, (SSD_INNER, SSD_GROUPS * SSD_STATE, SSD_GROUPS * SSD_STATE))
    v = to_heads(xs, SSD_HEADS)
    rep = SSD_HEADS // SSD_GROUPS
    bm = jnp.repeat(to_heads(bm, SSD_GROUPS), rep, axis=1)
    cm = jnp.repeat(to_heads(cm, SSD_GROUPS), rep, axis=1)
    dt = jax.nn.softplus(dt_raw.astype(jnp.float32) + dt_bias.astype(jnp.float32))
    dt = dt.transpose(0, 2, 1)
    log_a = dt * (-jnp.exp(a_log.astype(jnp.float32)))[None, :, None]
    y = chunk_recurrence(cm, bm * dt[..., None], v, log_a) + d_skip[None, :, None, None] * v
    y = merge_heads(y) * jax.nn.silu(z)
    b_, t_len, _ = y.shape
    yg = y.reshape(b_, t_len, SSD_GROUPS, SSD_NORM_GROUP)
    yg = rmsnorm(yg, jnp.ones((SSD_NORM_GROUP,), y.dtype)).reshape(b_, t_len, SSD_INNER)
    return yg * norm_w


def hgrn2_mixer(hq, hf, hi, hg, lower_bound, norm_w):
    q = to_heads(hq, HG_HEADS)
    zf = to_heads(hf, HG_HEADS).astype(jnp.float32)
    lb = lower_bound.reshape(HG_HEADS, 1, HG_DK)
    f = lb + (1.0 - lb) * jax.nn.sigmoid(zf)
    k = (1.0 - lb) * jax.nn.sigmoid(-zf)
    o = chunk_recurrence(q, k, to_heads(hi, HG_HEADS), jnp.log(f))
    o = rmsnorm(o, norm_w)
    return merge_heads(o) * jax.nn.silu(hg)


def fox_mixer(fq, fk, fv, f_raw, f_bias, qn_w, kn_w):
    q = rmsnorm(to_heads(fq, FOX_HEADS), qn_w)
    k = rmsnorm(to_heads(fk, FOX_HEADS), kn_w)
    v = to_heads(fv, FOX_HEADS)
    log_f = jax.nn.log_sigmoid(f_raw.astype(jnp.float32) + f_bias.astype(jnp.float32))
    cum_f = jnp.cumsum(log_f.transpose(0, 2, 1), axis=-1)
    scale = FOX_DIM ** -0.5
    t_len = q.shape[2]
    outs = []
    for start in range(0, t_len, Q_BLOCK):
        end = start + Q_BLOCK
        s = jnp.einsum('bhqd,bhkd->bhqk', q[:, :, start:end], k[:, :, :end]).astype(jnp.float32)
        s = s * scale + cum_f[:, :, start:end, None] - cum_f[:, :, None, :end]
        mask = (start + jnp.arange(Q_BLOCK))[:, None] >= jnp.arange(end)[None, :]
        p = jax.nn.softmax(jnp.where(mask, s, -jnp.inf), axis=-1)
        outs.append(jnp.einsum('bhqk,bhkd->bhqd', p.astype(v.dtype), v[:, :, :end]))
    return merge_heads(jnp.concatenate(outs, axis=2))


def swiglu(u, w_gate, w_up, w_down):
    return (jax.nn.silu(u @ w_gate) * (u @ w_up)) @ w_down


def setup_inputs(seed: int = 0) -> dict:
    key = jax.random.key(seed)
    ks = jax.random.split(key, 24)
    f32 = jnp.float32

    def nrm(k, shape, scale):
        return jax.random.normal(k, shape, f32) * scale

    def gain(k, shape):
        return 1.0 + 0.1 * jax.random.normal(k, shape, f32)

    dt0 = jnp.exp(jax.random.uniform(ks[12], (N_EVEN, SSD_HEADS), f32,
                                     math.log(1e-3), math.log(1e-1)))
    return {
        "x": nrm(ks[0], (BATCH, SEQ, D_MODEL), 1.0),
        "norm_mix": gain(ks[1], (DEPTH, D_MODEL)),
        "norm_ffn": gain(ks[2], (DEPTH, D_MODEL)),
        "ffn_w_gate": nrm(ks[3], (DEPTH, D_MODEL, FFN_HIDDEN), D_MODEL ** -0.5),
        "ffn_w_up": nrm(ks[4], (DEPTH, D_MODEL, FFN_HIDDEN), D_MODEL ** -0.5),
        "ffn_w_down": nrm(ks[5], (DEPTH, FFN_HIDDEN, D_MODEL), FFN_HIDDEN ** -0.5),
        "ab_w_in": nrm(ks[6], (N_EVEN, D_MODEL, AB_IN), D_MODEL ** -0.5),
        "ab_w_out": nrm(ks[7], (N_EVEN, AB_OUT, D_MODEL), AB_OUT ** -0.5),
        "ret_gn_w": gain(ks[8], (N_EVEN, RET_HEADS, RET_DV)),
        "ssd_conv_w": nrm(ks[9], (N_EVEN, SSD_CONV, SSD_CONV_DIM), SSD_CONV ** -0.5),
        "ssd_conv_b": nrm(ks[10], (N_EVEN, SSD_CONV_DIM), 0.02),
        "ssd_dt_bias": dt0 + jnp.log(-jnp.expm1(-dt0)),
        "ssd_a_log": jnp.log(jax.random.uniform(ks[11], (N_EVEN, SSD_HEADS), f32, 1.0, 16.0)),
        "ssd_d": gain(ks[13], (N_EVEN, SSD_HEADS)),
        "ssd_norm_w": gain(ks[14], (N_EVEN, SSD_INNER)),
        "cd_w_in": nrm(ks[15], (N_ODD, D_MODEL, CD_IN), D_MODEL ** -0.5),
        "cd_w_out": nrm(ks[16], (N_ODD, CD_OUT, D_MODEL), CD_OUT ** -0.5),
        "hg_lb_logits": nrm(ks[17], (N_ODD, HG_HEADS * HG_DK), 0.1),
        "hg_norm_w": gain(ks[18], (N_ODD, HG_DV)),
        "fox_f_bias": jax.random.uniform(ks[19], (N_ODD, FOX_HEADS), f32, 1.0, 5.0),
        "fox_q_norm_w": gain(ks[20], (N_ODD, FOX_DIM)),
        "fox_k_norm_w": gain(ks[21], (N_ODD, FOX_DIM)),
    }


def reference(x, norm_mix, norm_ffn, ffn_w_gate, ffn_w_up, ffn_w_down,
              ab_w_in, ab_w_out, ret_gn_w, ssd_conv_w, ssd_conv_b, ssd_dt_bias,
              ssd_a_log, ssd_d, ssd_norm_w, cd_w_in, cd_w_out, hg_lb_logits,
              hg_norm_w, fox_f_bias, fox_q_norm_w, fox_k_norm_w):
    lb_cum = jnp.cumsum(jax.nn.softmax(hg_lb_logits.astype(jnp.float32), axis=0), axis=0)
    lower_bounds = lb_cum - lb_cum[:1]

    h = x
    for layer in range(DEPTH):
        j = layer // 2
        u = rmsnorm(h, norm_mix[layer])
        if layer % 2 == 0:
            rq, rk, rv, rg, z, xbc, dt_raw = split_cols(u @ ab_w_in[j], AB_SIZES)
            y_ret = retention_mixer(rq, rk, rv, rg, ret_gn_w[j])
            y_ssd = ssd_mixer(z, xbc, dt_raw, ssd_conv_w[j], ssd_conv_b[j], ssd_dt_bias[j],
                              ssd_a_log[j], ssd_d[j], ssd_norm_w[j])
            h = h + jnp.concatenate([y_ret, y_ssd], axis=-1) @ ab_w_out[j]
        else:
            hq, hf, hi, hg, fq, fk, fv, f_raw = split_cols(u @ cd_w_in[j], CD_SIZES)
            y_hg = hgrn2_mixer(hq, hf, hi, hg, lower_bounds[j], hg_norm_w[j])
            y_fox = fox_mixer(fq, fk, fv, f_raw, fox_f_bias[j], fox_q_norm_w[j], fox_k_norm_w[j])
            h = h + jnp.concatenate([y_hg, y_fox], axis=-1) @ cd_w_out[j]
        u = rmsnorm(h, norm_ffn[layer])
        h = h + swiglu(u, ffn_w_gate[layer], ffn_w_up[layer], ffn_w_down[layer])
    return h
```

```python
import math
import numpy as np
from contextlib import ExitStack
import concourse.bass as bass
import concourse.mybir as mybir
from concourse.bass_utils import run_bass_kernel_spmd

F32 = mybir.dt.float32
BF16 = mybir.dt.bfloat16
AF = mybir.ActivationFunctionType
ALU = mybir.AluOpType
AX = mybir.AxisListType

D = 1024
KC = 8
FF = 2816
NJ = 22
EPS = 1e-6
ENGS = ["pe", "act", "dve", "pool", "sp"]
CENG = ["pe", "act", "dve", "pool"]


class Res:
    __slots__ = ("w", "rs")

    def __init__(self):
        self.w = None
        self.rs = []


def RL(n):
    return [Res() for _ in range(n)]


class Ev:
    __slots__ = ("sem", "val", "op", "blk")

    def __init__(self, blk, sem=None, val=None, op=None):
        self.blk, self.sem, self.val, self.op = blk, sem, val, op


class Op:
    __slots__ = ("eng", "fn", "waits", "marked", "ev", "incs")

    def __init__(self, eng, fn):
        self.eng, self.fn = eng, fn
        self.waits = []
        self.marked = False
        self.ev = None
        self.incs = None


class DSem:
    def __init__(self, sem):
        self.sem = sem
        self.count = 0


class Prog:
    def __init__(self, nc, es, same_engine_sync=True):
        self.nc = nc
        self.es = es
        self.ops = {e: [] for e in ENGS}
        self.csem = {e: es.enter_context(nc.semaphore("c_" + e)) for e in CENG}
        self.ccount = {e: 0 for e in CENG}
        self.bar = es.enter_context(nc.semaphore("bar"))
        self.nbar = 0
        self.same = same_engine_sync
        self.dsems = []
        self.blk = 0
        self.blk_dsems = set()

    def dsem(self):
        d = DSem(self.es.enter_context(self.nc.semaphore(f"d{len(self.dsems)}")))
        self.dsems.append(d)
        return d

    def _deps(self, op, reads, writes):
        evs = []
        for r in reads:
            if r.w is not None:
                evs.append(r.w)
        for r in writes:
            if r.w is not None:
                evs.append(r.w)
            evs.extend(r.rs)
        for ev in evs:
            if ev.blk != self.blk:
                continue
            if ev.op is not None:
                p = ev.op
                if p.eng == op.eng and (op.eng == "pe" or not self.same):
                    continue
                p.marked = True
            op.waits.append(ev)

    def _commit(self, ev, reads, writes):
        for r in reads:
            r.rs.append(ev)
        for r in writes:
            r.w = ev
            r.rs = []

    def op(self, eng, fn, reads=(), writes=()):
        o = Op(eng, fn)
        self._deps(o, reads, writes)
        o.ev = Ev(self.blk, op=o)
        self.ops[eng].append(o)
        self._commit(o.ev, reads, writes)
        return o

    def dma(self, q, out, in_, ds, reads=(), writes=(), slow=False):
        if slow:
            o = Op(q, lambda e: e.dma_start(out=out, in_=in_, allow_slow_non_contiguous=True))
        else:
            o = Op(q, lambda e: e.dma_start(out=out, in_=in_))
        self._deps(o, reads, writes)
        ds.count += 16
        self.blk_dsems.add(ds)
        o.incs = (ds.sem, 16)
        o.ev = Ev(self.blk, sem=ds.sem, val=ds.count)
        self.ops[q].append(o)
        self._commit(o.ev, reads, writes)
        return o

    def emit_block(self, final=False):
        nc = self.nc
        finals = []
        for e in CENG:
            ops = [o for o in self.ops[e] if o.fn is not None]
            if ops:
                ops[-1].marked = True
            c = self.ccount[e]
            for o in self.ops[e]:
                if o.incs is None and o.marked:
                    c += 1
                    o.ev.sem, o.ev.val = self.csem[e], c
            self.ccount[e] = c
            if ops:
                finals.append((self.csem[e], c))
        for ds in self.blk_dsems:
            finals.append((ds.sem, ds.count))
        self.nbar += 1
        nbar = self.nbar
        engobj = {"pe": "tensor", "act": "scalar", "dve": "vector", "pool": "gpsimd", "sp": "sync"}
        with nc.Block() as block:
            for e in ENGS:
                ops = self.ops[e]

                def body(eng, ops=ops, e=e):
                    waited = {}
                    for o in ops:
                        for ev in o.waits:
                            k = id(ev.sem)
                            if waited.get(k, 0) < ev.val:
                                eng.wait_ge(ev.sem, ev.val)
                                waited[k] = ev.val
                        ins = o.fn(eng)
                        if o.incs is not None:
                            ins.then_inc(o.incs[0], o.incs[1])
                        elif o.marked:
                            ins.then_inc(self.csem[e], 1)
                    if e == "sp":
                        for (s, v) in finals:
                            eng.wait_ge(s, v)
                        eng.sem_inc(self.bar, 1)
                    if not (final and e != "sp"):
                        eng.wait_ge(self.bar, nbar)

                getattr(block, engobj[e])(body)
        self.ops = {e: [] for e in ENGS}
        self.blk += 1
        self.blk_dsems = set()


class KB:
    pass


def _mm(P, out, lhsT, rhs, start, stop, reads, writes, **kw):
    return P.op("pe", lambda e: e.matmul(out, lhsT, rhs, start=start, stop=stop, **kw), reads=reads, writes=writes)


def build_program(T, layers, phases=("mix", "ffn"), test_ybuf=False):
    nc = bass.Bass("TRN2", target_bir_lowering=False)
    NT = T // 512
    K = KB()
    K.nc, K.T, K.NT = nc, T, NT
    dr = {}

    def din(name, shape, dt=F32):
        dr[name] = nc.dram_tensor(name, list(shape), dt, kind="ExternalInput").ap()
        return dr[name]

    din("xT", [D, T])
    din("norm_mix", [4, 128, KC])
    din("norm_ffn", [4, 128, KC])
    din("wg", [4, NJ, 128, KC, 128])
    din("wu", [4, NJ, 128, KC, 128])
    din("wd", [4, KC, 128, NJ, 128])
    din("w_out", [4, 128, KC, D])
    din("w_fox", [2, 128, KC, 1540])
    din("fox_vec", [2, 128, 3])
    din("cmask", [128, 128])
    din("w_hg", [2, 128, KC, 2048])
    din("hg_vec", [2, 128, 12])
    din("w_ret", [2, 128, KC, 3072])
    din("ret_vec", [2, 128, 12])
    din("rmask", [128, 512])
    din("w_ssd", [2, 128, KC, 1288])
    din("ssd_rows", [2, 128, 536])
    din("ssd_conv", [2, 128, 6, 5])
    din("smask", [128, 128])
    din("rot_tab", [128, 2, T])
    din("ident", [128, 128])
    out = nc.dram_tensor("out", [D, T], F32, kind="ExternalOutput").ap()
    if test_ybuf == "out":
        ybuf = nc.dram_tensor("ybuf", [D, T], BF16, kind="ExternalOutput").ap()
    elif test_ybuf:
        ybuf = nc.dram_tensor("ybuf", [D, T], BF16, kind="ExternalInput").ap()
    else:
        ybuf = nc.dram_tensor("ybuf", [D, T], BF16).ap()
    K.dr, K.out, K.ybuf = dr, out, ybuf

    with ExitStack() as es:
        P = Prog(nc, es)
        K.P = P
        K.bank = [es.enter_context(nc.psum_tensor(f"bank{i}", [128, 512], F32)) for i in range(8)]
        K.rbank = RL(8)
        K.ones_bf = es.enter_context(nc.sbuf_tensor("ones_bf", [128, 128], BF16))
        K.r_const = Res()
        P.op("pool", lambda e: e.memset(K.ones_bf[:], 1.0), writes=[K.r_const])
        K.r_h = [RL(NT) for _ in range(KC)]
        K.r_y = [RL(NT) for _ in range(KC)]
        K.dsem_pool = [P.dsem() for _ in range(40)]
        K.dsem_q = [P.dsem() for _ in range(8)]
        K.ident_bf = es.enter_context(nc.sbuf_tensor("ident_bf", [128, 128], BF16))
        K.ident_f = es.enter_context(nc.sbuf_tensor("ident_f", [128, 128], F32))
        K.cmask_bf = es.enter_context(nc.sbuf_tensor("cmask_bf", [128, 128], BF16))
        K.cmask_f = es.enter_context(nc.sbuf_tensor("cmask_f", [128, 128], F32))
        P.dma("pool", K.ident_bf[:], dr["ident"], K.dsem_q[7], writes=[K.r_const])
        P.dma("sp", K.ident_f[:], dr["ident"], K.dsem_pool[37], writes=[K.r_const])
        P.dma("pool", K.cmask_bf[:], dr["cmask"], K.dsem_q[7], writes=[K.r_const])
        P.dma("sp", K.cmask_f[:], dr["cmask"], K.dsem_pool[39], writes=[K.r_const])
        K.fd = nc.dram_tensor("fd", [8, T], F32).ap()
        K.r_fd = Res()

        plan = []
        for li, layer in enumerate(layers):
            if layer % 2 == 0:
                for ph in ("ret", "ssd"):
                    if ph in phases:
                        plan.append((ph, li, layer))
            else:
                for ph in ("hg", "fox"):
                    if ph in phases:
                        plan.append((ph, li, layer))
            if "ffn" in phases:
                plan.append(("ffn", li, layer))
        for pi, (ph, li, layer) in enumerate(plan):
            hin = dr["xT"] if li == 0 else out
            fin = pi == len(plan) - 1
            if ph == "ret":
                hg_phase(K, layer, hin, "ret", final=fin)
            elif ph == "hg":
                hg_phase(K, layer, hin, "hg", final=fin)
            elif ph == "ssd":
                ssd_phase(K, layer, hin, final=fin)
            elif ph == "fox":
                fox_phase(K, layer, hin, final=fin)
            else:
                ffn_phase(K, layer, hin, final=fin)
    return nc


def rmsnorm_tile(K, es_bufs, h_t, r_h_t, gain, r_gain, u_out, r_u, sq, r_sq, rstd, r_rstd, bank_i):
    P = K.P
    bank, rb = K.bank[bank_i], K.rbank[bank_i]
    for c in range(KC):
        P.op("act", lambda e, c=c: e.activation(out=sq[:, c, :], in_=h_t[:, c, :], func=AF.Square),
             reads=[r_h_t], writes=[r_sq])
    for c in range(KC):
        _mm(P, bank[:], K.ones_bf[:], sq[:, c, :], c == 0, c == KC - 1, [K.r_const, r_sq], [rb])
    P.op("act", lambda e: e.activation(out=rstd[:], in_=bank[:], func=AF.Sqrt, bias=EPS, scale=1.0 / D),
         reads=[rb], writes=[r_rstd])
    P.op("dve", lambda e: e.reciprocal(out=rstd[:], in_=rstd[:]), reads=[r_rstd], writes=[r_rstd])
    for c in range(KC):
        P.op("dve", lambda e, c=c: e.scalar_tensor_tensor(out=u_out(c), in0=h_t[:, c, :], scalar=gain[:, c:c + 1],
                                                         in1=rstd[:], op0=ALU.mult, op1=ALU.mult),
             reads=[r_h_t, r_gain, r_rstd], writes=[r_u])


def ffn_phase(K, layer, hin, final=False):
    nc, P, T, NT, dr, out = K.nc, K.P, K.T, K.NT, K.dr, K.out
    TG = min(1024, T)
    NG = T // TG
    TPG = TG // 512
    ds = K.dsem_pool
    with ExitStack() as es:
        def sb(name, shape, dt):
            return es.enter_context(nc.sbuf_tensor(f"f{layer}_{name}", shape, dt))
        wout = sb("wout", [128, KC, D], BF16); r_wout = Res()
        gain = sb("gain", [128, KC], F32); r_gain = Res()
        uT = sb("uT", [128, KC, TG], BF16); r_uT = RL(TPG)
        aT = sb("aT", [128, NJ, TG], BF16); r_aT = [RL(TPG) for _ in range(NJ)]
        ht = [sb(f"ht{i}", [128, KC, 512], F32) for i in range(2)]; r_ht = RL(2)
        yt = [sb(f"yt{i}", [128, KC, 512], BF16) for i in range(2)]; r_yt = RL(2)
        sq = sb("sq", [128, KC, 512], BF16); r_sq = Res()
        rstd = sb("rstd", [128, 512], F32); r_rstd = Res()
        wgc = [sb(f"wgc{i}", [128, KC, 128], BF16) for i in range(2)]; r_wgc = RL(2)
        wuc = [sb(f"wuc{i}", [128, KC, 128], BF16) for i in range(2)]; r_wuc = RL(2)
        wdc = [sb(f"wdc{i}", [128, NJ, 128], BF16) for i in range(2)]; r_wdc = RL(2)
        sg = [sb(f"sg{i}", [128, 512], F32) for i in range(2)]; r_sg = RL(2)
        hs = [sb(f"hs{i}", [128, 512], F32) for i in range(4)]; r_hs = RL(4)

        P.dma("pool", wout[:], dr["w_out"][layer], K.dsem_q[0], writes=[r_wout])
        P.dma("sp", gain[:], dr["norm_ffn"][layer], ds[1], writes=[r_gain])

        def hview(ap, t):
            return ap.rearrange("(c p) t -> p c t", p=128)[:, :, t * 512:(t + 1) * 512]

        def load_tile(tg):
            b = tg % 2
            rh = [K.r_h[c][tg] for c in range(KC)]
            ry = [K.r_y[c][tg] for c in range(KC)]
            P.dma("sp", ht[b][:], hview(hin, tg), ds[2 + b], reads=rh, writes=[r_ht[b]])
            P.dma("sp", yt[b][:], hview(K.ybuf, tg), ds[4 + b], reads=ry, writes=[r_yt[b]])

        wcount = [0]

        for g in range(NG):
            load_tile(g * TPG)
            for tl in range(TPG):
                tg = g * TPG + tl
                b = tg % 2
                if tl + 1 < TPG:
                    load_tile(tg + 1)
                for dc in range(KC):
                    bi = dc % 2
                    for kc in range(KC):
                        _mm(P, K.bank[bi][:], wout[:, kc, dc * 128:(dc + 1) * 128], yt[b][:, kc, :], kc == 0, kc == KC - 1,
                            [r_wout, r_yt[b]], [K.rbank[bi]])
                    P.op("dve", lambda e, dc=dc, bi=bi, b=b: e.tensor_tensor(out=ht[b][:, dc, :], in0=ht[b][:, dc, :],
                                                                           in1=K.bank[bi][:], op=ALU.add),
                         reads=[K.rbank[bi], r_ht[b]], writes=[r_ht[b]])
                P.dma("sp", hview(out, tg), ht[b][:], ds[6 + b], reads=[r_ht[b]],
                      writes=[K.r_h[c][tg] for c in range(KC)])
                rmsnorm_tile(K, None, ht[b], r_ht[b], gain, r_gain,
                             lambda c, tl=tl: uT[:, c, tl * 512:(tl + 1) * 512], r_uT[tl], sq, r_sq, rstd, r_rstd, 2)
            def load_w2(j):
                b = wcount[0] % 2
                wcount[0] += 1
                P.dma("pool", wgc[b][:], dr["wg"][layer, j], K.dsem_q[1 + b], writes=[r_wgc[b]])
                P.dma("pool", wuc[b][:], dr["wu"][layer, j], K.dsem_q[3 + b], writes=[r_wuc[b]])
                return b
            nb = load_w2(0)
            for j in range(NJ):
                b = nb
                if j + 1 < NJ:
                    nb = load_w2(j + 1)
                for tl in range(TPG):
                    pg, pu = 3 + 2 * (tl % 2), 4 + 2 * (tl % 2)
                    sl = slice(tl * 512, (tl + 1) * 512)
                    for kc in range(KC):
                        _mm(P, K.bank[pg][:], wgc[b][:, kc, :], uT[:, kc, sl], kc == 0, kc == KC - 1,
                            [r_wgc[b], r_uT[tl]], [K.rbank[pg]])
                    for kc in range(KC):
                        _mm(P, K.bank[pu][:], wuc[b][:, kc, :], uT[:, kc, sl], kc == 0, kc == KC - 1,
                            [r_wuc[b], r_uT[tl]], [K.rbank[pu]])
                    s = tl % 2
                    P.op("act", lambda e, s=s, pg=pg: e.activation(out=sg[s][:], in_=K.bank[pg][:], func=AF.Silu),
                         reads=[K.rbank[pg]], writes=[r_sg[s]])
                    P.op("dve", lambda e, s=s, pu=pu, j=j, sl=sl: e.tensor_tensor(out=aT[:, j, sl], in0=sg[s][:],
                                                                                 in1=K.bank[pu][:], op=ALU.mult),
                         reads=[r_sg[s], K.rbank[pu]], writes=[r_aT[j][tl]])
            P.dma("pool", wdc[0][:], dr["wd"][layer, 0], K.dsem_q[5], writes=[r_wdc[0]])
            cnt = 0
            for dc in range(KC):
                b = dc % 2
                if dc + 1 < KC:
                    P.dma("pool", wdc[1 - b][:], dr["wd"][layer, dc + 1], K.dsem_q[5 + (1 - b)], writes=[r_wdc[1 - b]])
                for tl in range(TPG):
                    tg = g * TPG + tl
                    bi = cnt % 2
                    hb = cnt % 4
                    cnt += 1
                    src = out[dc * 128:(dc + 1) * 128, tg * 512:(tg + 1) * 512]
                    P.dma("sp", hs[hb][:], src, ds[14 + hb], reads=[K.r_h[dc][tg]], writes=[r_hs[hb]])
                    for j in range(NJ):
                        _mm(P, K.bank[bi][:], wdc[b][:, j, :], aT[:, j, tl * 512:(tl + 1) * 512], j == 0, j == NJ - 1,
                            [r_wdc[b], r_aT[j][tl]], [K.rbank[bi]])
                    P.op("dve", lambda e, hb=hb, bi=bi: e.tensor_tensor(out=hs[hb][:], in0=hs[hb][:], in1=K.bank[bi][:],
                                                                       op=ALU.add),
                         reads=[K.rbank[bi], r_hs[hb]], writes=[r_hs[hb]])
                    P.dma("sp", src, hs[hb][:], ds[18 + hb], reads=[r_hs[hb]], writes=[K.r_h[dc][tg]])
        P.emit_block(final=final)


def act_rstd(P, out, r_out, in_, r_in, scale, tmp, r_tmp):
    P.op("act", lambda e: e.activation(out=tmp, in_=in_, func=AF.Ln, bias=EPS, scale=scale), reads=[r_in], writes=[r_tmp])
    P.op("act", lambda e: e.activation(out=out, in_=tmp, func=AF.Exp, scale=-0.5), reads=[r_tmp], writes=[r_out])


def rmsnorm_tile2(K, h_t, r_h_t, gain, r_gain, uT, r_u, sq, r_sq, rstd, r_rstd, tmp, r_tmp, bank_i):
    P = K.P
    bank, rb = K.bank[bank_i], K.rbank[bank_i]
    for c in range(KC):
        P.op("pool", lambda e, c=c: e.tensor_tensor(out=sq[:, c, :], in0=h_t[:, c, :], in1=h_t[:, c, :], op=ALU.mult),
             reads=[r_h_t], writes=[r_sq])
    for c in range(KC):
        _mm(P, bank[:], K.ones_bf[:], sq[:, c, :], c == 0, c == KC - 1, [K.r_const, r_sq], [rb])
    act_rstd(P, rstd[:], r_rstd, bank[:], rb, 1.0 / D, tmp[:], r_tmp)
    for c in range(KC):
        P.op("dve", lambda e, c=c: e.scalar_tensor_tensor(out=uT[:, c, :], in0=h_t[:, c, :], scalar=gain[:, c:c + 1],
                                                         in1=rstd[:], op0=ALU.mult, op1=ALU.mult),
             reads=[r_h_t, r_gain, r_rstd], writes=[r_u])


def hview(ap, t):
    return ap.rearrange("(c p) t -> p c t", p=128)[:, :, t * 512:(t + 1) * 512]


def fox_phase(K, layer, hin, final=False):
    nc, P, T, NT, dr = K.nc, K.P, K.T, K.NT, K.dr
    j = layer // 2
    NB = T // 128
    ds = K.dsem_pool
    SCALE = 128 ** -0.5
    with ExitStack() as es:
        def sb(name, shape, dt):
            return es.enter_context(nc.sbuf_tensor(f"x{layer}_{name}", shape, dt))
        w = sb("w", [128, KC, 1540], BF16); r_w = Res()
        gain = sb("gain", [128, KC], F32); r_gain = Res()
        fvec = sb("fvec", [128, 3], F32); r_fvec = Res()
        KT = sb("KT", [128, 4, T], BF16); r_KT = [RL(NT) for _ in range(4)]
        VA = sb("VA", [128, NB, 4, 129], BF16); r_VA = RL(NB); r_VAone = Res()
        Ftok = sb("Ftok", [128, NB, 4], F32); r_Ftok = RL(NB)
        Rq = sb("Rq", [128, NB, 4], F32); r_Rq = RL(NB)
        Bq = sb("Bq", [128, NB, 4], F32); r_Bq = Res()
        ht = [sb(f"ht{i}", [128, KC, 512], F32) for i in range(2)]; r_ht = RL(2)
        uT = sb("uT", [128, KC, 512], BF16); r_uT = Res()
        sq = sb("sq", [128, KC, 512], BF16); r_sq = Res()
        rstd = sb("rstd", [128, 512], F32); r_rstd = Res()
        tmp = sb("tmp", [128, 512], F32); r_tmp = Res()
        qn = sb("qn", [128, 4, 512], BF16); r_qn = RL(4)
        sqh = sb("sqh", [128, 512], BF16); r_sqh = Res()
        rq = sb("rq", [128, 512], F32); r_rq = Res()
        fx = [sb(f"fx{i}", [4, 512], F32) for i in range(4)]; r_fx = RL(4)
        Fc = [sb(f"Fc{i}", [4, 512], F32) for i in range(2)]; r_Fc = RL(2)
        onesf = sb("onesf", [4, 512], F32); r_onesf = Res()
        PT = [sb(f"PT{i}", [128, 4, 128], BF16) for i in range(3)]; r_PT = RL(3)
        ytok = sb("ytok", [128, 512], BF16); r_ytok = RL(4)
        rec = sb("rec", [128, 4], F32); r_rec = RL(4)
        yTt = [sb(f"yTt{i}", [128, 4, 512], BF16) for i in range(2)]; r_yTt = RL(2)

        P.dma("pool", w[:, :, 0:768], dr["w_fox"][j][:, :, 0:768], K.dsem_q[0], writes=[r_w])
        P.dma("pool", w[:, :, 768:1540], dr["w_fox"][j][:, :, 768:1540], K.dsem_q[0], writes=[r_w])
        P.dma("sp", gain[:], dr["norm_mix"][layer], ds[1], writes=[r_gain])
        P.dma("sp", fvec[:], dr["fox_vec"][j], ds[30], writes=[r_fvec])
        P.op("pool", lambda e: e.memset(onesf[:], 1.0), writes=[r_onesf])
        P.op("pool", lambda e: e.memset(VA[:, :, :, 128:129], 1.0), writes=[r_VAone])

        P.dma("sp", ht[0][:], hview(hin, 0), ds[2], reads=[K.r_h[c][0] for c in range(KC)], writes=[r_ht[0]])
        npt = 0
        for t in range(NT):
            b = t % 2
            if t + 1 < NT:
                P.dma("sp", ht[1 - b][:], hview(hin, t + 1), ds[2 + (1 - b)],
                      reads=[K.r_h[c][t + 1] for c in range(KC)], writes=[r_ht[1 - b]])
            rmsnorm_tile2(K, ht[b], r_ht[b], gain, r_gain, uT, r_uT, sq, r_sq, rstd, r_rstd, tmp, r_tmp, 7)
            tsl = slice(t * 512, (t + 1) * 512)
            pb, rpb = K.bank[6], K.rbank[6]
            for kc in range(KC):
                _mm(P, pb[0:4, :], w[:, kc, 1536:1540], uT[:, kc, :], kc == 0, kc == KC - 1, [r_w, r_uT], [rpb])
            P.op("dve", lambda e: e.tensor_scalar(out=fx[0][:], in0=pb[0:4, :], scalar1=fvec[0:4, 2:3], scalar2=None,
                                                  op0=ALU.add), reads=[rpb, r_fvec], writes=[r_fx[0]])
            P.op("dve", lambda e: e.scalar_tensor_tensor(out=fx[1][:], in0=fx[0][:], scalar=-1.0, in1=fx[0][:],
                                                         op0=ALU.mult, op1=ALU.min),
                 reads=[r_fx[0]], writes=[r_fx[1]])
            P.op("act", lambda e: e.activation(out=fx[1][:], in_=fx[1][:], func=AF.Exp),
                 reads=[r_fx[1]], writes=[r_fx[1]])
            P.op("act", lambda e: e.activation(out=fx[1][:], in_=fx[1][:], func=AF.Ln, bias=1.0),
                 reads=[r_fx[1]], writes=[r_fx[1]])
            P.op("dve", lambda e: e.tensor_scalar_min(out=fx[2][:], in0=fx[0][:], scalar1=0.0),
                 reads=[r_fx[0]], writes=[r_fx[2]])
            P.op("dve", lambda e: e.tensor_sub(out=fx[3][:], in0=fx[2][:], in1=fx[1][:]),
                 reads=[r_fx[1], r_fx[2]], writes=[r_fx[3]])
            init = 0.0 if t == 0 else Fc[1 - b][:, 511:512]
            P.op("dve", lambda e, b=b, init=init: e.tensor_tensor_scan(out=Fc[b][:], data0=onesf[:], data1=fx[3][:],
                                                                      initial=init, op0=ALU.mult, op1=ALU.add),
                 reads=[r_onesf, r_fx[3], r_Fc[1 - b]], writes=[r_Fc[b]])
            P.dma("sp", K.fd[0:4, tsl], Fc[b][:], ds[4], reads=[r_Fc[b]], writes=[K.r_fd])
            for bl in range(4):
                blk = t * 4 + bl
                c0 = blk * 128
                P.dma("sp", Ftok[:, blk, :], K.fd[0:4, c0:c0 + 128].rearrange("h s -> s h"), ds[12 + bl],
                      reads=[K.r_fd], writes=[r_Ftok[blk]], slow=True)
                P.dma("sp", Rq[:, blk, :], K.fd[0:4, c0 + 64:c0 + 65].rearrange("h o -> o h").broadcast_to([128, 4]),
                      ds[16 + bl], reads=[K.r_fd], writes=[r_Rq[blk]], slow=True)
            for h in range(4):
                for which in range(2):
                    c0 = which * 512 + h * 128
                    for kc in range(KC):
                        _mm(P, pb[:], w[:, kc, c0:c0 + 128], uT[:, kc, :], kc == 0, kc == KC - 1, [r_w, r_uT], [rpb])
                    P.op("act", lambda e: e.activation(out=sqh[:], in_=pb[:], func=AF.Square),
                         reads=[rpb], writes=[r_sqh])
                    nb_, rnb = K.bank[7], K.rbank[7]
                    _mm(P, nb_[:], K.ones_bf[:], sqh[:], True, True, [K.r_const, r_sqh], [rnb])
                    act_rstd(P, rq[:], r_rq, nb_[:], rnb, 1.0 / 128, tmp[:], r_tmp)
                    if which == 0:
                        dst, rd = qn[:, h, :], [r_qn[h]]
                    else:
                        dst, rd = KT[:, h, tsl], [r_KT[h][t]]
                    P.op("dve", lambda e, dst=dst, which=which: e.scalar_tensor_tensor(
                        out=dst, in0=pb[:], scalar=fvec[:, which:which + 1], in1=rq[:], op0=ALU.mult, op1=ALU.mult),
                        reads=[rpb, r_fvec, r_rq], writes=rd)
            for bl in range(4):
                blk = t * 4 + bl
                for kc in range(KC):
                    _mm(P, pb[:], uT[:, kc, bl * 128:(bl + 1) * 128], w[:, kc, 1024:1536], kc == 0, kc == KC - 1,
                        [r_w, r_uT], [rpb])
                P.op("dve", lambda e, blk=blk: e.tensor_copy(out=VA[:, blk, :, 0:128],
                                                            in_=pb[:].rearrange("p (h v) -> p h v", h=4)),
                     reads=[rpb, r_VAone], writes=[r_VA[blk]])
            yb = t % 2
            for ql in range(4):
                qb = t * 4 + ql
                P.op("dve", lambda e, qb=qb: e.tensor_tensor(
                    out=Bq[:, 0:qb + 1, :], in0=Rq[:, qb:qb + 1, :].to_broadcast([128, qb + 1, 4]),
                    in1=Ftok[:, 0:qb + 1, :], op=ALU.subtract),
                    reads=[r_Rq[qb]] + r_Ftok[0:qb + 1], writes=[r_Bq])
                for h in range(4):
                    ob, rob = K.bank[2 + h], K.rbank[2 + h]
                    for g0 in range(0, qb + 1, 4):
                        grp = list(range(g0, min(g0 + 4, qb + 1)))
                        sbi = npt % 2
                        pti = npt % 3
                        npt += 1
                        sbk, rsb = K.bank[sbi], K.rbank[sbi]
                        for i, kb in enumerate(grp):
                            _mm(P, sbk[:, i * 128:(i + 1) * 128], KT[:, h, kb * 128:(kb + 1) * 128],
                                qn[:, h, ql * 128:(ql + 1) * 128], True, True, [r_KT[h][kb // 4], r_qn[h]], [rsb])
                        for i, kb in enumerate(grp):
                            P.op("act", lambda e, i=i, kb=kb, h=h, sbk=sbk, pti=pti: e.activation(
                                out=PT[pti][:, i, :], in_=sbk[:, i * 128:(i + 1) * 128], func=AF.Exp,
                                bias=Bq[:, kb, h:h + 1], scale=SCALE), reads=[rsb, r_Bq], writes=[r_PT[pti]])
                            if kb == qb:
                                P.op("pool", lambda e, i=i, pti=pti: e.tensor_tensor(
                                    out=PT[pti][:, i, :], in0=PT[pti][:, i, :], in1=K.cmask_bf[:], op=ALU.mult),
                                    reads=[r_PT[pti], K.r_const], writes=[r_PT[pti]])
                        for i, kb in enumerate(grp):
                            _mm(P, ob[:, 0:129], PT[pti][:, i, :], VA[:, kb, h, :], kb == 0, kb == qb,
                                [r_PT[pti], r_VA[kb], r_VAone], [rob])
                    P.op("dve", lambda e, h=h, ob=ob: e.reciprocal(out=rec[:, h:h + 1], in_=ob[:, 128:129]),
                         reads=[rob], writes=[r_rec[h]])
                    P.op("dve", lambda e, h=h, ob=ob: e.tensor_scalar(
                        out=ytok[:, h * 128:(h + 1) * 128], in0=ob[:, 0:128], scalar1=rec[:, h:h + 1], scalar2=None,
                        op0=ALU.mult), reads=[rob, r_rec[h]], writes=[r_ytok[h]])
                tb = pb[:].bitcast(BF16)
                for h in range(4):
                    P.op("pe", lambda e, h=h: e.transpose(out=tb[:, h * 128:(h + 1) * 128],
                                                          in_=ytok[:, h * 128:(h + 1) * 128], identity=K.ident_bf[:]),
                         reads=[r_ytok[h], K.r_const], writes=[rpb])
                P.op("dve", lambda e, ql=ql, yb=yb: e.tensor_copy(
                    out=yTt[yb][:, :, ql * 128:(ql + 1) * 128], in_=tb[:, 0:512].rearrange("p (h s) -> p h s", h=4)),
                    reads=[rpb], writes=[r_yTt[yb]])
            dst = K.ybuf.rearrange("(c p) t -> p c t", p=128)[:, 4:8, tsl]
            P.dma("sp", dst, yTt[yb][:], ds[7 + yb], reads=[r_yTt[yb]], writes=[K.r_y[c][t] for c in range(4, 8)])
        P.emit_block(final=final)


def hg_phase(K, layer, hin, kind, final=False):
    nc, P, T, NT, dr = K.nc, K.P, K.T, K.NT, K.dr
    j = layer // 2
    ds = K.dsem_pool
    ret = kind == "ret"
    NCOL = 3072 if ret else 2048
    wname = "w_ret" if ret else "w_hg"
    with ExitStack() as es:
        def sb(name, shape, dt):
            return es.enter_context(nc.sbuf_tensor(f"g{layer}_{name}", shape, dt))
        w = sb("w", [128, KC, NCOL], BF16); r_w = Res()
        gain = sb("gain", [128, KC], F32); r_gain = Res()
        vec = sb("vec", [128, 12], F32); r_vec = Res()
        ht = [sb(f"ht{i}", [128, KC, 512], F32) for i in range(2)]; r_ht = RL(2)
        uT = sb("uT", [128, KC, 512], BF16); r_uT = Res()
        sq = sb("sq", [128, KC, 512], BF16); r_sq = Res()
        rstd = sb("rstd", [128, 512], F32); r_rstd = Res()
        tmp = sb("tmp", [128, 512], F32); r_tmp = Res()
        rmask = sb("rmask", [128, 512], F32); r_rmask = Res()
        qf = sb("qf", [128, 512], F32); r_qf = Res()
        kf = sb("kf", [128, 512], F32); r_kf = Res()
        lf = sb("lf", [128, 512], F32); r_lf = Res()
        cum = sb("cum", [128, 512], F32); r_cum = Res()
        e1 = sb("e1", [128, 512], F32); r_e1 = Res()
        ex = [sb(f"ex{i}", [128, 512], F32) for i in range(2)]; r_ex = RL(2)
        qh = sb("qh", [128, 512], BF16); r_qh = Res()
        kh = sb("kh", [128, 512], BF16); r_kh = Res()
        qi = sb("qi", [128, 512], BF16); r_qi = Res()
        ko = sb("ko", [128, 512], BF16); r_ko = Res()
        alast = sb("alast", [128, 8], F32); r_alast = Res()
        vtok = sb("vtok", [128, 4, 512], BF16); r_vtok = Res()
        kotok = sb("kotok", [128, 4, 128], BF16); r_kotok = Res()
        PT = [sb(f"PT{i}", [128, 64], BF16) for i in range(2)]; r_PT = RL(2)
        S_f = sb("S_f", [128, 4, 128], F32); r_Sf = RL(4)
        S_b = sb("S_b", [128, 4, 128], BF16); r_Sb = RL(4)
        sgt = sb("sgt", [128, 512], F32); r_sgt = Res()
        sqo = sb("sqo", [128, 512], BF16); r_sqo = Res()
        cen = sb("cen", [128, 512], F32); r_cen = Res()
        yTt = [sb(f"yTt{i}", [128, 4, 512], BF16) for i in range(2)]; r_yTt = RL(2)
        if ret:
            rot = [sb(f"rot{i}", [128, 2, 512], F32) for i in range(2)]; r_rot = RL(2)
            lfc = sb("lfc", [128, 4, 512], F32); r_lfc = Res()
            onesf = sb("onesf", [128, 512], F32); r_onesf = Res()
            meanb = sb("meanb", [128, 128], BF16); r_meanb = Res()

        P.dma("pool", w[:, :, 0:1024], dr[wname][j][:, :, 0:1024], K.dsem_q[0], writes=[r_w])
        P.dma("pool", w[:, :, 1024:2048], dr[wname][j][:, :, 1024:2048], K.dsem_q[0], writes=[r_w])
        if ret:
            P.dma("pool", w[:, :, 2048:3072], dr[wname][j][:, :, 2048:3072], K.dsem_q[0], writes=[r_w])
        P.dma("sp", gain[:], dr["norm_mix"][layer], ds[1], writes=[r_gain])
        if ret:
            P.dma("sp", vec[:], dr["ret_vec"][j], ds[30], writes=[r_vec])
        P.dma("sp", rmask[:], dr["rmask"], ds[31], writes=[r_rmask])
        if not ret:
            raw = sb("raw", [128, 12], F32); r_raw = Res()
            P.dma("sp", raw[:], dr["hg_vec"][j], ds[32], writes=[r_raw])
            if j == 0:
                P.op("dve", lambda e: e.tensor_scalar(out=vec[:, 0:4], in0=raw[:, 0:4], scalar1=0.0, scalar2=None,
                                                      op0=ALU.mult), reads=[r_raw, r_vec], writes=[r_vec])
            else:
                P.op("dve", lambda e: e.tensor_tensor(out=vec[:, 0:4], in0=raw[:, 4:8], in1=raw[:, 0:4], op=ALU.subtract),
                     reads=[r_raw, r_vec], writes=[r_vec])
                P.op("act", lambda e: e.activation(out=vec[:, 0:4], in_=vec[:, 0:4], func=AF.Sigmoid),
                     reads=[r_vec], writes=[r_vec])
            P.op("dve", lambda e: e.tensor_scalar(out=vec[:, 4:8], in0=vec[:, 0:4], scalar1=-1.0, scalar2=1.0,
                                                  op0=ALU.mult, op1=ALU.add), reads=[r_vec], writes=[r_vec])
            P.op("dve", lambda e: e.tensor_copy(out=vec[:, 8:9], in_=raw[:, 8:9]), reads=[r_raw, r_vec], writes=[r_vec])
        P.op("pool", lambda e: e.memset(S_f[:], 0.0), writes=r_Sf)
        P.op("pool", lambda e: e.memset(S_b[:], 0.0), writes=r_Sb)
        if ret:
            P.op("pool", lambda e: e.memset(onesf[:], 1.0), writes=[r_onesf])
            P.op("pool", lambda e: e.memset(meanb[:], 1.0 / 128), writes=[r_meanb])
            for h in range(4):
                P.op("dve", lambda e, h=h: e.tensor_scalar(out=lfc[:, h, :], in0=onesf[:], scalar1=vec[:, h:h + 1],
                                                          scalar2=None, op0=ALU.mult),
                     reads=[r_onesf, r_vec], writes=[r_lfc])

        pb, rpb = K.bank[6], K.rbank[6]
        tb = pb[:].bitcast(BF16)

        def proj_fm(c0):
            for kc in range(KC):
                _mm(P, pb[:], w[:, kc, c0:c0 + 128], uT[:, kc, :], kc == 0, kc == KC - 1, [r_w, r_uT], [rpb])

        P.dma("sp", ht[0][:], hview(hin, 0), ds[2], reads=[K.r_h[c][0] for c in range(KC)], writes=[r_ht[0]])
        nchunk = 0
        for t in range(NT):
            b = t % 2
            tsl = slice(t * 512, (t + 1) * 512)
            if t + 1 < NT:
                P.dma("sp", ht[1 - b][:], hview(hin, t + 1), ds[2 + (1 - b)],
                      reads=[K.r_h[c][t + 1] for c in range(KC)], writes=[r_ht[1 - b]])
            if ret:
                P.dma("sp", rot[b][:], dr["rot_tab"][:, :, tsl], ds[4 + b], writes=[r_rot[b]])
            rmsnorm_tile2(K, ht[b], r_ht[b], gain, r_gain, uT, r_uT, sq, r_sq, rstd, r_rstd, tmp, r_tmp, 7)
            for bl in range(4):
                for kc in range(KC):
                    _mm(P, pb[:], uT[:, kc, bl * 128:(bl + 1) * 128], w[:, kc, 1024:1536], kc == 0, kc == KC - 1,
                        [r_w, r_uT], [rpb])
                P.op("dve", lambda e, bl=bl: e.tensor_copy(out=vtok[:, bl, :], in_=pb[:]), reads=[rpb], writes=[r_vtok])
            yb = t % 2
            for h in range(4):
                if ret:
                    for which, dstf, rdst, sc in ((0, qf, r_qf, 1.0), (1, kf, r_kf, 0.125)):
                        proj_fm(which * 512 + h * 128)
                        P.op("dve", lambda e, b=b: e.tensor_tensor(out=e1[:], in0=pb[:], in1=rot[b][:, 0, :], op=ALU.mult),
                             reads=[rpb, r_rot[b]], writes=[r_e1])
                        proj_fm(2048 + which * 512 + h * 128)
                        P.op("dve", lambda e, b=b: e.tensor_tensor(out=cum[:], in0=pb[:], in1=rot[b][:, 1, :], op=ALU.mult),
                             reads=[rpb, r_rot[b]], writes=[r_cum])
                        P.op("dve", lambda e, dstf=dstf, sc=sc: e.scalar_tensor_tensor(
                            out=dstf[:], in0=e1[:], scalar=sc, in1=cum[:], op0=ALU.mult, op1=ALU.add),
                            reads=[r_e1, r_cum], writes=[rdst])
                        if sc != 1.0:
                            P.op("dve", lambda e, dstf=dstf, sc=sc: e.scalar_tensor_tensor(
                                out=dstf[:], in0=cum[:], scalar=sc - 1.0, in1=dstf[:], op0=ALU.mult, op1=ALU.add),
                                reads=[r_cum, rdst], writes=[rdst])
                    lft, r_lft = lfc[:, h, :], r_lfc
                else:
                    proj_fm(h * 128)
                    P.op("dve", lambda e: e.tensor_copy(out=qf[:], in_=pb[:]), reads=[rpb], writes=[r_qf])
                    proj_fm(512 + h * 128)
                    P.op("act", lambda e: e.activation(out=lf[:], in_=pb[:], func=AF.Sigmoid), reads=[rpb], writes=[r_lf])
                    P.op("dve", lambda e, h=h: e.tensor_scalar(out=lf[:], in0=lf[:], scalar1=vec[:, 4 + h:5 + h],
                                                              scalar2=vec[:, h:h + 1], op0=ALU.mult, op1=ALU.add),
                         reads=[r_lf, r_vec], writes=[r_lf])
                    P.op("dve", lambda e: e.tensor_scalar(out=kf[:], in0=lf[:], scalar1=-1.0, scalar2=1.0,
                                                          op0=ALU.mult, op1=ALU.add), reads=[r_lf], writes=[r_kf])
                    P.op("act", lambda e: e.activation(out=lf[:], in_=lf[:], func=AF.Ln), reads=[r_lf], writes=[r_lf])
                    lft, r_lft = lf[:], r_lf
                P.op("dve", lambda e, lft=lft: e.tensor_tensor_scan(out=cum[:], data0=rmask[:], data1=lft, initial=0.0,
                                                                    op0=ALU.mult, op1=ALU.add),
                     reads=[r_rmask, r_lft], writes=[r_cum])
                cum3 = cum[:].rearrange("p (c s) -> p c s", s=64)
                P.op("dve", lambda e: e.tensor_tensor(out=e1[:].rearrange("p (c s) -> p c s", s=64), in0=cum3,
                                                      in1=cum3[:, :, 31:32].to_broadcast([128, 8, 64]), op=ALU.subtract),
                     reads=[r_cum], writes=[r_e1])
                P.op("act", lambda e: e.activation(out=ex[0][:], in_=e1[:], func=AF.Exp), reads=[r_e1], writes=[r_ex[0]])
                P.op("dve", lambda e: e.tensor_tensor(out=qh[:], in0=qf[:], in1=ex[0][:], op=ALU.mult),
                     reads=[r_qf, r_ex[0]], writes=[r_qh])
                P.op("act", lambda e: e.activation(out=ex[1][:], in_=e1[:], func=AF.Exp, scale=-1.0),
                     reads=[r_e1], writes=[r_ex[1]])
                P.op("dve", lambda e: e.tensor_tensor(out=kh[:], in0=kf[:], in1=ex[1][:], op=ALU.mult),
                     reads=[r_kf, r_ex[1]], writes=[r_kh])
                P.op("act", lambda e: e.activation(out=ex[0][:], in_=cum[:], func=AF.Exp), reads=[r_cum], writes=[r_ex[0]])
                P.op("dve", lambda e: e.tensor_tensor(out=qi[:], in0=qf[:], in1=ex[0][:], op=ALU.mult),
                     reads=[r_qf, r_ex[0]], writes=[r_qi])
                P.op("dve", lambda e: e.tensor_tensor(out=e1[:].rearrange("p (c s) -> p c s", s=64), in0=cum3,
                                                      in1=cum3[:, :, 63:64].to_broadcast([128, 8, 64]), op=ALU.subtract),
                     reads=[r_cum], writes=[r_e1])
                P.op("act", lambda e: e.activation(out=ex[1][:], in_=e1[:], func=AF.Exp, scale=-1.0),
                     reads=[r_e1], writes=[r_ex[1]])
                P.op("dve", lambda e: e.tensor_tensor(out=ko[:], in0=kf[:], in1=ex[1][:], op=ALU.mult),
                     reads=[r_kf, r_ex[1]], writes=[r_ko])
                P.op("act", lambda e: e.activation(out=alast[:].rearrange("p (c o) -> p c o", o=1), in_=cum3[:, :, 63:64],
                                                   func=AF.Exp), reads=[r_cum], writes=[r_alast])
                for bl in range(4):
                    P.op("pe", lambda e, bl=bl: e.transpose(out=tb[:, bl * 128:(bl + 1) * 128],
                                                            in_=ko[:, bl * 128:(bl + 1) * 128], identity=K.ident_bf[:]),
                         reads=[r_ko, K.r_const], writes=[rpb])
                P.op("dve", lambda e: e.tensor_copy(out=kotok[:].rearrange("p b d -> p (b d)"), in_=tb[:, 0:512]),
                     reads=[rpb], writes=[r_kotok])
                ob, rob = K.bank[2 + (h % 2)], K.rbank[2 + (h % 2)]
                for c in range(8):
                    bl, p0 = c // 2, (c % 2) * 64
                    csl = slice(c * 64, (c + 1) * 64)
                    si = nchunk % 2
                    nchunk += 1
                    sbk, rsb = K.bank[si], K.rbank[si]
                    dbk, rdb = K.bank[4 + si], K.rbank[4 + si]
                    _mm(P, sbk[p0:p0 + 64, 0:64], kh[:, csl], qh[:, csl], True, True, [r_kh, r_qh], [rsb])
                    P.op("dve", lambda e, si=si, p0=p0, sbk=sbk: e.tensor_tensor(
                        out=PT[si][p0:p0 + 64, :], in0=sbk[p0:p0 + 64, 0:64], in1=K.cmask_f[p0:p0 + 64, p0:p0 + 64],
                        op=ALU.mult), reads=[rsb, K.r_const], writes=[r_PT[si]])
                    first = (t == 0 and c == 0)
                    _mm(P, ob[:, csl], vtok[p0:p0 + 64, bl, h * 128:(h + 1) * 128], PT[si][p0:p0 + 64, :], True, first,
                        [r_vtok, r_PT[si]], [rob])
                    if not first:
                        _mm(P, ob[:, csl], S_b[:, h, :], qi[:, csl], False, True, [r_Sb[h], r_qi], [rob])
                    _mm(P, dbk[:, 0:128], kotok[p0:p0 + 64, bl, :], vtok[p0:p0 + 64, bl, h * 128:(h + 1) * 128], True, True,
                        [r_kotok, r_vtok], [rdb])
                    P.op("dve", lambda e, h=h, c=c, dbk=dbk: e.scalar_tensor_tensor(
                        out=S_f[:, h, :], in0=S_f[:, h, :], scalar=alast[:, c:c + 1], in1=dbk[:, 0:128],
                        op0=ALU.mult, op1=ALU.add), reads=[r_Sf[h], r_alast, rdb], writes=[r_Sf[h]])
                    P.op("pool", lambda e, h=h: e.tensor_copy(out=S_b[:, h, :], in_=S_f[:, h, :]),
                         reads=[r_Sf[h]], writes=[r_Sb[h]])
                nb_, rnb = K.bank[7], K.rbank[7]
                if ret:
                    P.op("act", lambda e, ob=ob: e.copy(out=sqo[:], in_=ob[:]), reads=[rob], writes=[r_sqo])
                    _mm(P, nb_[:], meanb[:], sqo[:], True, True, [r_meanb, r_sqo], [rnb])
                    P.op("act", lambda e: e.copy(out=tmp[:], in_=nb_[:]), reads=[rnb], writes=[r_tmp])
                    P.op("dve", lambda e, ob=ob: e.tensor_tensor(out=cen[:], in0=ob[:], in1=tmp[:], op=ALU.subtract),
                         reads=[rob, r_tmp], writes=[r_cen])
                    P.op("act", lambda e: e.activation(out=sqo[:], in_=cen[:], func=AF.Square), reads=[r_cen], writes=[r_sqo])
                    osrc, r_osrc, nscale = cen[:], r_cen, 1.0
                    _mm(P, nb_[:], meanb[:], sqo[:], True, True, [r_meanb, r_sqo], [rnb])
                    nwcol = vec[:, 4 + h:5 + h]
                else:
                    P.op("act", lambda e, ob=ob: e.activation(out=sqo[:], in_=ob[:], func=AF.Square), reads=[rob], writes=[r_sqo])
                    osrc, r_osrc, nscale = ob[:], rob, 1.0 / 128
                    _mm(P, nb_[:], K.ones_bf[:], sqo[:], True, True, [K.r_const, r_sqo], [rnb])
                    nwcol = vec[:, 8:9]
                act_rstd(P, rstd[:], r_rstd, nb_[:], rnb, nscale, tmp[:], r_tmp)
                P.op("dve", lambda e, osrc=osrc, nwcol=nwcol: e.scalar_tensor_tensor(
                    out=cum[:], in0=osrc, scalar=nwcol, in1=rstd[:], op0=ALU.mult, op1=ALU.mult),
                    reads=[r_osrc, r_vec, r_rstd], writes=[r_cum])
                proj_fm(1536 + h * 128)
                P.op("act", lambda e: e.activation(out=sgt[:], in_=pb[:], func=AF.Silu), reads=[rpb], writes=[r_sgt])
                P.op("dve", lambda e, h=h, yb=yb: e.tensor_tensor(out=yTt[yb][:, h, :], in0=cum[:], in1=sgt[:], op=ALU.mult),
                     reads=[r_cum, r_sgt], writes=[r_yTt[yb]])
            dst = K.ybuf.rearrange("(c p) t -> p c t", p=128)[:, 0:4, tsl]
            P.dma("sp", dst, yTt[yb][:], ds[7 + yb], reads=[r_yTt[yb]], writes=[K.r_y[c][t] for c in range(4)])
        P.emit_block(final=final)


def ssd_phase(K, layer, hin, final=False):
    nc, P, T, NT, dr = K.nc, K.P, K.T, K.NT, K.dr
    j = layer // 2
    ds = K.dsem_pool
    with ExitStack() as es:
        def sb(name, shape, dt):
            return es.enter_context(nc.sbuf_tensor(f"s{layer}_{name}", shape, dt))
        w = sb("w", [128, KC, 1288], BF16); r_w = Res()
        gain = sb("gain", [128, KC], F32); r_gain = Res()
        rows = sb("rows", [128, 536], F32); r_rows = Res()
        cw = sb("cw", [128, 6, 5], F32); r_cw = Res()
        smask = sb("smask", [128, 128], F32); r_smask = Res()
        onesF = sb("onesF", [128, 128], F32); r_onesF = Res()
        negA = sb("negA", [128, 8], F32); r_negA = Res()
        ht = [sb(f"ht{i}", [128, KC, 512], F32) for i in range(2)]; r_ht = RL(2)
        uT = sb("uT", [128, KC, 512], BF16); r_uT = Res()
        sq = sb("sq", [128, KC, 512], BF16); r_sq = Res()
        rstd = sb("rstd", [128, 512], F32); r_rstd = Res()
        tmp = sb("tmp", [128, 512], F32); r_tmp = Res()
        xpad = sb("xpad", [128, 6, 515], F32); r_xpad = RL(6)
        acc = sb("acc", [128, 512], F32); r_acc = Res()
        xc = sb("xc", [128, 6, 512], BF16); r_xc = RL(6)
        vtok = sb("vtok", [128, 512], BF16); r_vtok = Res()
        btok = sb("btok", [128, 128], BF16); r_btok = Res()
        sz = sb("sz", [128, 512], F32); r_sz = Res()
        sm = [sb(f"sm{i}", [128, 8], F32) for i in range(8)]; r_sm = RL(8)
        sm16 = sb("sm16", [128, 16], F32); r_sm16 = Res()
        vp = sb("vp", [128, 512], BF16); r_vp = Res()
        vpp = sb("vpp", [128, 512], BF16); r_vpp = Res()
        LM = sb("LM", [128, 8, 128], F32); r_LM = Res()
        E = sb("E", [128, 8, 128], F32); r_E = Res()
        GM = sb("GM", [128, 2, 128], F32); r_GM = Res()
        PT = sb("PT", [128, 8, 128], BF16); r_PT = Res()
        o1 = sb("o1", [128, 512], F32); r_o1 = Res()
        o2 = sb("o2", [128, 512], F32); r_o2 = Res()
        ss = sb("ss", [128, 2], F32); r_ss = Res()
        ss2 = sb("ss2", [128, 2], F32); r_ss2 = Res()
        S_f = sb("S_f", [128, 512], F32); r_Sf = Res()
        S_b = sb("S_b", [128, 512], BF16); r_Sb = Res()
        ytok = sb("ytok", [128, 512], BF16); r_ytok = Res()
        yTt = [sb(f"yTt{i}", [128, 4, 512], BF16) for i in range(2)]; r_yTt = RL(2)

        P.dma("pool", w[:, :, 0:768], dr["w_ssd"][j][:, :, 0:768], K.dsem_q[0], writes=[r_w])
        P.dma("pool", w[:, :, 768:1288], dr["w_ssd"][j][:, :, 768:1288], K.dsem_q[0], writes=[r_w])
        P.dma("sp", gain[:], dr["norm_mix"][layer], ds[1], writes=[r_gain])
        P.dma("sp", rows[:], dr["ssd_rows"][j], ds[30], writes=[r_rows])
        P.dma("sp", cw[:], dr["ssd_conv"][j], ds[31], writes=[r_cw])
        P.dma("sp", smask[:], dr["smask"], ds[32], writes=[r_smask])
        P.op("pool", lambda e: e.memset(onesF[:], 1.0), writes=[r_onesF])
        P.op("pool", lambda e: e.memset(S_f[:], 0.0), writes=[r_Sf])
        P.op("pool", lambda e: e.memset(S_b[:], 0.0), writes=[r_Sb])
        P.op("pool", lambda e: e.memset(xpad[:], 0.0), writes=r_xpad)
        P.op("act", lambda e: e.activation(out=negA[:], in_=rows[:, 8:16], func=AF.Exp), reads=[r_rows], writes=[r_negA])
        P.op("dve", lambda e: e.tensor_scalar(out=negA[:], in0=negA[:], scalar1=-1.0, scalar2=None, op0=ALU.mult),
             reads=[r_negA], writes=[r_negA])

        pb, rpb = K.bank[6], K.rbank[6]
        tb = pb[:].bitcast(BF16)
        P.dma("sp", ht[0][:], hview(hin, 0), ds[2], reads=[K.r_h[c][0] for c in range(KC)], writes=[r_ht[0]])
        for t in range(NT):
            b = t % 2
            yb = t % 2
            tsl = slice(t * 512, (t + 1) * 512)
            if t + 1 < NT:
                P.dma("sp", ht[1 - b][:], hview(hin, t + 1), ds[2 + (1 - b)],
                      reads=[K.r_h[c][t + 1] for c in range(KC)], writes=[r_ht[1 - b]])
            rmsnorm_tile2(K, ht[b], r_ht[b], gain, r_gain, uT, r_uT, sq, r_sq, rstd, r_rstd, tmp, r_tmp, 7)
            for cc in range(6):
                c0 = 512 + cc * 128
                for kc in range(KC):
                    _mm(P, pb[:], w[:, kc, c0:c0 + 128], uT[:, kc, :], kc == 0, kc == KC - 1, [r_w, r_uT], [rpb])
                P.op("dve", lambda e, cc=cc: e.tensor_copy(out=xpad[:, cc, 3:515], in_=pb[:]), reads=[rpb], writes=[r_xpad[cc]])
                P.op("dve", lambda e, cc=cc: e.tensor_scalar(out=acc[:], in0=xpad[:, cc, 0:512], scalar1=cw[:, cc, 0:1],
                                                            scalar2=None, op0=ALU.mult),
                     reads=[r_xpad[cc], r_cw], writes=[r_acc])
                for k in range(1, 4):
                    P.op("dve", lambda e, cc=cc, k=k: e.scalar_tensor_tensor(
                        out=acc[:], in0=xpad[:, cc, k:k + 512], scalar=cw[:, cc, k:k + 1], in1=acc[:],
                        op0=ALU.mult, op1=ALU.add), reads=[r_xpad[cc], r_cw, r_acc], writes=[r_acc])
                P.op("act", lambda e, cc=cc: e.activation(out=xc[:, cc, :], in_=acc[:], func=AF.Silu, bias=cw[:, cc, 4:5]),
                     reads=[r_acc, r_cw], writes=[r_xc[cc]])
                P.op("dve", lambda e, cc=cc: e.tensor_copy(out=xpad[:, cc, 0:3], in_=xpad[:, cc, 512:515]),
                     reads=[r_xpad[cc]], writes=[r_xpad[cc]])
            for bl in range(4):
                bsl = slice(bl * 128, (bl + 1) * 128)
                for cc in range(4):
                    P.op("pe", lambda e, cc=cc, bsl=bsl: e.transpose(out=tb[:, cc * 128:(cc + 1) * 128], in_=xc[:, cc, bsl],
                                                                    identity=K.ident_bf[:]),
                         reads=[r_xc[cc], K.r_const], writes=[rpb])
                P.op("dve", lambda e: e.tensor_copy(out=vtok[:], in_=tb[:, 0:512]), reads=[rpb], writes=[r_vtok])
                P.op("pe", lambda e, bsl=bsl: e.transpose(out=tb[:, 0:128], in_=xc[:, 4, bsl], identity=K.ident_bf[:]),
                     reads=[r_xc[4], K.r_const], writes=[rpb])
                P.op("dve", lambda e: e.tensor_copy(out=btok[:], in_=tb[:, 0:128]), reads=[rpb], writes=[r_btok])
                for kc in range(KC):
                    _mm(P, pb[:], uT[:, kc, bsl], w[:, kc, 0:512], kc == 0, kc == KC - 1, [r_w, r_uT], [rpb])
                P.op("act", lambda e: e.activation(out=sz[:], in_=pb[:], func=AF.Silu), reads=[rpb], writes=[r_sz])
                for kc in range(KC):
                    _mm(P, pb[:, 0:8], uT[:, kc, bsl], w[:, kc, 1280:1288], kc == 0, kc == KC - 1, [r_w, r_uT], [rpb])
                x_, ax, mx, dt_, l_, cumt, wdec, ecum = sm
                P.op("dve", lambda e: e.tensor_tensor(out=x_[:], in0=pb[:, 0:8], in1=rows[:, 0:8], op=ALU.add),
                     reads=[rpb, r_rows], writes=[r_sm[0]])
                P.op("dve", lambda e: e.scalar_tensor_tensor(out=ax[:], in0=x_[:], scalar=-1.0, in1=x_[:], op0=ALU.mult,
                                                             op1=ALU.min), reads=[r_sm[0]], writes=[r_sm[1]])
                P.op("act", lambda e: e.activation(out=ax[:], in_=ax[:], func=AF.Exp), reads=[r_sm[1]], writes=[r_sm[1]])
                P.op("act", lambda e: e.activation(out=ax[:], in_=ax[:], func=AF.Ln, bias=1.0), reads=[r_sm[1]], writes=[r_sm[1]])
                P.op("dve", lambda e: e.scalar_tensor_tensor(out=dt_[:], in0=x_[:], scalar=0.0, in1=ax[:], op0=ALU.max,
                                                             op1=ALU.add), reads=[r_sm[0], r_sm[1]], writes=[r_sm[3]])
                P.op("dve", lambda e: e.tensor_tensor(out=l_[:], in0=dt_[:], in1=negA[:], op=ALU.mult),
                     reads=[r_sm[3], r_negA], writes=[r_sm[4]])
                cb, rcb = K.bank[2], K.rbank[2]
                _mm(P, cb[:, 256:264], K.cmask_f[:], l_[:], True, True, [K.r_const, r_sm[4]], [rcb])
                _mm(P, cb[:, 264:272], onesF[:], l_[:], True, True, [r_onesF, r_sm[4]], [rcb])
                P.op("dve", lambda e: e.tensor_copy(out=sm16[:], in_=cb[:, 256:272]), reads=[rcb], writes=[r_sm16])
                P.op("act", lambda e: e.activation(out=ecum[:], in_=sm16[:, 0:8], func=AF.Exp), reads=[r_sm16], writes=[r_sm[7]])
                P.op("dve", lambda e: e.tensor_tensor(out=wdec[:], in0=sm16[:, 8:16], in1=sm16[:, 0:8], op=ALU.subtract),
                     reads=[r_sm16], writes=[r_sm[6]])
                P.op("act", lambda e: e.activation(out=wdec[:], in_=wdec[:], func=AF.Exp), reads=[r_sm[6]], writes=[r_sm[6]])
                P.op("act", lambda e: e.activation(out=cumt[:], in_=sm16[:, 8:16], func=AF.Exp), reads=[r_sm16], writes=[r_sm[5]])
                v3 = vtok[:].rearrange("p (h d) -> p h d", h=8)
                P.op("dve", lambda e, v3=v3: e.tensor_tensor(out=vp[:].rearrange("p (h d) -> p h d", h=8), in0=v3,
                                                            in1=dt_[:].unsqueeze(2).to_broadcast([128, 8, 64]), op=ALU.mult),
                     reads=[r_vtok, r_sm[3]], writes=[r_vp])
                P.op("dve", lambda e: e.tensor_tensor(out=vpp[:].rearrange("p (h d) -> p h d", h=8),
                                                      in0=vp[:].rearrange("p (h d) -> p h d", h=8),
                                                      in1=wdec[:].unsqueeze(2).to_broadcast([128, 8, 64]), op=ALU.mult),
                     reads=[r_vp, r_sm[6]], writes=[r_vpp])
                P.op("dve", lambda e: e.tensor_tensor(out=LM[:], in0=smask[:].unsqueeze(1).to_broadcast([128, 8, 128]),
                                                      in1=l_[:].unsqueeze(2).to_broadcast([128, 8, 128]), op=ALU.mult),
                     reads=[r_smask, r_sm[4]], writes=[r_LM])
                for hh in range(2):
                    db, rdb = K.bank[hh], K.rbank[hh]
                    for h4 in range(4):
                        h = hh * 4 + h4
                        _mm(P, db[:, h4 * 128:(h4 + 1) * 128], LM[:, h, :], K.cmask_f[:], True, True, [r_LM, K.r_const], [rdb])
                    P.op("act", lambda e, hh=hh, db=db: e.activation(out=E[:, hh * 4:(hh + 1) * 4, :].rearrange("p h i -> p (h i)"),
                                                                   in_=db[:], func=AF.Exp), reads=[rdb], writes=[r_E])
                gbank = [(K.bank[2], K.rbank[2]), (K.bank[7], K.rbank[7])]
                for g in range(2):
                    gs = slice(g * 64, (g + 1) * 64)
                    gb_, rgb = gbank[g]
                    _mm(P, gb_[:, 0:128], xc[gs, 4, bsl], xc[gs, 5, bsl], True, True, [r_xc[4], r_xc[5]], [rgb])
                    P.op("dve", lambda e, g=g, gb_=gb_: e.tensor_tensor(out=GM[:, g, :], in0=gb_[:, 0:128], in1=K.cmask_f[:],
                                                                      op=ALU.mult), reads=[rgb, K.r_const], writes=[r_GM])
                for g in range(2):
                    P.op("dve", lambda e, g=g: e.tensor_tensor(out=PT[:, g * 4:(g + 1) * 4, :], in0=E[:, g * 4:(g + 1) * 4, :],
                                                              in1=GM[:, g:g + 1, :].to_broadcast([128, 4, 128]), op=ALU.mult),
                         reads=[r_E, r_GM], writes=[r_PT])
                ab, rab = K.bank[3], K.rbank[3]
                bb, rbb = K.bank[4], K.rbank[4]
                for h in range(8):
                    _mm(P, ab[:, h * 64:(h + 1) * 64], PT[:, h, :], vp[:, h * 64:(h + 1) * 64], True, True, [r_PT, r_vp], [rab])
                ibank = [(K.bank[4], K.rbank[4]), (K.bank[7], K.rbank[7])]
                for g in range(2):
                    gs = slice(g * 64, (g + 1) * 64)
                    ib_, rib = ibank[g]
                    _mm(P, ib_[:, g * 256:(g + 1) * 256], xc[gs, 5, bsl], S_b[gs, g * 256:(g + 1) * 256], True, True,
                        [r_xc[5], r_Sb], [rib])
                    P.op("dve", lambda e, g=g, ib_=ib_: e.tensor_tensor(
                        out=o1[:, g * 256:(g + 1) * 256].rearrange("p (h d) -> p h d", h=4),
                        in0=ib_[:, g * 256:(g + 1) * 256].rearrange("p (h d) -> p h d", h=4),
                        in1=ecum[:, g * 4:(g + 1) * 4].unsqueeze(2).to_broadcast([128, 4, 64]), op=ALU.mult),
                        reads=[rib, r_sm[7]], writes=[r_o1])
                P.op("dve", lambda e: e.tensor_tensor(out=o1[:], in0=o1[:], in1=ab[:], op=ALU.add),
                     reads=[r_o1, rab], writes=[r_o1])
                sbk, rsb = K.bank[5], K.rbank[5]
                _mm(P, sbk[:], btok[:], vpp[:], True, True, [r_btok, r_vpp], [rsb])
                for g in range(2):
                    gs = slice(g * 64, (g + 1) * 64)
                    sv = S_f[gs, g * 256:(g + 1) * 256].rearrange("p (h d) -> p h d", h=4)
                    P.op("dve", lambda e, g=g, gs=gs, sv=sv: e.tensor_tensor(
                        out=sv, in0=sv, in1=cumt[gs, g * 4:(g + 1) * 4].unsqueeze(2).to_broadcast([64, 4, 64]), op=ALU.mult),
                        reads=[r_Sf, r_sm[5]], writes=[r_Sf])
                    P.op("dve", lambda e, g=g, gs=gs: e.tensor_tensor(
                        out=S_f[gs, g * 256:(g + 1) * 256], in0=S_f[gs, g * 256:(g + 1) * 256],
                        in1=sbk[gs, g * 256:(g + 1) * 256], op=ALU.add), reads=[r_Sf, rsb], writes=[r_Sf])
                P.op("pool", lambda e: e.tensor_copy(out=S_b[:], in_=S_f[:]), reads=[r_Sf], writes=[r_Sb])
                P.op("dve", lambda e, v3=v3: e.tensor_tensor(out=o2[:].rearrange("p (h d) -> p h d", h=8), in0=v3,
                                                            in1=rows[:, 16:24].unsqueeze(2).to_broadcast([128, 8, 64]),
                                                            op=ALU.mult), reads=[r_vtok, r_rows], writes=[r_o2])
                P.op("dve", lambda e: e.tensor_tensor(out=o2[:], in0=o2[:], in1=o1[:], op=ALU.add),
                     reads=[r_o2, r_o1], writes=[r_o2])
                P.op("dve", lambda e: e.tensor_tensor(out=o2[:], in0=o2[:], in1=sz[:], op=ALU.mult),
                     reads=[r_o2, r_sz], writes=[r_o2])
                P.op("dve", lambda e: e.tensor_tensor(out=o1[:], in0=o2[:], in1=o2[:], op=ALU.mult),
                     reads=[r_o2, r_o1], writes=[r_o1])
                P.op("dve", lambda e: e.tensor_reduce(out=ss[:], in_=o1[:].rearrange("p (g d) -> p g d", g=2), axis=AX.X,
                                                      op=ALU.add), reads=[r_o1], writes=[r_ss])
                act_rstd(P, ss[:], r_ss, ss[:], r_ss, 1.0 / 256, ss2[:], r_ss2)
                P.op("dve", lambda e: e.tensor_tensor(out=o2[:].rearrange("p (g d) -> p g d", g=2),
                                                      in0=o2[:].rearrange("p (g d) -> p g d", g=2),
                                                      in1=ss[:].unsqueeze(2).to_broadcast([128, 2, 256]), op=ALU.mult),
                     reads=[r_o2, r_ss], writes=[r_o2])
                P.op("dve", lambda e: e.tensor_tensor(out=ytok[:], in0=o2[:], in1=rows[:, 24:536], op=ALU.mult),
                     reads=[r_o2, r_rows], writes=[r_ytok])
                for cc in range(4):
                    P.op("pe", lambda e, cc=cc: e.transpose(out=tb[:, cc * 128:(cc + 1) * 128],
                                                            in_=ytok[:, cc * 128:(cc + 1) * 128], identity=K.ident_bf[:]),
                         reads=[r_ytok, K.r_const], writes=[rpb])
                P.op("dve", lambda e, bsl=bsl, yb=yb: e.tensor_copy(out=yTt[yb][:, :, bsl],
                                                                   in_=tb[:, 0:512].rearrange("p (c s) -> p c s", c=4)),
                     reads=[rpb], writes=[r_yTt[yb]])
            dst = K.ybuf.rearrange("(c p) t -> p c t", p=128)[:, 4:8, tsl]
            P.dma("sp", dst, yTt[yb][:], ds[7 + yb], reads=[r_yTt[yb]], writes=[K.r_y[c][t] for c in range(4, 8)])
        P.emit_block(final=final)


ALL_PHASES = ("ret", "ssd", "hg", "fox", "ffn")


def _wl(W):
    return np.ascontiguousarray(W.reshape(W.shape[0], KC, 128, W.shape[2]).transpose(0, 2, 1, 3))


def _vec(v):
    return np.ascontiguousarray(v.reshape(v.shape[0], KC, 128).transpose(0, 2, 1))


def prepare_inputs(T, norm_mix, norm_ffn, ffn_w_gate, ffn_w_up, ffn_w_down, ab_w_in, ab_w_out, ret_gn_w, ssd_conv_w,
                   ssd_conv_b, ssd_dt_bias, ssd_a_log, ssd_d, ssd_norm_w, cd_w_in, cd_w_out, hg_lb_logits, hg_norm_w,
                   fox_f_bias, fox_q_norm_w, fox_k_norm_w):
    f32 = np.float32
    A = lambda a: np.asarray(a, dtype=f32)
    norm_mix, norm_ffn = A(norm_mix), A(norm_ffn)
    wg, wu, wd = A(ffn_w_gate), A(ffn_w_up), A(ffn_w_down)
    ab_in, ab_out, cd_in, cd_out = A(ab_w_in), A(ab_w_out), A(cd_w_in), A(cd_w_out)
    sh = {}
    sh["norm_mix"], sh["norm_ffn"] = _vec(norm_mix), _vec(norm_ffn)
    sh["wg"] = np.ascontiguousarray(wg.reshape(4, KC, 128, NJ, 128).transpose(0, 3, 2, 1, 4))
    sh["wu"] = np.ascontiguousarray(wu.reshape(4, KC, 128, NJ, 128).transpose(0, 3, 2, 1, 4))
    sh["wd"] = np.ascontiguousarray(wd.reshape(4, NJ, 128, KC, 128).transpose(0, 3, 2, 1, 4))
    w_out = np.stack([ab_out[0], cd_out[0], ab_out[1], cd_out[1]])
    sh["w_out"] = _wl(w_out)
    sh["w_fox"] = _wl(cd_in[:, :, 2048:3588])
    fv = np.zeros((2, 128, 3), f32)
    fv[:, :, 0] = A(fox_q_norm_w); fv[:, :, 1] = A(fox_k_norm_w); fv[:, 0:4, 2] = A(fox_f_bias)
    sh["fox_vec"] = fv
    sh["cmask"] = np.triu(np.ones((128, 128), f32))
    sh["smask"] = np.tril(np.ones((128, 128), f32), -1)
    sh["ident"] = np.eye(128, dtype=f32)
    sh["rmask"] = np.tile((np.arange(512) % 64 != 0).astype(f32), (128, 1))
    sh["w_hg"] = _wl(cd_in[:, :, 0:2048])
    hv = np.zeros((2, 128, 12), f32)
    lbl = A(hg_lb_logits)
    for jj in range(2):
        hv[jj, :, 0:4] = lbl[0].reshape(4, 128).T
        hv[jj, :, 4:8] = lbl[1].reshape(4, 128).T
        hv[jj, :, 8] = A(hg_norm_w)[jj]
    sh["hg_vec"] = hv

    def pad(Wx, swap):
        o = np.zeros((D, 512), f32)
        for h in range(4):
            blk = Wx[:, h * 64:(h + 1) * 64]
            if swap:
                blk = blk.reshape(D, 32, 2)[:, :, ::-1].reshape(D, 64)
            o[:, h * 128:h * 128 + 64] = blk
        return o
    wr = []
    for jj in range(2):
        Wq, Wk = ab_in[jj][:, 0:256], ab_in[jj][:, 256:512]
        wr.append(np.concatenate([pad(Wq, 0), pad(Wk, 0), ab_in[jj][:, 512:1024], ab_in[jj][:, 1024:1536],
                                  pad(Wq, 1), pad(Wk, 1)], 1))
    sh["w_ret"] = _wl(np.stack(wr))
    rv = np.zeros((2, 128, 12), f32)
    lg = np.log1p(-np.exp2(-5.0 - np.arange(4, dtype=np.float64)))
    rv[:, :, 0:4] = lg[None, None, :].astype(f32)
    rv[:, :, 4:8] = A(ret_gn_w).transpose(0, 2, 1)
    sh["ret_vec"] = rv
    freqs = (np.float32(10000.0) ** (-np.linspace(0.0, 1.0, 32, dtype=f32))).astype(f32)
    ang = (np.arange(T, dtype=f32)[:, None] * freqs[None, :]).astype(f32).astype(np.float64)
    rt = np.zeros((128, 2, T), f32)
    rt[0:64, 0, :] = np.repeat(np.cos(ang), 2, axis=1).T
    sn = np.repeat(np.sin(ang), 2, axis=1)
    sn[:, 0::2] *= -1
    rt[0:64, 1, :] = sn.T
    sh["rot_tab"] = rt
    sh["w_ssd"] = _wl(ab_in[:, :, 1536:2824])
    rows = np.zeros((2, 128, 536), f32)
    rows[:, :, 0:8] = A(ssd_dt_bias)[:, None, :]
    rows[:, :, 8:16] = A(ssd_a_log)[:, None, :]
    rows[:, :, 16:24] = A(ssd_d)[:, None, :]
    rows[:, :, 24:536] = A(ssd_norm_w)[:, None, :]
    sh["ssd_rows"] = rows
    cv = np.zeros((2, 128, 6, 5), f32)
    cwv, cbv = A(ssd_conv_w), A(ssd_conv_b)
    for jj in range(2):
        cv[jj, :, :, 0:4] = cwv[jj].T.reshape(6, 128, 4).transpose(1, 0, 2)
        cv[jj, :, :, 4] = cbv[jj].reshape(6, 128).T
    sh["ssd_conv"] = cv
    return sh


def kernel(x, **params):
    x = np.asarray(x, dtype=np.float32)
    B, T, _ = x.shape
    shared = prepare_inputs(T, **params)
    nc = build_program(T, [0, 1, 2, 3], phases=ALL_PHASES)
    in_maps = []
    for b in range(B):
        m = dict(shared)
        m["xT"] = np.ascontiguousarray(x[b].T)
        in_maps.append(m)
    res = run_bass_kernel_spmd(nc, in_maps, core_ids=list(range(B)))
    out = np.stack([np.ascontiguousarray(res.results[b]["out"].T) for b in range(B)], axis=0)
    return out.astype(np.float32)
```

```python
import math
import numpy as np
from contextlib import ExitStack
import concourse.bass as bass
import concourse.mybir as mybir
from concourse.bass_utils import run_bass_kernel_spmd

F32 = mybir.dt.float32
BF16 = mybir.dt.bfloat16
AF = mybir.ActivationFunctionType
ALU = mybir.AluOpType
AX = mybir.AxisListType

D = 1024
KC = 8
FF = 2816
NJ = 22
EPS = 1e-6
import os
NOSYNC_ENGINES = tuple(x for x in os.environ.get("KNOSYNC", "").split(",") if x)
ENGS = ["pe", "act", "dve", "pool", "sp"]
CENG = ["pe", "act", "dve", "pool"]


class Res:
    __slots__ = ("w", "rs")

    def __init__(self):
        self.w = None
        self.rs = []


def RL(n):
    return [Res() for _ in range(n)]


class Ev:
    __slots__ = ("sem", "val", "op", "blk")

    def __init__(self, blk, sem=None, val=None, op=None):
        self.blk, self.sem, self.val, self.op = blk, sem, val, op


class Op:
    __slots__ = ("eng", "fn", "waits", "marked", "ev", "incs")

    def __init__(self, eng, fn):
        self.eng, self.fn = eng, fn
        self.waits = []
        self.marked = False
        self.ev = None
        self.incs = None


class DSem:
    def __init__(self, sem):
        self.sem = sem
        self.count = 0


class Prog:
    def __init__(self, nc, es, same_engine_sync=True):
        self.nc = nc
        self.es = es
        self.ops = {e: [] for e in ENGS}
        self.csem = {e: es.enter_context(nc.semaphore("c_" + e)) for e in CENG}
        self.ccount = {e: 0 for e in CENG}
        self.bar = es.enter_context(nc.semaphore("bar"))
        self.nbar = 0
        self.same = same_engine_sync
        self.nosync = set(NOSYNC_ENGINES)
        self.dsems = []
        self.blk = 0
        self.blk_dsems = set()

    def dsem(self):
        d = DSem(self.es.enter_context(self.nc.semaphore(f"d{len(self.dsems)}")))
        self.dsems.append(d)
        return d

    def _deps(self, op, reads, writes):
        evs = []
        for r in reads:
            if r.w is not None:
                evs.append(r.w)
        for r in writes:
            if r.w is not None:
                evs.append(r.w)
            evs.extend(r.rs)
        for ev in evs:
            if ev.blk != self.blk:
                continue
            if ev.op is not None:
                p = ev.op
                if p.eng == op.eng and (op.eng == "pe" or op.eng in self.nosync):
                    continue
                p.marked = True
            op.waits.append(ev)

    def _commit(self, ev, reads, writes):
        for r in reads:
            r.rs.append(ev)
        for r in writes:
            r.w = ev
            r.rs = []

    def op(self, eng, fn, reads=(), writes=()):
        o = Op(eng, fn)
        self._deps(o, reads, writes)
        o.ev = Ev(self.blk, op=o)
        self.ops[eng].append(o)
        self._commit(o.ev, reads, writes)
        return o

    def dma(self, q, out, in_, ds, reads=(), writes=(), slow=False):
        if slow:
            o = Op(q, lambda e: e.dma_start(out=out, in_=in_, allow_slow_non_contiguous=True))
        else:
            o = Op(q, lambda e: e.dma_start(out=out, in_=in_))
        self._deps(o, reads, writes)
        ds.count += 16
        self.blk_dsems.add(ds)
        o.incs = (ds.sem, 16)
        o.ev = Ev(self.blk, sem=ds.sem, val=ds.count)
        self.ops[q].append(o)
        self._commit(o.ev, reads, writes)
        return o

    def emit_block(self, final=False):
        nc = self.nc
        finals = []
        for e in CENG:
            ops = [o for o in self.ops[e] if o.fn is not None]
            if ops:
                ops[-1].marked = True
            c = self.ccount[e]
            for o in self.ops[e]:
                if o.incs is None and o.marked:
                    c += 1
                    o.ev.sem, o.ev.val = self.csem[e], c
            self.ccount[e] = c
            if ops:
                finals.append((self.csem[e], c))
        for ds in self.blk_dsems:
            finals.append((ds.sem, ds.count))
        self.nbar += 1
        nbar = self.nbar
        engobj = {"pe": "tensor", "act": "scalar", "dve": "vector", "pool": "gpsimd", "sp": "sync"}
        with nc.Block() as block:
            for e in ENGS:
                ops = self.ops[e]

                def body(eng, ops=ops, e=e):
                    waited = {}
                    for o in ops:
                        for ev in o.waits:
                            k = id(ev.sem)
                            if waited.get(k, 0) < ev.val:
                                eng.wait_ge(ev.sem, ev.val)
                                waited[k] = ev.val
                        ins = o.fn(eng)
                        if o.incs is not None:
                            ins.then_inc(o.incs[0], o.incs[1])
                        elif o.marked:
                            ins.then_inc(self.csem[e], 1)
                    if e == "sp":
                        for (s, v) in finals:
                            eng.wait_ge(s, v)
                        eng.sem_inc(self.bar, 1)
                    if not (final and e != "sp"):
                        eng.wait_ge(self.bar, nbar)

                getattr(block, engobj[e])(body)
        self.ops = {e: [] for e in ENGS}
        self.blk += 1
        self.blk_dsems = set()


class KB:
    pass


def _mm(P, out, lhsT, rhs, start, stop, reads, writes, **kw):
    return P.op("pe", lambda e: e.matmul(out, lhsT, rhs, start=start, stop=stop, **kw), reads=reads, writes=writes)


def build_program(T, layers, phases=("mix", "ffn"), test_ybuf=False):
    nc = bass.Bass("TRN2", target_bir_lowering=False)
    NT = T // 512
    K = KB()
    K.nc, K.T, K.NT = nc, T, NT
    dr = {}

    def din(name, shape, dt=F32):
        dr[name] = nc.dram_tensor(name, list(shape), dt, kind="ExternalInput").ap()
        return dr[name]

    din("xT", [D, T])
    din("norm_mix", [4, 128, KC])
    din("norm_ffn", [4, 128, KC])
    din("wg", [4, NJ, 128, KC, 128])
    din("wu", [4, NJ, 128, KC, 128])
    din("wd", [4, KC, 128, NJ, 128])
    din("w_out", [4, 128, KC, D])
    din("w_fox", [2, 128, KC, 1540])
    din("fox_vec", [2, 128, 3])
    din("cmask", [128, 128])
    din("w_hg", [2, 128, KC, 2048])
    din("hg_vec", [2, 128, 12])
    din("w_ret", [2, 128, KC, 3072])
    din("ret_vec", [2, 128, 12])
    din("rmask", [128, 512])
    din("w_ssd", [2, 128, KC, 1288])
    din("ssd_rows", [2, 128, 536])
    din("ssd_conv", [2, 128, 6, 5])
    din("smask", [128, 128])
    din("rot_tab", [128, 2, T])
    din("ident", [128, 128])
    out = nc.dram_tensor("out", [D, T], F32, kind="ExternalOutput").ap()
    if test_ybuf == "out":
        ybuf = nc.dram_tensor("ybuf", [D, T], BF16, kind="ExternalOutput").ap()
    elif test_ybuf:
        ybuf = nc.dram_tensor("ybuf", [D, T], BF16, kind="ExternalInput").ap()
    else:
        ybuf = nc.dram_tensor("ybuf", [D, T], BF16).ap()
    K.dr, K.out, K.ybuf = dr, out, ybuf

    with ExitStack() as es:
        P = Prog(nc, es)
        K.P = P
        K.bank = [es.enter_context(nc.psum_tensor(f"bank{i}", [128, 512], F32)) for i in range(8)]
        K.rbank = RL(8)
        K.ones_bf = es.enter_context(nc.sbuf_tensor("ones_bf", [128, 128], BF16))
        K.r_const = Res()
        P.op("pool", lambda e: e.memset(K.ones_bf[:], 1.0), writes=[K.r_const])
        K.r_h = [RL(NT) for _ in range(KC)]
        K.r_y = [RL(NT) for _ in range(KC)]
        K.dsem_pool = [P.dsem() for _ in range(40)]
        K.dsem_q = [P.dsem() for _ in range(8)]
        K.ident_bf = es.enter_context(nc.sbuf_tensor("ident_bf", [128, 128], BF16))
        K.ident_f = es.enter_context(nc.sbuf_tensor("ident_f", [128, 128], F32))
        K.cmask_bf = es.enter_context(nc.sbuf_tensor("cmask_bf", [128, 128], BF16))
        K.cmask_f = es.enter_context(nc.sbuf_tensor("cmask_f", [128, 128], F32))
        P.dma("pool", K.ident_bf[:], dr["ident"], K.dsem_q[7], writes=[K.r_const])
        P.dma("sp", K.ident_f[:], dr["ident"], K.dsem_pool[37], writes=[K.r_const])
        P.dma("pool", K.cmask_bf[:], dr["cmask"], K.dsem_q[7], writes=[K.r_const])
        P.dma("sp", K.cmask_f[:], dr["cmask"], K.dsem_pool[39], writes=[K.r_const])
        K.fd = nc.dram_tensor("fd", [8, T], F32).ap()
        K.r_fd = Res()

        plan = []
        for li, layer in enumerate(layers):
            if layer % 2 == 0:
                for ph in ("ret", "ssd"):
                    if ph in phases:
                        plan.append((ph, li, layer))
            else:
                for ph in ("hg", "fox"):
                    if ph in phases:
                        plan.append((ph, li, layer))
            if "ffn" in phases:
                plan.append(("ffn", li, layer))
        for pi, (ph, li, layer) in enumerate(plan):
            hin = dr["xT"] if li == 0 else out
            fin = pi == len(plan) - 1
            if ph == "ret":
                hg_phase(K, layer, hin, "ret", final=fin)
            elif ph == "hg":
                hg_phase(K, layer, hin, "hg", final=fin)
            elif ph == "ssd":
                ssd_phase(K, layer, hin, final=fin)
            elif ph == "fox":
                fox_phase(K, layer, hin, final=fin)
            else:
                ffn_phase(K, layer, hin, final=fin)
    return nc


def rmsnorm_tile(K, es_bufs, h_t, r_h_t, gain, r_gain, u_out, r_u, sq, r_sq, rstd, r_rstd, bank_i):
    P = K.P
    bank, rb = K.bank[bank_i], K.rbank[bank_i]
    for c in range(KC):
        P.op("act", lambda e, c=c: e.activation(out=sq[:, c, :], in_=h_t[:, c, :], func=AF.Square),
             reads=[r_h_t], writes=[r_sq])
    for c in range(KC):
        _mm(P, bank[:], K.ones_bf[:], sq[:, c, :], c == 0, c == KC - 1, [K.r_const, r_sq], [rb])
    P.op("act", lambda e: e.activation(out=rstd[:], in_=bank[:], func=AF.Sqrt, bias=EPS, scale=1.0 / D),
         reads=[rb], writes=[r_rstd])
    P.op("dve", lambda e: e.reciprocal(out=rstd[:], in_=rstd[:]), reads=[r_rstd], writes=[r_rstd])
    for c in range(KC):
        P.op("dve", lambda e, c=c: e.scalar_tensor_tensor(out=u_out(c), in0=h_t[:, c, :], scalar=gain[:, c:c + 1],
                                                         in1=rstd[:], op0=ALU.mult, op1=ALU.mult),
             reads=[r_h_t, r_gain, r_rstd], writes=[r_u])


def ffn_phase(K, layer, hin, final=False):
    nc, P, T, NT, dr, out = K.nc, K.P, K.T, K.NT, K.dr, K.out
    TG = min(1024, T)
    NG = T // TG
    TPG = TG // 512
    ds = K.dsem_pool
    with ExitStack() as es:
        def sb(name, shape, dt):
            return es.enter_context(nc.sbuf_tensor(f"f{layer}_{name}", shape, dt))
        wout = sb("wout", [128, KC, D], BF16); r_wout = Res()
        gain = sb("gain", [128, KC], F32); r_gain = Res()
        uT = sb("uT", [128, KC, TG], BF16); r_uT = RL(TPG)
        aT = sb("aT", [128, NJ, TG], BF16); r_aT = [RL(TPG) for _ in range(NJ)]
        ht = [sb(f"ht{i}", [128, KC, 512], F32) for i in range(2)]; r_ht = RL(2)
        yt = [sb(f"yt{i}", [128, KC, 512], BF16) for i in range(2)]; r_yt = RL(2)
        sq = sb("sq", [128, KC, 512], BF16); r_sq = Res()
        rstd = sb("rstd", [128, 512], F32); r_rstd = Res()
        wgc = [sb(f"wgc{i}", [128, KC, 128], BF16) for i in range(2)]; r_wgc = RL(2)
        wuc = [sb(f"wuc{i}", [128, KC, 128], BF16) for i in range(2)]; r_wuc = RL(2)
        wdc = [sb(f"wdc{i}", [128, NJ, 128], BF16) for i in range(2)]; r_wdc = RL(2)
        sg = [sb(f"sg{i}", [128, 512], F32) for i in range(2)]; r_sg = RL(2)
        hs = [sb(f"hs{i}", [128, 512], F32) for i in range(4)]; r_hs = RL(4)

        P.dma("pool", wout[:], dr["w_out"][layer], K.dsem_q[0], writes=[r_wout])
        P.dma("sp", gain[:], dr["norm_ffn"][layer], ds[1], writes=[r_gain])

        def hview(ap, t):
            return ap.rearrange("(c p) t -> p c t", p=128)[:, :, t * 512:(t + 1) * 512]

        def load_tile(tg):
            b = tg % 2
            rh = [K.r_h[c][tg] for c in range(KC)]
            ry = [K.r_y[c][tg] for c in range(KC)]
            P.dma("sp", ht[b][:], hview(hin, tg), ds[2 + b], reads=rh, writes=[r_ht[b]])
            P.dma("sp", yt[b][:], hview(K.ybuf, tg), ds[4 + b], reads=ry, writes=[r_yt[b]])

        wcount = [0]

        for g in range(NG):
            load_tile(g * TPG)
            for tl in range(TPG):
                tg = g * TPG + tl
                b = tg % 2
                if tl + 1 < TPG:
                    load_tile(tg + 1)
                for dc in range(KC):
                    bi = dc % 2
                    for kc in range(KC):
                        _mm(P, K.bank[bi][:], wout[:, kc, dc * 128:(dc + 1) * 128], yt[b][:, kc, :], kc == 0, kc == KC - 1,
                            [r_wout, r_yt[b]], [K.rbank[bi]])
                    P.op("dve", lambda e, dc=dc, bi=bi, b=b: e.tensor_tensor(out=ht[b][:, dc, :], in0=ht[b][:, dc, :],
                                                                           in1=K.bank[bi][:], op=ALU.add),
                         reads=[K.rbank[bi], r_ht[b]], writes=[r_ht[b]])
                P.dma("sp", hview(out, tg), ht[b][:], ds[6 + b], reads=[r_ht[b]],
                      writes=[K.r_h[c][tg] for c in range(KC)])
                rmsnorm_tile(K, None, ht[b], r_ht[b], gain, r_gain,
                             lambda c, tl=tl: uT[:, c, tl * 512:(tl + 1) * 512], r_uT[tl], sq, r_sq, rstd, r_rstd, 2)
            def load_w2(j):
                b = wcount[0] % 2
                wcount[0] += 1
                P.dma("pool", wgc[b][:], dr["wg"][layer, j], K.dsem_q[1 + b], writes=[r_wgc[b]])
                P.dma("pool", wuc[b][:], dr["wu"][layer, j], K.dsem_q[3 + b], writes=[r_wuc[b]])
                return b
            nb = load_w2(0)
            for j in range(NJ):
                b = nb
                if j + 1 < NJ:
                    nb = load_w2(j + 1)
                for tl in range(TPG):
                    pg, pu = 3 + 2 * (tl % 2), 4 + 2 * (tl % 2)
                    sl = slice(tl * 512, (tl + 1) * 512)
                    for kc in range(KC):
                        _mm(P, K.bank[pg][:], wgc[b][:, kc, :], uT[:, kc, sl], kc == 0, kc == KC - 1,
                            [r_wgc[b], r_uT[tl]], [K.rbank[pg]])
                    for kc in range(KC):
                        _mm(P, K.bank[pu][:], wuc[b][:, kc, :], uT[:, kc, sl], kc == 0, kc == KC - 1,
                            [r_wuc[b], r_uT[tl]], [K.rbank[pu]])
                    s = tl % 2
                    P.op("act", lambda e, s=s, pg=pg: e.activation(out=sg[s][:], in_=K.bank[pg][:], func=AF.Silu),
                         reads=[K.rbank[pg]], writes=[r_sg[s]])
                    P.op("dve", lambda e, s=s, pu=pu, j=j, sl=sl: e.tensor_tensor(out=aT[:, j, sl], in0=sg[s][:],
                                                                                 in1=K.bank[pu][:], op=ALU.mult),
                         reads=[r_sg[s], K.rbank[pu]], writes=[r_aT[j][tl]])
            P.dma("pool", wdc[0][:], dr["wd"][layer, 0], K.dsem_q[5], writes=[r_wdc[0]])
            cnt = 0
            for dc in range(KC):
                b = dc % 2
                if dc + 1 < KC:
                    P.dma("pool", wdc[1 - b][:], dr["wd"][layer, dc + 1], K.dsem_q[5 + (1 - b)], writes=[r_wdc[1 - b]])
                for tl in range(TPG):
                    tg = g * TPG + tl
                    bi = cnt % 2
                    hb = cnt % 4
                    cnt += 1
                    src = out[dc * 128:(dc + 1) * 128, tg * 512:(tg + 1) * 512]
                    P.dma("sp", hs[hb][:], src, ds[14 + hb], reads=[K.r_h[dc][tg]], writes=[r_hs[hb]])
                    for j in range(NJ):
                        _mm(P, K.bank[bi][:], wdc[b][:, j, :], aT[:, j, tl * 512:(tl + 1) * 512], j == 0, j == NJ - 1,
                            [r_wdc[b], r_aT[j][tl]], [K.rbank[bi]])
                    P.op("dve", lambda e, hb=hb, bi=bi: e.tensor_tensor(out=hs[hb][:], in0=hs[hb][:], in1=K.bank[bi][:],
                                                                       op=ALU.add),
                         reads=[K.rbank[bi], r_hs[hb]], writes=[r_hs[hb]])
                    P.dma("sp", src, hs[hb][:], ds[18 + hb], reads=[r_hs[hb]], writes=[K.r_h[dc][tg]])
        P.emit_block(final=final)


def act_rstd(P, out, r_out, in_, r_in, scale, tmp, r_tmp):
    P.op("act", lambda e: e.activation(out=tmp, in_=in_, func=AF.Ln, bias=EPS, scale=scale), reads=[r_in], writes=[r_tmp])
    P.op("act", lambda e: e.activation(out=out, in_=tmp, func=AF.Exp, scale=-0.5), reads=[r_tmp], writes=[r_out])


def rmsnorm_tile2(K, h_t, r_h_t, gain, r_gain, uT, r_u, sq, r_sq, rstd, r_rstd, tmp, r_tmp, bank_i):
    P = K.P
    bank, rb = K.bank[bank_i], K.rbank[bank_i]
    for c in range(KC):
        P.op("pool", lambda e, c=c: e.tensor_tensor(out=sq[:, c, :], in0=h_t[:, c, :], in1=h_t[:, c, :], op=ALU.mult),
             reads=[r_h_t], writes=[r_sq])
    for c in range(KC):
        _mm(P, bank[:], K.ones_bf[:], sq[:, c, :], c == 0, c == KC - 1, [K.r_const, r_sq], [rb])
    act_rstd(P, rstd[:], r_rstd, bank[:], rb, 1.0 / D, tmp[:], r_tmp)
    for c in range(KC):
        P.op("dve", lambda e, c=c: e.scalar_tensor_tensor(out=uT[:, c, :], in0=h_t[:, c, :], scalar=gain[:, c:c + 1],
                                                         in1=rstd[:], op0=ALU.mult, op1=ALU.mult),
             reads=[r_h_t, r_gain, r_rstd], writes=[r_u])


def hview(ap, t):
    return ap.rearrange("(c p) t -> p c t", p=128)[:, :, t * 512:(t + 1) * 512]


def fox_phase(K, layer, hin, final=False):
    nc, P, T, NT, dr = K.nc, K.P, K.T, K.NT, K.dr
    j = layer // 2
    NB = T // 128
    ds = K.dsem_pool
    SCALE = 128 ** -0.5
    with ExitStack() as es:
        def sb(name, shape, dt):
            return es.enter_context(nc.sbuf_tensor(f"x{layer}_{name}", shape, dt))
        w = sb("w", [128, KC, 1540], BF16); r_w = Res()
        gain = sb("gain", [128, KC], F32); r_gain = Res()
        fvec = sb("fvec", [128, 3], F32); r_fvec = Res()
        KT = sb("KT", [128, 4, T], BF16); r_KT = [RL(NT) for _ in range(4)]
        VA = sb("VA", [128, NB, 4, 129], BF16); r_VA = RL(NB); r_VAone = Res()
        Ftok = sb("Ftok", [128, NB, 4], F32); r_Ftok = RL(NB)
        Rq = sb("Rq", [128, NB, 4], F32); r_Rq = RL(NB)
        Bq = sb("Bq", [128, NB, 4], F32); r_Bq = Res()
        ht = [sb(f"ht{i}", [128, KC, 512], F32) for i in range(2)]; r_ht = RL(2)
        uT = sb("uT", [128, KC, 512], BF16); r_uT = Res()
        sq = sb("sq", [128, KC, 512], BF16); r_sq = Res()
        rstd = sb("rstd", [128, 512], F32); r_rstd = Res()
        tmp = sb("tmp", [128, 512], F32); r_tmp = Res()
        qn = sb("qn", [128, 4, 512], BF16); r_qn = RL(4)
        sqh = sb("sqh", [128, 512], BF16); r_sqh = Res()
        rq = sb("rq", [128, 512], F32); r_rq = Res()
        fx = [sb(f"fx{i}", [4, 512], F32) for i in range(4)]; r_fx = RL(4)
        Fc = [sb(f"Fc{i}", [4, 512], F32) for i in range(2)]; r_Fc = RL(2)
        onesf = sb("onesf", [4, 512], F32); r_onesf = Res()
        PT = [sb(f"PT{i}", [128, 512], BF16) for i in range(3)]; r_PT = RL(3)
        ytok = sb("ytok", [128, 4, 512], BF16); r_ytok = RL(4)
        Rt = sb("Rt", [128, 4], F32); r_Rt = Res()
        Boff = sb("Boff", [128, NB, 4], F32); r_Boff = Res()
        Bin = sb("Bin", [128, 4, 4, 4], F32); r_Bin = Res()
        cfac = sb("cfac", [128, 4, 4], F32); r_cfac = Res()
        otmp = sb("otmp", [128, 132], F32); r_otmp = Res()
        nslot = [0]
        rec = sb("rec", [128, 4], F32); r_rec = RL(4)
        yTt = [sb(f"yTt{i}", [128, 4, 512], BF16) for i in range(2)]; r_yTt = RL(2)

        P.dma("pool", w[:, :, 0:768], dr["w_fox"][j][:, :, 0:768], K.dsem_q[0], writes=[r_w])
        P.dma("pool", w[:, :, 768:1540], dr["w_fox"][j][:, :, 768:1540], K.dsem_q[0], writes=[r_w])
        P.dma("sp", gain[:], dr["norm_mix"][layer], ds[1], writes=[r_gain])
        P.dma("sp", fvec[:], dr["fox_vec"][j], ds[30], writes=[r_fvec])
        P.op("pool", lambda e: e.memset(onesf[:], 1.0), writes=[r_onesf])
        P.op("pool", lambda e: e.memset(VA[:, :, :, 128:129], 1.0), writes=[r_VAone])

        P.dma("sp", ht[0][:], hview(hin, 0), ds[2], reads=[K.r_h[c][0] for c in range(KC)], writes=[r_ht[0]])
        npt = 0
        for t in range(NT):
            b = t % 2
            if t + 1 < NT:
                P.dma("sp", ht[1 - b][:], hview(hin, t + 1), ds[2 + (1 - b)],
                      reads=[K.r_h[c][t + 1] for c in range(KC)], writes=[r_ht[1 - b]])
            rmsnorm_tile2(K, ht[b], r_ht[b], gain, r_gain, uT, r_uT, sq, r_sq, rstd, r_rstd, tmp, r_tmp, 7)
            tsl = slice(t * 512, (t + 1) * 512)
            pb, rpb = K.bank[6], K.rbank[6]
            for kc in range(KC):
                _mm(P, pb[0:4, :], w[:, kc, 1536:1540], uT[:, kc, :], kc == 0, kc == KC - 1, [r_w, r_uT], [rpb])
            P.op("dve", lambda e: e.tensor_scalar(out=fx[0][:], in0=pb[0:4, :], scalar1=fvec[0:4, 2:3], scalar2=None,
                                                  op0=ALU.add), reads=[rpb, r_fvec], writes=[r_fx[0]])
            P.op("dve", lambda e: e.scalar_tensor_tensor(out=fx[1][:], in0=fx[0][:], scalar=-1.0, in1=fx[0][:],
                                                         op0=ALU.mult, op1=ALU.min),
                 reads=[r_fx[0]], writes=[r_fx[1]])
            P.op("act", lambda e: e.activation(out=fx[1][:], in_=fx[1][:], func=AF.Exp),
                 reads=[r_fx[1]], writes=[r_fx[1]])
            P.op("act", lambda e: e.activation(out=fx[1][:], in_=fx[1][:], func=AF.Ln, bias=1.0),
                 reads=[r_fx[1]], writes=[r_fx[1]])
            P.op("dve", lambda e: e.tensor_scalar_min(out=fx[2][:], in0=fx[0][:], scalar1=0.0),
                 reads=[r_fx[0]], writes=[r_fx[2]])
            P.op("dve", lambda e: e.tensor_sub(out=fx[3][:], in0=fx[2][:], in1=fx[1][:]),
                 reads=[r_fx[1], r_fx[2]], writes=[r_fx[3]])
            init = 0.0 if t == 0 else Fc[1 - b][:, 511:512]
            P.op("dve", lambda e, b=b, init=init: e.tensor_tensor_scan(out=Fc[b][:], data0=onesf[:], data1=fx[3][:],
                                                                      initial=init, op0=ALU.mult, op1=ALU.add),
                 reads=[r_onesf, r_fx[3], r_Fc[1 - b]], writes=[r_Fc[b]])
            P.dma("sp", K.fd[0:4, tsl], Fc[b][:], ds[4], reads=[r_Fc[b]], writes=[K.r_fd])
            for bl in range(4):
                blk = t * 4 + bl
                c0 = blk * 128
                P.dma("sp", Ftok[:, blk, :], K.fd[0:4, c0:c0 + 128].rearrange("h s -> s h"), ds[12 + bl],
                      reads=[K.r_fd], writes=[r_Ftok[blk]], slow=True)
                P.dma("sp", Rq[:, blk, :], K.fd[0:4, c0 + 64:c0 + 65].rearrange("h o -> o h").broadcast_to([128, 4]),
                      ds[16 + bl], reads=[K.r_fd], writes=[r_Rq[blk]], slow=True)
            for h in range(4):
                for which in range(2):
                    c0 = which * 512 + h * 128
                    for kc in range(KC):
                        _mm(P, pb[:], w[:, kc, c0:c0 + 128], uT[:, kc, :], kc == 0, kc == KC - 1, [r_w, r_uT], [rpb])
                    P.op("act", lambda e: e.activation(out=sqh[:], in_=pb[:], func=AF.Square),
                         reads=[rpb], writes=[r_sqh])
                    nb_, rnb = K.bank[7], K.rbank[7]
                    _mm(P, nb_[:], K.ones_bf[:], sqh[:], True, True, [K.r_const, r_sqh], [rnb])
                    act_rstd(P, rq[:], r_rq, nb_[:], rnb, 1.0 / 128, tmp[:], r_tmp)
                    if which == 0:
                        dst, rd = qn[:, h, :], [r_qn[h]]
                    else:
                        dst, rd = KT[:, h, tsl], [r_KT[h][t]]
                    P.op("dve", lambda e, dst=dst, which=which: e.scalar_tensor_tensor(
                        out=dst, in0=pb[:], scalar=fvec[:, which:which + 1], in1=rq[:], op0=ALU.mult, op1=ALU.mult),
                        reads=[rpb, r_fvec, r_rq], writes=rd)
            for bl in range(4):
                blk = t * 4 + bl
                for kc in range(KC):
                    _mm(P, pb[:], uT[:, kc, bl * 128:(bl + 1) * 128], w[:, kc, 1024:1536], kc == 0, kc == KC - 1,
                        [r_w, r_uT], [rpb])
                P.op("dve", lambda e, blk=blk: e.tensor_copy(out=VA[:, blk, :, 0:128],
                                                            in_=pb[:].rearrange("p (h v) -> p h v", h=4)),
                     reads=[rpb, r_VAone], writes=[r_VA[blk]])
            yb = t % 2
            t4 = t * 4
            if t > 0:
                P.dma("sp", Rt[:], K.fd[0:4, t * 512:t * 512 + 1].rearrange("h o -> o h").broadcast_to([128, 4]),
                      ds[20], reads=[K.r_fd], writes=[r_Rt], slow=True)
                P.op("dve", lambda e, t4=t4: e.tensor_tensor(
                    out=Boff[:, 0:t4, :], in0=Rt[:].unsqueeze(1).to_broadcast([128, t4, 4]), in1=Ftok[:, 0:t4, :],
                    op=ALU.subtract), reads=[r_Rt] + r_Ftok[0:t4], writes=[r_Boff])
                P.op("dve", lambda e, t4=t4: e.tensor_tensor(
                    out=cfac[:], in0=Rq[:, t4:t4 + 4, :], in1=Rt[:].unsqueeze(1).to_broadcast([128, 4, 4]),
                    op=ALU.subtract), reads=[r_Rt] + r_Rq[t4:t4 + 4], writes=[r_cfac])
                P.op("act", lambda e: e.activation(out=cfac[:], in_=cfac[:], func=AF.Exp), reads=[r_cfac], writes=[r_cfac])
            P.op("dve", lambda e, t4=t4: e.tensor_tensor(
                out=Bin[:], in0=Rq[:, t4:t4 + 4, :].unsqueeze(1).to_broadcast([128, 4, 4, 4]),
                in1=Ftok[:, t4:t4 + 4, :].unsqueeze(2).to_broadcast([128, 4, 4, 4]), op=ALU.subtract),
                reads=r_Rq[t4:t4 + 4] + r_Ftok[t4:t4 + 4], writes=[r_Bin])

            units = []
            for h in range(4):
                for kb in range(t4):
                    units.append(("off", h, kb))
                for kl in range(4):
                    units.append(("in", h, kl))
            started = {}

            def oreg(h, r):
                bi = (2 if h % 2 == 0 else 5) + r // 3
                c0 = (r % 3) * 129
                return bi, K.bank[bi][:, c0:c0 + 129], K.rbank[bi]

            def pv(h, r, lhsT, rhs, reads):
                bi, reg, rb = oreg(h, r)
                first = not started.get((h, bi), False)
                started[(h, bi)] = True
                _mm(P, reg, lhsT, rhs, first, False, reads, [rb], skip_group_check=True)

            def emit_scores(u, slot):
                kind, h, kk = u
                sbk, rsb = K.bank[slot % 2], K.rbank[slot % 2]
                if kind == "off":
                    _mm(P, sbk[:], KT[:, h, kk * 128:(kk + 1) * 128], qn[:, h, :], True, True,
                        [r_KT[h][kk // 4], r_qn[h]], [rsb])
                else:
                    kb = t4 + kk
                    n = (4 - kk) * 128
                    _mm(P, sbk[:, 0:n], KT[:, h, kb * 128:(kb + 1) * 128], qn[:, h, kk * 128:512], True, True,
                        [r_KT[h][t], r_qn[h]], [rsb])

            def emit_exp(u, slot):
                kind, h, kk = u
                sbk, rsb = K.bank[slot % 2], K.rbank[slot % 2]
                pt, rpt = PT[slot % 3], r_PT[slot % 3]
                if kind == "off":
                    P.op("act", lambda e: e.activation(out=pt[:], in_=sbk[:], func=AF.Exp, bias=Boff[:, kk, h:h + 1],
                                                       scale=SCALE), reads=[rsb, r_Boff], writes=[rpt])
                else:
                    for ql in range(kk, 4):
                        i = ql - kk
                        P.op("act", lambda e, i=i, ql=ql: e.activation(
                            out=pt[:, i * 128:(i + 1) * 128], in_=sbk[:, i * 128:(i + 1) * 128], func=AF.Exp,
                            bias=Bin[:, kk, ql, h:h + 1], scale=SCALE), reads=[rsb, r_Bin], writes=[rpt])
                    P.op("pool", lambda e: e.tensor_tensor(out=pt[:, 0:128], in0=pt[:, 0:128], in1=K.cmask_bf[:], op=ALU.mult),
                         reads=[rpt, K.r_const], writes=[rpt])

            def emit_pv(u, slot):
                kind, h, kk = u
                pt, rpt = PT[slot % 3], r_PT[slot % 3]
                if kind == "off":
                    for ql in range(4):
                        pv(h, ql, pt[:, ql * 128:(ql + 1) * 128], VA[:, kk, h, :], [rpt, r_VA[kk], r_VAone])
                else:
                    kb = t4 + kk
                    for ql in range(kk, 4):
                        i = ql - kk
                        pv(h, 4 + ql, pt[:, i * 128:(i + 1) * 128], VA[:, kb, h, :], [rpt, r_VA[kb], r_VAone])
                    if kk == 3:
                        finish_head(h)

            def finish_head(h):
                for ql in range(4):
                    _, oin, rin = oreg(h, 4 + ql)
                    if t > 0:
                        _, oof, rof = oreg(h, ql)
                        P.op("dve", lambda e, oof=oof, ql=ql: e.tensor_scalar(
                            out=otmp[:, 0:129], in0=oof, scalar1=cfac[:, ql, h:h + 1], scalar2=None, op0=ALU.mult),
                            reads=[rof, r_cfac], writes=[r_otmp])
                        P.op("dve", lambda e, oin=oin: e.tensor_tensor(out=otmp[:, 0:129], in0=otmp[:, 0:129], in1=oin, op=ALU.add),
                             reads=[rin, r_otmp], writes=[r_otmp])
                        src, rsrc = otmp[:, 0:129], [r_otmp]
                    else:
                        src, rsrc = oin, [rin]
                    P.op("dve", lambda e, src=src: e.reciprocal(out=rec[:, h:h + 1], in_=src[:, 128:129]),
                         reads=rsrc, writes=[r_rec[h]])
                    P.op("dve", lambda e, src=src, ql=ql: e.tensor_scalar(
                        out=ytok[:, ql, h * 128:(h + 1) * 128], in0=src[:, 0:128], scalar1=rec[:, h:h + 1], scalar2=None,
                        op0=ALU.mult), reads=rsrc + [r_rec[h]], writes=[r_ytok[ql]])

            prev = None
            for u in units:
                slot = nslot[0]
                nslot[0] += 1
                emit_scores(u, slot)
                if prev is not None:
                    emit_pv(*prev)
                emit_exp(u, slot)
                prev = (u, slot)
            emit_pv(*prev)
            tb = pb[:].bitcast(BF16)
            for ql in range(4):
                for h in range(4):
                    P.op("pe", lambda e, h=h, ql=ql: e.transpose(out=tb[:, h * 128:(h + 1) * 128],
                                                                in_=ytok[:, ql, h * 128:(h + 1) * 128], identity=K.ident_bf[:]),
                         reads=[r_ytok[ql], K.r_const], writes=[rpb])
                P.op("dve", lambda e, ql=ql, yb=yb: e.tensor_copy(
                    out=yTt[yb][:, :, ql * 128:(ql + 1) * 128], in_=tb[:, 0:512].rearrange("p (h s) -> p h s", h=4)),
                    reads=[rpb], writes=[r_yTt[yb]])
            dst = K.ybuf.rearrange("(c p) t -> p c t", p=128)[:, 4:8, tsl]
            P.dma("sp", dst, yTt[yb][:], ds[7 + yb], reads=[r_yTt[yb]], writes=[K.r_y[c][t] for c in range(4, 8)])
        P.emit_block(final=final)


def hg_phase(K, layer, hin, kind, final=False):
    nc, P, T, NT, dr = K.nc, K.P, K.T, K.NT, K.dr
    j = layer // 2
    ds = K.dsem_pool
    ret = kind == "ret"
    NCOL = 3072 if ret else 2048
    wname = "w_ret" if ret else "w_hg"
    with ExitStack() as es:
        def sb(name, shape, dt):
            return es.enter_context(nc.sbuf_tensor(f"g{layer}_{name}", shape, dt))
        w = sb("w", [128, KC, NCOL], BF16); r_w = Res()
        gain = sb("gain", [128, KC], F32); r_gain = Res()
        vec = sb("vec", [128, 12], F32); r_vec = Res()
        ht = [sb(f"ht{i}", [128, KC, 512], F32) for i in range(2)]; r_ht = RL(2)
        uT = sb("uT", [128, KC, 512], BF16); r_uT = Res()
        sq = sb("sq", [128, KC, 512], BF16); r_sq = Res()
        rstd = sb("rstd", [128, 512], F32); r_rstd = Res()
        tmp = sb("tmp", [128, 512], F32); r_tmp = Res()
        rmask = sb("rmask", [128, 512], F32); r_rmask = Res()
        qf = sb("qf", [128, 512], F32); r_qf = Res()
        kf = sb("kf", [128, 512], F32); r_kf = Res()
        lf = sb("lf", [128, 512], F32); r_lf = Res()
        cum = sb("cum", [128, 512], F32); r_cum = Res()
        e1 = sb("e1", [128, 512], F32); r_e1 = Res()
        ex = [sb(f"ex{i}", [128, 512], F32) for i in range(2)]; r_ex = RL(2)
        qh = sb("qh", [128, 512], BF16); r_qh = Res()
        kh = sb("kh", [128, 512], BF16); r_kh = Res()
        qi = sb("qi", [128, 512], BF16); r_qi = Res()
        ko = sb("ko", [128, 512], BF16); r_ko = Res()
        alast = sb("alast", [128, 8], F32); r_alast = Res()
        vtok = sb("vtok", [128, 4, 512], BF16); r_vtok = Res()
        kotok = sb("kotok", [128, 4, 128], BF16); r_kotok = Res()
        PT = [sb(f"PT{i}", [128, 64], BF16) for i in range(2)]; r_PT = RL(2)
        S_f = sb("S_f", [128, 4, 128], F32); r_Sf = RL(4)
        S_b = sb("S_b", [128, 4, 128], BF16); r_Sb = RL(4)
        sgt = sb("sgt", [128, 512], F32); r_sgt = Res()
        sqo = sb("sqo", [128, 512], BF16); r_sqo = Res()
        cen = sb("cen", [128, 512], F32); r_cen = Res()
        yTt = [sb(f"yTt{i}", [128, 4, 512], BF16) for i in range(2)]; r_yTt = RL(2)
        if ret:
            rot = [sb(f"rot{i}", [128, 2, 512], F32) for i in range(2)]; r_rot = RL(2)
            lfc = sb("lfc", [128, 4, 512], F32); r_lfc = Res()
            onesf = sb("onesf", [128, 512], F32); r_onesf = Res()
            meanb = sb("meanb", [128, 128], BF16); r_meanb = Res()

        P.dma("pool", w[:, :, 0:1024], dr[wname][j][:, :, 0:1024], K.dsem_q[0], writes=[r_w])
        P.dma("pool", w[:, :, 1024:2048], dr[wname][j][:, :, 1024:2048], K.dsem_q[0], writes=[r_w])
        if ret:
            P.dma("pool", w[:, :, 2048:3072], dr[wname][j][:, :, 2048:3072], K.dsem_q[0], writes=[r_w])
        P.dma("sp", gain[:], dr["norm_mix"][layer], ds[1], writes=[r_gain])
        if ret:
            P.dma("sp", vec[:], dr["ret_vec"][j], ds[30], writes=[r_vec])
        P.dma("sp", rmask[:], dr["rmask"], ds[31], writes=[r_rmask])
        if not ret:
            raw = sb("raw", [128, 12], F32); r_raw = Res()
            P.dma("sp", raw[:], dr["hg_vec"][j], ds[32], writes=[r_raw])
            if j == 0:
                P.op("dve", lambda e: e.tensor_scalar(out=vec[:, 0:4], in0=raw[:, 0:4], scalar1=0.0, scalar2=None,
                                                      op0=ALU.mult), reads=[r_raw, r_vec], writes=[r_vec])
            else:
                P.op("dve", lambda e: e.tensor_tensor(out=vec[:, 0:4], in0=raw[:, 4:8], in1=raw[:, 0:4], op=ALU.subtract),
                     reads=[r_raw, r_vec], writes=[r_vec])
                P.op("act", lambda e: e.activation(out=vec[:, 0:4], in_=vec[:, 0:4], func=AF.Sigmoid),
                     reads=[r_vec], writes=[r_vec])
            P.op("dve", lambda e: e.tensor_scalar(out=vec[:, 4:8], in0=vec[:, 0:4], scalar1=-1.0, scalar2=1.0,
                                                  op0=ALU.mult, op1=ALU.add), reads=[r_vec], writes=[r_vec])
            P.op("dve", lambda e: e.tensor_copy(out=vec[:, 8:9], in_=raw[:, 8:9]), reads=[r_raw, r_vec], writes=[r_vec])
        P.op("pool", lambda e: e.memset(S_f[:], 0.0), writes=r_Sf)
        P.op("pool", lambda e: e.memset(S_b[:], 0.0), writes=r_Sb)
        if ret:
            P.op("pool", lambda e: e.memset(onesf[:], 1.0), writes=[r_onesf])
            P.op("pool", lambda e: e.memset(meanb[:], 1.0 / 128), writes=[r_meanb])
            for h in range(4):
                P.op("dve", lambda e, h=h: e.tensor_scalar(out=lfc[:, h, :], in0=onesf[:], scalar1=vec[:, h:h + 1],
                                                          scalar2=None, op0=ALU.mult),
                     reads=[r_onesf, r_vec], writes=[r_lfc])

        pb, rpb = K.bank[6], K.rbank[6]
        tb = pb[:].bitcast(BF16)

        def proj_fm(c0):
            for kc in range(KC):
                _mm(P, pb[:], w[:, kc, c0:c0 + 128], uT[:, kc, :], kc == 0, kc == KC - 1, [r_w, r_uT], [rpb])

        P.dma("sp", ht[0][:], hview(hin, 0), ds[2], reads=[K.r_h[c][0] for c in range(KC)], writes=[r_ht[0]])
        nchunk = 0
        for t in range(NT):
            b = t % 2
            tsl = slice(t * 512, (t + 1) * 512)
            if t + 1 < NT:
                P.dma("sp", ht[1 - b][:], hview(hin, t + 1), ds[2 + (1 - b)],
                      reads=[K.r_h[c][t + 1] for c in range(KC)], writes=[r_ht[1 - b]])
            if ret:
                P.dma("sp", rot[b][:], dr["rot_tab"][:, :, tsl], ds[4 + b], writes=[r_rot[b]])
            rmsnorm_tile2(K, ht[b], r_ht[b], gain, r_gain, uT, r_uT, sq, r_sq, rstd, r_rstd, tmp, r_tmp, 7)
            for bl in range(4):
                for kc in range(KC):
                    _mm(P, pb[:], uT[:, kc, bl * 128:(bl + 1) * 128], w[:, kc, 1024:1536], kc == 0, kc == KC - 1,
                        [r_w, r_uT], [rpb])
                P.op("dve", lambda e, bl=bl: e.tensor_copy(out=vtok[:, bl, :], in_=pb[:]), reads=[rpb], writes=[r_vtok])
            yb = t % 2
            for h in range(4):
                if ret:
                    for which, dstf, rdst, sc in ((0, qf, r_qf, 1.0), (1, kf, r_kf, 0.125)):
                        proj_fm(which * 512 + h * 128)
                        P.op("dve", lambda e, b=b: e.tensor_tensor(out=e1[:], in0=pb[:], in1=rot[b][:, 0, :], op=ALU.mult),
                             reads=[rpb, r_rot[b]], writes=[r_e1])
                        proj_fm(2048 + which * 512 + h * 128)
                        P.op("dve", lambda e, b=b: e.tensor_tensor(out=cum[:], in0=pb[:], in1=rot[b][:, 1, :], op=ALU.mult),
                             reads=[rpb, r_rot[b]], writes=[r_cum])
                        P.op("dve", lambda e, dstf=dstf, sc=sc: e.scalar_tensor_tensor(
                            out=dstf[:], in0=e1[:], scalar=sc, in1=cum[:], op0=ALU.mult, op1=ALU.add),
                            reads=[r_e1, r_cum], writes=[rdst])
                        if sc != 1.0:
                            P.op("dve", lambda e, dstf=dstf, sc=sc: e.scalar_tensor_tensor(
                                out=dstf[:], in0=cum[:], scalar=sc - 1.0, in1=dstf[:], op0=ALU.mult, op1=ALU.add),
                                reads=[r_cum, rdst], writes=[rdst])
                    lft, r_lft = lfc[:, h, :], r_lfc
                else:
                    proj_fm(h * 128)
                    P.op("dve", lambda e: e.tensor_copy(out=qf[:], in_=pb[:]), reads=[rpb], writes=[r_qf])
                    proj_fm(512 + h * 128)
                    P.op("act", lambda e: e.activation(out=lf[:], in_=pb[:], func=AF.Sigmoid), reads=[rpb], writes=[r_lf])
                    P.op("dve", lambda e, h=h: e.tensor_scalar(out=lf[:], in0=lf[:], scalar1=vec[:, 4 + h:5 + h],
                                                              scalar2=vec[:, h:h + 1], op0=ALU.mult, op1=ALU.add),
                         reads=[r_lf, r_vec], writes=[r_lf])
                    P.op("dve", lambda e: e.tensor_scalar(out=kf[:], in0=lf[:], scalar1=-1.0, scalar2=1.0,
                                                          op0=ALU.mult, op1=ALU.add), reads=[r_lf], writes=[r_kf])
                    P.op("act", lambda e: e.activation(out=lf[:], in_=lf[:], func=AF.Ln), reads=[r_lf], writes=[r_lf])
                    lft, r_lft = lf[:], r_lf
                P.op("dve", lambda e, lft=lft: e.tensor_tensor_scan(out=cum[:], data0=rmask[:], data1=lft, initial=0.0,
                                                                    op0=ALU.mult, op1=ALU.add),
                     reads=[r_rmask, r_lft], writes=[r_cum])
                cum3 = cum[:].rearrange("p (c s) -> p c s", s=64)
                P.op("dve", lambda e: e.tensor_tensor(out=e1[:].rearrange("p (c s) -> p c s", s=64), in0=cum3,
                                                      in1=cum3[:, :, 31:32].to_broadcast([128, 8, 64]), op=ALU.subtract),
                     reads=[r_cum], writes=[r_e1])
                P.op("act", lambda e: e.activation(out=ex[0][:], in_=e1[:], func=AF.Exp), reads=[r_e1], writes=[r_ex[0]])
                P.op("dve", lambda e: e.tensor_tensor(out=qh[:], in0=qf[:], in1=ex[0][:], op=ALU.mult),
                     reads=[r_qf, r_ex[0]], writes=[r_qh])
                P.op("act", lambda e: e.activation(out=ex[1][:], in_=e1[:], func=AF.Exp, scale=-1.0),
                     reads=[r_e1], writes=[r_ex[1]])
                P.op("dve", lambda e: e.tensor_tensor(out=kh[:], in0=kf[:], in1=ex[1][:], op=ALU.mult),
                     reads=[r_kf, r_ex[1]], writes=[r_kh])
                P.op("act", lambda e: e.activation(out=ex[0][:], in_=cum[:], func=AF.Exp), reads=[r_cum], writes=[r_ex[0]])
                P.op("dve", lambda e: e.tensor_tensor(out=qi[:], in0=qf[:], in1=ex[0][:], op=ALU.mult),
                     reads=[r_qf, r_ex[0]], writes=[r_qi])
                P.op("dve", lambda e: e.tensor_tensor(out=e1[:].rearrange("p (c s) -> p c s", s=64), in0=cum3,
                                                      in1=cum3[:, :, 63:64].to_broadcast([128, 8, 64]), op=ALU.subtract),
                     reads=[r_cum], writes=[r_e1])
                P.op("act", lambda e: e.activation(out=ex[1][:], in_=e1[:], func=AF.Exp, scale=-1.0),
                     reads=[r_e1], writes=[r_ex[1]])
                P.op("dve", lambda e: e.tensor_tensor(out=ko[:], in0=kf[:], in1=ex[1][:], op=ALU.mult),
                     reads=[r_kf, r_ex[1]], writes=[r_ko])
                P.op("act", lambda e: e.activation(out=alast[:].rearrange("p (c o) -> p c o", o=1), in_=cum3[:, :, 63:64],
                                                   func=AF.Exp), reads=[r_cum], writes=[r_alast])
                for bl in range(4):
                    P.op("pe", lambda e, bl=bl: e.transpose(out=tb[:, bl * 128:(bl + 1) * 128],
                                                            in_=ko[:, bl * 128:(bl + 1) * 128], identity=K.ident_bf[:]),
                         reads=[r_ko, K.r_const], writes=[rpb])
                P.op("dve", lambda e: e.tensor_copy(out=kotok[:].rearrange("p b d -> p (b d)"), in_=tb[:, 0:512]),
                     reads=[rpb], writes=[r_kotok])
                ob, rob = K.bank[2 + (h % 2)], K.rbank[2 + (h % 2)]
                for c in range(8):
                    bl, p0 = c // 2, (c % 2) * 64
                    csl = slice(c * 64, (c + 1) * 64)
                    si = nchunk % 2
                    nchunk += 1
                    sbk, rsb = K.bank[si], K.rbank[si]
                    dbk, rdb = K.bank[4 + si], K.rbank[4 + si]
                    _mm(P, sbk[p0:p0 + 64, 0:64], kh[:, csl], qh[:, csl], True, True, [r_kh, r_qh], [rsb])
                    P.op("dve", lambda e, si=si, p0=p0, sbk=sbk: e.tensor_tensor(
                        out=PT[si][p0:p0 + 64, :], in0=sbk[p0:p0 + 64, 0:64], in1=K.cmask_f[p0:p0 + 64, p0:p0 + 64],
                        op=ALU.mult), reads=[rsb, K.r_const], writes=[r_PT[si]])
                    first = (t == 0 and c == 0)
                    _mm(P, ob[:, csl], vtok[p0:p0 + 64, bl, h * 128:(h + 1) * 128], PT[si][p0:p0 + 64, :], True, first,
                        [r_vtok, r_PT[si]], [rob])
                    if not first:
                        _mm(P, ob[:, csl], S_b[:, h, :], qi[:, csl], False, True, [r_Sb[h], r_qi], [rob])
                    _mm(P, dbk[:, 0:128], kotok[p0:p0 + 64, bl, :], vtok[p0:p0 + 64, bl, h * 128:(h + 1) * 128], True, True,
                        [r_kotok, r_vtok], [rdb])
                    P.op("dve", lambda e, h=h, c=c, dbk=dbk: e.scalar_tensor_tensor(
                        out=S_f[:, h, :], in0=S_f[:, h, :], scalar=alast[:, c:c + 1], in1=dbk[:, 0:128],
                        op0=ALU.mult, op1=ALU.add), reads=[r_Sf[h], r_alast, rdb], writes=[r_Sf[h]])
                    P.op("pool", lambda e, h=h: e.tensor_copy(out=S_b[:, h, :], in_=S_f[:, h, :]),
                         reads=[r_Sf[h]], writes=[r_Sb[h]])
                nb_, rnb = K.bank[7], K.rbank[7]
                if ret:
                    P.op("act", lambda e, ob=ob: e.copy(out=sqo[:], in_=ob[:]), reads=[rob], writes=[r_sqo])
                    _mm(P, nb_[:], meanb[:], sqo[:], True, True, [r_meanb, r_sqo], [rnb])
                    P.op("act", lambda e: e.copy(out=tmp[:], in_=nb_[:]), reads=[rnb], writes=[r_tmp])
                    P.op("dve", lambda e, ob=ob: e.tensor_tensor(out=cen[:], in0=ob[:], in1=tmp[:], op=ALU.subtract),
                         reads=[rob, r_tmp], writes=[r_cen])
                    P.op("act", lambda e: e.activation(out=sqo[:], in_=cen[:], func=AF.Square), reads=[r_cen], writes=[r_sqo])
                    osrc, r_osrc, nscale = cen[:], r_cen, 1.0
                    _mm(P, nb_[:], meanb[:], sqo[:], True, True, [r_meanb, r_sqo], [rnb])
                    nwcol = vec[:, 4 + h:5 + h]
                else:
                    P.op("act", lambda e, ob=ob: e.activation(out=sqo[:], in_=ob[:], func=AF.Square), reads=[rob], writes=[r_sqo])
                    osrc, r_osrc, nscale = ob[:], rob, 1.0 / 128
                    _mm(P, nb_[:], K.ones_bf[:], sqo[:], True, True, [K.r_const, r_sqo], [rnb])
                    nwcol = vec[:, 8:9]
                act_rstd(P, rstd[:], r_rstd, nb_[:], rnb, nscale, tmp[:], r_tmp)
                P.op("dve", lambda e, osrc=osrc, nwcol=nwcol: e.scalar_tensor_tensor(
                    out=cum[:], in0=osrc, scalar=nwcol, in1=rstd[:], op0=ALU.mult, op1=ALU.mult),
                    reads=[r_osrc, r_vec, r_rstd], writes=[r_cum])
                proj_fm(1536 + h * 128)
                P.op("act", lambda e: e.activation(out=sgt[:], in_=pb[:], func=AF.Silu), reads=[rpb], writes=[r_sgt])
                P.op("dve", lambda e, h=h, yb=yb: e.tensor_tensor(out=yTt[yb][:, h, :], in0=cum[:], in1=sgt[:], op=ALU.mult),
                     reads=[r_cum, r_sgt], writes=[r_yTt[yb]])
            dst = K.ybuf.rearrange("(c p) t -> p c t", p=128)[:, 0:4, tsl]
            P.dma("sp", dst, yTt[yb][:], ds[7 + yb], reads=[r_yTt[yb]], writes=[K.r_y[c][t] for c in range(4)])
        P.emit_block(final=final)


def ssd_phase(K, layer, hin, final=False):
    nc, P, T, NT, dr = K.nc, K.P, K.T, K.NT, K.dr
    j = layer // 2
    ds = K.dsem_pool
    with ExitStack() as es:
        def sb(name, shape, dt):
            return es.enter_context(nc.sbuf_tensor(f"s{layer}_{name}", shape, dt))
        w = sb("w", [128, KC, 1288], BF16); r_w = Res()
        gain = sb("gain", [128, KC], F32); r_gain = Res()
        rows = sb("rows", [128, 536], F32); r_rows = Res()
        cw = sb("cw", [128, 6, 5], F32); r_cw = Res()
        smask = sb("smask", [128, 128], F32); r_smask = Res()
        onesF = sb("onesF", [128, 128], F32); r_onesF = Res()
        negA = sb("negA", [128, 8], F32); r_negA = Res()
        ht = [sb(f"ht{i}", [128, KC, 512], F32) for i in range(2)]; r_ht = RL(2)
        uT = sb("uT", [128, KC, 512], BF16); r_uT = Res()
        sq = sb("sq", [128, KC, 512], BF16); r_sq = Res()
        rstd = sb("rstd", [128, 512], F32); r_rstd = Res()
        tmp = sb("tmp", [128, 512], F32); r_tmp = Res()
        xpad = sb("xpad", [128, 6, 515], F32); r_xpad = RL(6)
        acc = sb("acc", [128, 512], F32); r_acc = Res()
        xc = sb("xc", [128, 6, 512], BF16); r_xc = RL(6)
        vtok = sb("vtok", [128, 512], BF16); r_vtok = Res()
        btok = sb("btok", [128, 128], BF16); r_btok = Res()
        sz = sb("sz", [128, 512], F32); r_sz = Res()
        sm = [sb(f"sm{i}", [128, 8], F32) for i in range(8)]; r_sm = RL(8)
        sm16 = sb("sm16", [128, 16], F32); r_sm16 = Res()
        vp = sb("vp", [128, 512], BF16); r_vp = Res()
        vpp = sb("vpp", [128, 512], BF16); r_vpp = Res()
        LM = sb("LM", [128, 8, 128], F32); r_LM = Res()
        E = sb("E", [128, 8, 128], F32); r_E = Res()
        GM = sb("GM", [128, 2, 128], F32); r_GM = Res()
        PT = sb("PT", [128, 8, 128], BF16); r_PT = Res()
        o1 = sb("o1", [128, 512], F32); r_o1 = Res()
        o2 = sb("o2", [128, 512], F32); r_o2 = Res()
        ss = sb("ss", [128, 2], F32); r_ss = Res()
        ss2 = sb("ss2", [128, 2], F32); r_ss2 = Res()
        S_f = sb("S_f", [128, 512], F32); r_Sf = Res()
        S_b = sb("S_b", [128, 512], BF16); r_Sb = Res()
        ytok = sb("ytok", [128, 512], BF16); r_ytok = Res()
        yTt = [sb(f"yTt{i}", [128, 4, 512], BF16) for i in range(2)]; r_yTt = RL(2)

        P.dma("pool", w[:, :, 0:768], dr["w_ssd"][j][:, :, 0:768], K.dsem_q[0], writes=[r_w])
        P.dma("pool", w[:, :, 768:1288], dr["w_ssd"][j][:, :, 768:1288], K.dsem_q[0], writes=[r_w])
        P.dma("sp", gain[:], dr["norm_mix"][layer], ds[1], writes=[r_gain])
        P.dma("sp", rows[:], dr["ssd_rows"][j], ds[30], writes=[r_rows])
        P.dma("sp", cw[:], dr["ssd_conv"][j], ds[31], writes=[r_cw])
        P.dma("sp", smask[:], dr["smask"], ds[32], writes=[r_smask])
        P.op("pool", lambda e: e.memset(onesF[:], 1.0), writes=[r_onesF])
        P.op("pool", lambda e: e.memset(S_f[:], 0.0), writes=[r_Sf])
        P.op("pool", lambda e: e.memset(S_b[:], 0.0), writes=[r_Sb])
        P.op("pool", lambda e: e.memset(xpad[:], 0.0), writes=r_xpad)
        P.op("act", lambda e: e.activation(out=negA[:], in_=rows[:, 8:16], func=AF.Exp), reads=[r_rows], writes=[r_negA])
        P.op("dve", lambda e: e.tensor_scalar(out=negA[:], in0=negA[:], scalar1=-1.0, scalar2=None, op0=ALU.mult),
             reads=[r_negA], writes=[r_negA])

        pb, rpb = K.bank[6], K.rbank[6]
        tb = pb[:].bitcast(BF16)
        P.dma("sp", ht[0][:], hview(hin, 0), ds[2], reads=[K.r_h[c][0] for c in range(KC)], writes=[r_ht[0]])
        for t in range(NT):
            b = t % 2
            yb = t % 2
            tsl = slice(t * 512, (t + 1) * 512)
            if t + 1 < NT:
                P.dma("sp", ht[1 - b][:], hview(hin, t + 1), ds[2 + (1 - b)],
                      reads=[K.r_h[c][t + 1] for c in range(KC)], writes=[r_ht[1 - b]])
            rmsnorm_tile2(K, ht[b], r_ht[b], gain, r_gain, uT, r_uT, sq, r_sq, rstd, r_rstd, tmp, r_tmp, 7)
            for cc in range(6):
                c0 = 512 + cc * 128
                for kc in range(KC):
                    _mm(P, pb[:], w[:, kc, c0:c0 + 128], uT[:, kc, :], kc == 0, kc == KC - 1, [r_w, r_uT], [rpb])
                P.op("dve", lambda e, cc=cc: e.tensor_copy(out=xpad[:, cc, 3:515], in_=pb[:]), reads=[rpb], writes=[r_xpad[cc]])
                P.op("dve", lambda e, cc=cc: e.tensor_scalar(out=acc[:], in0=xpad[:, cc, 0:512], scalar1=cw[:, cc, 0:1],
                                                            scalar2=None, op0=ALU.mult),
                     reads=[r_xpad[cc], r_cw], writes=[r_acc])
                for k in range(1, 4):
                    P.op("dve", lambda e, cc=cc, k=k: e.scalar_tensor_tensor(
                        out=acc[:], in0=xpad[:, cc, k:k + 512], scalar=cw[:, cc, k:k + 1], in1=acc[:],
                        op0=ALU.mult, op1=ALU.add), reads=[r_xpad[cc], r_cw, r_acc], writes=[r_acc])
                P.op("act", lambda e, cc=cc: e.activation(out=xc[:, cc, :], in_=acc[:], func=AF.Silu, bias=cw[:, cc, 4:5]),
                     reads=[r_acc, r_cw], writes=[r_xc[cc]])
                P.op("dve", lambda e, cc=cc: e.tensor_copy(out=xpad[:, cc, 0:3], in_=xpad[:, cc, 512:515]),
                     reads=[r_xpad[cc]], writes=[r_xpad[cc]])
            for bl in range(4):
                bsl = slice(bl * 128, (bl + 1) * 128)
                for cc in range(4):
                    P.op("pe", lambda e, cc=cc, bsl=bsl: e.transpose(out=tb[:, cc * 128:(cc + 1) * 128], in_=xc[:, cc, bsl],
                                                                    identity=K.ident_bf[:]),
                         reads=[r_xc[cc], K.r_const], writes=[rpb])
                P.op("dve", lambda e: e.tensor_copy(out=vtok[:], in_=tb[:, 0:512]), reads=[rpb], writes=[r_vtok])
                P.op("pe", lambda e, bsl=bsl: e.transpose(out=tb[:, 0:128], in_=xc[:, 4, bsl], identity=K.ident_bf[:]),
                     reads=[r_xc[4], K.r_const], writes=[rpb])
                P.op("dve", lambda e: e.tensor_copy(out=btok[:], in_=tb[:, 0:128]), reads=[rpb], writes=[r_btok])
                for kc in range(KC):
                    _mm(P, pb[:], uT[:, kc, bsl], w[:, kc, 0:512], kc == 0, kc == KC - 1, [r_w, r_uT], [rpb])
                P.op("act", lambda e: e.activation(out=sz[:], in_=pb[:], func=AF.Silu), reads=[rpb], writes=[r_sz])
                for kc in range(KC):
                    _mm(P, pb[:, 0:8], uT[:, kc, bsl], w[:, kc, 1280:1288], kc == 0, kc == KC - 1, [r_w, r_uT], [rpb])
                x_, ax, mx, dt_, l_, cumt, wdec, ecum = sm
                P.op("dve", lambda e: e.tensor_tensor(out=x_[:], in0=pb[:, 0:8], in1=rows[:, 0:8], op=ALU.add),
                     reads=[rpb, r_rows], writes=[r_sm[0]])
                P.op("dve", lambda e: e.scalar_tensor_tensor(out=ax[:], in0=x_[:], scalar=-1.0, in1=x_[:], op0=ALU.mult,
                                                             op1=ALU.min), reads=[r_sm[0]], writes=[r_sm[1]])
                P.op("act", lambda e: e.activation(out=ax[:], in_=ax[:], func=AF.Exp), reads=[r_sm[1]], writes=[r_sm[1]])
                P.op("act", lambda e: e.activation(out=ax[:], in_=ax[:], func=AF.Ln, bias=1.0), reads=[r_sm[1]], writes=[r_sm[1]])
                P.op("dve", lambda e: e.scalar_tensor_tensor(out=dt_[:], in0=x_[:], scalar=0.0, in1=ax[:], op0=ALU.max,
                                                             op1=ALU.add), reads=[r_sm[0], r_sm[1]], writes=[r_sm[3]])
                P.op("dve", lambda e: e.tensor_tensor(out=l_[:], in0=dt_[:], in1=negA[:], op=ALU.mult),
                     reads=[r_sm[3], r_negA], writes=[r_sm[4]])
                cb, rcb = K.bank[2], K.rbank[2]
                _mm(P, cb[:, 256:264], K.cmask_f[:], l_[:], True, True, [K.r_const, r_sm[4]], [rcb])
                _mm(P, cb[:, 264:272], onesF[:], l_[:], True, True, [r_onesF, r_sm[4]], [rcb])
                P.op("dve", lambda e: e.tensor_copy(out=sm16[:], in_=cb[:, 256:272]), reads=[rcb], writes=[r_sm16])
                P.op("act", lambda e: e.activation(out=ecum[:], in_=sm16[:, 0:8], func=AF.Exp), reads=[r_sm16], writes=[r_sm[7]])
                P.op("dve", lambda e: e.tensor_tensor(out=wdec[:], in0=sm16[:, 8:16], in1=sm16[:, 0:8], op=ALU.subtract),
                     reads=[r_sm16], writes=[r_sm[6]])
                P.op("act", lambda e: e.activation(out=wdec[:], in_=wdec[:], func=AF.Exp), reads=[r_sm[6]], writes=[r_sm[6]])
                P.op("act", lambda e: e.activation(out=cumt[:], in_=sm16[:, 8:16], func=AF.Exp), reads=[r_sm16], writes=[r_sm[5]])
                v3 = vtok[:].rearrange("p (h d) -> p h d", h=8)
                P.op("dve", lambda e, v3=v3: e.tensor_tensor(out=vp[:].rearrange("p (h d) -> p h d", h=8), in0=v3,
                                                            in1=dt_[:].unsqueeze(2).to_broadcast([128, 8, 64]), op=ALU.mult),
                     reads=[r_vtok, r_sm[3]], writes=[r_vp])
                P.op("dve", lambda e: e.tensor_tensor(out=vpp[:].rearrange("p (h d) -> p h d", h=8),
                                                      in0=vp[:].rearrange("p (h d) -> p h d", h=8),
                                                      in1=wdec[:].unsqueeze(2).to_broadcast([128, 8, 64]), op=ALU.mult),
                     reads=[r_vp, r_sm[6]], writes=[r_vpp])
                P.op("dve", lambda e: e.tensor_tensor(out=LM[:], in0=smask[:].unsqueeze(1).to_broadcast([128, 8, 128]),
                                                      in1=l_[:].unsqueeze(2).to_broadcast([128, 8, 128]), op=ALU.mult),
                     reads=[r_smask, r_sm[4]], writes=[r_LM])
                for hh in range(2):
                    db, rdb = K.bank[hh], K.rbank[hh]
                    for h4 in range(4):
                        h = hh * 4 + h4
                        _mm(P, db[:, h4 * 128:(h4 + 1) * 128], LM[:, h, :], K.cmask_f[:], True, True, [r_LM, K.r_const], [rdb])
                    P.op("act", lambda e, hh=hh, db=db: e.activation(out=E[:, hh * 4:(hh + 1) * 4, :].rearrange("p h i -> p (h i)"),
                                                                   in_=db[:], func=AF.Exp), reads=[rdb], writes=[r_E])
                gbank = [(K.bank[2], K.rbank[2]), (K.bank[7], K.rbank[7])]
                for g in range(2):
                    gs = slice(g * 64, (g + 1) * 64)
                    gb_, rgb = gbank[g]
                    _mm(P, gb_[:, 0:128], xc[gs, 4, bsl], xc[gs, 5, bsl], True, True, [r_xc[4], r_xc[5]], [rgb])
                    P.op("dve", lambda e, g=g, gb_=gb_: e.tensor_tensor(out=GM[:, g, :], in0=gb_[:, 0:128], in1=K.cmask_f[:],
                                                                      op=ALU.mult), reads=[rgb, K.r_const], writes=[r_GM])
                for g in range(2):
                    P.op("dve", lambda e, g=g: e.tensor_tensor(out=PT[:, g * 4:(g + 1) * 4, :], in0=E[:, g * 4:(g + 1) * 4, :],
                                                              in1=GM[:, g:g + 1, :].to_broadcast([128, 4, 128]), op=ALU.mult),
                         reads=[r_E, r_GM], writes=[r_PT])
                ab, rab = K.bank[3], K.rbank[3]
                bb, rbb = K.bank[4], K.rbank[4]
                for h in range(8):
                    _mm(P, ab[:, h * 64:(h + 1) * 64], PT[:, h, :], vp[:, h * 64:(h + 1) * 64], True, True, [r_PT, r_vp], [rab])
                ibank = [(K.bank[4], K.rbank[4]), (K.bank[7], K.rbank[7])]
                for g in range(2):
                    gs = slice(g * 64, (g + 1) * 64)
                    ib_, rib = ibank[g]
                    _mm(P, ib_[:, g * 256:(g + 1) * 256], xc[gs, 5, bsl], S_b[gs, g * 256:(g + 1) * 256], True, True,
                        [r_xc[5], r_Sb], [rib])
                    P.op("dve", lambda e, g=g, ib_=ib_: e.tensor_tensor(
                        out=o1[:, g * 256:(g + 1) * 256].rearrange("p (h d) -> p h d", h=4),
                        in0=ib_[:, g * 256:(g + 1) * 256].rearrange("p (h d) -> p h d", h=4),
                        in1=ecum[:, g * 4:(g + 1) * 4].unsqueeze(2).to_broadcast([128, 4, 64]), op=ALU.mult),
                        reads=[rib, r_sm[7]], writes=[r_o1])
                P.op("dve", lambda e: e.tensor_tensor(out=o1[:], in0=o1[:], in1=ab[:], op=ALU.add),
                     reads=[r_o1, rab], writes=[r_o1])
                sbk, rsb = K.bank[5], K.rbank[5]
                _mm(P, sbk[:], btok[:], vpp[:], True, True, [r_btok, r_vpp], [rsb])
                for g in range(2):
                    gs = slice(g * 64, (g + 1) * 64)
                    sv = S_f[gs, g * 256:(g + 1) * 256].rearrange("p (h d) -> p h d", h=4)
                    P.op("dve", lambda e, g=g, gs=gs, sv=sv: e.tensor_tensor(
                        out=sv, in0=sv, in1=cumt[gs, g * 4:(g + 1) * 4].unsqueeze(2).to_broadcast([64, 4, 64]), op=ALU.mult),
                        reads=[r_Sf, r_sm[5]], writes=[r_Sf])
                    P.op("dve", lambda e, g=g, gs=gs: e.tensor_tensor(
                        out=S_f[gs, g * 256:(g + 1) * 256], in0=S_f[gs, g * 256:(g + 1) * 256],
                        in1=sbk[gs, g * 256:(g + 1) * 256], op=ALU.add), reads=[r_Sf, rsb], writes=[r_Sf])
                P.op("pool", lambda e: e.tensor_copy(out=S_b[:], in_=S_f[:]), reads=[r_Sf], writes=[r_Sb])
                P.op("dve", lambda e, v3=v3: e.tensor_tensor(out=o2[:].rearrange("p (h d) -> p h d", h=8), in0=v3,
                                                            in1=rows[:, 16:24].unsqueeze(2).to_broadcast([128, 8, 64]),
                                                            op=ALU.mult), reads=[r_vtok, r_rows], writes=[r_o2])
                P.op("dve", lambda e: e.tensor_tensor(out=o2[:], in0=o2[:], in1=o1[:], op=ALU.add),
                     reads=[r_o2, r_o1], writes=[r_o2])
                P.op("dve", lambda e: e.tensor_tensor(out=o2[:], in0=o2[:], in1=sz[:], op=ALU.mult),
                     reads=[r_o2, r_sz], writes=[r_o2])
                P.op("dve", lambda e: e.tensor_tensor(out=o1[:], in0=o2[:], in1=o2[:], op=ALU.mult),
                     reads=[r_o2, r_o1], writes=[r_o1])
                P.op("dve", lambda e: e.tensor_reduce(out=ss[:], in_=o1[:].rearrange("p (g d) -> p g d", g=2), axis=AX.X,
                                                      op=ALU.add), reads=[r_o1], writes=[r_ss])
                act_rstd(P, ss[:], r_ss, ss[:], r_ss, 1.0 / 256, ss2[:], r_ss2)
                P.op("dve", lambda e: e.tensor_tensor(out=o2[:].rearrange("p (g d) -> p g d", g=2),
                                                      in0=o2[:].rearrange("p (g d) -> p g d", g=2),
                                                      in1=ss[:].unsqueeze(2).to_broadcast([128, 2, 256]), op=ALU.mult),
                     reads=[r_o2, r_ss], writes=[r_o2])
                P.op("dve", lambda e: e.tensor_tensor(out=ytok[:], in0=o2[:], in1=rows[:, 24:536], op=ALU.mult),
                     reads=[r_o2, r_rows], writes=[r_ytok])
                for cc in range(4):
                    P.op("pe", lambda e, cc=cc: e.transpose(out=tb[:, cc * 128:(cc + 1) * 128],
                                                            in_=ytok[:, cc * 128:(cc + 1) * 128], identity=K.ident_bf[:]),
                         reads=[r_ytok, K.r_const], writes=[rpb])
                P.op("dve", lambda e, bsl=bsl, yb=yb: e.tensor_copy(out=yTt[yb][:, :, bsl],
                                                                   in_=tb[:, 0:512].rearrange("p (c s) -> p c s", c=4)),
                     reads=[rpb], writes=[r_yTt[yb]])
            dst = K.ybuf.rearrange("(c p) t -> p c t", p=128)[:, 4:8, tsl]
            P.dma("sp", dst, yTt[yb][:], ds[7 + yb], reads=[r_yTt[yb]], writes=[K.r_y[c][t] for c in range(4, 8)])
        P.emit_block(final=final)


ALL_PHASES = ("ret", "ssd", "hg", "fox", "ffn")


def _wl(W):
    return np.ascontiguousarray(W.reshape(W.shape[0], KC, 128, W.shape[2]).transpose(0, 2, 1, 3))


def _vec(v):
    return np.ascontiguousarray(v.reshape(v.shape[0], KC, 128).transpose(0, 2, 1))


def prepare_inputs(T, norm_mix, norm_ffn, ffn_w_gate, ffn_w_up, ffn_w_down, ab_w_in, ab_w_out, ret_gn_w, ssd_conv_w,
                   ssd_conv_b, ssd_dt_bias, ssd_a_log, ssd_d, ssd_norm_w, cd_w_in, cd_w_out, hg_lb_logits, hg_norm_w,
                   fox_f_bias, fox_q_norm_w, fox_k_norm_w):
    f32 = np.float32
    A = lambda a: np.asarray(a, dtype=f32)
    norm_mix, norm_ffn = A(norm_mix), A(norm_ffn)
    wg, wu, wd = A(ffn_w_gate), A(ffn_w_up), A(ffn_w_down)
    ab_in, ab_out, cd_in, cd_out = A(ab_w_in), A(ab_w_out), A(cd_w_in), A(cd_w_out)
    sh = {}
    sh["norm_mix"], sh["norm_ffn"] = _vec(norm_mix), _vec(norm_ffn)
    sh["wg"] = np.ascontiguousarray(wg.reshape(4, KC, 128, NJ, 128).transpose(0, 3, 2, 1, 4))
    sh["wu"] = np.ascontiguousarray(wu.reshape(4, KC, 128, NJ, 128).transpose(0, 3, 2, 1, 4))
    sh["wd"] = np.ascontiguousarray(wd.reshape(4, NJ, 128, KC, 128).transpose(0, 3, 2, 1, 4))
    w_out = np.stack([ab_out[0], cd_out[0], ab_out[1], cd_out[1]])
    sh["w_out"] = _wl(w_out)
    sh["w_fox"] = _wl(cd_in[:, :, 2048:3588])
    fv = np.zeros((2, 128, 3), f32)
    fv[:, :, 0] = A(fox_q_norm_w); fv[:, :, 1] = A(fox_k_norm_w); fv[:, 0:4, 2] = A(fox_f_bias)
    sh["fox_vec"] = fv
    sh["cmask"] = np.triu(np.ones((128, 128), f32))
    sh["smask"] = np.tril(np.ones((128, 128), f32), -1)
    sh["ident"] = np.eye(128, dtype=f32)
    sh["rmask"] = np.tile((np.arange(512) % 64 != 0).astype(f32), (128, 1))
    sh["w_hg"] = _wl(cd_in[:, :, 0:2048])
    hv = np.zeros((2, 128, 12), f32)
    lbl = A(hg_lb_logits)
    for jj in range(2):
        hv[jj, :, 0:4] = lbl[0].reshape(4, 128).T
        hv[jj, :, 4:8] = lbl[1].reshape(4, 128).T
        hv[jj, :, 8] = A(hg_norm_w)[jj]
    sh["hg_vec"] = hv

    def pad(Wx, swap):
        o = np.zeros((D, 512), f32)
        for h in range(4):
            blk = Wx[:, h * 64:(h + 1) * 64]
            if swap:
                blk = blk.reshape(D, 32, 2)[:, :, ::-1].reshape(D, 64)
            o[:, h * 128:h * 128 + 64] = blk
        return o
    wr = []
    for jj in range(2):
        Wq, Wk = ab_in[jj][:, 0:256], ab_in[jj][:, 256:512]
        wr.append(np.concatenate([pad(Wq, 0), pad(Wk, 0), ab_in[jj][:, 512:1024], ab_in[jj][:, 1024:1536],
                                  pad(Wq, 1), pad(Wk, 1)], 1))
    sh["w_ret"] = _wl(np.stack(wr))
    rv = np.zeros((2, 128, 12), f32)
    lg = np.log1p(-np.exp2(-5.0 - np.arange(4, dtype=np.float64)))
    rv[:, :, 0:4] = lg[None, None, :].astype(f32)
    rv[:, :, 4:8] = A(ret_gn_w).transpose(0, 2, 1)
    sh["ret_vec"] = rv
    freqs = (np.float32(10000.0) ** (-np.linspace(0.0, 1.0, 32, dtype=f32))).astype(f32)
    ang = (np.arange(T, dtype=f32)[:, None] * freqs[None, :]).astype(f32).astype(np.float64)
    rt = np.zeros((128, 2, T), f32)
    rt[0:64, 0, :] = np.repeat(np.cos(ang), 2, axis=1).T
    sn = np.repeat(np.sin(ang), 2, axis=1)
    sn[:, 0::2] *= -1
    rt[0:64, 1, :] = sn.T
    sh["rot_tab"] = rt
    sh["w_ssd"] = _wl(ab_in[:, :, 1536:2824])
    rows = np.zeros((2, 128, 536), f32)
    rows[:, :, 0:8] = A(ssd_dt_bias)[:, None, :]
    rows[:, :, 8:16] = A(ssd_a_log)[:, None, :]
    rows[:, :, 16:24] = A(ssd_d)[:, None, :]
    rows[:, :, 24:536] = A(ssd_norm_w)[:, None, :]
    sh["ssd_rows"] = rows
    cv = np.zeros((2, 128, 6, 5), f32)
    cwv, cbv = A(ssd_conv_w), A(ssd_conv_b)
    for jj in range(2):
        cv[jj, :, :, 0:4] = cwv[jj].T.reshape(6, 128, 4).transpose(1, 0, 2)
        cv[jj, :, :, 4] = cbv[jj].reshape(6, 128).T
    sh["ssd_conv"] = cv
    return sh


def kernel(x, **params):
    x = np.asarray(x, dtype=np.float32)
    B, T, _ = x.shape
    shared = prepare_inputs(T, **params)
    nc = build_program(T, [0, 1, 2, 3], phases=ALL_PHASES)
    in_maps = []
    for b in range(B):
        m = dict(shared)
        m["xT"] = np.ascontiguousarray(x[b].T)
        in_maps.append(m)
    res = run_bass_kernel_spmd(nc, in_maps, core_ids=list(range(B)))
    out = np.stack([np.ascontiguousarray(res.results[b]["out"].T) for b in range(B)], axis=0)
    return out.astype(np.float32)
```

```python
import math
import numpy as np
from contextlib import ExitStack
import concourse.bass as bass
import concourse.mybir as mybir
from concourse.bass_utils import run_bass_kernel_spmd

F32 = mybir.dt.float32
BF16 = mybir.dt.bfloat16
AF = mybir.ActivationFunctionType
ALU = mybir.AluOpType
AX = mybir.AxisListType

D = 1024
KC = 8
FF = 2816
NJ = 22
EPS = 1e-6
import os
NOSYNC_ENGINES = tuple(x for x in os.environ.get("KNOSYNC", "").split(",") if x)
ENGS = ["pe", "act", "dve", "pool", "sp"]
CENG = ["pe", "act", "dve", "pool"]


class Res:
    __slots__ = ("w", "rs")

    def __init__(self):
        self.w = None
        self.rs = []


def RL(n):
    return [Res() for _ in range(n)]


class Ev:
    __slots__ = ("sem", "val", "op", "blk")

    def __init__(self, blk, sem=None, val=None, op=None):
        self.blk, self.sem, self.val, self.op = blk, sem, val, op


class Op:
    __slots__ = ("eng", "fn", "waits", "marked", "ev", "incs")

    def __init__(self, eng, fn):
        self.eng, self.fn = eng, fn
        self.waits = []
        self.marked = False
        self.ev = None
        self.incs = None


class DSem:
    def __init__(self, sem):
        self.sem = sem
        self.count = 0


class Prog:
    def __init__(self, nc, es, same_engine_sync=True):
        self.nc = nc
        self.es = es
        self.ops = {e: [] for e in ENGS}
        self.csem = {e: es.enter_context(nc.semaphore("c_" + e)) for e in CENG}
        self.ccount = {e: 0 for e in CENG}
        self.bar = es.enter_context(nc.semaphore("bar"))
        self.nbar = 0
        self.same = same_engine_sync
        self.nosync = set(NOSYNC_ENGINES)
        self.dsems = []
        self.blk = 0
        self.blk_dsems = set()

    def dsem(self):
        d = DSem(self.es.enter_context(self.nc.semaphore(f"d{len(self.dsems)}")))
        self.dsems.append(d)
        return d

    def _deps(self, op, reads, writes):
        evs = []
        for r in reads:
            if r.w is not None:
                evs.append(r.w)
        for r in writes:
            if r.w is not None:
                evs.append(r.w)
            evs.extend(r.rs)
        for ev in evs:
            if ev.blk != self.blk:
                continue
            if ev.op is not None:
                p = ev.op
                if p.eng == op.eng and (op.eng == "pe" or op.eng in self.nosync):
                    continue
                p.marked = True
            op.waits.append(ev)

    def _commit(self, ev, reads, writes):
        for r in reads:
            r.rs.append(ev)
        for r in writes:
            r.w = ev
            r.rs = []

    def op(self, eng, fn, reads=(), writes=()):
        o = Op(eng, fn)
        self._deps(o, reads, writes)
        o.ev = Ev(self.blk, op=o)
        self.ops[eng].append(o)
        self._commit(o.ev, reads, writes)
        return o

    def dma(self, q, out, in_, ds, reads=(), writes=(), slow=False):
        if slow:
            o = Op(q, lambda e: e.dma_start(out=out, in_=in_, allow_slow_non_contiguous=True))
        else:
            o = Op(q, lambda e: e.dma_start(out=out, in_=in_))
        self._deps(o, reads, writes)
        ds.count += 16
        self.blk_dsems.add(ds)
        o.incs = (ds.sem, 16)
        o.ev = Ev(self.blk, sem=ds.sem, val=ds.count)
        self.ops[q].append(o)
        self._commit(o.ev, reads, writes)
        return o

    def emit_block(self, final=False):
        nc = self.nc
        finals = []
        for e in CENG:
            ops = [o for o in self.ops[e] if o.fn is not None]
            if ops:
                ops[-1].marked = True
            c = self.ccount[e]
            for o in self.ops[e]:
                if o.incs is None and o.marked:
                    c += 1
                    o.ev.sem, o.ev.val = self.csem[e], c
            self.ccount[e] = c
            if ops:
                finals.append((self.csem[e], c))
        for ds in self.blk_dsems:
            finals.append((ds.sem, ds.count))
        self.nbar += 1
        nbar = self.nbar
        engobj = {"pe": "tensor", "act": "scalar", "dve": "vector", "pool": "gpsimd", "sp": "sync"}
        with nc.Block() as block:
            for e in ENGS:
                ops = self.ops[e]

                def body(eng, ops=ops, e=e):
                    waited = {}
                    for o in ops:
                        for ev in o.waits:
                            k = id(ev.sem)
                            if waited.get(k, 0) < ev.val:
                                eng.wait_ge(ev.sem, ev.val)
                                waited[k] = ev.val
                        ins = o.fn(eng)
                        if o.incs is not None:
                            ins.then_inc(o.incs[0], o.incs[1])
                        elif o.marked:
                            ins.then_inc(self.csem[e], 1)
                    if e == "sp":
                        for (s, v) in finals:
                            eng.wait_ge(s, v)
                        eng.sem_inc(self.bar, 1)
                    if not (final and e != "sp"):
                        eng.wait_ge(self.bar, nbar)

                getattr(block, engobj[e])(body)
        self.ops = {e: [] for e in ENGS}
        self.blk += 1
        self.blk_dsems = set()


class KB:
    pass


def _mm(P, out, lhsT, rhs, start, stop, reads, writes, **kw):
    return P.op("pe", lambda e: e.matmul(out, lhsT, rhs, start=start, stop=stop, **kw), reads=reads, writes=writes)


def build_program(T, layers, phases=("mix", "ffn"), test_ybuf=False):
    nc = bass.Bass("TRN2", target_bir_lowering=False)
    NT = T // 512
    K = KB()
    K.nc, K.T, K.NT = nc, T, NT
    dr = {}

    def din(name, shape, dt=F32):
        dr[name] = nc.dram_tensor(name, list(shape), dt, kind="ExternalInput").ap()
        return dr[name]

    din("xT", [D, T])
    din("norm_mix", [4, 128, KC])
    din("norm_ffn", [4, 128, KC])
    din("wg", [4, NJ, 128, KC, 128])
    din("wu", [4, NJ, 128, KC, 128])
    din("wd", [4, KC, 128, NJ, 128])
    din("w_out", [4, 128, KC, D])
    din("w_fox", [2, 128, KC, 1540])
    din("fox_vec", [2, 128, 3])
    din("cmask", [128, 128])
    din("w_hg", [2, 128, KC, 2048])
    din("hg_vec", [2, 128, 12])
    din("w_ret", [2, 128, KC, 3072])
    din("ret_vec", [2, 128, 12])
    din("rmask", [128, 512])
    din("w_ssd", [2, 128, KC, 1288])
    din("ssd_rows", [2, 128, 536])
    din("ssd_conv", [2, 128, 6, 5])
    din("smask", [128, 128])
    din("rot_tab", [128, 2, T])
    din("ident", [128, 128])
    out = nc.dram_tensor("out", [D, T], F32, kind="ExternalOutput").ap()
    if test_ybuf == "out":
        ybuf = nc.dram_tensor("ybuf", [D, T], BF16, kind="ExternalOutput").ap()
    elif test_ybuf:
        ybuf = nc.dram_tensor("ybuf", [D, T], BF16, kind="ExternalInput").ap()
    else:
        ybuf = nc.dram_tensor("ybuf", [D, T], BF16).ap()
    K.dr, K.out, K.ybuf = dr, out, ybuf

    with ExitStack() as es:
        P = Prog(nc, es)
        K.P = P
        K.bank = [es.enter_context(nc.psum_tensor(f"bank{i}", [128, 512], F32)) for i in range(8)]
        K.rbank = RL(8)
        K.ones_bf = es.enter_context(nc.sbuf_tensor("ones_bf", [128, 128], BF16))
        K.r_const = Res()
        P.op("pool", lambda e: e.memset(K.ones_bf[:], 1.0), writes=[K.r_const])
        K.r_h = [RL(NT) for _ in range(KC)]
        K.r_y = [RL(NT) for _ in range(KC)]
        K.dsem_pool = [P.dsem() for _ in range(40)]
        K.dsem_q = [P.dsem() for _ in range(8)]
        K.ident_bf = es.enter_context(nc.sbuf_tensor("ident_bf", [128, 128], BF16))
        K.ident_f = es.enter_context(nc.sbuf_tensor("ident_f", [128, 128], F32))
        K.cmask_bf = es.enter_context(nc.sbuf_tensor("cmask_bf", [128, 128], BF16))
        K.cmask_f = es.enter_context(nc.sbuf_tensor("cmask_f", [128, 128], F32))
        P.dma("pool", K.ident_bf[:], dr["ident"], K.dsem_q[7], writes=[K.r_const])
        P.dma("sp", K.ident_f[:], dr["ident"], K.dsem_pool[37], writes=[K.r_const])
        P.dma("pool", K.cmask_bf[:], dr["cmask"], K.dsem_q[7], writes=[K.r_const])
        P.dma("sp", K.cmask_f[:], dr["cmask"], K.dsem_pool[39], writes=[K.r_const])
        K.fd = nc.dram_tensor("fd", [8, T], F32).ap()
        K.r_fd = Res()

        plan = []
        for li, layer in enumerate(layers):
            if layer % 2 == 0:
                for ph in ("ret", "ssd"):
                    if ph in phases:
                        plan.append((ph, li, layer))
            else:
                for ph in ("hg", "fox"):
                    if ph in phases:
                        plan.append((ph, li, layer))
            if "ffn" in phases:
                plan.append(("ffn", li, layer))
        for pi, (ph, li, layer) in enumerate(plan):
            hin = dr["xT"] if li == 0 else out
            fin = pi == len(plan) - 1
            if ph == "ret":
                hg_phase(K, layer, hin, "ret", final=fin)
            elif ph == "hg":
                hg_phase(K, layer, hin, "hg", final=fin)
            elif ph == "ssd":
                ssd_phase(K, layer, hin, final=fin)
            elif ph == "fox":
                fox_phase(K, layer, hin, final=fin)
            else:
                ffn_phase(K, layer, hin, final=fin)
    return nc


def rmsnorm_tile(K, es_bufs, h_t, r_h_t, gain, r_gain, u_out, r_u, sq, r_sq, rstd, r_rstd, bank_i):
    P = K.P
    bank, rb = K.bank[bank_i], K.rbank[bank_i]
    for c in range(KC):
        P.op("act", lambda e, c=c: e.activation(out=sq[:, c, :], in_=h_t[:, c, :], func=AF.Square),
             reads=[r_h_t], writes=[r_sq])
    for c in range(KC):
        _mm(P, bank[:], K.ones_bf[:], sq[:, c, :], c == 0, c == KC - 1, [K.r_const, r_sq], [rb])
    P.op("act", lambda e: e.activation(out=rstd[:], in_=bank[:], func=AF.Sqrt, bias=EPS, scale=1.0 / D),
         reads=[rb], writes=[r_rstd])
    P.op("dve", lambda e: e.reciprocal(out=rstd[:], in_=rstd[:]), reads=[r_rstd], writes=[r_rstd])
    for c in range(KC):
        P.op("dve", lambda e, c=c: e.scalar_tensor_tensor(out=u_out(c), in0=h_t[:, c, :], scalar=gain[:, c:c + 1],
                                                         in1=rstd[:], op0=ALU.mult, op1=ALU.mult),
             reads=[r_h_t, r_gain, r_rstd], writes=[r_u])


def ffn_phase(K, layer, hin, final=False):
    nc, P, T, NT, dr, out = K.nc, K.P, K.T, K.NT, K.dr, K.out
    TG = min(1024, T)
    NG = T // TG
    TPG = TG // 512
    ds = K.dsem_pool
    with ExitStack() as es:
        def sb(name, shape, dt):
            return es.enter_context(nc.sbuf_tensor(f"f{layer}_{name}", shape, dt))
        wout = sb("wout", [128, KC, D], BF16); r_wout = Res()
        gain = sb("gain", [128, KC], F32); r_gain = Res()
        uT = sb("uT", [128, KC, TG], BF16); r_uT = RL(TPG)
        aT = sb("aT", [128, NJ, TG], BF16); r_aT = [RL(TPG) for _ in range(NJ)]
        ht = [sb(f"ht{i}", [128, KC, 512], F32) for i in range(2)]; r_ht = RL(2)
        yt = [sb(f"yt{i}", [128, KC, 512], BF16) for i in range(2)]; r_yt = RL(2)
        sq = sb("sq", [128, KC, 512], BF16); r_sq = Res()
        rstd = sb("rstd", [128, 512], F32); r_rstd = Res()
        wgc = [sb(f"wgc{i}", [128, KC, 128], BF16) for i in range(2)]; r_wgc = RL(2)
        wuc = [sb(f"wuc{i}", [128, KC, 128], BF16) for i in range(2)]; r_wuc = RL(2)
        wdc = [sb(f"wdc{i}", [128, NJ, 128], BF16) for i in range(2)]; r_wdc = RL(2)
        sg = [sb(f"sg{i}", [128, 512], F32) for i in range(2)]; r_sg = RL(2)
        hs = [sb(f"hs{i}", [128, 512], F32) for i in range(4)]; r_hs = RL(4)

        P.dma("pool", wout[:], dr["w_out"][layer], K.dsem_q[0], writes=[r_wout])
        P.dma("sp", gain[:], dr["norm_ffn"][layer], ds[1], writes=[r_gain])

        def hview(ap, t):
            return ap.rearrange("(c p) t -> p c t", p=128)[:, :, t * 512:(t + 1) * 512]

        def load_tile(tg):
            b = tg % 2
            rh = [K.r_h[c][tg] for c in range(KC)]
            ry = [K.r_y[c][tg] for c in range(KC)]
            P.dma("sp", ht[b][:], hview(hin, tg), ds[2 + b], reads=rh, writes=[r_ht[b]])
            P.dma("sp", yt[b][:], hview(K.ybuf, tg), ds[4 + b], reads=ry, writes=[r_yt[b]])

        wcount = [0]

        for g in range(NG):
            load_tile(g * TPG)
            for tl in range(TPG):
                tg = g * TPG + tl
                b = tg % 2
                if tl + 1 < TPG:
                    load_tile(tg + 1)
                for dc in range(KC):
                    bi = dc % 2
                    for kc in range(KC):
                        _mm(P, K.bank[bi][:], wout[:, kc, dc * 128:(dc + 1) * 128], yt[b][:, kc, :], kc == 0, kc == KC - 1,
                            [r_wout, r_yt[b]], [K.rbank[bi]])
                    P.op("dve", lambda e, dc=dc, bi=bi, b=b: e.tensor_tensor(out=ht[b][:, dc, :], in0=ht[b][:, dc, :],
                                                                           in1=K.bank[bi][:], op=ALU.add),
                         reads=[K.rbank[bi], r_ht[b]], writes=[r_ht[b]])
                P.dma("sp", hview(out, tg), ht[b][:], ds[6 + b], reads=[r_ht[b]],
                      writes=[K.r_h[c][tg] for c in range(KC)])
                rmsnorm_tile(K, None, ht[b], r_ht[b], gain, r_gain,
                             lambda c, tl=tl: uT[:, c, tl * 512:(tl + 1) * 512], r_uT[tl], sq, r_sq, rstd, r_rstd, 2)
            def load_w2(j):
                b = wcount[0] % 2
                wcount[0] += 1
                P.dma("pool", wgc[b][:], dr["wg"][layer, j], K.dsem_q[1 + b], writes=[r_wgc[b]])
                P.dma("pool", wuc[b][:], dr["wu"][layer, j], K.dsem_q[3 + b], writes=[r_wuc[b]])
                return b
            nb = load_w2(0)
            for j in range(NJ):
                b = nb
                if j + 1 < NJ:
                    nb = load_w2(j + 1)
                for tl in range(TPG):
                    pg, pu = 3 + 2 * (tl % 2), 4 + 2 * (tl % 2)
                    sl = slice(tl * 512, (tl + 1) * 512)
                    for kc in range(KC):
                        _mm(P, K.bank[pg][:], wgc[b][:, kc, :], uT[:, kc, sl], kc == 0, kc == KC - 1,
                            [r_wgc[b], r_uT[tl]], [K.rbank[pg]])
                    for kc in range(KC):
                        _mm(P, K.bank[pu][:], wuc[b][:, kc, :], uT[:, kc, sl], kc == 0, kc == KC - 1,
                            [r_wuc[b], r_uT[tl]], [K.rbank[pu]])
                    s = tl % 2
                    P.op("act", lambda e, s=s, pg=pg: e.activation(out=sg[s][:], in_=K.bank[pg][:], func=AF.Silu),
                         reads=[K.rbank[pg]], writes=[r_sg[s]])
                    P.op("dve", lambda e, s=s, pu=pu, j=j, sl=sl: e.tensor_tensor(out=aT[:, j, sl], in0=sg[s][:],
                                                                                 in1=K.bank[pu][:], op=ALU.mult),
                         reads=[r_sg[s], K.rbank[pu]], writes=[r_aT[j][tl]])
            P.dma("pool", wdc[0][:], dr["wd"][layer, 0], K.dsem_q[5], writes=[r_wdc[0]])
            cnt = 0
            for dc in range(KC):
                b = dc % 2
                if dc + 1 < KC:
                    P.dma("pool", wdc[1 - b][:], dr["wd"][layer, dc + 1], K.dsem_q[5 + (1 - b)], writes=[r_wdc[1 - b]])
                for tl in range(TPG):
                    tg = g * TPG + tl
                    bi = cnt % 2
                    hb = cnt % 4
                    cnt += 1
                    src = out[dc * 128:(dc + 1) * 128, tg * 512:(tg + 1) * 512]
                    P.dma("sp", hs[hb][:], src, ds[14 + hb], reads=[K.r_h[dc][tg]], writes=[r_hs[hb]])
                    for j in range(NJ):
                        _mm(P, K.bank[bi][:], wdc[b][:, j, :], aT[:, j, tl * 512:(tl + 1) * 512], j == 0, j == NJ - 1,
                            [r_wdc[b], r_aT[j][tl]], [K.rbank[bi]])
                    P.op("dve", lambda e, hb=hb, bi=bi: e.tensor_tensor(out=hs[hb][:], in0=hs[hb][:], in1=K.bank[bi][:],
                                                                       op=ALU.add),
                         reads=[K.rbank[bi], r_hs[hb]], writes=[r_hs[hb]])
                    P.dma("sp", src, hs[hb][:], ds[18 + hb], reads=[r_hs[hb]], writes=[K.r_h[dc][tg]])
        P.emit_block(final=final)


def act_rstd(P, out, r_out, in_, r_in, scale, tmp, r_tmp):
    P.op("act", lambda e: e.activation(out=tmp, in_=in_, func=AF.Ln, bias=EPS, scale=scale), reads=[r_in], writes=[r_tmp])
    P.op("act", lambda e: e.activation(out=out, in_=tmp, func=AF.Exp, scale=-0.5), reads=[r_tmp], writes=[r_out])


def rmsnorm_tile2(K, h_t, r_h_t, gain, r_gain, uT, r_u, sq, r_sq, rstd, r_rstd, tmp, r_tmp, bank_i):
    P = K.P
    bank, rb = K.bank[bank_i], K.rbank[bank_i]
    for c in range(KC):
        P.op("pool", lambda e, c=c: e.tensor_tensor(out=sq[:, c, :], in0=h_t[:, c, :], in1=h_t[:, c, :], op=ALU.mult),
             reads=[r_h_t], writes=[r_sq])
    for c in range(KC):
        _mm(P, bank[:], K.ones_bf[:], sq[:, c, :], c == 0, c == KC - 1, [K.r_const, r_sq], [rb])
    act_rstd(P, rstd[:], r_rstd, bank[:], rb, 1.0 / D, tmp[:], r_tmp)
    for c in range(KC):
        P.op("dve", lambda e, c=c: e.scalar_tensor_tensor(out=uT[:, c, :], in0=h_t[:, c, :], scalar=gain[:, c:c + 1],
                                                         in1=rstd[:], op0=ALU.mult, op1=ALU.mult),
             reads=[r_h_t, r_gain, r_rstd], writes=[r_u])


def hview(ap, t):
    return ap.rearrange("(c p) t -> p c t", p=128)[:, :, t * 512:(t + 1) * 512]


def fox_phase(K, layer, hin, final=False):
    nc, P, T, NT, dr = K.nc, K.P, K.T, K.NT, K.dr
    j = layer // 2
    NB = T // 128
    ds = K.dsem_pool
    SCALE = 128 ** -0.5
    with ExitStack() as es:
        def sb(name, shape, dt):
            return es.enter_context(nc.sbuf_tensor(f"x{layer}_{name}", shape, dt))
        w = sb("w", [128, KC, 1540], BF16); r_w = Res()
        gain = sb("gain", [128, KC], F32); r_gain = Res()
        fvec = sb("fvec", [128, 3], F32); r_fvec = Res()
        KT = sb("KT", [128, 4, T], BF16); r_KT = [RL(NT) for _ in range(4)]
        VA = sb("VA", [128, NB, 4, 129], BF16); r_VA = RL(NB); r_VAone = Res()
        Ftok = sb("Ftok", [128, NB, 4], F32); r_Ftok = RL(NB)
        Rq = sb("Rq", [128, NB, 4], F32); r_Rq = RL(NB)
        Bq = sb("Bq", [128, NB, 4], F32); r_Bq = Res()
        ht = [sb(f"ht{i}", [128, KC, 512], F32) for i in range(2)]; r_ht = RL(2)
        uT = sb("uT", [128, KC, 512], BF16); r_uT = Res()
        sq = sb("sq", [128, KC, 512], BF16); r_sq = Res()
        rstd = sb("rstd", [128, 512], F32); r_rstd = Res()
        tmp = sb("tmp", [128, 512], F32); r_tmp = Res()
        qn = sb("qn", [128, 4, 512], BF16); r_qn = RL(4)
        sqh = sb("sqh", [128, 512], BF16); r_sqh = Res()
        rq = sb("rq", [128, 512], F32); r_rq = Res()
        fx = [sb(f"fx{i}", [4, 512], F32) for i in range(4)]; r_fx = RL(4)
        Fc = [sb(f"Fc{i}", [4, 512], F32) for i in range(2)]; r_Fc = RL(2)
        onesf = sb("onesf", [4, 512], F32); r_onesf = Res()
        PT = [sb(f"PT{i}", [128, 512], BF16) for i in range(3)]; r_PT = RL(3)
        ytok = sb("ytok", [128, 4, 512], BF16); r_ytok = RL(4)
        Rt = sb("Rt", [128, 4], F32); r_Rt = Res()
        Boff = sb("Boff", [128, NB, 4], F32); r_Boff = Res()
        Bin = sb("Bin", [128, 4, 4, 4], F32); r_Bin = Res()
        cfac = sb("cfac", [128, 4, 4], F32); r_cfac = Res()
        otmp = sb("otmp", [128, 132], F32); r_otmp = Res()
        nslot = [0]
        rec = sb("rec", [128, 4], F32); r_rec = RL(4)
        yTt = [sb(f"yTt{i}", [128, 4, 512], BF16) for i in range(2)]; r_yTt = RL(2)

        P.dma("pool", w[:, :, 0:768], dr["w_fox"][j][:, :, 0:768], K.dsem_q[0], writes=[r_w])
        P.dma("pool", w[:, :, 768:1540], dr["w_fox"][j][:, :, 768:1540], K.dsem_q[0], writes=[r_w])
        P.dma("sp", gain[:], dr["norm_mix"][layer], ds[1], writes=[r_gain])
        P.dma("sp", fvec[:], dr["fox_vec"][j], ds[30], writes=[r_fvec])
        P.op("pool", lambda e: e.memset(onesf[:], 1.0), writes=[r_onesf])
        P.op("pool", lambda e: e.memset(VA[:, :, :, 128:129], 1.0), writes=[r_VAone])

        P.dma("sp", ht[0][:], hview(hin, 0), ds[2], reads=[K.r_h[c][0] for c in range(KC)], writes=[r_ht[0]])
        npt = 0
        for t in range(NT):
            b = t % 2
            if t + 1 < NT:
                P.dma("sp", ht[1 - b][:], hview(hin, t + 1), ds[2 + (1 - b)],
                      reads=[K.r_h[c][t + 1] for c in range(KC)], writes=[r_ht[1 - b]])
            rmsnorm_tile2(K, ht[b], r_ht[b], gain, r_gain, uT, r_uT, sq, r_sq, rstd, r_rstd, tmp, r_tmp, 7)
            tsl = slice(t * 512, (t + 1) * 512)
            pb, rpb = K.bank[6], K.rbank[6]
            for kc in range(KC):
                _mm(P, pb[0:4, :], w[:, kc, 1536:1540], uT[:, kc, :], kc == 0, kc == KC - 1, [r_w, r_uT], [rpb])
            P.op("dve", lambda e: e.tensor_scalar(out=fx[0][:], in0=pb[0:4, :], scalar1=fvec[0:4, 2:3], scalar2=None,
                                                  op0=ALU.add), reads=[rpb, r_fvec], writes=[r_fx[0]])
            P.op("dve", lambda e: e.scalar_tensor_tensor(out=fx[1][:], in0=fx[0][:], scalar=-1.0, in1=fx[0][:],
                                                         op0=ALU.mult, op1=ALU.min),
                 reads=[r_fx[0]], writes=[r_fx[1]])
            P.op("act", lambda e: e.activation(out=fx[1][:], in_=fx[1][:], func=AF.Exp),
                 reads=[r_fx[1]], writes=[r_fx[1]])
            P.op("act", lambda e: e.activation(out=fx[1][:], in_=fx[1][:], func=AF.Ln, bias=1.0),
                 reads=[r_fx[1]], writes=[r_fx[1]])
            P.op("dve", lambda e: e.tensor_scalar_min(out=fx[2][:], in0=fx[0][:], scalar1=0.0),
                 reads=[r_fx[0]], writes=[r_fx[2]])
            P.op("dve", lambda e: e.tensor_sub(out=fx[3][:], in0=fx[2][:], in1=fx[1][:]),
                 reads=[r_fx[1], r_fx[2]], writes=[r_fx[3]])
            init = 0.0 if t == 0 else Fc[1 - b][:, 511:512]
            P.op("dve", lambda e, b=b, init=init: e.tensor_tensor_scan(out=Fc[b][:], data0=onesf[:], data1=fx[3][:],
                                                                      initial=init, op0=ALU.mult, op1=ALU.add),
                 reads=[r_onesf, r_fx[3], r_Fc[1 - b]], writes=[r_Fc[b]])
            P.dma("sp", K.fd[0:4, tsl], Fc[b][:], ds[4], reads=[r_Fc[b]], writes=[K.r_fd])
            for bl in range(4):
                blk = t * 4 + bl
                c0 = blk * 128
                P.dma("sp", Ftok[:, blk, :], K.fd[0:4, c0:c0 + 128].rearrange("h s -> s h"), ds[12 + bl],
                      reads=[K.r_fd], writes=[r_Ftok[blk]], slow=True)
                P.dma("sp", Rq[:, blk, :], K.fd[0:4, c0 + 64:c0 + 65].rearrange("h o -> o h").broadcast_to([128, 4]),
                      ds[16 + bl], reads=[K.r_fd], writes=[r_Rq[blk]], slow=True)
            for h in range(4):
                for which in range(2):
                    c0 = which * 512 + h * 128
                    for kc in range(KC):
                        _mm(P, pb[:], w[:, kc, c0:c0 + 128], uT[:, kc, :], kc == 0, kc == KC - 1, [r_w, r_uT], [rpb])
                    P.op("act", lambda e: e.activation(out=sqh[:], in_=pb[:], func=AF.Square),
                         reads=[rpb], writes=[r_sqh])
                    nb_, rnb = K.bank[7], K.rbank[7]
                    _mm(P, nb_[:], K.ones_bf[:], sqh[:], True, True, [K.r_const, r_sqh], [rnb])
                    act_rstd(P, rq[:], r_rq, nb_[:], rnb, 1.0 / 128, tmp[:], r_tmp)
                    if which == 0:
                        dst, rd = qn[:, h, :], [r_qn[h]]
                    else:
                        dst, rd = KT[:, h, tsl], [r_KT[h][t]]
                    P.op("dve", lambda e, dst=dst, which=which: e.scalar_tensor_tensor(
                        out=dst, in0=pb[:], scalar=fvec[:, which:which + 1], in1=rq[:], op0=ALU.mult, op1=ALU.mult),
                        reads=[rpb, r_fvec, r_rq], writes=rd)
            for bl in range(4):
                blk = t * 4 + bl
                for kc in range(KC):
                    _mm(P, pb[:], uT[:, kc, bl * 128:(bl + 1) * 128], w[:, kc, 1024:1536], kc == 0, kc == KC - 1,
                        [r_w, r_uT], [rpb])
                P.op("dve", lambda e, blk=blk: e.tensor_copy(out=VA[:, blk, :, 0:128],
                                                            in_=pb[:].rearrange("p (h v) -> p h v", h=4)),
                     reads=[rpb, r_VAone], writes=[r_VA[blk]])
            yb = t % 2
            t4 = t * 4
            if t > 0:
                P.dma("sp", Rt[:], K.fd[0:4, t * 512:t * 512 + 1].rearrange("h o -> o h").broadcast_to([128, 4]),
                      ds[20], reads=[K.r_fd], writes=[r_Rt], slow=True)
                P.op("dve", lambda e, t4=t4: e.tensor_tensor(
                    out=Boff[:, 0:t4, :], in0=Rt[:].unsqueeze(1).to_broadcast([128, t4, 4]), in1=Ftok[:, 0:t4, :],
                    op=ALU.subtract), reads=[r_Rt] + r_Ftok[0:t4], writes=[r_Boff])
                P.op("dve", lambda e, t4=t4: e.tensor_tensor(
                    out=cfac[:], in0=Rq[:, t4:t4 + 4, :], in1=Rt[:].unsqueeze(1).to_broadcast([128, 4, 4]),
                    op=ALU.subtract), reads=[r_Rt] + r_Rq[t4:t4 + 4], writes=[r_cfac])
                P.op("act", lambda e: e.activation(out=cfac[:], in_=cfac[:], func=AF.Exp), reads=[r_cfac], writes=[r_cfac])
            P.op("dve", lambda e, t4=t4: e.tensor_tensor(
                out=Bin[:], in0=Rq[:, t4:t4 + 4, :].unsqueeze(1).to_broadcast([128, 4, 4, 4]),
                in1=Ftok[:, t4:t4 + 4, :].unsqueeze(2).to_broadcast([128, 4, 4, 4]), op=ALU.subtract),
                reads=r_Rq[t4:t4 + 4] + r_Ftok[t4:t4 + 4], writes=[r_Bin])

            units = []
            for h in range(4):
                for kb in range(t4):
                    units.append(("off", h, kb))
                for kl in range(4):
                    units.append(("in", h, kl))
            started = {}

            def oreg(h, r):
                bi = (2 if h % 2 == 0 else 5) + r // 3
                c0 = (r % 3) * 129
                return bi, K.bank[bi][:, c0:c0 + 129], K.rbank[bi]

            def pv(h, r, lhsT, rhs, reads):
                bi, reg, rb = oreg(h, r)
                first = not started.get((h, bi), False)
                started[(h, bi)] = True
                _mm(P, reg, lhsT, rhs, first, False, reads, [rb], skip_group_check=True)

            def emit_scores(u, slot):
                kind, h, kk = u
                sbk, rsb = K.bank[slot % 2], K.rbank[slot % 2]
                if kind == "off":
                    _mm(P, sbk[:], KT[:, h, kk * 128:(kk + 1) * 128], qn[:, h, :], True, True,
                        [r_KT[h][kk // 4], r_qn[h]], [rsb])
                else:
                    kb = t4 + kk
                    n = (4 - kk) * 128
                    _mm(P, sbk[:, 0:n], KT[:, h, kb * 128:(kb + 1) * 128], qn[:, h, kk * 128:512], True, True,
                        [r_KT[h][t], r_qn[h]], [rsb])

            def emit_exp(u, slot):
                kind, h, kk = u
                sbk, rsb = K.bank[slot % 2], K.rbank[slot % 2]
                pt, rpt = PT[slot % 3], r_PT[slot % 3]
                if kind == "off":
                    P.op("act", lambda e: e.activation(out=pt[:], in_=sbk[:], func=AF.Exp, bias=Boff[:, kk, h:h + 1],
                                                       scale=SCALE), reads=[rsb, r_Boff], writes=[rpt])
                else:
                    for ql in range(kk, 4):
                        i = ql - kk
                        P.op("act", lambda e, i=i, ql=ql: e.activation(
                            out=pt[:, i * 128:(i + 1) * 128], in_=sbk[:, i * 128:(i + 1) * 128], func=AF.Exp,
                            bias=Bin[:, kk, ql, h:h + 1], scale=SCALE), reads=[rsb, r_Bin], writes=[rpt])
                    P.op("pool", lambda e: e.tensor_tensor(out=pt[:, 0:128], in0=pt[:, 0:128], in1=K.cmask_bf[:], op=ALU.mult),
                         reads=[rpt, K.r_const], writes=[rpt])

            def emit_pv(u, slot):
                kind, h, kk = u
                pt, rpt = PT[slot % 3], r_PT[slot % 3]
                if kind == "off":
                    for ql in range(4):
                        pv(h, ql, pt[:, ql * 128:(ql + 1) * 128], VA[:, kk, h, :], [rpt, r_VA[kk], r_VAone])
                else:
                    kb = t4 + kk
                    for ql in range(kk, 4):
                        i = ql - kk
                        pv(h, 4 + ql, pt[:, i * 128:(i + 1) * 128], VA[:, kb, h, :], [rpt, r_VA[kb], r_VAone])
                    if kk == 3:
                        finish_head(h)

            def finish_head(h):
                for ql in range(4):
                    _, oin, rin = oreg(h, 4 + ql)
                    if t > 0:
                        _, oof, rof = oreg(h, ql)
                        P.op("dve", lambda e, oof=oof, ql=ql: e.tensor_scalar(
                            out=otmp[:, 0:129], in0=oof, scalar1=cfac[:, ql, h:h + 1], scalar2=None, op0=ALU.mult),
                            reads=[rof, r_cfac], writes=[r_otmp])
                        P.op("dve", lambda e, oin=oin: e.tensor_tensor(out=otmp[:, 0:129], in0=otmp[:, 0:129], in1=oin, op=ALU.add),
                             reads=[rin, r_otmp], writes=[r_otmp])
                        src, rsrc = otmp[:, 0:129], [r_otmp]
                    else:
                        src, rsrc = oin, [rin]
                    P.op("dve", lambda e, src=src: e.reciprocal(out=rec[:, h:h + 1], in_=src[:, 128:129]),
                         reads=rsrc, writes=[r_rec[h]])
                    P.op("dve", lambda e, src=src, ql=ql: e.tensor_scalar(
                        out=ytok[:, ql, h * 128:(h + 1) * 128], in0=src[:, 0:128], scalar1=rec[:, h:h + 1], scalar2=None,
                        op0=ALU.mult), reads=rsrc + [r_rec[h]], writes=[r_ytok[ql]])

            prev = None
            for u in units:
                slot = nslot[0]
                nslot[0] += 1
                emit_scores(u, slot)
                if prev is not None:
                    emit_pv(*prev)
                emit_exp(u, slot)
                prev = (u, slot)
            emit_pv(*prev)
            tb = pb[:].bitcast(BF16)
            for ql in range(4):
                for h in range(4):
                    P.op("pe", lambda e, h=h, ql=ql: e.transpose(out=tb[:, h * 128:(h + 1) * 128],
                                                                in_=ytok[:, ql, h * 128:(h + 1) * 128], identity=K.ident_bf[:]),
                         reads=[r_ytok[ql], K.r_const], writes=[rpb])
                P.op("dve", lambda e, ql=ql, yb=yb: e.tensor_copy(
                    out=yTt[yb][:, :, ql * 128:(ql + 1) * 128], in_=tb[:, 0:512].rearrange("p (h s) -> p h s", h=4)),
                    reads=[rpb], writes=[r_yTt[yb]])
            dst = K.ybuf.rearrange("(c p) t -> p c t", p=128)[:, 4:8, tsl]
            P.dma("sp", dst, yTt[yb][:], ds[7 + yb], reads=[r_yTt[yb]], writes=[K.r_y[c][t] for c in range(4, 8)])
        P.emit_block(final=final)


def hg_phase(K, layer, hin, kind, final=False):
    nc, P, T, NT, dr = K.nc, K.P, K.T, K.NT, K.dr
    j = layer // 2
    ds = K.dsem_pool
    ret = kind == "ret"
    NCOL = 3072 if ret else 2048
    wname = "w_ret" if ret else "w_hg"
    with ExitStack() as es:
        def sb(name, shape, dt):
            return es.enter_context(nc.sbuf_tensor(f"g{layer}_{name}", shape, dt))
        w = sb("w", [128, KC, NCOL], BF16); r_w = Res()
        gain = sb("gain", [128, KC], F32); r_gain = Res()
        vec = sb("vec", [128, 12], F32); r_vec = Res()
        ht = [sb(f"ht{i}", [128, KC, 512], F32) for i in range(2)]; r_ht = RL(2)
        uT = [sb(f"uT{i}", [128, KC, 512], BF16) for i in range(2)]; r_uT = RL(2)
        sq = sb("sq", [128, KC, 512], BF16); r_sq = Res()
        rstd = sb("rstd", [128, 512], F32); r_rstd = Res()
        tmp = sb("tmp", [128, 512], F32); r_tmp = Res()
        rmask = sb("rmask", [128, 512], F32); r_rmask = Res()
        m2 = sb("m2", [128, 64], F32); r_m2 = Res()
        qf = sb("qf", [128, 512], F32); r_qf = Res()
        kf = sb("kf", [128, 512], F32); r_kf = Res()
        lf = sb("lf", [128, 512], F32); r_lf = Res()
        cum = sb("cum", [128, 512], F32); r_cum = Res()
        e1 = sb("e1", [128, 512], F32); r_e1 = Res()
        ex = [sb(f"ex{i}", [128, 512], F32) for i in range(2)]; r_ex = RL(2)
        qh = [sb(f"qh{i}", [128, 512], BF16) for i in range(2)]; r_qh = RL(2)
        kh = [sb(f"kh{i}", [128, 512], BF16) for i in range(2)]; r_kh = RL(2)
        qi = [sb(f"qi{i}", [128, 512], BF16) for i in range(2)]; r_qi = RL(2)
        ko = [sb(f"ko{i}", [128, 512], BF16) for i in range(2)]; r_ko = RL(2)
        alast = [sb(f"alast{i}", [128, 8], F32) for i in range(2)]; r_alast = RL(2)
        kotok = [sb(f"kotok{i}", [128, 4, 128], BF16) for i in range(2)]; r_kotok = RL(2)
        vtok = [sb(f"vtok{i}", [128, 4, 512], BF16) for i in range(2)]; r_vtok = RL(2)
        sgt = [sb(f"sgt{i}", [128, 4, 512], BF16) for i in range(2)]; r_sgt = RL(2)
        PT = sb("PT", [128, 4, 64], BF16); r_PT = Res()
        S_f = sb("S_f", [128, 4, 2, 128], F32); r_Sf = [RL(2) for _ in range(4)]
        S_b = sb("S_b", [128, 4, 8, 128], BF16); r_Sb = [RL(8) for _ in range(4)]
        sqo = sb("sqo", [128, 512], BF16); r_sqo = Res()
        cen = sb("cen", [128, 512], F32); r_cen = Res()
        cn = sb("cn", [128, 512], F32); r_cn = Res()
        rstd2 = sb("rstd2", [128, 512], F32); r_rstd2 = Res()
        tmp2 = sb("tmp2", [128, 512], F32); r_tmp2 = Res()
        yTt = [sb(f"yTt{i}", [128, 4, 512], BF16) for i in range(2)]; r_yTt = RL(2)
        if ret:
            rot = [sb(f"rot{i}", [128, 2, 512], F32) for i in range(2)]; r_rot = RL(2)
            rtab = sb("rtab", [128, 4, 4, 64], F32); r_rtab = Res()
            rtmp = sb("rtmp", [128, 3, 64], F32); r_rtmp = Res()
            ral = sb("ral", [128, 4, 8], F32); r_ral = Res()
            meanb = sb("meanb", [128, 128], BF16); r_meanb = Res()

        P.dma("pool", w[:, :, 0:1024], dr[wname][j][:, :, 0:1024], K.dsem_q[0], writes=[r_w])
        P.dma("pool", w[:, :, 1024:2048], dr[wname][j][:, :, 1024:2048], K.dsem_q[0], writes=[r_w])
        if ret:
            P.dma("pool", w[:, :, 2048:3072], dr[wname][j][:, :, 2048:3072], K.dsem_q[0], writes=[r_w])
        P.dma("sp", gain[:], dr["norm_mix"][layer], ds[1], writes=[r_gain])
        if ret:
            P.dma("sp", vec[:], dr["ret_vec"][j], ds[30], writes=[r_vec])
        P.dma("sp", rmask[:], dr["rmask"], ds[31], writes=[r_rmask])
        P.op("dve", lambda e: e.tensor_copy(out=m2[0:64, :], in_=K.cmask_f[0:64, 0:64]), reads=[K.r_const], writes=[r_m2])
        P.op("dve", lambda e: e.tensor_copy(out=m2[64:128, :], in_=K.cmask_f[64:128, 64:128]), reads=[K.r_const, r_m2],
             writes=[r_m2])
        if not ret:
            raw = sb("raw", [128, 12], F32); r_raw = Res()
            P.dma("sp", raw[:], dr["hg_vec"][j], ds[32], writes=[r_raw])
            if j == 0:
                P.op("dve", lambda e: e.tensor_scalar(out=vec[:, 0:4], in0=raw[:, 0:4], scalar1=0.0, scalar2=None,
                                                      op0=ALU.mult), reads=[r_raw, r_vec], writes=[r_vec])
            else:
                P.op("dve", lambda e: e.tensor_tensor(out=vec[:, 0:4], in0=raw[:, 4:8], in1=raw[:, 0:4], op=ALU.subtract),
                     reads=[r_raw, r_vec], writes=[r_vec])
                P.op("act", lambda e: e.activation(out=vec[:, 0:4], in_=vec[:, 0:4], func=AF.Sigmoid),
                     reads=[r_vec], writes=[r_vec])
            P.op("dve", lambda e: e.tensor_scalar(out=vec[:, 4:8], in0=vec[:, 0:4], scalar1=-1.0, scalar2=1.0,
                                                  op0=ALU.mult, op1=ALU.add), reads=[r_vec], writes=[r_vec])
            P.op("dve", lambda e: e.tensor_copy(out=vec[:, 8:9], in_=raw[:, 8:9]), reads=[r_raw, r_vec], writes=[r_vec])
        P.op("pool", lambda e: e.memset(S_f[:], 0.0), writes=[x for l in r_Sf for x in l])
        P.op("pool", lambda e: e.memset(S_b[:], 0.0), writes=[x for l in r_Sb for x in l])

        def decay_factors(cum3, n, r_c, outs, al_out, r_al):
            n64 = n * 64
            e13 = e1[:, 0:n64].rearrange("p (c s) -> p c s", s=64)
            P.op("dve", lambda e: e.tensor_tensor(out=e13, in0=cum3, in1=cum3[:, :, 31:32].to_broadcast([128, n, 64]),
                                                  op=ALU.subtract), reads=[r_c], writes=[r_e1])
            P.op("act", lambda e: e.activation(out=ex[0][:, 0:n64], in_=e1[:, 0:n64], func=AF.Exp), reads=[r_e1], writes=[r_ex[0]])
            outs[0](ex[0][:, 0:n64], r_ex[0])
            P.op("act", lambda e: e.activation(out=ex[1][:, 0:n64], in_=e1[:, 0:n64], func=AF.Exp, scale=-1.0),
                 reads=[r_e1], writes=[r_ex[1]])
            outs[1](ex[1][:, 0:n64], r_ex[1])
            yield
            P.op("act", lambda e: e.activation(out=ex[0][:, 0:n64].rearrange("p (c s) -> p c s", s=64), in_=cum3, func=AF.Exp),
                 reads=[r_c], writes=[r_ex[0]])
            outs[2](ex[0][:, 0:n64], r_ex[0])
            P.op("dve", lambda e: e.tensor_tensor(out=e13, in0=cum3, in1=cum3[:, :, 63:64].to_broadcast([128, n, 64]),
                                                  op=ALU.subtract), reads=[r_c], writes=[r_e1])
            P.op("act", lambda e: e.activation(out=ex[1][:, 0:n64], in_=e1[:, 0:n64], func=AF.Exp, scale=-1.0),
                 reads=[r_e1], writes=[r_ex[1]])
            outs[3](ex[1][:, 0:n64], r_ex[1])
            P.op("act", lambda e: e.activation(out=al_out, in_=cum3[:, :, 63:64], func=AF.Exp), reads=[r_c], writes=[r_al])
            yield

        if ret:
            P.op("pool", lambda e: e.memset(meanb[:], 1.0 / 128), writes=[r_meanb])
            for h in range(4):
                P.op("dve", lambda e, h=h: e.tensor_scalar(out=rtmp[:, 0, :], in0=rmask[:, 0:64], scalar1=0.0,
                                                          scalar2=vec[:, h:h + 1], op0=ALU.mult, op1=ALU.add),
                     reads=[r_rmask, r_vec, r_rtmp], writes=[r_rtmp])
                P.op("dve", lambda e: e.tensor_tensor_scan(out=cum[:, 0:64], data0=rmask[:, 0:64], data1=rtmp[:, 0, :],
                                                           initial=0.0, op0=ALU.mult, op1=ALU.add),
                     reads=[r_rmask, r_rtmp], writes=[r_cum])

                def mk(i, h=h):
                    def f(ap, r):
                        P.op("dve", lambda e: e.tensor_copy(out=rtab[:, h, i, :], in_=ap), reads=[r, r_rtab], writes=[r_rtab])
                    return f
                for _ in decay_factors(cum[:, 0:64].rearrange("p (c s) -> p c s", s=64), 1, r_cum, [mk(0), mk(1), mk(2), mk(3)],
                                       ral[:, h, 0:1].rearrange("p (c o) -> p c o", o=1), r_ral):
                    pass
                P.op("dve", lambda e, h=h: e.tensor_copy(out=ral[:, h, 1:8], in_=ral[:, h, 0:1].to_broadcast([128, 7])),
                     reads=[r_ral], writes=[r_ral])

        pb, rpb = K.bank[6], K.rbank[6]
        tb = pb[:].bitcast(BF16)
        gbk, rgb = K.bank[1], K.rbank[1]
        sbk, rsb = K.bank[0], K.rbank[0]

        def proj_fm(bank, rbank, b, c0):
            for kc in range(KC):
                _mm(P, bank[:], w[:, kc, c0:c0 + 128], uT[b][:, kc, :], kc == 0, kc == KC - 1, [r_w, r_uT[b]], [rbank])

        def tile_prologue(t):
            b = t % 2
            tsl = slice(t * 512, (t + 1) * 512)
            if t == 0:
                P.dma("sp", ht[0][:], hview(hin, 0), ds[2], reads=[K.r_h[c][0] for c in range(KC)], writes=[r_ht[0]])
            if t + 1 < NT:
                P.dma("sp", ht[1 - b][:], hview(hin, t + 1), ds[2 + (1 - b)],
                      reads=[K.r_h[c][t + 1] for c in range(KC)], writes=[r_ht[1 - b]])
            if ret:
                P.dma("sp", rot[b][:], dr["rot_tab"][:, :, tsl], ds[4 + b], writes=[r_rot[b]])
            rmsnorm_tile2(K, ht[b], r_ht[b], gain, r_gain, uT[b], r_uT[b], sq, r_sq, rstd, r_rstd, tmp, r_tmp, 7)
            yield
            for bl in range(4):
                for kc in range(KC):
                    _mm(P, pb[:], uT[b][:, kc, bl * 128:(bl + 1) * 128], w[:, kc, 1024:1536], kc == 0, kc == KC - 1,
                        [r_w, r_uT[b]], [rpb])
                P.op("act", lambda e, bl=bl, b=b: e.copy(out=vtok[b][:, bl, :], in_=pb[:]), reads=[rpb], writes=[r_vtok[b]])
                yield
            for h in range(4):
                proj_fm(gbk, rgb, b, 1536 + h * 128)
                P.op("act", lambda e, h=h, b=b: e.activation(out=sgt[b][:, h, :], in_=gbk[:], func=AF.Silu),
                     reads=[rgb], writes=[r_sgt[b]])
                yield

        def stageA(t, h, s):
            b = t % 2
            if h == 0:
                yield from tile_prologue(t)
            if ret:
                for which, dstf, rdst in ((0, qf, r_qf), (1, kf, r_kf)):
                    proj_fm(pb, rpb, b, which * 512 + h * 128)
                    P.op("dve", lambda e, b=b: e.tensor_tensor(out=e1[:], in0=pb[:], in1=rot[b][:, 0, :], op=ALU.mult),
                         reads=[rpb, r_rot[b]], writes=[r_e1])
                    proj_fm(pb, rpb, b, 2048 + which * 512 + h * 128)
                    P.op("dve", lambda e, b=b: e.tensor_tensor(out=cum[:], in0=pb[:], in1=rot[b][:, 1, :], op=ALU.mult),
                         reads=[rpb, r_rot[b]], writes=[r_cum])
                    P.op("pool", lambda e, dstf=dstf: e.tensor_tensor(out=dstf[:], in0=e1[:], in1=cum[:], op=ALU.add),
                         reads=[r_e1, r_cum], writes=[rdst])
                    yield

                def tabv(i):
                    return rtab[:, h, i:i + 1, :].to_broadcast([128, 8, 64])

                def v3(a):
                    return a.rearrange("p (c s) -> p c s", s=64)
                P.op("pool", lambda e: e.tensor_tensor(out=v3(qh[s][:]), in0=v3(qf[:]), in1=tabv(0), op=ALU.mult),
                     reads=[r_qf, r_rtab], writes=[r_qh[s]])
                P.op("dve", lambda e: e.scalar_tensor_tensor(out=v3(kh[s][:]), in0=v3(kf[:]), scalar=0.125, in1=tabv(1),
                                                             op0=ALU.mult, op1=ALU.mult), reads=[r_kf, r_rtab], writes=[r_kh[s]])
                yield
                P.op("pool", lambda e: e.tensor_tensor(out=v3(qi[s][:]), in0=v3(qf[:]), in1=tabv(2), op=ALU.mult),
                     reads=[r_qf, r_rtab], writes=[r_qi[s]])
                P.op("dve", lambda e: e.scalar_tensor_tensor(out=v3(ko[s][:]), in0=v3(kf[:]), scalar=0.125, in1=tabv(3),
                                                             op0=ALU.mult, op1=ALU.mult), reads=[r_kf, r_rtab], writes=[r_ko[s]])
                al, r_al = ral[:, h, :], r_ral
                yield
            else:
                proj_fm(pb, rpb, b, h * 128)
                P.op("act", lambda e: e.copy(out=qf[:], in_=pb[:]), reads=[rpb], writes=[r_qf])
                proj_fm(pb, rpb, b, 512 + h * 128)
                P.op("act", lambda e: e.activation(out=lf[:], in_=pb[:], func=AF.Exp, scale=-1.0), reads=[rpb], writes=[r_lf])
                P.op("act", lambda e: e.activation(out=lf[:], in_=lf[:], func=AF.Ln, bias=1.0), reads=[r_lf], writes=[r_lf])
                yield
                P.op("act", lambda e: e.activation(out=lf[:], in_=lf[:], func=AF.Exp, scale=-1.0), reads=[r_lf], writes=[r_lf])
                P.op("dve", lambda e: e.tensor_scalar(out=lf[:], in0=lf[:], scalar1=vec[:, 4 + h:5 + h],
                                                      scalar2=vec[:, h:h + 1], op0=ALU.mult, op1=ALU.add),
                     reads=[r_lf, r_vec], writes=[r_lf])
                P.op("pool", lambda e: e.tensor_scalar(out=kf[:], in0=lf[:], scalar1=-1.0, scalar2=1.0,
                                                       op0=ALU.mult, op1=ALU.add), reads=[r_lf], writes=[r_kf])
                P.op("act", lambda e: e.activation(out=lf[:], in_=lf[:], func=AF.Ln), reads=[r_lf], writes=[r_lf])
                yield
                P.op("dve", lambda e: e.tensor_tensor_scan(out=cum[:], data0=rmask[:], data1=lf[:], initial=0.0,
                                                           op0=ALU.mult, op1=ALU.add),
                     reads=[r_rmask, r_lf], writes=[r_cum])
                yield

                def o_qh(ap, r):
                    P.op("pool", lambda e: e.tensor_tensor(out=qh[s][:], in0=qf[:], in1=ap, op=ALU.mult),
                         reads=[r_qf, r], writes=[r_qh[s]])

                def o_kh(ap, r):
                    P.op("dve", lambda e: e.tensor_tensor(out=kh[s][:], in0=kf[:], in1=ap, op=ALU.mult),
                         reads=[r_kf, r], writes=[r_kh[s]])

                def o_qi(ap, r):
                    P.op("pool", lambda e: e.tensor_tensor(out=qi[s][:], in0=qf[:], in1=ap, op=ALU.mult),
                         reads=[r_qf, r], writes=[r_qi[s]])

                def o_ko(ap, r):
                    P.op("dve", lambda e: e.tensor_tensor(out=ko[s][:], in0=kf[:], in1=ap, op=ALU.mult),
                         reads=[r_kf, r], writes=[r_ko[s]])
                yield from decay_factors(cum[:].rearrange("p (c s) -> p c s", s=64), 8, r_cum, [o_qh, o_kh, o_qi, o_ko],
                                         alast[s][:].rearrange("p (c o) -> p c o", o=1), r_alast[s])
                al, r_al = alast[s][:], r_alast[s]
            K._al[(t, h)] = (al, r_al)
            for bl in range(4):
                P.op("pe", lambda e, bl=bl: e.transpose(out=tb[:, bl * 128:(bl + 1) * 128],
                                                        in_=ko[s][:, bl * 128:(bl + 1) * 128], identity=K.ident_bf[:]),
                     reads=[r_ko[s], K.r_const], writes=[rpb])
            P.op("act", lambda e: e.copy(out=kotok[s][:].rearrange("p b d -> p (b d)"), in_=tb[:, 0:512]),
                 reads=[rpb], writes=[r_kotok[s]])
            yield

        def stageBC(t, h, s):
            b = t % 2
            yb = t % 2
            hs = slice(h * 128, (h + 1) * 128)
            al, r_al = K._al[(t, h)]
            ob, rob = K.bank[2 + (h % 2)], K.rbank[2 + (h % 2)]
            for c in range(8):
                bl, p0 = c // 2, (c % 2) * 64
                dbk, rdb = K.bank[4 + (c % 2)], K.rbank[4 + (c % 2)]
                _mm(P, dbk[:, (c // 2) * 128:(c // 2 + 1) * 128], kotok[s][p0:p0 + 64, bl, :], vtok[b][p0:p0 + 64, bl, hs],
                    True, True, [r_kotok[s], r_vtok[b]], [rdb])
            yield
            for c in range(8):
                p0 = (c % 2) * 64
                csl = slice(c * 64, (c + 1) * 64)
                _mm(P, sbk[p0:p0 + 64, (c // 2) * 64:(c // 2 + 1) * 64], kh[s][:, csl], qh[s][:, csl], True, True,
                    [r_kh[s], r_qh[s]], [rsb])
            P.op("dve", lambda e: e.tensor_tensor(out=PT[:], in0=sbk[:, 0:256].rearrange("p (c s) -> p c s", s=64),
                                                  in1=m2[:].unsqueeze(1).to_broadcast([128, 4, 64]), op=ALU.mult),
                 reads=[rsb, r_m2], writes=[r_PT])
            yield
            for c in range(8):
                dbk, rdb = K.bank[4 + (c % 2)], K.rbank[4 + (c % 2)]
                src, dst = (c + 1) % 2, c % 2
                P.op("dve", lambda e, c=c, src=src, dst=dst, dbk=dbk: e.scalar_tensor_tensor(
                    out=S_f[:, h, dst, :], in0=S_f[:, h, src, :], scalar=al[:, c:c + 1],
                    in1=dbk[:, (c // 2) * 128:(c // 2 + 1) * 128], op0=ALU.mult, op1=ALU.add),
                    reads=[r_Sf[h][src], r_al, rdb], writes=[r_Sf[h][dst]])
                if c < 7:
                    P.op("pool", lambda e, c=c, dst=dst: e.tensor_copy(out=S_b[:, h, c + 1, :], in_=S_f[:, h, dst, :]),
                         reads=[r_Sf[h][dst]], writes=[r_Sb[h][c + 1]])
                if c % 2 == 1:
                    yield
            for c in range(8):
                bl, p0 = c // 2, (c % 2) * 64
                csl = slice(c * 64, (c + 1) * 64)
                _mm(P, ob[:, csl], vtok[b][p0:p0 + 64, bl, hs], PT[p0:p0 + 64, c // 2, :], True, False,
                    [r_vtok[b], r_PT], [rob])
                _mm(P, ob[:, csl], S_b[:, h, c, :], qi[s][:, csl], False, True, [r_Sb[h][c], r_qi[s]], [rob])
                if c % 2 == 1:
                    yield
            P.op("pool", lambda e: e.tensor_copy(out=S_b[:, h, 0, :], in_=S_f[:, h, 1, :]),
                 reads=[r_Sf[h][1]], writes=[r_Sb[h][0]])
            nb_, rnb = K.bank[7], K.rbank[7]
            if ret:
                P.op("act", lambda e: e.copy(out=sqo[:], in_=ob[:]), reads=[rob], writes=[r_sqo])
                _mm(P, nb_[:], meanb[:], sqo[:], True, True, [r_meanb, r_sqo], [rnb])
                P.op("act", lambda e: e.copy(out=tmp2[:], in_=nb_[:]), reads=[rnb], writes=[r_tmp2])
                yield
                P.op("dve", lambda e: e.tensor_tensor(out=cen[:], in0=ob[:], in1=tmp2[:], op=ALU.subtract),
                     reads=[rob, r_tmp2], writes=[r_cen])
                P.op("act", lambda e: e.activation(out=sqo[:], in_=cen[:], func=AF.Square), reads=[r_cen], writes=[r_sqo])
                osrc, r_osrc, nscale = cen[:], r_cen, 1.0
                _mm(P, nb_[:], meanb[:], sqo[:], True, True, [r_meanb, r_sqo], [rnb])
                nwcol = vec[:, 4 + h:5 + h]
            else:
                P.op("act", lambda e: e.activation(out=sqo[:], in_=ob[:], func=AF.Square), reads=[rob], writes=[r_sqo])
                osrc, r_osrc, nscale = ob[:], rob, 1.0 / 128
                _mm(P, nb_[:], K.ones_bf[:], sqo[:], True, True, [K.r_const, r_sqo], [rnb])
                nwcol = vec[:, 8:9]
            yield
            act_rstd(P, rstd2[:], r_rstd2, nb_[:], rnb, nscale, tmp2[:], r_tmp2)
            P.op("dve", lambda e: e.scalar_tensor_tensor(out=cn[:], in0=osrc, scalar=nwcol, in1=rstd2[:], op0=ALU.mult,
                                                         op1=ALU.mult), reads=[r_osrc, r_vec, r_rstd2], writes=[r_cn])
            P.op("pool", lambda e: e.tensor_tensor(out=yTt[yb][:, h, :], in0=cn[:], in1=sgt[b][:, h, :], op=ALU.mult),
                 reads=[r_cn, r_sgt[b]], writes=[r_yTt[yb]])
            yield
            if h == 3:
                tsl = slice(t * 512, (t + 1) * 512)
                dst = K.ybuf.rearrange("(c p) t -> p c t", p=128)[:, 0:4, tsl]
                P.dma("sp", dst, yTt[yb][:], ds[7 + yb], reads=[r_yTt[yb]], writes=[K.r_y[c][t] for c in range(4)])

        K._al = {}
        items = [(t, h) for t in range(NT) for h in range(4)]
        for _ in stageA(items[0][0], items[0][1], 0):
            pass
        for i, (t, h) in enumerate(items):
            gens = [stageBC(t, h, i % 2)]
            if i + 1 < len(items):
                gens.append(stageA(items[i + 1][0], items[i + 1][1], (i + 1) % 2))
            while gens:
                for g in list(gens):
                    try:
                        next(g)
                    except StopIteration:
                        gens.remove(g)
        P.emit_block(final=final)


def ssd_phase(K, layer, hin, final=False):
    nc, P, T, NT, dr = K.nc, K.P, K.T, K.NT, K.dr
    j = layer // 2
    ds = K.dsem_pool
    with ExitStack() as es:
        def sb(name, shape, dt):
            return es.enter_context(nc.sbuf_tensor(f"s{layer}_{name}", shape, dt))
        w = sb("w", [128, KC, 1288], BF16); r_w = Res()
        gain = sb("gain", [128, KC], F32); r_gain = Res()
        rows = sb("rows", [128, 536], F32); r_rows = Res()
        cw = sb("cw", [128, 6, 5], F32); r_cw = Res()
        smask = sb("smask", [128, 128], F32); r_smask = Res()
        onesF = sb("onesF", [128, 128], F32); r_onesF = Res()
        negA = sb("negA", [128, 8], F32); r_negA = Res()
        ht = [sb(f"ht{i}", [128, KC, 512], F32) for i in range(2)]; r_ht = RL(2)
        uT = sb("uT", [128, KC, 512], BF16); r_uT = Res()
        sq = sb("sq", [128, KC, 512], BF16); r_sq = Res()
        rstd = sb("rstd", [128, 512], F32); r_rstd = Res()
        tmp = sb("tmp", [128, 512], F32); r_tmp = Res()
        xpad = sb("xpad", [128, 6, 515], F32); r_xpad = RL(6)
        acc = sb("acc", [128, 512], F32); r_acc = Res()
        xc = sb("xc", [128, 6, 512], BF16); r_xc = RL(6)
        vtok = sb("vtok", [128, 512], BF16); r_vtok = Res()
        btok = sb("btok", [128, 128], BF16); r_btok = Res()
        sz = sb("sz", [128, 512], F32); r_sz = Res()
        sm = [sb(f"sm{i}", [128, 8], F32) for i in range(8)]; r_sm = RL(8)
        sm16 = sb("sm16", [128, 16], F32); r_sm16 = Res()
        vp = sb("vp", [128, 512], BF16); r_vp = Res()
        vpp = sb("vpp", [128, 512], BF16); r_vpp = Res()
        LM = sb("LM", [128, 8, 128], F32); r_LM = Res()
        E = sb("E", [128, 8, 128], F32); r_E = Res()
        GM = sb("GM", [128, 2, 128], F32); r_GM = Res()
        PT = sb("PT", [128, 8, 128], BF16); r_PT = Res()
        o1 = sb("o1", [128, 512], F32); r_o1 = Res()
        o2 = sb("o2", [128, 512], F32); r_o2 = Res()
        ss = sb("ss", [128, 2], F32); r_ss = Res()
        ss2 = sb("ss2", [128, 2], F32); r_ss2 = Res()
        S_f = sb("S_f", [128, 512], F32); r_Sf = Res()
        S_b = sb("S_b", [128, 512], BF16); r_Sb = Res()
        ytok = sb("ytok", [128, 512], BF16); r_ytok = Res()
        yTt = [sb(f"yTt{i}", [128, 4, 512], BF16) for i in range(2)]; r_yTt = RL(2)

        P.dma("pool", w[:, :, 0:768], dr["w_ssd"][j][:, :, 0:768], K.dsem_q[0], writes=[r_w])
        P.dma("pool", w[:, :, 768:1288], dr["w_ssd"][j][:, :, 768:1288], K.dsem_q[0], writes=[r_w])
        P.dma("sp", gain[:], dr["norm_mix"][layer], ds[1], writes=[r_gain])
        P.dma("sp", rows[:], dr["ssd_rows"][j], ds[30], writes=[r_rows])
        P.dma("sp", cw[:], dr["ssd_conv"][j], ds[31], writes=[r_cw])
        P.dma("sp", smask[:], dr["smask"], ds[32], writes=[r_smask])
        P.op("pool", lambda e: e.memset(onesF[:], 1.0), writes=[r_onesF])
        P.op("pool", lambda e: e.memset(S_f[:], 0.0), writes=[r_Sf])
        P.op("pool", lambda e: e.memset(S_b[:], 0.0), writes=[r_Sb])
        P.op("pool", lambda e: e.memset(xpad[:], 0.0), writes=r_xpad)
        P.op("act", lambda e: e.activation(out=negA[:], in_=rows[:, 8:16], func=AF.Exp), reads=[r_rows], writes=[r_negA])
        P.op("dve", lambda e: e.tensor_scalar(out=negA[:], in0=negA[:], scalar1=-1.0, scalar2=None, op0=ALU.mult),
             reads=[r_negA], writes=[r_negA])

        pb, rpb = K.bank[6], K.rbank[6]
        tb = pb[:].bitcast(BF16)
        P.dma("sp", ht[0][:], hview(hin, 0), ds[2], reads=[K.r_h[c][0] for c in range(KC)], writes=[r_ht[0]])
        for t in range(NT):
            b = t % 2
            yb = t % 2
            tsl = slice(t * 512, (t + 1) * 512)
            if t + 1 < NT:
                P.dma("sp", ht[1 - b][:], hview(hin, t + 1), ds[2 + (1 - b)],
                      reads=[K.r_h[c][t + 1] for c in range(KC)], writes=[r_ht[1 - b]])
            rmsnorm_tile2(K, ht[b], r_ht[b], gain, r_gain, uT, r_uT, sq, r_sq, rstd, r_rstd, tmp, r_tmp, 7)
            for cc in range(6):
                c0 = 512 + cc * 128
                for kc in range(KC):
                    _mm(P, pb[:], w[:, kc, c0:c0 + 128], uT[:, kc, :], kc == 0, kc == KC - 1, [r_w, r_uT], [rpb])
                P.op("dve", lambda e, cc=cc: e.tensor_copy(out=xpad[:, cc, 3:515], in_=pb[:]), reads=[rpb], writes=[r_xpad[cc]])
                P.op("dve", lambda e, cc=cc: e.tensor_scalar(out=acc[:], in0=xpad[:, cc, 0:512], scalar1=cw[:, cc, 0:1],
                                                            scalar2=None, op0=ALU.mult),
                     reads=[r_xpad[cc], r_cw], writes=[r_acc])
                for k in range(1, 4):
                    P.op("dve", lambda e, cc=cc, k=k: e.scalar_tensor_tensor(
                        out=acc[:], in0=xpad[:, cc, k:k + 512], scalar=cw[:, cc, k:k + 1], in1=acc[:],
                        op0=ALU.mult, op1=ALU.add), reads=[r_xpad[cc], r_cw, r_acc], writes=[r_acc])
                P.op("act", lambda e, cc=cc: e.activation(out=xc[:, cc, :], in_=acc[:], func=AF.Silu, bias=cw[:, cc, 4:5]),
                     reads=[r_acc, r_cw], writes=[r_xc[cc]])
                P.op("dve", lambda e, cc=cc: e.tensor_copy(out=xpad[:, cc, 0:3], in_=xpad[:, cc, 512:515]),
                     reads=[r_xpad[cc]], writes=[r_xpad[cc]])
            for bl in range(4):
                bsl = slice(bl * 128, (bl + 1) * 128)
                for cc in range(4):
                    P.op("pe", lambda e, cc=cc, bsl=bsl: e.transpose(out=tb[:, cc * 128:(cc + 1) * 128], in_=xc[:, cc, bsl],
                                                                    identity=K.ident_bf[:]),
                         reads=[r_xc[cc], K.r_const], writes=[rpb])
                P.op("dve", lambda e: e.tensor_copy(out=vtok[:], in_=tb[:, 0:512]), reads=[rpb], writes=[r_vtok])
                P.op("pe", lambda e, bsl=bsl: e.transpose(out=tb[:, 0:128], in_=xc[:, 4, bsl], identity=K.ident_bf[:]),
                     reads=[r_xc[4], K.r_const], writes=[rpb])
                P.op("dve", lambda e: e.tensor_copy(out=btok[:], in_=tb[:, 0:128]), reads=[rpb], writes=[r_btok])
                for kc in range(KC):
                    _mm(P, pb[:], uT[:, kc, bsl], w[:, kc, 0:512], kc == 0, kc == KC - 1, [r_w, r_uT], [rpb])
                P.op("act", lambda e: e.activation(out=sz[:], in_=pb[:], func=AF.Silu), reads=[rpb], writes=[r_sz])
                for kc in range(KC):
                    _mm(P, pb[:, 0:8], uT[:, kc, bsl], w[:, kc, 1280:1288], kc == 0, kc == KC - 1, [r_w, r_uT], [rpb])
                x_, ax, mx, dt_, l_, cumt, wdec, ecum = sm
                P.op("dve", lambda e: e.tensor_tensor(out=x_[:], in0=pb[:, 0:8], in1=rows[:, 0:8], op=ALU.add),
                     reads=[rpb, r_rows], writes=[r_sm[0]])
                P.op("dve", lambda e: e.scalar_tensor_tensor(out=ax[:], in0=x_[:], scalar=-1.0, in1=x_[:], op0=ALU.mult,
                                                             op1=ALU.min), reads=[r_sm[0]], writes=[r_sm[1]])
                P.op("act", lambda e: e.activation(out=ax[:], in_=ax[:], func=AF.Exp), reads=[r_sm[1]], writes=[r_sm[1]])
                P.op("act", lambda e: e.activation(out=ax[:], in_=ax[:], func=AF.Ln, bias=1.0), reads=[r_sm[1]], writes=[r_sm[1]])
                P.op("dve", lambda e: e.scalar_tensor_tensor(out=dt_[:], in0=x_[:], scalar=0.0, in1=ax[:], op0=ALU.max,
                                                             op1=ALU.add), reads=[r_sm[0], r_sm[1]], writes=[r_sm[3]])
                P.op("dve", lambda e: e.tensor_tensor(out=l_[:], in0=dt_[:], in1=negA[:], op=ALU.mult),
                     reads=[r_sm[3], r_negA], writes=[r_sm[4]])
                cb, rcb = K.bank[2], K.rbank[2]
                _mm(P, cb[:, 256:264], K.cmask_f[:], l_[:], True, True, [K.r_const, r_sm[4]], [rcb])
                _mm(P, cb[:, 264:272], onesF[:], l_[:], True, True, [r_onesF, r_sm[4]], [rcb])
                P.op("dve", lambda e: e.tensor_copy(out=sm16[:], in_=cb[:, 256:272]), reads=[rcb], writes=[r_sm16])
                P.op("act", lambda e: e.activation(out=ecum[:], in_=sm16[:, 0:8], func=AF.Exp), reads=[r_sm16], writes=[r_sm[7]])
                P.op("dve", lambda e: e.tensor_tensor(out=wdec[:], in0=sm16[:, 8:16], in1=sm16[:, 0:8], op=ALU.subtract),
                     reads=[r_sm16], writes=[r_sm[6]])
                P.op("act", lambda e: e.activation(out=wdec[:], in_=wdec[:], func=AF.Exp), reads=[r_sm[6]], writes=[r_sm[6]])
                P.op("act", lambda e: e.activation(out=cumt[:], in_=sm16[:, 8:16], func=AF.Exp), reads=[r_sm16], writes=[r_sm[5]])
                v3 = vtok[:].rearrange("p (h d) -> p h d", h=8)
                P.op("dve", lambda e, v3=v3: e.tensor_tensor(out=vp[:].rearrange("p (h d) -> p h d", h=8), in0=v3,
                                                            in1=dt_[:].unsqueeze(2).to_broadcast([128, 8, 64]), op=ALU.mult),
                     reads=[r_vtok, r_sm[3]], writes=[r_vp])
                P.op("dve", lambda e: e.tensor_tensor(out=vpp[:].rearrange("p (h d) -> p h d", h=8),
                                                      in0=vp[:].rearrange("p (h d) -> p h d", h=8),
                                                      in1=wdec[:].unsqueeze(2).to_broadcast([128, 8, 64]), op=ALU.mult),
                     reads=[r_vp, r_sm[6]], writes=[r_vpp])
                P.op("dve", lambda e: e.tensor_tensor(out=LM[:], in0=smask[:].unsqueeze(1).to_broadcast([128, 8, 128]),
                                                      in1=l_[:].unsqueeze(2).to_broadcast([128, 8, 128]), op=ALU.mult),
                     reads=[r_smask, r_sm[4]], writes=[r_LM])
                for hh in range(2):
                    db, rdb = K.bank[hh], K.rbank[hh]
                    for h4 in range(4):
                        h = hh * 4 + h4
                        _mm(P, db[:, h4 * 128:(h4 + 1) * 128], LM[:, h, :], K.cmask_f[:], True, True, [r_LM, K.r_const], [rdb])
                    P.op("act", lambda e, hh=hh, db=db: e.activation(out=E[:, hh * 4:(hh + 1) * 4, :].rearrange("p h i -> p (h i)"),
                                                                   in_=db[:], func=AF.Exp), reads=[rdb], writes=[r_E])
                gbank = [(K.bank[2], K.rbank[2]), (K.bank[7], K.rbank[7])]
                for g in range(2):
                    gs = slice(g * 64, (g + 1) * 64)
                    gb_, rgb = gbank[g]
                    _mm(P, gb_[:, 0:128], xc[gs, 4, bsl], xc[gs, 5, bsl], True, True, [r_xc[4], r_xc[5]], [rgb])
                    P.op("dve", lambda e, g=g, gb_=gb_: e.tensor_tensor(out=GM[:, g, :], in0=gb_[:, 0:128], in1=K.cmask_f[:],
                                                                      op=ALU.mult), reads=[rgb, K.r_const], writes=[r_GM])
                for g in range(2):
                    P.op("dve", lambda e, g=g: e.tensor_tensor(out=PT[:, g * 4:(g + 1) * 4, :], in0=E[:, g * 4:(g + 1) * 4, :],
                                                              in1=GM[:, g:g + 1, :].to_broadcast([128, 4, 128]), op=ALU.mult),
                         reads=[r_E, r_GM], writes=[r_PT])
                ab, rab = K.bank[3], K.rbank[3]
                bb, rbb = K.bank[4], K.rbank[4]
                for h in range(8):
                    _mm(P, ab[:, h * 64:(h + 1) * 64], PT[:, h, :], vp[:, h * 64:(h + 1) * 64], True, True, [r_PT, r_vp], [rab])
                ibank = [(K.bank[4], K.rbank[4]), (K.bank[7], K.rbank[7])]
                for g in range(2):
                    gs = slice(g * 64, (g + 1) * 64)
                    ib_, rib = ibank[g]
                    _mm(P, ib_[:, g * 256:(g + 1) * 256], xc[gs, 5, bsl], S_b[gs, g * 256:(g + 1) * 256], True, True,
                        [r_xc[5], r_Sb], [rib])
                    P.op("dve", lambda e, g=g, ib_=ib_: e.tensor_tensor(
                        out=o1[:, g * 256:(g + 1) * 256].rearrange("p (h d) -> p h d", h=4),
                        in0=ib_[:, g * 256:(g + 1) * 256].rearrange("p (h d) -> p h d", h=4),
                        in1=ecum[:, g * 4:(g + 1) * 4].unsqueeze(2).to_broadcast([128, 4, 64]), op=ALU.mult),
                        reads=[rib, r_sm[7]], writes=[r_o1])
                P.op("dve", lambda e: e.tensor_tensor(out=o1[:], in0=o1[:], in1=ab[:], op=ALU.add),
                     reads=[r_o1, rab], writes=[r_o1])
                sbk, rsb = K.bank[5], K.rbank[5]
                _mm(P, sbk[:], btok[:], vpp[:], True, True, [r_btok, r_vpp], [rsb])
                for g in range(2):
                    gs = slice(g * 64, (g + 1) * 64)
                    sv = S_f[gs, g * 256:(g + 1) * 256].rearrange("p (h d) -> p h d", h=4)
                    P.op("dve", lambda e, g=g, gs=gs, sv=sv: e.tensor_tensor(
                        out=sv, in0=sv, in1=cumt[gs, g * 4:(g + 1) * 4].unsqueeze(2).to_broadcast([64, 4, 64]), op=ALU.mult),
                        reads=[r_Sf, r_sm[5]], writes=[r_Sf])
                    P.op("dve", lambda e, g=g, gs=gs: e.tensor_tensor(
                        out=S_f[gs, g * 256:(g + 1) * 256], in0=S_f[gs, g * 256:(g + 1) * 256],
                        in1=sbk[gs, g * 256:(g + 1) * 256], op=ALU.add), reads=[r_Sf, rsb], writes=[r_Sf])
                P.op("pool", lambda e: e.tensor_copy(out=S_b[:], in_=S_f[:]), reads=[r_Sf], writes=[r_Sb])
                P.op("dve", lambda e, v3=v3: e.tensor_tensor(out=o2[:].rearrange("p (h d) -> p h d", h=8), in0=v3,
                                                            in1=rows[:, 16:24].unsqueeze(2).to_broadcast([128, 8, 64]),
                                                            op=ALU.mult), reads=[r_vtok, r_rows], writes=[r_o2])
                P.op("dve", lambda e: e.tensor_tensor(out=o2[:], in0=o2[:], in1=o1[:], op=ALU.add),
                     reads=[r_o2, r_o1], writes=[r_o2])
                P.op("dve", lambda e: e.tensor_tensor(out=o2[:], in0=o2[:], in1=sz[:], op=ALU.mult),
                     reads=[r_o2, r_sz], writes=[r_o2])
                P.op("dve", lambda e: e.tensor_tensor(out=o1[:], in0=o2[:], in1=o2[:], op=ALU.mult),
                     reads=[r_o2, r_o1], writes=[r_o1])
                P.op("dve", lambda e: e.tensor_reduce(out=ss[:], in_=o1[:].rearrange("p (g d) -> p g d", g=2), axis=AX.X,
                                                      op=ALU.add), reads=[r_o1], writes=[r_ss])
                act_rstd(P, ss[:], r_ss, ss[:], r_ss, 1.0 / 256, ss2[:], r_ss2)
                P.op("dve", lambda e: e.tensor_tensor(out=o2[:].rearrange("p (g d) -> p g d", g=2),
                                                      in0=o2[:].rearrange("p (g d) -> p g d", g=2),
                                                      in1=ss[:].unsqueeze(2).to_broadcast([128, 2, 256]), op=ALU.mult),
                     reads=[r_o2, r_ss], writes=[r_o2])
                P.op("dve", lambda e: e.tensor_tensor(out=ytok[:], in0=o2[:], in1=rows[:, 24:536], op=ALU.mult),
                     reads=[r_o2, r_rows], writes=[r_ytok])
                for cc in range(4):
                    P.op("pe", lambda e, cc=cc: e.transpose(out=tb[:, cc * 128:(cc + 1) * 128],
                                                            in_=ytok[:, cc * 128:(cc + 1) * 128], identity=K.ident_bf[:]),
                         reads=[r_ytok, K.r_const], writes=[rpb])
                P.op("dve", lambda e, bsl=bsl, yb=yb: e.tensor_copy(out=yTt[yb][:, :, bsl],
                                                                   in_=tb[:, 0:512].rearrange("p (c s) -> p c s", c=4)),
                     reads=[rpb], writes=[r_yTt[yb]])
            dst = K.ybuf.rearrange("(c p) t -> p c t", p=128)[:, 4:8, tsl]
            P.dma("sp", dst, yTt[yb][:], ds[7 + yb], reads=[r_yTt[yb]], writes=[K.r_y[c][t] for c in range(4, 8)])
        P.emit_block(final=final)


ALL_PHASES = ("ret", "ssd", "hg", "fox", "ffn")


def _wl(W):
    return np.ascontiguousarray(W.reshape(W.shape[0], KC, 128, W.shape[2]).transpose(0, 2, 1, 3))


def _vec(v):
    return np.ascontiguousarray(v.reshape(v.shape[0], KC, 128).transpose(0, 2, 1))


def prepare_inputs(T, norm_mix, norm_ffn, ffn_w_gate, ffn_w_up, ffn_w_down, ab_w_in, ab_w_out, ret_gn_w, ssd_conv_w,
                   ssd_conv_b, ssd_dt_bias, ssd_a_log, ssd_d, ssd_norm_w, cd_w_in, cd_w_out, hg_lb_logits, hg_norm_w,
                   fox_f_bias, fox_q_norm_w, fox_k_norm_w):
    f32 = np.float32
    A = lambda a: np.asarray(a, dtype=f32)
    norm_mix, norm_ffn = A(norm_mix), A(norm_ffn)
    wg, wu, wd = A(ffn_w_gate), A(ffn_w_up), A(ffn_w_down)
    ab_in, ab_out, cd_in, cd_out = A(ab_w_in), A(ab_w_out), A(cd_w_in), A(cd_w_out)
    sh = {}
    sh["norm_mix"], sh["norm_ffn"] = _vec(norm_mix), _vec(norm_ffn)
    sh["wg"] = np.ascontiguousarray(wg.reshape(4, KC, 128, NJ, 128).transpose(0, 3, 2, 1, 4))
    sh["wu"] = np.ascontiguousarray(wu.reshape(4, KC, 128, NJ, 128).transpose(0, 3, 2, 1, 4))
    sh["wd"] = np.ascontiguousarray(wd.reshape(4, NJ, 128, KC, 128).transpose(0, 3, 2, 1, 4))
    w_out = np.stack([ab_out[0], cd_out[0], ab_out[1], cd_out[1]])
    sh["w_out"] = _wl(w_out)
    sh["w_fox"] = _wl(cd_in[:, :, 2048:3588])
    fv = np.zeros((2, 128, 3), f32)
    fv[:, :, 0] = A(fox_q_norm_w); fv[:, :, 1] = A(fox_k_norm_w); fv[:, 0:4, 2] = A(fox_f_bias)
    sh["fox_vec"] = fv
    sh["cmask"] = np.triu(np.ones((128, 128), f32))
    sh["smask"] = np.tril(np.ones((128, 128), f32), -1)
    sh["ident"] = np.eye(128, dtype=f32)
    sh["rmask"] = np.tile((np.arange(512) % 64 != 0).astype(f32), (128, 1))
    sh["w_hg"] = _wl(cd_in[:, :, 0:2048])
    hv = np.zeros((2, 128, 12), f32)
    lbl = A(hg_lb_logits)
    for jj in range(2):
        hv[jj, :, 0:4] = lbl[0].reshape(4, 128).T
        hv[jj, :, 4:8] = lbl[1].reshape(4, 128).T
        hv[jj, :, 8] = A(hg_norm_w)[jj]
    sh["hg_vec"] = hv

    def pad(Wx, swap):
        o = np.zeros((D, 512), f32)
        for h in range(4):
            blk = Wx[:, h * 64:(h + 1) * 64]
            if swap:
                blk = blk.reshape(D, 32, 2)[:, :, ::-1].reshape(D, 64)
            o[:, h * 128:h * 128 + 64] = blk
        return o
    wr = []
    for jj in range(2):
        Wq, Wk = ab_in[jj][:, 0:256], ab_in[jj][:, 256:512]
        wr.append(np.concatenate([pad(Wq, 0), pad(Wk, 0), ab_in[jj][:, 512:1024], ab_in[jj][:, 1024:1536],
                                  pad(Wq, 1), pad(Wk, 1)], 1))
    sh["w_ret"] = _wl(np.stack(wr))
    rv = np.zeros((2, 128, 12), f32)
    lg = np.log1p(-np.exp2(-5.0 - np.arange(4, dtype=np.float64)))
    rv[:, :, 0:4] = lg[None, None, :].astype(f32)
    rv[:, :, 4:8] = A(ret_gn_w).transpose(0, 2, 1)
    sh["ret_vec"] = rv
    freqs = (np.float32(10000.0) ** (-np.linspace(0.0, 1.0, 32, dtype=f32))).astype(f32)
    ang = (np.arange(T, dtype=f32)[:, None] * freqs[None, :]).astype(f32).astype(np.float64)
    rt = np.zeros((128, 2, T), f32)
    rt[0:64, 0, :] = np.repeat(np.cos(ang), 2, axis=1).T
    sn = np.repeat(np.sin(ang), 2, axis=1)
    sn[:, 0::2] *= -1
    rt[0:64, 1, :] = sn.T
    sh["rot_tab"] = rt
    sh["w_ssd"] = _wl(ab_in[:, :, 1536:2824])
    rows = np.zeros((2, 128, 536), f32)
    rows[:, :, 0:8] = A(ssd_dt_bias)[:, None, :]
    rows[:, :, 8:16] = A(ssd_a_log)[:, None, :]
    rows[:, :, 16:24] = A(ssd_d)[:, None, :]
    rows[:, :, 24:536] = A(ssd_norm_w)[:, None, :]
    sh["ssd_rows"] = rows
    cv = np.zeros((2, 128, 6, 5), f32)
    cwv, cbv = A(ssd_conv_w), A(ssd_conv_b)
    for jj in range(2):
        cv[jj, :, :, 0:4] = cwv[jj].T.reshape(6, 128, 4).transpose(1, 0, 2)
        cv[jj, :, :, 4] = cbv[jj].reshape(6, 128).T
    sh["ssd_conv"] = cv
    return sh


def kernel(x, **params):
    x = np.asarray(x, dtype=np.float32)
    B, T, _ = x.shape
    shared = prepare_inputs(T, **params)
    nc = build_program(T, [0, 1, 2, 3], phases=ALL_PHASES)
    in_maps = []
    for b in range(B):
        m = dict(shared)
        m["xT"] = np.ascontiguousarray(x[b].T)
        in_maps.append(m)
    res = run_bass_kernel_spmd(nc, in_maps, core_ids=list(range(B)))
    out = np.stack([np.ascontiguousarray(res.results[b]["out"].T) for b in range(B)], axis=0)
    return out.astype(np.float32)
```

```python
import math
import numpy as np
from contextlib import ExitStack
import concourse.bass as bass
import concourse.mybir as mybir
from concourse.bass_utils import run_bass_kernel_spmd

F32 = mybir.dt.float32
BF16 = mybir.dt.bfloat16
AF = mybir.ActivationFunctionType
ALU = mybir.AluOpType
AX = mybir.AxisListType

D = 1024
KC = 8
FF = 2816
NJ = 22
EPS = 1e-6
import os
NOSYNC_ENGINES = tuple(x for x in os.environ.get("KNOSYNC", "").split(",") if x)
ENGS = ["pe", "act", "dve", "pool", "sp"]
CENG = ["pe", "act", "dve", "pool"]


class Res:
    __slots__ = ("w", "rs")

    def __init__(self):
        self.w = None
        self.rs = []


def RL(n):
    return [Res() for _ in range(n)]


class Ev:
    __slots__ = ("sem", "val", "op", "blk")

    def __init__(self, blk, sem=None, val=None, op=None):
        self.blk, self.sem, self.val, self.op = blk, sem, val, op


class Op:
    __slots__ = ("eng", "fn", "waits", "marked", "ev", "incs")

    def __init__(self, eng, fn):
        self.eng, self.fn = eng, fn
        self.waits = []
        self.marked = False
        self.ev = None
        self.incs = None


class DSem:
    def __init__(self, sem):
        self.sem = sem
        self.count = 0


class Prog:
    def __init__(self, nc, es, same_engine_sync=True):
        self.nc = nc
        self.es = es
        self.ops = {e: [] for e in ENGS}
        self.csem = {e: es.enter_context(nc.semaphore("c_" + e)) for e in CENG}
        self.ccount = {e: 0 for e in CENG}
        self.bar = es.enter_context(nc.semaphore("bar"))
        self.nbar = 0
        self.same = same_engine_sync
        self.nosync = set(NOSYNC_ENGINES)
        self.dsems = []
        self.blk = 0
        self.blk_dsems = set()

    def dsem(self):
        d = DSem(self.es.enter_context(self.nc.semaphore(f"d{len(self.dsems)}")))
        self.dsems.append(d)
        return d

    def _deps(self, op, reads, writes):
        evs = []
        for r in reads:
            if r.w is not None:
                evs.append(r.w)
        for r in writes:
            if r.w is not None:
                evs.append(r.w)
            evs.extend(r.rs)
        for ev in evs:
            if ev.blk != self.blk:
                continue
            if ev.op is not None:
                p = ev.op
                if p.eng == op.eng and (op.eng == "pe" or op.eng in self.nosync):
                    continue
                p.marked = True
            op.waits.append(ev)

    def _commit(self, ev, reads, writes):
        for r in reads:
            r.rs.append(ev)
        for r in writes:
            r.w = ev
            r.rs = []

    def op(self, eng, fn, reads=(), writes=()):
        o = Op(eng, fn)
        self._deps(o, reads, writes)
        o.ev = Ev(self.blk, op=o)
        self.ops[eng].append(o)
        self._commit(o.ev, reads, writes)
        return o

    def dma(self, q, out, in_, ds, reads=(), writes=(), slow=False):
        if slow:
            o = Op(q, lambda e: e.dma_start(out=out, in_=in_, allow_slow_non_contiguous=True))
        else:
            o = Op(q, lambda e: e.dma_start(out=out, in_=in_))
        self._deps(o, reads, writes)
        ds.count += 16
        self.blk_dsems.add(ds)
        o.incs = (ds.sem, 16)
        o.ev = Ev(self.blk, sem=ds.sem, val=ds.count)
        self.ops[q].append(o)
        self._commit(o.ev, reads, writes)
        return o

    def emit_block(self, final=False):
        nc = self.nc
        finals = []
        for e in CENG:
            ops = [o for o in self.ops[e] if o.fn is not None]
            if ops:
                ops[-1].marked = True
            c = self.ccount[e]
            for o in self.ops[e]:
                if o.incs is None and o.marked:
                    c += 1
                    o.ev.sem, o.ev.val = self.csem[e], c
            self.ccount[e] = c
            if ops:
                finals.append((self.csem[e], c))
        for ds in self.blk_dsems:
            finals.append((ds.sem, ds.count))
        self.nbar += 1
        nbar = self.nbar
        engobj = {"pe": "tensor", "act": "scalar", "dve": "vector", "pool": "gpsimd", "sp": "sync"}
        with nc.Block() as block:
            for e in ENGS:
                ops = self.ops[e]

                def body(eng, ops=ops, e=e):
                    waited = {}
                    for o in ops:
                        for ev in o.waits:
                            k = id(ev.sem)
                            if waited.get(k, 0) < ev.val:
                                eng.wait_ge(ev.sem, ev.val)
                                waited[k] = ev.val
                        ins = o.fn(eng)
                        if o.incs is not None:
                            ins.then_inc(o.incs[0], o.incs[1])
                        elif o.marked:
                            ins.then_inc(self.csem[e], 1)
                    if e == "sp":
                        for (s, v) in finals:
                            eng.wait_ge(s, v)
                        eng.sem_inc(self.bar, 1)
                    if not (final and e != "sp"):
                        eng.wait_ge(self.bar, nbar)

                getattr(block, engobj[e])(body)
        self.ops = {e: [] for e in ENGS}
        self.blk += 1
        self.blk_dsems = set()


class KB:
    pass


def _mm(P, out, lhsT, rhs, start, stop, reads, writes, **kw):
    return P.op("pe", lambda e: e.matmul(out, lhsT, rhs, start=start, stop=stop, **kw), reads=reads, writes=writes)


def build_program(T, layers, phases=("mix", "ffn"), test_ybuf=False):
    nc = bass.Bass("TRN2", target_bir_lowering=False)
    NT = T // 512
    K = KB()
    K.nc, K.T, K.NT = nc, T, NT
    dr = {}

    def din(name, shape, dt=F32):
        dr[name] = nc.dram_tensor(name, list(shape), dt, kind="ExternalInput").ap()
        return dr[name]

    din("xT", [D, T])
    din("norm_mix", [4, 128, KC])
    din("norm_ffn", [4, 128, KC])
    din("wg", [4, NJ, 128, KC, 128])
    din("wu", [4, NJ, 128, KC, 128])
    din("wd", [4, KC, 128, NJ, 128])
    din("w_out", [4, 128, KC, D])
    din("w_fox", [2, 128, KC, 1540])
    din("fox_vec", [2, 128, 3])
    din("cmask", [128, 128])
    din("w_hg", [2, 128, KC, 2048])
    din("hg_vec", [2, 128, 12])
    din("w_ret", [2, 128, KC, 3072])
    din("ret_vec", [2, 128, 12])
    din("rmask", [128, 512])
    din("w_ssd", [2, 128, KC, 1288])
    din("ssd_rows", [2, 128, 536])
    din("ssd_conv", [2, 128, 6, 5])
    din("smask", [128, 128])
    din("rot_tab", [128, 2, T])
    din("ident", [128, 128])
    out = nc.dram_tensor("out", [D, T], F32, kind="ExternalOutput").ap()
    if test_ybuf == "out":
        ybuf = nc.dram_tensor("ybuf", [D, T], BF16, kind="ExternalOutput").ap()
    elif test_ybuf:
        ybuf = nc.dram_tensor("ybuf", [D, T], BF16, kind="ExternalInput").ap()
    else:
        ybuf = nc.dram_tensor("ybuf", [D, T], BF16).ap()
    K.dr, K.out, K.ybuf = dr, out, ybuf

    with ExitStack() as es:
        P = Prog(nc, es)
        K.P = P
        K.bank = [es.enter_context(nc.psum_tensor(f"bank{i}", [128, 512], F32)) for i in range(8)]
        K.rbank = RL(8)
        K.ones_bf = es.enter_context(nc.sbuf_tensor("ones_bf", [128, 128], BF16))
        K.r_const = Res()
        P.op("pool", lambda e: e.memset(K.ones_bf[:], 1.0), writes=[K.r_const])
        K.r_h = [RL(NT) for _ in range(KC)]
        K.r_y = [RL(NT) for _ in range(KC)]
        K.dsem_pool = [P.dsem() for _ in range(40)]
        K.dsem_q = [P.dsem() for _ in range(8)]
        K.ident_bf = es.enter_context(nc.sbuf_tensor("ident_bf", [128, 128], BF16))
        K.ident_f = es.enter_context(nc.sbuf_tensor("ident_f", [128, 128], F32))
        K.cmask_bf = es.enter_context(nc.sbuf_tensor("cmask_bf", [128, 128], BF16))
        K.cmask_f = es.enter_context(nc.sbuf_tensor("cmask_f", [128, 128], F32))
        P.dma("pool", K.ident_bf[:], dr["ident"], K.dsem_q[7], writes=[K.r_const])
        P.dma("sp", K.ident_f[:], dr["ident"], K.dsem_pool[37], writes=[K.r_const])
        P.dma("pool", K.cmask_bf[:], dr["cmask"], K.dsem_q[7], writes=[K.r_const])
        P.dma("sp", K.cmask_f[:], dr["cmask"], K.dsem_pool[39], writes=[K.r_const])
        K.fd = nc.dram_tensor("fd", [8, T], F32).ap()
        K.r_fd = Res()

        plan = []
        for li, layer in enumerate(layers):
            if layer % 2 == 0:
                for ph in ("ret", "ssd"):
                    if ph in phases:
                        plan.append((ph, li, layer))
            else:
                for ph in ("hg", "fox"):
                    if ph in phases:
                        plan.append((ph, li, layer))
            if "ffn" in phases:
                plan.append(("ffn", li, layer))
        for pi, (ph, li, layer) in enumerate(plan):
            hin = dr["xT"] if li == 0 else out
            fin = pi == len(plan) - 1
            if ph == "ret":
                hg_phase(K, layer, hin, "ret", final=fin)
            elif ph == "hg":
                hg_phase(K, layer, hin, "hg", final=fin)
            elif ph == "ssd":
                ssd_phase(K, layer, hin, final=fin)
            elif ph == "fox":
                fox_phase(K, layer, hin, final=fin)
            else:
                ffn_phase(K, layer, hin, final=fin)
    return nc


def rmsnorm_tile(K, es_bufs, h_t, r_h_t, gain, r_gain, u_out, r_u, sq, r_sq, rstd, r_rstd, bank_i):
    P = K.P
    bank, rb = K.bank[bank_i], K.rbank[bank_i]
    for c in range(KC):
        P.op("act", lambda e, c=c: e.activation(out=sq[:, c, :], in_=h_t[:, c, :], func=AF.Square),
             reads=[r_h_t], writes=[r_sq])
    for c in range(KC):
        _mm(P, bank[:], K.ones_bf[:], sq[:, c, :], c == 0, c == KC - 1, [K.r_const, r_sq], [rb])
    P.op("act", lambda e: e.activation(out=rstd[:], in_=bank[:], func=AF.Sqrt, bias=EPS, scale=1.0 / D),
         reads=[rb], writes=[r_rstd])
    P.op("dve", lambda e: e.reciprocal(out=rstd[:], in_=rstd[:]), reads=[r_rstd], writes=[r_rstd])
    for c in range(KC):
        P.op("dve", lambda e, c=c: e.scalar_tensor_tensor(out=u_out(c), in0=h_t[:, c, :], scalar=gain[:, c:c + 1],
                                                         in1=rstd[:], op0=ALU.mult, op1=ALU.mult),
             reads=[r_h_t, r_gain, r_rstd], writes=[r_u])


def ffn_phase(K, layer, hin, final=False):
    nc, P, T, NT, dr, out = K.nc, K.P, K.T, K.NT, K.dr, K.out
    TG = min(1024, T)
    NG = T // TG
    TPG = TG // 512
    ds = K.dsem_pool
    with ExitStack() as es:
        def sb(name, shape, dt):
            return es.enter_context(nc.sbuf_tensor(f"f{layer}_{name}", shape, dt))
        wout = sb("wout", [128, KC, D], BF16); r_wout = Res()
        gain = sb("gain", [128, KC], F32); r_gain = Res()
        uT = sb("uT", [128, KC, TG], BF16); r_uT = RL(TPG)
        aT = sb("aT", [128, NJ, TG], BF16); r_aT = [RL(TPG) for _ in range(NJ)]
        ht = [sb(f"ht{i}", [128, KC, 512], F32) for i in range(2)]; r_ht = RL(2)
        yt = [sb(f"yt{i}", [128, KC, 512], BF16) for i in range(2)]; r_yt = RL(2)
        sq = sb("sq", [128, KC, 512], BF16); r_sq = Res()
        rstd = sb("rstd", [128, 512], F32); r_rstd = Res()
        wgc = [sb(f"wgc{i}", [128, KC, 128], BF16) for i in range(2)]; r_wgc = RL(2)
        wuc = [sb(f"wuc{i}", [128, KC, 128], BF16) for i in range(2)]; r_wuc = RL(2)
        wdc = [sb(f"wdc{i}", [128, NJ, 128], BF16) for i in range(2)]; r_wdc = RL(2)
        sg = [sb(f"sg{i}", [128, 512], F32) for i in range(2)]; r_sg = RL(2)
        hs = [sb(f"hs{i}", [128, 512], F32) for i in range(4)]; r_hs = RL(4)

        P.dma("pool", wout[:], dr["w_out"][layer], K.dsem_q[0], writes=[r_wout])
        P.dma("sp", gain[:], dr["norm_ffn"][layer], ds[1], writes=[r_gain])

        def hview(ap, t):
            return ap.rearrange("(c p) t -> p c t", p=128)[:, :, t * 512:(t + 1) * 512]

        def load_tile(tg):
            b = tg % 2
            rh = [K.r_h[c][tg] for c in range(KC)]
            ry = [K.r_y[c][tg] for c in range(KC)]
            P.dma("sp", ht[b][:], hview(hin, tg), ds[2 + b], reads=rh, writes=[r_ht[b]])
            P.dma("sp", yt[b][:], hview(K.ybuf, tg), ds[4 + b], reads=ry, writes=[r_yt[b]])

        wcount = [0]

        for g in range(NG):
            load_tile(g * TPG)
            for tl in range(TPG):
                tg = g * TPG + tl
                b = tg % 2
                if tl + 1 < TPG:
                    load_tile(tg + 1)
                for dc in range(KC):
                    bi = dc % 2
                    for kc in range(KC):
                        _mm(P, K.bank[bi][:], wout[:, kc, dc * 128:(dc + 1) * 128], yt[b][:, kc, :], kc == 0, kc == KC - 1,
                            [r_wout, r_yt[b]], [K.rbank[bi]])
                    P.op("dve", lambda e, dc=dc, bi=bi, b=b: e.tensor_tensor(out=ht[b][:, dc, :], in0=ht[b][:, dc, :],
                                                                           in1=K.bank[bi][:], op=ALU.add),
                         reads=[K.rbank[bi], r_ht[b]], writes=[r_ht[b]])
                P.dma("sp", hview(out, tg), ht[b][:], ds[6 + b], reads=[r_ht[b]],
                      writes=[K.r_h[c][tg] for c in range(KC)])
                rmsnorm_tile(K, None, ht[b], r_ht[b], gain, r_gain,
                             lambda c, tl=tl: uT[:, c, tl * 512:(tl + 1) * 512], r_uT[tl], sq, r_sq, rstd, r_rstd, 2)
            def load_w2(j):
                b = wcount[0] % 2
                wcount[0] += 1
                P.dma("pool", wgc[b][:], dr["wg"][layer, j], K.dsem_q[1 + b], writes=[r_wgc[b]])
                P.dma("pool", wuc[b][:], dr["wu"][layer, j], K.dsem_q[3 + b], writes=[r_wuc[b]])
                return b
            nb = load_w2(0)
            for j in range(NJ):
                b = nb
                if j + 1 < NJ:
                    nb = load_w2(j + 1)
                for tl in range(TPG):
                    pg, pu = 3 + 2 * (tl % 2), 4 + 2 * (tl % 2)
                    sl = slice(tl * 512, (tl + 1) * 512)
                    for kc in range(KC):
                        _mm(P, K.bank[pg][:], wgc[b][:, kc, :], uT[:, kc, sl], kc == 0, kc == KC - 1,
                            [r_wgc[b], r_uT[tl]], [K.rbank[pg]])
                    for kc in range(KC):
                        _mm(P, K.bank[pu][:], wuc[b][:, kc, :], uT[:, kc, sl], kc == 0, kc == KC - 1,
                            [r_wuc[b], r_uT[tl]], [K.rbank[pu]])
                    s = tl % 2
                    P.op("act", lambda e, s=s, pg=pg: e.activation(out=sg[s][:], in_=K.bank[pg][:], func=AF.Silu),
                         reads=[K.rbank[pg]], writes=[r_sg[s]])
                    P.op("dve", lambda e, s=s, pu=pu, j=j, sl=sl: e.tensor_tensor(out=aT[:, j, sl], in0=sg[s][:],
                                                                                 in1=K.bank[pu][:], op=ALU.mult),
                         reads=[r_sg[s], K.rbank[pu]], writes=[r_aT[j][tl]])
            P.dma("pool", wdc[0][:], dr["wd"][layer, 0], K.dsem_q[5], writes=[r_wdc[0]])
            cnt = 0
            for dc in range(KC):
                b = dc % 2
                if dc + 1 < KC:
                    P.dma("pool", wdc[1 - b][:], dr["wd"][layer, dc + 1], K.dsem_q[5 + (1 - b)], writes=[r_wdc[1 - b]])
                for tl in range(TPG):
                    tg = g * TPG + tl
                    bi = cnt % 2
                    hb = cnt % 4
                    cnt += 1
                    src = out[dc * 128:(dc + 1) * 128, tg * 512:(tg + 1) * 512]
                    P.dma("sp", hs[hb][:], src, ds[14 + hb], reads=[K.r_h[dc][tg]], writes=[r_hs[hb]])
                    for j in range(NJ):
                        _mm(P, K.bank[bi][:], wdc[b][:, j, :], aT[:, j, tl * 512:(tl + 1) * 512], j == 0, j == NJ - 1,
                            [r_wdc[b], r_aT[j][tl]], [K.rbank[bi]])
                    P.op("dve", lambda e, hb=hb, bi=bi: e.tensor_tensor(out=hs[hb][:], in0=hs[hb][:], in1=K.bank[bi][:],
                                                                       op=ALU.add),
                         reads=[K.rbank[bi], r_hs[hb]], writes=[r_hs[hb]])
                    P.dma("sp", src, hs[hb][:], ds[18 + hb], reads=[r_hs[hb]], writes=[K.r_h[dc][tg]])
        P.emit_block(final=final)


def act_rstd(P, out, r_out, in_, r_in, scale, tmp, r_tmp):
    P.op("act", lambda e: e.activation(out=tmp, in_=in_, func=AF.Ln, bias=EPS, scale=scale), reads=[r_in], writes=[r_tmp])
    P.op("act", lambda e: e.activation(out=out, in_=tmp, func=AF.Exp, scale=-0.5), reads=[r_tmp], writes=[r_out])


def rmsnorm_tile2(K, h_t, r_h_t, gain, r_gain, uT, r_u, sq, r_sq, rstd, r_rstd, tmp, r_tmp, bank_i):
    P = K.P
    bank, rb = K.bank[bank_i], K.rbank[bank_i]
    if not hasattr(K, "_sqres"):
        K._sqres = {}
    rs = K._sqres.setdefault(id(r_sq), RL(KC))
    eng_of = ["act", "act", "act", "dve", "act", "dve", "pool", "act"]
    for c in range(KC):
        if eng_of[c] == "act":
            P.op("act", lambda e, c=c: e.activation(out=sq[:, c, :], in_=h_t[:, c, :], func=AF.Square),
                 reads=[r_h_t], writes=[rs[c]])
        else:
            P.op(eng_of[c], lambda e, c=c: e.tensor_tensor(out=sq[:, c, :], in0=h_t[:, c, :], in1=h_t[:, c, :], op=ALU.mult),
                 reads=[r_h_t], writes=[rs[c]])
    for c in range(KC):
        _mm(P, bank[:], K.ones_bf[:], sq[:, c, :], c == 0, c == KC - 1, [K.r_const, rs[c]], [rb])
    act_rstd(P, rstd[:], r_rstd, bank[:], rb, 1.0 / D, tmp[:], r_tmp)
    for c in range(KC):
        P.op("dve", lambda e, c=c: e.scalar_tensor_tensor(out=uT[:, c, :], in0=h_t[:, c, :], scalar=gain[:, c:c + 1],
                                                         in1=rstd[:], op0=ALU.mult, op1=ALU.add if False else ALU.mult),
             reads=[r_h_t, r_gain, r_rstd], writes=[r_u])


def hview(ap, t):
    return ap.rearrange("(c p) t -> p c t", p=128)[:, :, t * 512:(t + 1) * 512]


def fox_phase(K, layer, hin, final=False):
    nc, P, T, NT, dr = K.nc, K.P, K.T, K.NT, K.dr
    j = layer // 2
    NB = T // 128
    ds = K.dsem_pool
    SCALE = 128 ** -0.5
    with ExitStack() as es:
        def sb(name, shape, dt):
            return es.enter_context(nc.sbuf_tensor(f"x{layer}_{name}", shape, dt))
        w = sb("w", [128, KC, 1540], BF16); r_w = Res()
        gain = sb("gain", [128, KC], F32); r_gain = Res()
        fvec = sb("fvec", [128, 3], F32); r_fvec = Res()
        KT = sb("KT", [128, 4, T], BF16); r_KT = [RL(NT) for _ in range(4)]
        VA = sb("VA", [128, NB, 4, 129], BF16); r_VA = RL(NB); r_VAone = Res()
        Ftok = sb("Ftok", [128, NB, 4], F32); r_Ftok = RL(NB)
        Rq = sb("Rq", [128, NB, 4], F32); r_Rq = RL(NB)
        Bq = sb("Bq", [128, NB, 4], F32); r_Bq = Res()
        ht = [sb(f"ht{i}", [128, KC, 512], F32) for i in range(2)]; r_ht = RL(2)
        uT = sb("uT", [128, KC, 512], BF16); r_uT = Res()
        sq = sb("sq", [128, KC, 512], BF16); r_sq = Res()
        rstd = sb("rstd", [128, 512], F32); r_rstd = Res()
        tmp = sb("tmp", [128, 512], F32); r_tmp = Res()
        qn = sb("qn", [128, 4, 512], BF16); r_qn = RL(4)
        sqh = sb("sqh", [128, 512], BF16); r_sqh = Res()
        rq = sb("rq", [128, 512], F32); r_rq = Res()
        fx = [sb(f"fx{i}", [4, 512], F32) for i in range(4)]; r_fx = RL(4)
        Fc = [sb(f"Fc{i}", [4, 512], F32) for i in range(2)]; r_Fc = RL(2)
        onesf = sb("onesf", [4, 512], F32); r_onesf = Res()
        PT = [sb(f"PT{i}", [128, 512], BF16) for i in range(3)]; r_PT = RL(3)
        ytok = sb("ytok", [128, 4, 512], BF16); r_ytok = RL(4)
        Rt = sb("Rt", [128, 4], F32); r_Rt = Res()
        Boff = sb("Boff", [128, NB, 4], F32); r_Boff = Res()
        Bin = sb("Bin", [128, 4, 4, 4], F32); r_Bin = Res()
        cfac = sb("cfac", [128, 4, 4], F32); r_cfac = Res()
        otmp = sb("otmp", [128, 132], F32); r_otmp = Res()
        nslot = [0]
        rec = sb("rec", [128, 4], F32); r_rec = RL(4)
        yTt = [sb(f"yTt{i}", [128, 4, 512], BF16) for i in range(2)]; r_yTt = RL(2)

        P.dma("pool", w[:, :, 0:768], dr["w_fox"][j][:, :, 0:768], K.dsem_q[0], writes=[r_w])
        P.dma("pool", w[:, :, 768:1540], dr["w_fox"][j][:, :, 768:1540], K.dsem_q[0], writes=[r_w])
        P.dma("sp", gain[:], dr["norm_mix"][layer], ds[1], writes=[r_gain])
        P.dma("sp", fvec[:], dr["fox_vec"][j], ds[30], writes=[r_fvec])
        P.op("pool", lambda e: e.memset(onesf[:], 1.0), writes=[r_onesf])
        P.op("pool", lambda e: e.memset(VA[:, :, :, 128:129], 1.0), writes=[r_VAone])

        P.dma("sp", ht[0][:], hview(hin, 0), ds[2], reads=[K.r_h[c][0] for c in range(KC)], writes=[r_ht[0]])
        npt = 0
        for t in range(NT):
            b = t % 2
            if t + 1 < NT:
                P.dma("sp", ht[1 - b][:], hview(hin, t + 1), ds[2 + (1 - b)],
                      reads=[K.r_h[c][t + 1] for c in range(KC)], writes=[r_ht[1 - b]])
            rmsnorm_tile2(K, ht[b], r_ht[b], gain, r_gain, uT, r_uT, sq, r_sq, rstd, r_rstd, tmp, r_tmp, 7)
            tsl = slice(t * 512, (t + 1) * 512)
            pb, rpb = K.bank[6], K.rbank[6]
            for kc in range(KC):
                _mm(P, pb[0:4, :], w[:, kc, 1536:1540], uT[:, kc, :], kc == 0, kc == KC - 1, [r_w, r_uT], [rpb])
            P.op("dve", lambda e: e.tensor_scalar(out=fx[0][:], in0=pb[0:4, :], scalar1=fvec[0:4, 2:3], scalar2=None,
                                                  op0=ALU.add), reads=[rpb, r_fvec], writes=[r_fx[0]])
            P.op("dve", lambda e: e.scalar_tensor_tensor(out=fx[1][:], in0=fx[0][:], scalar=-1.0, in1=fx[0][:],
                                                         op0=ALU.mult, op1=ALU.min),
                 reads=[r_fx[0]], writes=[r_fx[1]])
            P.op("act", lambda e: e.activation(out=fx[1][:], in_=fx[1][:], func=AF.Exp),
                 reads=[r_fx[1]], writes=[r_fx[1]])
            P.op("act", lambda e: e.activation(out=fx[1][:], in_=fx[1][:], func=AF.Ln, bias=1.0),
                 reads=[r_fx[1]], writes=[r_fx[1]])
            P.op("dve", lambda e: e.tensor_scalar_min(out=fx[2][:], in0=fx[0][:], scalar1=0.0),
                 reads=[r_fx[0]], writes=[r_fx[2]])
            P.op("dve", lambda e: e.tensor_sub(out=fx[3][:], in0=fx[2][:], in1=fx[1][:]),
                 reads=[r_fx[1], r_fx[2]], writes=[r_fx[3]])
            init = 0.0 if t == 0 else Fc[1 - b][:, 511:512]
            P.op("dve", lambda e, b=b, init=init: e.tensor_tensor_scan(out=Fc[b][:], data0=onesf[:], data1=fx[3][:],
                                                                      initial=init, op0=ALU.mult, op1=ALU.add),
                 reads=[r_onesf, r_fx[3], r_Fc[1 - b]], writes=[r_Fc[b]])
            P.dma("sp", K.fd[0:4, tsl], Fc[b][:], ds[4], reads=[r_Fc[b]], writes=[K.r_fd])
            for bl in range(4):
                blk = t * 4 + bl
                c0 = blk * 128
                P.dma("sp", Ftok[:, blk, :], K.fd[0:4, c0:c0 + 128].rearrange("h s -> s h"), ds[12 + bl],
                      reads=[K.r_fd], writes=[r_Ftok[blk]], slow=True)
                P.dma("sp", Rq[:, blk, :], K.fd[0:4, c0 + 64:c0 + 65].rearrange("h o -> o h").broadcast_to([128, 4]),
                      ds[16 + bl], reads=[K.r_fd], writes=[r_Rq[blk]], slow=True)
            for h in range(4):
                for which in range(2):
                    c0 = which * 512 + h * 128
                    for kc in range(KC):
                        _mm(P, pb[:], w[:, kc, c0:c0 + 128], uT[:, kc, :], kc == 0, kc == KC - 1, [r_w, r_uT], [rpb])
                    P.op("act", lambda e: e.activation(out=sqh[:], in_=pb[:], func=AF.Square),
                         reads=[rpb], writes=[r_sqh])
                    nb_, rnb = K.bank[7], K.rbank[7]
                    _mm(P, nb_[:], K.ones_bf[:], sqh[:], True, True, [K.r_const, r_sqh], [rnb])
                    act_rstd(P, rq[:], r_rq, nb_[:], rnb, 1.0 / 128, tmp[:], r_tmp)
                    if which == 0:
                        dst, rd = qn[:, h, :], [r_qn[h]]
                    else:
                        dst, rd = KT[:, h, tsl], [r_KT[h][t]]
                    P.op("dve", lambda e, dst=dst, which=which: e.scalar_tensor_tensor(
                        out=dst, in0=pb[:], scalar=fvec[:, which:which + 1], in1=rq[:], op0=ALU.mult, op1=ALU.mult),
                        reads=[rpb, r_fvec, r_rq], writes=rd)
            for bl in range(4):
                blk = t * 4 + bl
                for kc in range(KC):
                    _mm(P, pb[:], uT[:, kc, bl * 128:(bl + 1) * 128], w[:, kc, 1024:1536], kc == 0, kc == KC - 1,
                        [r_w, r_uT], [rpb])
                P.op("dve", lambda e, blk=blk: e.tensor_copy(out=VA[:, blk, :, 0:128],
                                                            in_=pb[:].rearrange("p (h v) -> p h v", h=4)),
                     reads=[rpb, r_VAone], writes=[r_VA[blk]])
            yb = t % 2
            t4 = t * 4
            if t > 0:
                P.dma("sp", Rt[:], K.fd[0:4, t * 512:t * 512 + 1].rearrange("h o -> o h").broadcast_to([128, 4]),
                      ds[20], reads=[K.r_fd], writes=[r_Rt], slow=True)
                P.op("dve", lambda e, t4=t4: e.tensor_tensor(
                    out=Boff[:, 0:t4, :], in0=Rt[:].unsqueeze(1).to_broadcast([128, t4, 4]), in1=Ftok[:, 0:t4, :],
                    op=ALU.subtract), reads=[r_Rt] + r_Ftok[0:t4], writes=[r_Boff])
                P.op("dve", lambda e, t4=t4: e.tensor_tensor(
                    out=cfac[:], in0=Rq[:, t4:t4 + 4, :], in1=Rt[:].unsqueeze(1).to_broadcast([128, 4, 4]),
                    op=ALU.subtract), reads=[r_Rt] + r_Rq[t4:t4 + 4], writes=[r_cfac])
                P.op("act", lambda e: e.activation(out=cfac[:], in_=cfac[:], func=AF.Exp), reads=[r_cfac], writes=[r_cfac])
            P.op("dve", lambda e, t4=t4: e.tensor_tensor(
                out=Bin[:], in0=Rq[:, t4:t4 + 4, :].unsqueeze(1).to_broadcast([128, 4, 4, 4]),
                in1=Ftok[:, t4:t4 + 4, :].unsqueeze(2).to_broadcast([128, 4, 4, 4]), op=ALU.subtract),
                reads=r_Rq[t4:t4 + 4] + r_Ftok[t4:t4 + 4], writes=[r_Bin])

            units = []
            for h in range(4):
                for kb in range(t4):
                    units.append(("off", h, kb))
                for kl in range(4):
                    units.append(("in", h, kl))
            started = {}

            def oreg(h, r):
                bi = (2 if h % 2 == 0 else 5) + r // 3
                c0 = (r % 3) * 129
                return bi, K.bank[bi][:, c0:c0 + 129], K.rbank[bi]

            def pv(h, r, lhsT, rhs, reads):
                bi, reg, rb = oreg(h, r)
                first = not started.get((h, bi), False)
                started[(h, bi)] = True
                _mm(P, reg, lhsT, rhs, first, False, reads, [rb], skip_group_check=True)

            def emit_scores(u, slot):
                kind, h, kk = u
                sbk, rsb = K.bank[slot % 2], K.rbank[slot % 2]
                if kind == "off":
                    _mm(P, sbk[:], KT[:, h, kk * 128:(kk + 1) * 128], qn[:, h, :], True, True,
                        [r_KT[h][kk // 4], r_qn[h]], [rsb])
                else:
                    kb = t4 + kk
                    n = (4 - kk) * 128
                    _mm(P, sbk[:, 0:n], KT[:, h, kb * 128:(kb + 1) * 128], qn[:, h, kk * 128:512], True, True,
                        [r_KT[h][t], r_qn[h]], [rsb])

            def emit_exp(u, slot):
                kind, h, kk = u
                sbk, rsb = K.bank[slot % 2], K.rbank[slot % 2]
                pt, rpt = PT[slot % 3], r_PT[slot % 3]
                if kind == "off":
                    P.op("act", lambda e: e.activation(out=pt[:], in_=sbk[:], func=AF.Exp, bias=Boff[:, kk, h:h + 1],
                                                       scale=SCALE), reads=[rsb, r_Boff], writes=[rpt])
                else:
                    for ql in range(kk, 4):
                        i = ql - kk
                        P.op("act", lambda e, i=i, ql=ql: e.activation(
                            out=pt[:, i * 128:(i + 1) * 128], in_=sbk[:, i * 128:(i + 1) * 128], func=AF.Exp,
                            bias=Bin[:, kk, ql, h:h + 1], scale=SCALE), reads=[rsb, r_Bin], writes=[rpt])
                    P.op("pool", lambda e: e.tensor_tensor(out=pt[:, 0:128], in0=pt[:, 0:128], in1=K.cmask_bf[:], op=ALU.mult),
                         reads=[rpt, K.r_const], writes=[rpt])

            def emit_pv(u, slot):
                kind, h, kk = u
                pt, rpt = PT[slot % 3], r_PT[slot % 3]
                if kind == "off":
                    for ql in range(4):
                        pv(h, ql, pt[:, ql * 128:(ql + 1) * 128], VA[:, kk, h, :], [rpt, r_VA[kk], r_VAone])
                else:
                    kb = t4 + kk
                    for ql in range(kk, 4):
                        i = ql - kk
                        pv(h, 4 + ql, pt[:, i * 128:(i + 1) * 128], VA[:, kb, h, :], [rpt, r_VA[kb], r_VAone])
                    if kk == 3:
                        finish_head(h)

            def finish_head(h):
                for ql in range(4):
                    _, oin, rin = oreg(h, 4 + ql)
                    if t > 0:
                        _, oof, rof = oreg(h, ql)
                        P.op("dve", lambda e, oof=oof, ql=ql: e.tensor_scalar(
                            out=otmp[:, 0:129], in0=oof, scalar1=cfac[:, ql, h:h + 1], scalar2=None, op0=ALU.mult),
                            reads=[rof, r_cfac], writes=[r_otmp])
                        P.op("dve", lambda e, oin=oin: e.tensor_tensor(out=otmp[:, 0:129], in0=otmp[:, 0:129], in1=oin, op=ALU.add),
                             reads=[rin, r_otmp], writes=[r_otmp])
                        src, rsrc = otmp[:, 0:129], [r_otmp]
                    else:
                        src, rsrc = oin, [rin]
                    P.op("dve", lambda e, src=src: e.reciprocal(out=rec[:, h:h + 1], in_=src[:, 128:129]),
                         reads=rsrc, writes=[r_rec[h]])
                    P.op("dve", lambda e, src=src, ql=ql: e.tensor_scalar(
                        out=ytok[:, ql, h * 128:(h + 1) * 128], in0=src[:, 0:128], scalar1=rec[:, h:h + 1], scalar2=None,
                        op0=ALU.mult), reads=rsrc + [r_rec[h]], writes=[r_ytok[ql]])

            prev = None
            for u in units:
                slot = nslot[0]
                nslot[0] += 1
                emit_scores(u, slot)
                if prev is not None:
                    emit_pv(*prev)
                emit_exp(u, slot)
                prev = (u, slot)
            emit_pv(*prev)
            tb = pb[:].bitcast(BF16)
            for ql in range(4):
                for h in range(4):
                    P.op("pe", lambda e, h=h, ql=ql: e.transpose(out=tb[:, h * 128:(h + 1) * 128],
                                                                in_=ytok[:, ql, h * 128:(h + 1) * 128], identity=K.ident_bf[:]),
                         reads=[r_ytok[ql], K.r_const], writes=[rpb])
                P.op("dve", lambda e, ql=ql, yb=yb: e.tensor_copy(
                    out=yTt[yb][:, :, ql * 128:(ql + 1) * 128], in_=tb[:, 0:512].rearrange("p (h s) -> p h s", h=4)),
                    reads=[rpb], writes=[r_yTt[yb]])
            dst = K.ybuf.rearrange("(c p) t -> p c t", p=128)[:, 4:8, tsl]
            P.dma("sp", dst, yTt[yb][:], ds[7 + yb], reads=[r_yTt[yb]], writes=[K.r_y[c][t] for c in range(4, 8)])
        P.emit_block(final=final)


def hg_phase(K, layer, hin, kind, final=False):
    nc, P, T, NT, dr = K.nc, K.P, K.T, K.NT, K.dr
    j = layer // 2
    ds = K.dsem_pool
    ret = kind == "ret"
    NCOL = 3072 if ret else 2048
    wname = "w_ret" if ret else "w_hg"
    with ExitStack() as es:
        def sb(name, shape, dt):
            return es.enter_context(nc.sbuf_tensor(f"g{layer}_{name}", shape, dt))
        w = sb("w", [128, KC, NCOL], BF16); r_w = Res()
        gain = sb("gain", [128, KC], F32); r_gain = Res()
        vec = sb("vec", [128, 12], F32); r_vec = Res()
        ht = [sb(f"ht{i}", [128, KC, 512], F32) for i in range(2)]; r_ht = RL(2)
        uT = [sb(f"uT{i}", [128, KC, 512], BF16) for i in range(2)]; r_uT = RL(2)
        sq = sb("sq", [128, KC, 512], BF16); r_sq = Res()
        rstd = sb("rstd", [128, 512], F32); r_rstd = Res()
        tmp = sb("tmp", [128, 512], F32); r_tmp = Res()
        rmask = sb("rmask", [128, 512], F32); r_rmask = Res()
        m2 = sb("m2", [128, 64], F32); r_m2 = Res()
        qfl = [sb(f"qf{i}", [128, 512], F32) for i in range(2)]; r_qfl = RL(2)
        kfl = [sb(f"kf{i}", [128, 512], F32) for i in range(2)]; r_kfl = RL(2)
        lfl = [sb(f"lf{i}", [128, 512], F32) for i in range(2)]; r_lfl = RL(2)
        cum = sb("cum", [128, 512], F32); r_cum = Res()
        e1 = sb("e1", [128, 512], F32); r_e1 = Res()
        ex = [sb(f"ex{i}", [128, 512], F32) for i in range(2)]; r_ex = RL(2)
        qh = [sb(f"qh{i}", [128, 512], BF16) for i in range(2)]; r_qh = RL(2)
        kh = [sb(f"kh{i}", [128, 512], BF16) for i in range(2)]; r_kh = RL(2)
        qi = [sb(f"qi{i}", [128, 512], BF16) for i in range(2)]; r_qi = RL(2)
        ko = [sb(f"ko{i}", [128, 512], BF16) for i in range(2)]; r_ko = RL(2)
        alast = [sb(f"alast{i}", [128, 8], F32) for i in range(2)]; r_alast = RL(2)
        kotok = [sb(f"kotok{i}", [128, 4, 128], BF16) for i in range(2)]; r_kotok = RL(2)
        vtok = [sb(f"vtok{i}", [128, 4, 512], BF16) for i in range(2)]; r_vtok = RL(2)
        sgt = [sb(f"sgt{i}", [128, 4, 512], BF16) for i in range(2)]; r_sgt = RL(2)
        PT = sb("PT", [128, 4, 64], BF16); r_PT = Res()
        S_f = sb("S_f", [128, 4, 2, 128], F32); r_Sf = [RL(2) for _ in range(4)]
        S_b = sb("S_b", [128, 4, 8, 128], BF16); r_Sb = [RL(8) for _ in range(4)]
        sqo = sb("sqo", [128, 512], BF16); r_sqo = Res()
        cen = sb("cen", [128, 512], F32); r_cen = Res()
        cn = sb("cn", [128, 512], F32); r_cn = Res()
        rstd2 = sb("rstd2", [128, 512], F32); r_rstd2 = Res()
        tmp2 = sb("tmp2", [128, 512], F32); r_tmp2 = Res()
        yTt = [sb(f"yTt{i}", [128, 4, 512], BF16) for i in range(2)]; r_yTt = RL(2)
        if ret:
            rot = [sb(f"rot{i}", [128, 2, 512], F32) for i in range(2)]; r_rot = RL(2)
            rtab = sb("rtab", [128, 4, 4, 64], F32); r_rtab = Res()
            rtmp = sb("rtmp", [128, 3, 64], F32); r_rtmp = Res()
            ral = sb("ral", [128, 4, 8], F32); r_ral = Res()
            meanb = sb("meanb", [128, 128], BF16); r_meanb = Res()

        P.dma("pool", w[:, :, 0:1024], dr[wname][j][:, :, 0:1024], K.dsem_q[0], writes=[r_w])
        P.dma("pool", w[:, :, 1024:2048], dr[wname][j][:, :, 1024:2048], K.dsem_q[0], writes=[r_w])
        if ret:
            P.dma("pool", w[:, :, 2048:3072], dr[wname][j][:, :, 2048:3072], K.dsem_q[0], writes=[r_w])
        P.dma("sp", gain[:], dr["norm_mix"][layer], ds[1], writes=[r_gain])
        if ret:
            P.dma("sp", vec[:], dr["ret_vec"][j], ds[30], writes=[r_vec])
        P.dma("sp", rmask[:], dr["rmask"], ds[31], writes=[r_rmask])
        P.op("dve", lambda e: e.tensor_copy(out=m2[0:64, :], in_=K.cmask_f[0:64, 0:64]), reads=[K.r_const], writes=[r_m2])
        P.op("dve", lambda e: e.tensor_copy(out=m2[64:128, :], in_=K.cmask_f[64:128, 64:128]), reads=[K.r_const, r_m2],
             writes=[r_m2])
        if not ret:
            raw = sb("raw", [128, 12], F32); r_raw = Res()
            P.dma("sp", raw[:], dr["hg_vec"][j], ds[32], writes=[r_raw])
            if j == 0:
                P.op("dve", lambda e: e.tensor_scalar(out=vec[:, 0:4], in0=raw[:, 0:4], scalar1=0.0, scalar2=None,
                                                      op0=ALU.mult), reads=[r_raw, r_vec], writes=[r_vec])
            else:
                P.op("dve", lambda e: e.tensor_tensor(out=vec[:, 0:4], in0=raw[:, 4:8], in1=raw[:, 0:4], op=ALU.subtract),
                     reads=[r_raw, r_vec], writes=[r_vec])
                P.op("act", lambda e: e.activation(out=vec[:, 0:4], in_=vec[:, 0:4], func=AF.Sigmoid),
                     reads=[r_vec], writes=[r_vec])
            P.op("dve", lambda e: e.tensor_scalar(out=vec[:, 4:8], in0=vec[:, 0:4], scalar1=-1.0, scalar2=1.0,
                                                  op0=ALU.mult, op1=ALU.add), reads=[r_vec], writes=[r_vec])
            P.op("dve", lambda e: e.tensor_copy(out=vec[:, 8:9], in_=raw[:, 8:9]), reads=[r_raw, r_vec], writes=[r_vec])
        P.op("pool", lambda e: e.memset(S_f[:], 0.0), writes=[x for l in r_Sf for x in l])
        P.op("pool", lambda e: e.memset(S_b[:], 0.0), writes=[x for l in r_Sb for x in l])

        def decay_factors(cum3, n, r_c, outs, al_out, r_al):
            n64 = n * 64
            e13 = e1[:, 0:n64].rearrange("p (c s) -> p c s", s=64)
            P.op("dve", lambda e: e.tensor_tensor(out=e13, in0=cum3, in1=cum3[:, :, 31:32].to_broadcast([128, n, 64]),
                                                  op=ALU.subtract), reads=[r_c], writes=[r_e1])
            P.op("act", lambda e: e.activation(out=ex[0][:, 0:n64], in_=e1[:, 0:n64], func=AF.Exp), reads=[r_e1], writes=[r_ex[0]])
            outs[0](ex[0][:, 0:n64], r_ex[0])
            P.op("act", lambda e: e.activation(out=ex[1][:, 0:n64], in_=e1[:, 0:n64], func=AF.Exp, scale=-1.0),
                 reads=[r_e1], writes=[r_ex[1]])
            outs[1](ex[1][:, 0:n64], r_ex[1])
            yield
            P.op("act", lambda e: e.activation(out=ex[0][:, 0:n64].rearrange("p (c s) -> p c s", s=64), in_=cum3, func=AF.Exp),
                 reads=[r_c], writes=[r_ex[0]])
            outs[2](ex[0][:, 0:n64], r_ex[0])
            P.op("dve", lambda e: e.tensor_tensor(out=e13, in0=cum3, in1=cum3[:, :, 63:64].to_broadcast([128, n, 64]),
                                                  op=ALU.subtract), reads=[r_c], writes=[r_e1])
            P.op("act", lambda e: e.activation(out=ex[1][:, 0:n64], in_=e1[:, 0:n64], func=AF.Exp, scale=-1.0),
                 reads=[r_e1], writes=[r_ex[1]])
            outs[3](ex[1][:, 0:n64], r_ex[1])
            P.op("act", lambda e: e.activation(out=al_out, in_=cum3[:, :, 63:64], func=AF.Exp), reads=[r_c], writes=[r_al])
            yield

        if ret:
            P.op("pool", lambda e: e.memset(meanb[:], 1.0 / 128), writes=[r_meanb])
            for h in range(4):
                P.op("dve", lambda e, h=h: e.tensor_scalar(out=rtmp[:, 0, :], in0=rmask[:, 0:64], scalar1=0.0,
                                                          scalar2=vec[:, h:h + 1], op0=ALU.mult, op1=ALU.add),
                     reads=[r_rmask, r_vec, r_rtmp], writes=[r_rtmp])
                P.op("dve", lambda e: e.tensor_tensor_scan(out=cum[:, 0:64], data0=rmask[:, 0:64], data1=rtmp[:, 0, :],
                                                           initial=0.0, op0=ALU.mult, op1=ALU.add),
                     reads=[r_rmask, r_rtmp], writes=[r_cum])

                def mk(i, h=h):
                    def f(ap, r):
                        P.op("dve", lambda e: e.tensor_copy(out=rtab[:, h, i, :], in_=ap), reads=[r, r_rtab], writes=[r_rtab])
                    return f
                for _ in decay_factors(cum[:, 0:64].rearrange("p (c s) -> p c s", s=64), 1, r_cum, [mk(0), mk(1), mk(2), mk(3)],
                                       ral[:, h, 0:1].rearrange("p (c o) -> p c o", o=1), r_ral):
                    pass
                P.op("dve", lambda e, h=h: e.tensor_copy(out=ral[:, h, 1:8], in_=ral[:, h, 0:1].to_broadcast([128, 7])),
                     reads=[r_ral], writes=[r_ral])

        pb, rpb = K.bank[6], K.rbank[6]
        tb = pb[:].bitcast(BF16)
        gbk, rgb = K.bank[1], K.rbank[1]
        sbk, rsb = K.bank[0], K.rbank[0]

        def proj_fm(bank, rbank, b, c0):
            for kc in range(KC):
                _mm(P, bank[:], w[:, kc, c0:c0 + 128], uT[b][:, kc, :], kc == 0, kc == KC - 1, [r_w, r_uT[b]], [rbank])

        def tile_prologue(t):
            b = t % 2
            tsl = slice(t * 512, (t + 1) * 512)
            if t == 0:
                P.dma("sp", ht[0][:], hview(hin, 0), ds[2], reads=[K.r_h[c][0] for c in range(KC)], writes=[r_ht[0]])
            if t + 1 < NT:
                P.dma("sp", ht[1 - b][:], hview(hin, t + 1), ds[2 + (1 - b)],
                      reads=[K.r_h[c][t + 1] for c in range(KC)], writes=[r_ht[1 - b]])
            if ret:
                P.dma("sp", rot[b][:], dr["rot_tab"][:, :, tsl], ds[4 + b], writes=[r_rot[b]])
            rmsnorm_tile2(K, ht[b], r_ht[b], gain, r_gain, uT[b], r_uT[b], sq, r_sq, rstd, r_rstd, tmp, r_tmp, 7)
            yield
            pbanks = [(K.bank[1], K.rbank[1]), (K.bank[7], K.rbank[7])]
            for bl in range(4):
                bk, rbk = pbanks[bl % 2]
                for kc in range(KC):
                    _mm(P, bk[:], uT[b][:, kc, bl * 128:(bl + 1) * 128], w[:, kc, 1024:1536], kc == 0, kc == KC - 1,
                        [r_w, r_uT[b]], [rbk])
                P.op("act", lambda e, bl=bl, b=b, bk=bk: e.copy(out=vtok[b][:, bl, :], in_=bk[:]), reads=[rbk],
                     writes=[r_vtok[b]])
                yield
            for h in range(4):
                bk, rbk = pbanks[h % 2]
                proj_fm(bk, rbk, b, 1536 + h * 128)
                P.op("act", lambda e, h=h, b=b, bk=bk: e.activation(out=sgt[b][:, h, :], in_=bk[:], func=AF.Silu),
                     reads=[rbk], writes=[r_sgt[b]])
                yield

        def stageA1(t, h, i):
            b = t % 2
            a = i % 2
            qf, kf, lf = qfl[a], kfl[a], lfl[a]
            r_qf, r_kf, r_lf = r_qfl[a], r_kfl[a], r_lfl[a]
            if ret:
                for which, dstf, rdst in ((0, qf, r_qf), (1, kf, r_kf)):
                    proj_fm(pb, rpb, b, which * 512 + h * 128)
                    P.op("dve", lambda e, b=b: e.tensor_tensor(out=lf[:], in0=pb[:], in1=rot[b][:, 0, :], op=ALU.mult),
                         reads=[rpb, r_rot[b]], writes=[r_lf])
                    proj_fm(pb, rpb, b, 2048 + which * 512 + h * 128)
                    P.op("dve", lambda e, b=b, dstf=dstf: e.tensor_tensor(out=dstf[:], in0=pb[:], in1=rot[b][:, 1, :],
                                                                         op=ALU.mult),
                         reads=[rpb, r_rot[b]], writes=[rdst])
                    P.op("pool", lambda e, dstf=dstf: e.tensor_tensor(out=dstf[:], in0=dstf[:], in1=lf[:], op=ALU.add),
                         reads=[r_lf, rdst], writes=[rdst])
                    yield
            else:
                proj_fm(pb, rpb, b, h * 128)
                P.op("act", lambda e: e.copy(out=qf[:], in_=pb[:]), reads=[rpb], writes=[r_qf])
                proj_fm(pb, rpb, b, 512 + h * 128)
                P.op("act", lambda e: e.activation(out=lf[:], in_=pb[:], func=AF.Exp, scale=-1.0), reads=[rpb], writes=[r_lf])
                yield
                P.op("act", lambda e: e.activation(out=lf[:], in_=lf[:], func=AF.Ln, bias=1.0), reads=[r_lf], writes=[r_lf])
                yield
                P.op("act", lambda e: e.activation(out=lf[:], in_=lf[:], func=AF.Exp, scale=-1.0), reads=[r_lf], writes=[r_lf])
                yield
                P.op("dve", lambda e: e.tensor_scalar(out=lf[:], in0=lf[:], scalar1=vec[:, 4 + h:5 + h],
                                                      scalar2=vec[:, h:h + 1], op0=ALU.mult, op1=ALU.add),
                     reads=[r_lf, r_vec], writes=[r_lf])
                yield
                P.op("pool", lambda e: e.tensor_scalar(out=kf[:], in0=lf[:], scalar1=-1.0, scalar2=1.0,
                                                       op0=ALU.mult, op1=ALU.add), reads=[r_lf], writes=[r_kf])
                P.op("act", lambda e: e.activation(out=lf[:], in_=lf[:], func=AF.Ln), reads=[r_lf], writes=[r_lf])
                yield

        def stageA2(t, h, i):
            b = t % 2
            a = i % 2
            s = i % 2
            qf, kf, lf = qfl[a], kfl[a], lfl[a]
            r_qf, r_kf, r_lf = r_qfl[a], r_kfl[a], r_lfl[a]
            if ret:
                def tabv(k):
                    return rtab[:, h, k:k + 1, :].to_broadcast([128, 8, 64])

                def v3(x):
                    return x.rearrange("p (c s) -> p c s", s=64)
                P.op("pool", lambda e: e.tensor_tensor(out=v3(qh[s][:]), in0=v3(qf[:]), in1=tabv(0), op=ALU.mult),
                     reads=[r_qf, r_rtab], writes=[r_qh[s]])
                P.op("dve", lambda e: e.scalar_tensor_tensor(out=v3(kh[s][:]), in0=v3(kf[:]), scalar=0.125, in1=tabv(1),
                                                             op0=ALU.mult, op1=ALU.mult), reads=[r_kf, r_rtab], writes=[r_kh[s]])
                yield
                P.op("pool", lambda e: e.tensor_tensor(out=v3(qi[s][:]), in0=v3(qf[:]), in1=tabv(2), op=ALU.mult),
                     reads=[r_qf, r_rtab], writes=[r_qi[s]])
                P.op("dve", lambda e: e.scalar_tensor_tensor(out=v3(ko[s][:]), in0=v3(kf[:]), scalar=0.125, in1=tabv(3),
                                                             op0=ALU.mult, op1=ALU.mult), reads=[r_kf, r_rtab], writes=[r_ko[s]])
                al, r_al = ral[:, h, :], r_ral
                yield
            else:
                P.op("dve", lambda e: e.tensor_tensor_scan(out=cum[:], data0=rmask[:], data1=lf[:], initial=0.0,
                                                           op0=ALU.mult, op1=ALU.add),
                     reads=[r_rmask, r_lf], writes=[r_cum])
                yield

                def o_qh(ap, r):
                    P.op("pool", lambda e: e.tensor_tensor(out=qh[s][:], in0=qf[:], in1=ap, op=ALU.mult),
                         reads=[r_qf, r], writes=[r_qh[s]])

                def o_kh(ap, r):
                    P.op("dve", lambda e: e.tensor_tensor(out=kh[s][:], in0=kf[:], in1=ap, op=ALU.mult),
                         reads=[r_kf, r], writes=[r_kh[s]])

                def o_qi(ap, r):
                    P.op("pool", lambda e: e.tensor_tensor(out=qi[s][:], in0=qf[:], in1=ap, op=ALU.mult),
                         reads=[r_qf, r], writes=[r_qi[s]])

                def o_ko(ap, r):
                    P.op("dve", lambda e: e.tensor_tensor(out=ko[s][:], in0=kf[:], in1=ap, op=ALU.mult),
                         reads=[r_kf, r], writes=[r_ko[s]])
                yield from decay_factors(cum[:].rearrange("p (c s) -> p c s", s=64), 8, r_cum, [o_qh, o_kh, o_qi, o_ko],
                                         alast[s][:].rearrange("p (c o) -> p c o", o=1), r_alast[s])
                al, r_al = alast[s][:], r_alast[s]
            K._al[(t, h)] = (al, r_al)
            for bl in range(4):
                P.op("pe", lambda e, bl=bl: e.transpose(out=tb[:, bl * 128:(bl + 1) * 128],
                                                        in_=ko[s][:, bl * 128:(bl + 1) * 128], identity=K.ident_bf[:]),
                     reads=[r_ko[s], K.r_const], writes=[rpb])
            P.op("act", lambda e: e.copy(out=kotok[s][:].rearrange("p b d -> p (b d)"), in_=tb[:, 0:512]),
                 reads=[rpb], writes=[r_kotok[s]])
            yield

        def stageBC(t, h, s):
            b = t % 2
            yb = t % 2
            hs = slice(h * 128, (h + 1) * 128)
            al, r_al = K._al[(t, h)]
            ob, rob = K.bank[2 + (h % 2)], K.rbank[2 + (h % 2)]
            for c in range(8):
                bl, p0 = c // 2, (c % 2) * 64
                dbk, rdb = K.bank[4 + (c % 2)], K.rbank[4 + (c % 2)]
                _mm(P, dbk[:, (c // 2) * 128:(c // 2 + 1) * 128], kotok[s][p0:p0 + 64, bl, :], vtok[b][p0:p0 + 64, bl, hs],
                    True, True, [r_kotok[s], r_vtok[b]], [rdb])
            yield
            for c in range(8):
                p0 = (c % 2) * 64
                csl = slice(c * 64, (c + 1) * 64)
                _mm(P, sbk[p0:p0 + 64, (c // 2) * 64:(c // 2 + 1) * 64], kh[s][:, csl], qh[s][:, csl], True, True,
                    [r_kh[s], r_qh[s]], [rsb])
            P.op("dve", lambda e: e.tensor_tensor(out=PT[:], in0=sbk[:, 0:256].rearrange("p (c s) -> p c s", s=64),
                                                  in1=m2[:].unsqueeze(1).to_broadcast([128, 4, 64]), op=ALU.mult),
                 reads=[rsb, r_m2], writes=[r_PT])
            yield
            for c in range(8):
                dbk, rdb = K.bank[4 + (c % 2)], K.rbank[4 + (c % 2)]
                src, dst = (c + 1) % 2, c % 2
                P.op("dve", lambda e, c=c, src=src, dst=dst, dbk=dbk: e.scalar_tensor_tensor(
                    out=S_f[:, h, dst, :], in0=S_f[:, h, src, :], scalar=al[:, c:c + 1],
                    in1=dbk[:, (c // 2) * 128:(c // 2 + 1) * 128], op0=ALU.mult, op1=ALU.add),
                    reads=[r_Sf[h][src], r_al, rdb], writes=[r_Sf[h][dst]])
                if c < 7:
                    P.op("pool", lambda e, c=c, dst=dst: e.tensor_copy(out=S_b[:, h, c + 1, :], in_=S_f[:, h, dst, :]),
                         reads=[r_Sf[h][dst]], writes=[r_Sb[h][c + 1]])
                if c % 2 == 1:
                    yield
            for c in range(8):
                bl, p0 = c // 2, (c % 2) * 64
                csl = slice(c * 64, (c + 1) * 64)
                _mm(P, ob[:, csl], vtok[b][p0:p0 + 64, bl, hs], PT[p0:p0 + 64, c // 2, :], True, False,
                    [r_vtok[b], r_PT], [rob])
                _mm(P, ob[:, csl], S_b[:, h, c, :], qi[s][:, csl], False, True, [r_Sb[h][c], r_qi[s]], [rob])
                if c % 2 == 1:
                    yield
            P.op("pool", lambda e: e.tensor_copy(out=S_b[:, h, 0, :], in_=S_f[:, h, 1, :]),
                 reads=[r_Sf[h][1]], writes=[r_Sb[h][0]])
            nb_, rnb = K.bank[7], K.rbank[7]
            if ret:
                P.op("act", lambda e: e.copy(out=sqo[:], in_=ob[:]), reads=[rob], writes=[r_sqo])
                _mm(P, nb_[:], meanb[:], sqo[:], True, True, [r_meanb, r_sqo], [rnb])
                P.op("act", lambda e: e.copy(out=tmp2[:], in_=nb_[:]), reads=[rnb], writes=[r_tmp2])
                yield
                P.op("dve", lambda e: e.tensor_tensor(out=cen[:], in0=ob[:], in1=tmp2[:], op=ALU.subtract),
                     reads=[rob, r_tmp2], writes=[r_cen])
                P.op("act", lambda e: e.activation(out=sqo[:], in_=cen[:], func=AF.Square), reads=[r_cen], writes=[r_sqo])
                osrc, r_osrc, nscale = cen[:], r_cen, 1.0
                _mm(P, nb_[:], meanb[:], sqo[:], True, True, [r_meanb, r_sqo], [rnb])
                nwcol = vec[:, 4 + h:5 + h]
            else:
                P.op("act", lambda e: e.activation(out=sqo[:], in_=ob[:], func=AF.Square), reads=[rob], writes=[r_sqo])
                osrc, r_osrc, nscale = ob[:], rob, 1.0 / 128
                _mm(P, nb_[:], K.ones_bf[:], sqo[:], True, True, [K.r_const, r_sqo], [rnb])
                nwcol = vec[:, 8:9]
            yield
            act_rstd(P, rstd2[:], r_rstd2, nb_[:], rnb, nscale, tmp2[:], r_tmp2)
            P.op("dve", lambda e: e.scalar_tensor_tensor(out=cn[:], in0=osrc, scalar=nwcol, in1=rstd2[:], op0=ALU.mult,
                                                         op1=ALU.mult), reads=[r_osrc, r_vec, r_rstd2], writes=[r_cn])
            P.op("pool", lambda e: e.tensor_tensor(out=yTt[yb][:, h, :], in0=cn[:], in1=sgt[b][:, h, :], op=ALU.mult),
                 reads=[r_cn, r_sgt[b]], writes=[r_yTt[yb]])
            yield
            if h == 3:
                tsl = slice(t * 512, (t + 1) * 512)
                dst = K.ybuf.rearrange("(c p) t -> p c t", p=128)[:, 0:4, tsl]
                P.dma("sp", dst, yTt[yb][:], ds[7 + yb], reads=[r_yTt[yb]], writes=[K.r_y[c][t] for c in range(4)])

        K._al = {}
        items = [(t, h) for t in range(NT) for h in range(4)]
        n_items = len(items)

        def run_all(g):
            for _ in g:
                pass
        run_all(tile_prologue(0))
        run_all(stageA1(items[0][0], items[0][1], 0))
        run_all(stageA2(items[0][0], items[0][1], 0))
        if n_items > 1:
            run_all(stageA1(items[1][0], items[1][1], 1))
        bg = None
        for i, (t, h) in enumerate(items):
            if h == 0 and t + 1 < NT:
                bg = tile_prologue(t + 1)
            if h == 2 and bg is not None:
                run_all(bg)
                bg = None
            gens = [stageBC(t, h, i % 2)]
            if i + 1 < n_items:
                gens.append(stageA2(items[i + 1][0], items[i + 1][1], i + 1))
            if i + 2 < n_items:
                gens.append(stageA1(items[i + 2][0], items[i + 2][1], i + 2))
            while gens:
                for g in list(gens):
                    try:
                        next(g)
                    except StopIteration:
                        gens.remove(g)
                if bg is not None:
                    try:
                        next(bg)
                    except StopIteration:
                        bg = None
        P.emit_block(final=final)


def ssd_phase(K, layer, hin, final=False):
    nc, P, T, NT, dr = K.nc, K.P, K.T, K.NT, K.dr
    j = layer // 2
    ds = K.dsem_pool
    with ExitStack() as es:
        def sb(name, shape, dt):
            return es.enter_context(nc.sbuf_tensor(f"s{layer}_{name}", shape, dt))
        w = sb("w", [128, KC, 1288], BF16); r_w = Res()
        gain = sb("gain", [128, KC], F32); r_gain = Res()
        rows = sb("rows", [128, 536], F32); r_rows = Res()
        cw = sb("cw", [128, 6, 5], F32); r_cw = Res()
        smask = sb("smask", [128, 128], F32); r_smask = Res()
        onesF = sb("onesF", [128, 128], F32); r_onesF = Res()
        negA = sb("negA", [128, 8], F32); r_negA = Res()
        ht = [sb(f"ht{i}", [128, KC, 512], F32) for i in range(2)]; r_ht = RL(2)
        uT = sb("uT", [128, KC, 512], BF16); r_uT = Res()
        sq = sb("sq", [128, KC, 512], BF16); r_sq = Res()
        rstd = sb("rstd", [128, 512], F32); r_rstd = Res()
        tmp = sb("tmp", [128, 512], F32); r_tmp = Res()
        xpad = sb("xpad", [128, 6, 515], F32); r_xpad = RL(6)
        acc = sb("acc", [128, 512], F32); r_acc = Res()
        xc = sb("xc", [128, 6, 512], BF16); r_xc = RL(6)
        vtok = sb("vtok", [128, 512], BF16); r_vtok = Res()
        btok = sb("btok", [128, 128], BF16); r_btok = Res()
        sz = sb("sz", [128, 512], F32); r_sz = Res()
        sm = [sb(f"sm{i}", [128, 8], F32) for i in range(8)]; r_sm = RL(8)
        sm16 = sb("sm16", [128, 16], F32); r_sm16 = Res()
        vp = sb("vp", [128, 512], BF16); r_vp = Res()
        vpp = sb("vpp", [128, 512], BF16); r_vpp = Res()
        LM = sb("LM", [128, 8, 128], F32); r_LM = Res()
        E = sb("E", [128, 8, 128], F32); r_E = Res()
        GM = sb("GM", [128, 2, 128], F32); r_GM = Res()
        PT = sb("PT", [128, 8, 128], BF16); r_PT = Res()
        o1 = sb("o1", [128, 512], F32); r_o1 = Res()
        o2 = sb("o2", [128, 512], F32); r_o2 = Res()
        ss = sb("ss", [128, 2], F32); r_ss = Res()
        ss2 = sb("ss2", [128, 2], F32); r_ss2 = Res()
        S_f = sb("S_f", [128, 512], F32); r_Sf = Res()
        S_b = sb("S_b", [128, 512], BF16); r_Sb = Res()
        ytok = sb("ytok", [128, 512], BF16); r_ytok = Res()
        yTt = [sb(f"yTt{i}", [128, 4, 512], BF16) for i in range(2)]; r_yTt = RL(2)

        P.dma("pool", w[:, :, 0:768], dr["w_ssd"][j][:, :, 0:768], K.dsem_q[0], writes=[r_w])
        P.dma("pool", w[:, :, 768:1288], dr["w_ssd"][j][:, :, 768:1288], K.dsem_q[0], writes=[r_w])
        P.dma("sp", gain[:], dr["norm_mix"][layer], ds[1], writes=[r_gain])
        P.dma("sp", rows[:], dr["ssd_rows"][j], ds[30], writes=[r_rows])
        P.dma("sp", cw[:], dr["ssd_conv"][j], ds[31], writes=[r_cw])
        P.dma("sp", smask[:], dr["smask"], ds[32], writes=[r_smask])
        P.op("pool", lambda e: e.memset(onesF[:], 1.0), writes=[r_onesF])
        P.op("pool", lambda e: e.memset(S_f[:], 0.0), writes=[r_Sf])
        P.op("pool", lambda e: e.memset(S_b[:], 0.0), writes=[r_Sb])
        P.op("pool", lambda e: e.memset(xpad[:], 0.0), writes=r_xpad)
        P.op("act", lambda e: e.activation(out=negA[:], in_=rows[:, 8:16], func=AF.Exp), reads=[r_rows], writes=[r_negA])
        P.op("dve", lambda e: e.tensor_scalar(out=negA[:], in0=negA[:], scalar1=-1.0, scalar2=None, op0=ALU.mult),
             reads=[r_negA], writes=[r_negA])

        pb, rpb = K.bank[6], K.rbank[6]
        tb = pb[:].bitcast(BF16)
        P.dma("sp", ht[0][:], hview(hin, 0), ds[2], reads=[K.r_h[c][0] for c in range(KC)], writes=[r_ht[0]])
        for t in range(NT):
            b = t % 2
            yb = t % 2
            tsl = slice(t * 512, (t + 1) * 512)
            if t + 1 < NT:
                P.dma("sp", ht[1 - b][:], hview(hin, t + 1), ds[2 + (1 - b)],
                      reads=[K.r_h[c][t + 1] for c in range(KC)], writes=[r_ht[1 - b]])
            rmsnorm_tile2(K, ht[b], r_ht[b], gain, r_gain, uT, r_uT, sq, r_sq, rstd, r_rstd, tmp, r_tmp, 7)
            for cc in range(6):
                c0 = 512 + cc * 128
                for kc in range(KC):
                    _mm(P, pb[:], w[:, kc, c0:c0 + 128], uT[:, kc, :], kc == 0, kc == KC - 1, [r_w, r_uT], [rpb])
                P.op("dve", lambda e, cc=cc: e.tensor_copy(out=xpad[:, cc, 3:515], in_=pb[:]), reads=[rpb], writes=[r_xpad[cc]])
                P.op("dve", lambda e, cc=cc: e.tensor_scalar(out=acc[:], in0=xpad[:, cc, 0:512], scalar1=cw[:, cc, 0:1],
                                                            scalar2=None, op0=ALU.mult),
                     reads=[r_xpad[cc], r_cw], writes=[r_acc])
                for k in range(1, 4):
                    P.op("dve", lambda e, cc=cc, k=k: e.scalar_tensor_tensor(
                        out=acc[:], in0=xpad[:, cc, k:k + 512], scalar=cw[:, cc, k:k + 1], in1=acc[:],
                        op0=ALU.mult, op1=ALU.add), reads=[r_xpad[cc], r_cw, r_acc], writes=[r_acc])
                P.op("act", lambda e, cc=cc: e.activation(out=xc[:, cc, :], in_=acc[:], func=AF.Silu, bias=cw[:, cc, 4:5]),
                     reads=[r_acc, r_cw], writes=[r_xc[cc]])
                P.op("dve", lambda e, cc=cc: e.tensor_copy(out=xpad[:, cc, 0:3], in_=xpad[:, cc, 512:515]),
                     reads=[r_xpad[cc]], writes=[r_xpad[cc]])
            for bl in range(4):
                bsl = slice(bl * 128, (bl + 1) * 128)
                for cc in range(4):
                    P.op("pe", lambda e, cc=cc, bsl=bsl: e.transpose(out=tb[:, cc * 128:(cc + 1) * 128], in_=xc[:, cc, bsl],
                                                                    identity=K.ident_bf[:]),
                         reads=[r_xc[cc], K.r_const], writes=[rpb])
                P.op("dve", lambda e: e.tensor_copy(out=vtok[:], in_=tb[:, 0:512]), reads=[rpb], writes=[r_vtok])
                P.op("pe", lambda e, bsl=bsl: e.transpose(out=tb[:, 0:128], in_=xc[:, 4, bsl], identity=K.ident_bf[:]),
                     reads=[r_xc[4], K.r_const], writes=[rpb])
                P.op("dve", lambda e: e.tensor_copy(out=btok[:], in_=tb[:, 0:128]), reads=[rpb], writes=[r_btok])
                for kc in range(KC):
                    _mm(P, pb[:], uT[:, kc, bsl], w[:, kc, 0:512], kc == 0, kc == KC - 1, [r_w, r_uT], [rpb])
                P.op("act", lambda e: e.activation(out=sz[:], in_=pb[:], func=AF.Silu), reads=[rpb], writes=[r_sz])
                for kc in range(KC):
                    _mm(P, pb[:, 0:8], uT[:, kc, bsl], w[:, kc, 1280:1288], kc == 0, kc == KC - 1, [r_w, r_uT], [rpb])
                x_, ax, mx, dt_, l_, cumt, wdec, ecum = sm
                P.op("dve", lambda e: e.tensor_tensor(out=x_[:], in0=pb[:, 0:8], in1=rows[:, 0:8], op=ALU.add),
                     reads=[rpb, r_rows], writes=[r_sm[0]])
                P.op("dve", lambda e: e.scalar_tensor_tensor(out=ax[:], in0=x_[:], scalar=-1.0, in1=x_[:], op0=ALU.mult,
                                                             op1=ALU.min), reads=[r_sm[0]], writes=[r_sm[1]])
                P.op("act", lambda e: e.activation(out=ax[:], in_=ax[:], func=AF.Exp), reads=[r_sm[1]], writes=[r_sm[1]])
                P.op("act", lambda e: e.activation(out=ax[:], in_=ax[:], func=AF.Ln, bias=1.0), reads=[r_sm[1]], writes=[r_sm[1]])
                P.op("dve", lambda e: e.scalar_tensor_tensor(out=dt_[:], in0=x_[:], scalar=0.0, in1=ax[:], op0=ALU.max,
                                                             op1=ALU.add), reads=[r_sm[0], r_sm[1]], writes=[r_sm[3]])
                P.op("dve", lambda e: e.tensor_tensor(out=l_[:], in0=dt_[:], in1=negA[:], op=ALU.mult),
                     reads=[r_sm[3], r_negA], writes=[r_sm[4]])
                cb, rcb = K.bank[2], K.rbank[2]
                _mm(P, cb[:, 256:264], K.cmask_f[:], l_[:], True, True, [K.r_const, r_sm[4]], [rcb])
                _mm(P, cb[:, 264:272], onesF[:], l_[:], True, True, [r_onesF, r_sm[4]], [rcb])
                P.op("dve", lambda e: e.tensor_copy(out=sm16[:], in_=cb[:, 256:272]), reads=[rcb], writes=[r_sm16])
                P.op("act", lambda e: e.activation(out=ecum[:], in_=sm16[:, 0:8], func=AF.Exp), reads=[r_sm16], writes=[r_sm[7]])
                P.op("dve", lambda e: e.tensor_tensor(out=wdec[:], in0=sm16[:, 8:16], in1=sm16[:, 0:8], op=ALU.subtract),
                     reads=[r_sm16], writes=[r_sm[6]])
                P.op("act", lambda e: e.activation(out=wdec[:], in_=wdec[:], func=AF.Exp), reads=[r_sm[6]], writes=[r_sm[6]])
                P.op("act", lambda e: e.activation(out=cumt[:], in_=sm16[:, 8:16], func=AF.Exp), reads=[r_sm16], writes=[r_sm[5]])
                v3 = vtok[:].rearrange("p (h d) -> p h d", h=8)
                P.op("dve", lambda e, v3=v3: e.tensor_tensor(out=vp[:].rearrange("p (h d) -> p h d", h=8), in0=v3,
                                                            in1=dt_[:].unsqueeze(2).to_broadcast([128, 8, 64]), op=ALU.mult),
                     reads=[r_vtok, r_sm[3]], writes=[r_vp])
                P.op("dve", lambda e: e.tensor_tensor(out=vpp[:].rearrange("p (h d) -> p h d", h=8),
                                                      in0=vp[:].rearrange("p (h d) -> p h d", h=8),
                                                      in1=wdec[:].unsqueeze(2).to_broadcast([128, 8, 64]), op=ALU.mult),
                     reads=[r_vp, r_sm[6]], writes=[r_vpp])
                P.op("dve", lambda e: e.tensor_tensor(out=LM[:], in0=smask[:].unsqueeze(1).to_broadcast([128, 8, 128]),
                                                      in1=l_[:].unsqueeze(2).to_broadcast([128, 8, 128]), op=ALU.mult),
                     reads=[r_smask, r_sm[4]], writes=[r_LM])
                for hh in range(2):
                    db, rdb = K.bank[hh], K.rbank[hh]
                    for h4 in range(4):
                        h = hh * 4 + h4
                        _mm(P, db[:, h4 * 128:(h4 + 1) * 128], LM[:, h, :], K.cmask_f[:], True, True, [r_LM, K.r_const], [rdb])
                    P.op("act", lambda e, hh=hh, db=db: e.activation(out=E[:, hh * 4:(hh + 1) * 4, :].rearrange("p h i -> p (h i)"),
                                                                   in_=db[:], func=AF.Exp), reads=[rdb], writes=[r_E])
                gbank = [(K.bank[2], K.rbank[2]), (K.bank[7], K.rbank[7])]
                for g in range(2):
                    gs = slice(g * 64, (g + 1) * 64)
                    gb_, rgb = gbank[g]
                    _mm(P, gb_[:, 0:128], xc[gs, 4, bsl], xc[gs, 5, bsl], True, True, [r_xc[4], r_xc[5]], [rgb])
                    P.op("dve", lambda e, g=g, gb_=gb_: e.tensor_tensor(out=GM[:, g, :], in0=gb_[:, 0:128], in1=K.cmask_f[:],
                                                                      op=ALU.mult), reads=[rgb, K.r_const], writes=[r_GM])
                for g in range(2):
                    P.op("dve", lambda e, g=g: e.tensor_tensor(out=PT[:, g * 4:(g + 1) * 4, :], in0=E[:, g * 4:(g + 1) * 4, :],
                                                              in1=GM[:, g:g + 1, :].to_broadcast([128, 4, 128]), op=ALU.mult),
                         reads=[r_E, r_GM], writes=[r_PT])
                ab, rab = K.bank[3], K.rbank[3]
                bb, rbb = K.bank[4], K.rbank[4]
                for h in range(8):
                    _mm(P, ab[:, h * 64:(h + 1) * 64], PT[:, h, :], vp[:, h * 64:(h + 1) * 64], True, True, [r_PT, r_vp], [rab])
                ibank = [(K.bank[4], K.rbank[4]), (K.bank[7], K.rbank[7])]
                for g in range(2):
                    gs = slice(g * 64, (g + 1) * 64)
                    ib_, rib = ibank[g]
                    _mm(P, ib_[:, g * 256:(g + 1) * 256], xc[gs, 5, bsl], S_b[gs, g * 256:(g + 1) * 256], True, True,
                        [r_xc[5], r_Sb], [rib])
                    P.op("dve", lambda e, g=g, ib_=ib_: e.tensor_tensor(
                        out=o1[:, g * 256:(g + 1) * 256].rearrange("p (h d) -> p h d", h=4),
                        in0=ib_[:, g * 256:(g + 1) * 256].rearrange("p (h d) -> p h d", h=4),
                        in1=ecum[:, g * 4:(g + 1) * 4].unsqueeze(2).to_broadcast([128, 4, 64]), op=ALU.mult),
                        reads=[rib, r_sm[7]], writes=[r_o1])
                P.op("dve", lambda e: e.tensor_tensor(out=o1[:], in0=o1[:], in1=ab[:], op=ALU.add),
                     reads=[r_o1, rab], writes=[r_o1])
                sbk, rsb = K.bank[5], K.rbank[5]
                _mm(P, sbk[:], btok[:], vpp[:], True, True, [r_btok, r_vpp], [rsb])
                for g in range(2):
                    gs = slice(g * 64, (g + 1) * 64)
                    sv = S_f[gs, g * 256:(g + 1) * 256].rearrange("p (h d) -> p h d", h=4)
                    P.op("dve", lambda e, g=g, gs=gs, sv=sv: e.tensor_tensor(
                        out=sv, in0=sv, in1=cumt[gs, g * 4:(g + 1) * 4].unsqueeze(2).to_broadcast([64, 4, 64]), op=ALU.mult),
                        reads=[r_Sf, r_sm[5]], writes=[r_Sf])
                    P.op("dve", lambda e, g=g, gs=gs: e.tensor_tensor(
                        out=S_f[gs, g * 256:(g + 1) * 256], in0=S_f[gs, g * 256:(g + 1) * 256],
                        in1=sbk[gs, g * 256:(g + 1) * 256], op=ALU.add), reads=[r_Sf, rsb], writes=[r_Sf])
                P.op("pool", lambda e: e.tensor_copy(out=S_b[:], in_=S_f[:]), reads=[r_Sf], writes=[r_Sb])
                P.op("dve", lambda e, v3=v3: e.tensor_tensor(out=o2[:].rearrange("p (h d) -> p h d", h=8), in0=v3,
                                                            in1=rows[:, 16:24].unsqueeze(2).to_broadcast([128, 8, 64]),
                                                            op=ALU.mult), reads=[r_vtok, r_rows], writes=[r_o2])
                P.op("dve", lambda e: e.tensor_tensor(out=o2[:], in0=o2[:], in1=o1[:], op=ALU.add),
                     reads=[r_o2, r_o1], writes=[r_o2])
                P.op("dve", lambda e: e.tensor_tensor(out=o2[:], in0=o2[:], in1=sz[:], op=ALU.mult),
                     reads=[r_o2, r_sz], writes=[r_o2])
                P.op("dve", lambda e: e.tensor_tensor(out=o1[:], in0=o2[:], in1=o2[:], op=ALU.mult),
                     reads=[r_o2, r_o1], writes=[r_o1])
                P.op("dve", lambda e: e.tensor_reduce(out=ss[:], in_=o1[:].rearrange("p (g d) -> p g d", g=2), axis=AX.X,
                                                      op=ALU.add), reads=[r_o1], writes=[r_ss])
                act_rstd(P, ss[:], r_ss, ss[:], r_ss, 1.0 / 256, ss2[:], r_ss2)
                P.op("dve", lambda e: e.tensor_tensor(out=o2[:].rearrange("p (g d) -> p g d", g=2),
                                                      in0=o2[:].rearrange("p (g d) -> p g d", g=2),
                                                      in1=ss[:].unsqueeze(2).to_broadcast([128, 2, 256]), op=ALU.mult),
                     reads=[r_o2, r_ss], writes=[r_o2])
                P.op("dve", lambda e: e.tensor_tensor(out=ytok[:], in0=o2[:], in1=rows[:, 24:536], op=ALU.mult),
                     reads=[r_o2, r_rows], writes=[r_ytok])
                for cc in range(4):
                    P.op("pe", lambda e, cc=cc: e.transpose(out=tb[:, cc * 128:(cc + 1) * 128],
                                                            in_=ytok[:, cc * 128:(cc + 1) * 128], identity=K.ident_bf[:]),
                         reads=[r_ytok, K.r_const], writes=[rpb])
                P.op("dve", lambda e, bsl=bsl, yb=yb: e.tensor_copy(out=yTt[yb][:, :, bsl],
                                                                   in_=tb[:, 0:512].rearrange("p (c s) -> p c s", c=4)),
                     reads=[rpb], writes=[r_yTt[yb]])
            dst = K.ybuf.rearrange("(c p) t -> p c t", p=128)[:, 4:8, tsl]
            P.dma("sp", dst, yTt[yb][:], ds[7 + yb], reads=[r_yTt[yb]], writes=[K.r_y[c][t] for c in range(4, 8)])
        P.emit_block(final=final)


ALL_PHASES = ("ret", "ssd", "hg", "fox", "ffn")


def _wl(W):
    return np.ascontiguousarray(W.reshape(W.shape[0], KC, 128, W.shape[2]).transpose(0, 2, 1, 3))


def _vec(v):
    return np.ascontiguousarray(v.reshape(v.shape[0], KC, 128).transpose(0, 2, 1))


def prepare_inputs(T, norm_mix, norm_ffn, ffn_w_gate, ffn_w_up, ffn_w_down, ab_w_in, ab_w_out, ret_gn_w, ssd_conv_w,
                   ssd_conv_b, ssd_dt_bias, ssd_a_log, ssd_d, ssd_norm_w, cd_w_in, cd_w_out, hg_lb_logits, hg_norm_w,
                   fox_f_bias, fox_q_norm_w, fox_k_norm_w):
    f32 = np.float32
    A = lambda a: np.asarray(a, dtype=f32)
    norm_mix, norm_ffn = A(norm_mix), A(norm_ffn)
    wg, wu, wd = A(ffn_w_gate), A(ffn_w_up), A(ffn_w_down)
    ab_in, ab_out, cd_in, cd_out = A(ab_w_in), A(ab_w_out), A(cd_w_in), A(cd_w_out)
    sh = {}
    sh["norm_mix"], sh["norm_ffn"] = _vec(norm_mix), _vec(norm_ffn)
    sh["wg"] = np.ascontiguousarray(wg.reshape(4, KC, 128, NJ, 128).transpose(0, 3, 2, 1, 4))
    sh["wu"] = np.ascontiguousarray(wu.reshape(4, KC, 128, NJ, 128).transpose(0, 3, 2, 1, 4))
    sh["wd"] = np.ascontiguousarray(wd.reshape(4, NJ, 128, KC, 128).transpose(0, 3, 2, 1, 4))
    w_out = np.stack([ab_out[0], cd_out[0], ab_out[1], cd_out[1]])
    sh["w_out"] = _wl(w_out)
    sh["w_fox"] = _wl(cd_in[:, :, 2048:3588])
    fv = np.zeros((2, 128, 3), f32)
    fv[:, :, 0] = A(fox_q_norm_w); fv[:, :, 1] = A(fox_k_norm_w); fv[:, 0:4, 2] = A(fox_f_bias)
    sh["fox_vec"] = fv
    sh["cmask"] = np.triu(np.ones((128, 128), f32))
    sh["smask"] = np.tril(np.ones((128, 128), f32), -1)
    sh["ident"] = np.eye(128, dtype=f32)
    sh["rmask"] = np.tile((np.arange(512) % 64 != 0).astype(f32), (128, 1))
    sh["w_hg"] = _wl(cd_in[:, :, 0:2048])
    hv = np.zeros((2, 128, 12), f32)
    lbl = A(hg_lb_logits)
    for jj in range(2):
        hv[jj, :, 0:4] = lbl[0].reshape(4, 128).T
        hv[jj, :, 4:8] = lbl[1].reshape(4, 128).T
        hv[jj, :, 8] = A(hg_norm_w)[jj]
    sh["hg_vec"] = hv

    def pad(Wx, swap):
        o = np.zeros((D, 512), f32)
        for h in range(4):
            blk = Wx[:, h * 64:(h + 1) * 64]
            if swap:
                blk = blk.reshape(D, 32, 2)[:, :, ::-1].reshape(D, 64)
            o[:, h * 128:h * 128 + 64] = blk
        return o
    wr = []
    for jj in range(2):
        Wq, Wk = ab_in[jj][:, 0:256], ab_in[jj][:, 256:512]
        wr.append(np.concatenate([pad(Wq, 0), pad(Wk, 0), ab_in[jj][:, 512:1024], ab_in[jj][:, 1024:1536],
                                  pad(Wq, 1), pad(Wk, 1)], 1))
    sh["w_ret"] = _wl(np.stack(wr))
    rv = np.zeros((2, 128, 12), f32)
    lg = np.log1p(-np.exp2(-5.0 - np.arange(4, dtype=np.float64)))
    rv[:, :, 0:4] = lg[None, None, :].astype(f32)
    rv[:, :, 4:8] = A(ret_gn_w).transpose(0, 2, 1)
    sh["ret_vec"] = rv
    freqs = (np.float32(10000.0) ** (-np.linspace(0.0, 1.0, 32, dtype=f32))).astype(f32)
    ang = (np.arange(T, dtype=f32)[:, None] * freqs[None, :]).astype(f32).astype(np.float64)
    rt = np.zeros((128, 2, T), f32)
    rt[0:64, 0, :] = np.repeat(np.cos(ang), 2, axis=1).T
    sn = np.repeat(np.sin(ang), 2, axis=1)
    sn[:, 0::2] *= -1
    rt[0:64, 1, :] = sn.T
    sh["rot_tab"] = rt
    sh["w_ssd"] = _wl(ab_in[:, :, 1536:2824])
    rows = np.zeros((2, 128, 536), f32)
    rows[:, :, 0:8] = A(ssd_dt_bias)[:, None, :]
    rows[:, :, 8:16] = A(ssd_a_log)[:, None, :]
    rows[:, :, 16:24] = A(ssd_d)[:, None, :]
    rows[:, :, 24:536] = A(ssd_norm_w)[:, None, :]
    sh["ssd_rows"] = rows
    cv = np.zeros((2, 128, 6, 5), f32)
    cwv, cbv = A(ssd_conv_w), A(ssd_conv_b)
    for jj in range(2):
        cv[jj, :, :, 0:4] = cwv[jj].T.reshape(6, 128, 4).transpose(1, 0, 2)
        cv[jj, :, :, 4] = cbv[jj].reshape(6, 128).T
    sh["ssd_conv"] = cv
    return sh


def kernel(x, **params):
    x = np.asarray(x, dtype=np.float32)
    B, T, _ = x.shape
    shared = prepare_inputs(T, **params)
    nc = build_program(T, [0, 1, 2, 3], phases=ALL_PHASES)
    in_maps = []
    for b in range(B):
        m = dict(shared)
        m["xT"] = np.ascontiguousarray(x[b].T)
        in_maps.append(m)
    res = run_bass_kernel_spmd(nc, in_maps, core_ids=list(range(B)))
    out = np.stack([np.ascontiguousarray(res.results[b]["out"].T) for b in range(B)], axis=0)
    return out.astype(np.float32)
```

```python
import math
import numpy as np
from contextlib import ExitStack
import concourse.bass as bass
import concourse.mybir as mybir
from concourse.bass_utils import run_bass_kernel_spmd

F32 = mybir.dt.float32
BF16 = mybir.dt.bfloat16
AF = mybir.ActivationFunctionType
ALU = mybir.AluOpType
AX = mybir.AxisListType

D = 1024
KC = 8
FF = 2816
NJ = 22
EPS = 1e-6
import os
NOSYNC_ENGINES = tuple(x for x in os.environ.get("KNOSYNC", "").split(",") if x)
ENGS = ["pe", "act", "dve", "pool", "sp"]
CENG = ["pe", "act", "dve", "pool"]


class Res:
    __slots__ = ("w", "rs")

    def __init__(self):
        self.w = None
        self.rs = []


def RL(n):
    return [Res() for _ in range(n)]


class Ev:
    __slots__ = ("sem", "val", "op", "blk")

    def __init__(self, blk, sem=None, val=None, op=None):
        self.blk, self.sem, self.val, self.op = blk, sem, val, op


class Op:
    __slots__ = ("eng", "fn", "waits", "marked", "ev", "incs")

    def __init__(self, eng, fn):
        self.eng, self.fn = eng, fn
        self.waits = []
        self.marked = False
        self.ev = None
        self.incs = None


class DSem:
    def __init__(self, sem):
        self.sem = sem
        self.count = 0


class Prog:
    def __init__(self, nc, es, same_engine_sync=True):
        self.nc = nc
        self.es = es
        self.ops = {e: [] for e in ENGS}
        self.csem = {e: es.enter_context(nc.semaphore("c_" + e)) for e in CENG}
        self.ccount = {e: 0 for e in CENG}
        self.bar = es.enter_context(nc.semaphore("bar"))
        self.nbar = 0
        self.same = same_engine_sync
        self.nosync = set(NOSYNC_ENGINES)
        self.dsems = []
        self.blk = 0
        self.blk_dsems = set()

    def dsem(self):
        d = DSem(self.es.enter_context(self.nc.semaphore(f"d{len(self.dsems)}")))
        self.dsems.append(d)
        return d

    def _deps(self, op, reads, writes):
        evs = []
        for r in reads:
            if r.w is not None:
                evs.append(r.w)
        for r in writes:
            if r.w is not None:
                evs.append(r.w)
            evs.extend(r.rs)
        for ev in evs:
            if ev.blk != self.blk:
                continue
            if ev.op is not None:
                p = ev.op
                if p.eng == op.eng and (op.eng == "pe" or op.eng in self.nosync):
                    continue
                p.marked = True
            op.waits.append(ev)

    def _commit(self, ev, reads, writes):
        for r in reads:
            r.rs.append(ev)
        for r in writes:
            r.w = ev
            r.rs = []

    def op(self, eng, fn, reads=(), writes=()):
        o = Op(eng, fn)
        self._deps(o, reads, writes)
        o.ev = Ev(self.blk, op=o)
        self.ops[eng].append(o)
        self._commit(o.ev, reads, writes)
        return o

    def dma(self, q, out, in_, ds, reads=(), writes=(), slow=False):
        if slow:
            o = Op(q, lambda e: e.dma_start(out=out, in_=in_, allow_slow_non_contiguous=True))
        else:
            o = Op(q, lambda e: e.dma_start(out=out, in_=in_))
        self._deps(o, reads, writes)
        ds.count += 16
        self.blk_dsems.add(ds)
        o.incs = (ds.sem, 16)
        o.ev = Ev(self.blk, sem=ds.sem, val=ds.count)
        self.ops[q].append(o)
        self._commit(o.ev, reads, writes)
        return o

    def emit_block(self, final=False):
        nc = self.nc
        finals = []
        for e in CENG:
            ops = [o for o in self.ops[e] if o.fn is not None]
            if ops:
                ops[-1].marked = True
            c = self.ccount[e]
            for o in self.ops[e]:
                if o.incs is None and o.marked:
                    c += 1
                    o.ev.sem, o.ev.val = self.csem[e], c
            self.ccount[e] = c
            if ops:
                finals.append((self.csem[e], c))
        for ds in self.blk_dsems:
            finals.append((ds.sem, ds.count))
        self.nbar += 1
        nbar = self.nbar
        engobj = {"pe": "tensor", "act": "scalar", "dve": "vector", "pool": "gpsimd", "sp": "sync"}
        with nc.Block() as block:
            for e in ENGS:
                ops = self.ops[e]

                def body(eng, ops=ops, e=e):
                    waited = {}
                    for o in ops:
                        for ev in o.waits:
                            k = id(ev.sem)
                            if waited.get(k, 0) < ev.val:
                                eng.wait_ge(ev.sem, ev.val)
                                waited[k] = ev.val
                        ins = o.fn(eng)
                        if o.incs is not None:
                            ins.then_inc(o.incs[0], o.incs[1])
                        elif o.marked:
                            ins.then_inc(self.csem[e], 1)
                    if e == "sp":
                        for (s, v) in finals:
                            eng.wait_ge(s, v)
                        eng.sem_inc(self.bar, 1)
                    if not (final and e != "sp"):
                        eng.wait_ge(self.bar, nbar)

                getattr(block, engobj[e])(body)
        self.ops = {e: [] for e in ENGS}
        self.blk += 1
        self.blk_dsems = set()


class KB:
    pass


def _mm(P, out, lhsT, rhs, start, stop, reads, writes, **kw):
    return P.op("pe", lambda e: e.matmul(out, lhsT, rhs, start=start, stop=stop, **kw), reads=reads, writes=writes)


def build_program(T, layers, phases=("mix", "ffn"), test_ybuf=False):
    nc = bass.Bass("TRN2", target_bir_lowering=False)
    NT = T // 512
    K = KB()
    K.nc, K.T, K.NT = nc, T, NT
    dr = {}

    def din(name, shape, dt=F32):
        dr[name] = nc.dram_tensor(name, list(shape), dt, kind="ExternalInput").ap()
        return dr[name]

    din("xT", [D, T])
    din("norm_mix", [4, 128, KC])
    din("norm_ffn", [4, 128, KC])
    din("wg", [4, NJ, 128, KC, 128])
    din("wu", [4, NJ, 128, KC, 128])
    din("wd", [4, KC, 128, NJ, 128])
    din("w_out", [4, 128, KC, D])
    din("w_fox", [2, 128, KC, 1540])
    din("fox_vec", [2, 128, 3])
    din("cmask", [128, 128])
    din("w_hg", [2, 128, KC, 2048])
    din("hg_vec", [2, 128, 12])
    din("w_ret", [2, 128, KC, 2048])
    din("ret_vec", [2, 128, 12])
    din("rmask", [128, 512])
    din("w_ssd", [2, 128, KC, 1288])
    din("ssd_rows", [2, 128, 536])
    din("ssd_conv", [2, 128, 6, 6])
    din("smask", [128, 128])
    din("rot_tab", [128, 4, T])
    din("ident", [128, 128])
    out = nc.dram_tensor("out", [D, T], F32, kind="ExternalOutput").ap()
    if test_ybuf == "out":
        ybuf = nc.dram_tensor("ybuf", [D, T], BF16, kind="ExternalOutput").ap()
    elif test_ybuf:
        ybuf = nc.dram_tensor("ybuf", [D, T], BF16, kind="ExternalInput").ap()
    else:
        ybuf = nc.dram_tensor("ybuf", [D, T], BF16).ap()
    K.dr, K.out, K.ybuf = dr, out, ybuf

    with ExitStack() as es:
        P = Prog(nc, es)
        K.P = P
        K.bank = [es.enter_context(nc.psum_tensor(f"bank{i}", [128, 512], F32)) for i in range(8)]
        K.rbank = RL(8)
        K.ones_bf = es.enter_context(nc.sbuf_tensor("ones_bf", [128, 128], BF16))
        K.r_const = Res()
        P.op("pool", lambda e: e.memset(K.ones_bf[:], 1.0), writes=[K.r_const])
        K.r_h = [RL(NT) for _ in range(KC)]
        K.r_y = [RL(NT) for _ in range(KC)]
        K.dsem_pool = [P.dsem() for _ in range(40)]
        K.dsem_q = [P.dsem() for _ in range(8)]
        K.ident_bf = es.enter_context(nc.sbuf_tensor("ident_bf", [128, 128], BF16))
        K.ident_f = es.enter_context(nc.sbuf_tensor("ident_f", [128, 128], F32))
        K.cmask_bf = es.enter_context(nc.sbuf_tensor("cmask_bf", [128, 128], BF16))
        K.cmask_f = es.enter_context(nc.sbuf_tensor("cmask_f", [128, 128], F32))
        P.dma("pool", K.ident_bf[:], dr["ident"], K.dsem_q[7], writes=[K.r_const])
        P.dma("sp", K.ident_f[:], dr["ident"], K.dsem_pool[37], writes=[K.r_const])
        P.dma("pool", K.cmask_bf[:], dr["cmask"], K.dsem_q[7], writes=[K.r_const])
        P.dma("sp", K.cmask_f[:], dr["cmask"], K.dsem_pool[39], writes=[K.r_const])
        K.fd = nc.dram_tensor("fd", [8, T], F32).ap()
        K.r_fd = Res()

        plan = []
        for li, layer in enumerate(layers):
            if layer % 2 == 0:
                for ph in ("ret", "ssd"):
                    if ph in phases:
                        plan.append((ph, li, layer))
            else:
                for ph in ("hg", "fox"):
                    if ph in phases:
                        plan.append((ph, li, layer))
            if "ffn" in phases:
                plan.append(("ffn", li, layer))
        for pi, (ph, li, layer) in enumerate(plan):
            hin = dr["xT"] if li == 0 else out
            fin = pi == len(plan) - 1
            if ph == "ret":
                hg_phase(K, layer, hin, "ret", final=fin)
            elif ph == "hg":
                hg_phase(K, layer, hin, "hg", final=fin)
            elif ph == "ssd":
                ssd_phase(K, layer, hin, final=fin)
            elif ph == "fox":
                fox_phase(K, layer, hin, final=fin)
            else:
                ffn_phase(K, layer, hin, final=fin)
    return nc


def rmsnorm_tile(K, es_bufs, h_t, r_h_t, gain, r_gain, u_out, r_u, sq, r_sq, rstd, r_rstd, bank_i):
    P = K.P
    bank, rb = K.bank[bank_i], K.rbank[bank_i]
    for c in range(KC):
        P.op("act", lambda e, c=c: e.activation(out=sq[:, c, :], in_=h_t[:, c, :], func=AF.Square),
             reads=[r_h_t], writes=[r_sq])
    for c in range(KC):
        _mm(P, bank[:], K.ones_bf[:], sq[:, c, :], c == 0, c == KC - 1, [K.r_const, r_sq], [rb])
    P.op("act", lambda e: e.activation(out=rstd[:], in_=bank[:], func=AF.Sqrt, bias=EPS, scale=1.0 / D),
         reads=[rb], writes=[r_rstd])
    P.op("dve", lambda e: e.reciprocal(out=rstd[:], in_=rstd[:]), reads=[r_rstd], writes=[r_rstd])
    for c in range(KC):
        P.op("dve", lambda e, c=c: e.scalar_tensor_tensor(out=u_out(c), in0=h_t[:, c, :], scalar=gain[:, c:c + 1],
                                                         in1=rstd[:], op0=ALU.mult, op1=ALU.mult),
             reads=[r_h_t, r_gain, r_rstd], writes=[r_u])


def ffn_phase(K, layer, hin, final=False):
    nc, P, T, NT, dr, out = K.nc, K.P, K.T, K.NT, K.dr, K.out
    TG = min(1024, T)
    NG = T // TG
    TPG = TG // 512
    ds = K.dsem_pool
    with ExitStack() as es:
        def sb(name, shape, dt):
            return es.enter_context(nc.sbuf_tensor(f"f{layer}_{name}", shape, dt))
        wout = sb("wout", [128, KC, D], BF16); r_wout = Res()
        gain = sb("gain", [128, KC], F32); r_gain = Res()
        uT = [sb(f"uT{i}", [128, KC, TG], BF16) for i in range(2)]; r_uT = [RL(TPG) for _ in range(2)]
        aT = sb("aT", [128, NJ, TG], BF16); r_aT = [RL(TPG) for _ in range(NJ)]
        ht = [sb(f"ht{i}", [128, KC, 512], F32) for i in range(2)]; r_ht = RL(2)
        yt = [sb(f"yt{i}", [128, KC, 512], BF16) for i in range(2)]; r_yt = RL(2)
        sq = sb("sq", [128, KC, 512], BF16); r_sq = Res()
        rstd = sb("rstd", [128, 512], F32); r_rstd = Res()
        wgc = [sb(f"wgc{i}", [128, KC, 128], BF16) for i in range(2)]; r_wgc = RL(2)
        wuc = [sb(f"wuc{i}", [128, KC, 128], BF16) for i in range(2)]; r_wuc = RL(2)
        wdc = [sb(f"wdc{i}", [128, NJ, 128], BF16) for i in range(2)]; r_wdc = RL(2)
        sg = [sb(f"sg{i}", [128, 512], F32) for i in range(2)]; r_sg = RL(2)
        hs = [sb(f"hs{i}", [128, 512], F32) for i in range(4)]; r_hs = RL(4)

        P.dma("pool", wout[:], dr["w_out"][layer], K.dsem_q[0], writes=[r_wout])
        P.dma("sp", gain[:], dr["norm_ffn"][layer], ds[1], writes=[r_gain])

        def hview(ap, t):
            return ap.rearrange("(c p) t -> p c t", p=128)[:, :, t * 512:(t + 1) * 512]

        def load_tile(tg):
            b = tg % 2
            rh = [K.r_h[c][tg] for c in range(KC)]
            ry = [K.r_y[c][tg] for c in range(KC)]
            P.dma("sp", ht[b][:], hview(hin, tg), ds[2 + b], reads=rh, writes=[r_ht[b]])
            P.dma("sp", yt[b][:], hview(K.ybuf, tg), ds[4 + b], reads=ry, writes=[r_yt[b]])

        wcount = [0]
        tmpn = sb("tmpn", [128, 512], F32); r_tmpn = Res()

        def stage1(g):
            gp = g % 2
            load_tile(g * TPG)
            for tl in range(TPG):
                tg = g * TPG + tl
                b = tg % 2
                if tl + 1 < TPG:
                    load_tile(tg + 1)
                for dc in range(KC):
                    bi = dc % 2
                    for kc in range(KC):
                        _mm(P, K.bank[bi][:], wout[:, kc, dc * 128:(dc + 1) * 128], yt[b][:, kc, :], kc == 0, kc == KC - 1,
                            [r_wout, r_yt[b]], [K.rbank[bi]])
                    P.op("dve", lambda e, dc=dc, bi=bi, b=b: e.tensor_tensor(out=ht[b][:, dc, :], in0=ht[b][:, dc, :],
                                                                           in1=K.bank[bi][:], op=ALU.add),
                         reads=[K.rbank[bi], r_ht[b]], writes=[r_ht[b]])
                    yield
                P.dma("sp", hview(out, tg), ht[b][:], ds[6 + b], reads=[r_ht[b]],
                      writes=[K.r_h[c][tg] for c in range(KC)])
                rmsnorm_tile2(K, ht[b], r_ht[b], gain, r_gain, uT[gp][:, :, tl * 512:(tl + 1) * 512], r_uT[gp][tl],
                              sq, r_sq, rstd, r_rstd, tmpn, r_tmpn, 2)
                yield

        def run_all(gen):
            for _ in gen:
                pass

        run_all(stage1(0))
        for g in range(NG):
            gp = g % 2
            uTg, r_uTg = uT[gp], r_uT[gp]
            def load_w2(j):
                b = wcount[0] % 2
                wcount[0] += 1
                P.dma("pool", wgc[b][:], dr["wg"][layer, j], K.dsem_q[1 + b], writes=[r_wgc[b]])
                P.dma("pool", wuc[b][:], dr["wu"][layer, j], K.dsem_q[3 + b], writes=[r_wuc[b]])
                return b
            nb = load_w2(0)
            for j in range(NJ):
                b = nb
                if j + 1 < NJ:
                    nb = load_w2(j + 1)
                for tl in range(TPG):
                    pg, pu = 3 + 2 * (tl % 2), 4 + 2 * (tl % 2)
                    sl = slice(tl * 512, (tl + 1) * 512)
                    for kc in range(KC):
                        _mm(P, K.bank[pg][:], wgc[b][:, kc, :], uTg[:, kc, sl], kc == 0, kc == KC - 1,
                            [r_wgc[b], r_uTg[tl]], [K.rbank[pg]])
                    for kc in range(KC):
                        _mm(P, K.bank[pu][:], wuc[b][:, kc, :], uTg[:, kc, sl], kc == 0, kc == KC - 1,
                            [r_wuc[b], r_uTg[tl]], [K.rbank[pu]])
                    s = tl % 2
                    P.op("act", lambda e, s=s, pg=pg: e.activation(out=sg[s][:], in_=K.bank[pg][:], func=AF.Silu),
                         reads=[K.rbank[pg]], writes=[r_sg[s]])
                    P.op("dve", lambda e, s=s, pu=pu, j=j, sl=sl: e.tensor_tensor(out=aT[:, j, sl], in0=sg[s][:],
                                                                                 in1=K.bank[pu][:], op=ALU.mult),
                         reads=[r_sg[s], K.rbank[pu]], writes=[r_aT[j][tl]])
            bg = stage1(g + 1) if g + 1 < NG else None
            P.dma("pool", wdc[0][:], dr["wd"][layer, 0], K.dsem_q[5], writes=[r_wdc[0]])
            cnt = 0
            for dc in range(KC):
                b = dc % 2
                if dc + 1 < KC:
                    P.dma("pool", wdc[1 - b][:], dr["wd"][layer, dc + 1], K.dsem_q[5 + (1 - b)], writes=[r_wdc[1 - b]])
                for tl in range(TPG):
                    tg = g * TPG + tl
                    bi = 3 + cnt % 2
                    hb = cnt % 4
                    cnt += 1
                    src = out[dc * 128:(dc + 1) * 128, tg * 512:(tg + 1) * 512]
                    P.dma("sp", hs[hb][:], src, ds[14 + hb], reads=[K.r_h[dc][tg]], writes=[r_hs[hb]])
                    for j in range(NJ):
                        _mm(P, K.bank[bi][:], wdc[b][:, j, :], aT[:, j, tl * 512:(tl + 1) * 512], j == 0, j == NJ - 1,
                            [r_wdc[b], r_aT[j][tl]], [K.rbank[bi]])
                    P.op("dve", lambda e, hb=hb, bi=bi: e.tensor_tensor(out=hs[hb][:], in0=hs[hb][:], in1=K.bank[bi][:],
                                                                       op=ALU.add),
                         reads=[K.rbank[bi], r_hs[hb]], writes=[r_hs[hb]])
                    P.dma("sp", src, hs[hb][:], ds[18 + hb], reads=[r_hs[hb]], writes=[K.r_h[dc][tg]])
                    if bg is not None:
                        try:
                            next(bg)
                            next(bg)
                        except StopIteration:
                            bg = None
            if bg is not None:
                run_all(bg)
        P.emit_block(final=final)


def act_rstd(P, out, r_out, in_, r_in, scale, tmp, r_tmp):
    P.op("act", lambda e: e.activation(out=tmp, in_=in_, func=AF.Ln, bias=EPS, scale=scale), reads=[r_in], writes=[r_tmp])
    P.op("act", lambda e: e.activation(out=out, in_=tmp, func=AF.Exp, scale=-0.5), reads=[r_tmp], writes=[r_out])


def rmsnorm_tile2(K, h_t, r_h_t, gain, r_gain, uT, r_u, sq, r_sq, rstd, r_rstd, tmp, r_tmp, bank_i):
    P = K.P
    bank, rb = K.bank[bank_i], K.rbank[bank_i]
    if not hasattr(K, "_sqres"):
        K._sqres = {}
    rs = K._sqres.setdefault(id(r_sq), RL(KC))
    eng_of = ["act", "act", "act", "dve", "act", "dve", "pool", "act"]
    for c in range(KC):
        if eng_of[c] == "act":
            P.op("act", lambda e, c=c: e.activation(out=sq[:, c, :], in_=h_t[:, c, :], func=AF.Square),
                 reads=[r_h_t], writes=[rs[c]])
        else:
            P.op(eng_of[c], lambda e, c=c: e.tensor_tensor(out=sq[:, c, :], in0=h_t[:, c, :], in1=h_t[:, c, :], op=ALU.mult),
                 reads=[r_h_t], writes=[rs[c]])
    for c in range(KC):
        _mm(P, bank[:], K.ones_bf[:], sq[:, c, :], c == 0, c == KC - 1, [K.r_const, rs[c]], [rb])
    act_rstd(P, rstd[:], r_rstd, bank[:], rb, 1.0 / D, tmp[:], r_tmp)
    for c in range(KC):
        P.op("dve", lambda e, c=c: e.scalar_tensor_tensor(out=uT[:, c, :], in0=h_t[:, c, :], scalar=gain[:, c:c + 1],
                                                         in1=rstd[:], op0=ALU.mult, op1=ALU.add if False else ALU.mult),
             reads=[r_h_t, r_gain, r_rstd], writes=[r_u])


def hview(ap, t):
    return ap.rearrange("(c p) t -> p c t", p=128)[:, :, t * 512:(t + 1) * 512]


def fox_phase(K, layer, hin, final=False):
    nc, P, T, NT, dr = K.nc, K.P, K.T, K.NT, K.dr
    j = layer // 2
    NB = T // 128
    ds = K.dsem_pool
    SCALE = 128 ** -0.5
    with ExitStack() as es:
        def sb(name, shape, dt):
            return es.enter_context(nc.sbuf_tensor(f"x{layer}_{name}", shape, dt))
        w = sb("w", [128, KC, 1540], BF16); r_w = Res()
        gain = sb("gain", [128, KC], F32); r_gain = Res()
        fvec = sb("fvec", [128, 3], F32); r_fvec = Res()
        KT = sb("KT", [128, 4, T], BF16); r_KT = [RL(NT) for _ in range(4)]
        VA = sb("VA", [128, NB, 4, 129], BF16); r_VA = RL(NB); r_VAone = Res()
        Ftok = sb("Ftok", [128, NB, 4], F32); r_Ftok = RL(NB)
        Rq = sb("Rq", [128, NB, 4], F32); r_Rq = RL(NB)
        Bq = sb("Bq", [128, NB, 4], F32); r_Bq = Res()
        ht = [sb(f"ht{i}", [128, KC, 512], F32) for i in range(2)]; r_ht = RL(2)
        uT = sb("uT", [128, KC, 512], BF16); r_uT = Res()
        sq = sb("sq", [128, KC, 512], BF16); r_sq = Res()
        rstd = sb("rstd", [128, 512], F32); r_rstd = Res()
        tmp = sb("tmp", [128, 512], F32); r_tmp = Res()
        qn = sb("qn", [128, 4, 512], BF16); r_qn = RL(4)
        sqh = sb("sqh", [128, 512], BF16); r_sqh = Res()
        rq = sb("rq", [128, 512], F32); r_rq = Res()
        fx = [sb(f"fx{i}", [4, 512], F32) for i in range(4)]; r_fx = RL(4)
        Fc = [sb(f"Fc{i}", [4, 512], F32) for i in range(2)]; r_Fc = RL(2)
        onesf = sb("onesf", [4, 512], F32); r_onesf = Res()
        PT = [sb(f"PT{i}", [128, 512], BF16) for i in range(3)]; r_PT = RL(3)
        ytok = sb("ytok", [128, 4, 512], BF16); r_ytok = RL(4)
        Rt = sb("Rt", [128, 4], F32); r_Rt = Res()
        Boff = sb("Boff", [128, NB, 4], F32); r_Boff = Res()
        Bin = sb("Bin", [128, 4, 4, 4], F32); r_Bin = Res()
        cfac = sb("cfac", [128, 4, 4], F32); r_cfac = Res()
        otmp = sb("otmp", [128, 132], F32); r_otmp = Res()
        nslot = [0]
        rec = sb("rec", [128, 4], F32); r_rec = RL(4)
        yTt = [sb(f"yTt{i}", [128, 4, 512], BF16) for i in range(2)]; r_yTt = RL(2)

        P.dma("pool", w[:, :, 0:768], dr["w_fox"][j][:, :, 0:768], K.dsem_q[0], writes=[r_w])
        P.dma("pool", w[:, :, 768:1540], dr["w_fox"][j][:, :, 768:1540], K.dsem_q[0], writes=[r_w])
        P.dma("sp", gain[:], dr["norm_mix"][layer], ds[1], writes=[r_gain])
        P.dma("sp", fvec[:], dr["fox_vec"][j], ds[30], writes=[r_fvec])
        P.op("pool", lambda e: e.memset(onesf[:], 1.0), writes=[r_onesf])
        P.op("pool", lambda e: e.memset(VA[:, :, :, 128:129], 1.0), writes=[r_VAone])

        P.dma("sp", ht[0][:], hview(hin, 0), ds[2], reads=[K.r_h[c][0] for c in range(KC)], writes=[r_ht[0]])
        npt = 0
        for t in range(NT):
            b = t % 2
            if t + 1 < NT:
                P.dma("sp", ht[1 - b][:], hview(hin, t + 1), ds[2 + (1 - b)],
                      reads=[K.r_h[c][t + 1] for c in range(KC)], writes=[r_ht[1 - b]])
            rmsnorm_tile2(K, ht[b], r_ht[b], gain, r_gain, uT, r_uT, sq, r_sq, rstd, r_rstd, tmp, r_tmp, 7)
            tsl = slice(t * 512, (t + 1) * 512)
            pb, rpb = K.bank[6], K.rbank[6]
            for kc in range(KC):
                _mm(P, pb[0:4, :], w[:, kc, 1536:1540], uT[:, kc, :], kc == 0, kc == KC - 1, [r_w, r_uT], [rpb])
            P.op("dve", lambda e: e.tensor_scalar(out=fx[0][:], in0=pb[0:4, :], scalar1=fvec[0:4, 2:3], scalar2=None,
                                                  op0=ALU.add), reads=[rpb, r_fvec], writes=[r_fx[0]])
            P.op("dve", lambda e: e.scalar_tensor_tensor(out=fx[1][:], in0=fx[0][:], scalar=-1.0, in1=fx[0][:],
                                                         op0=ALU.mult, op1=ALU.min),
                 reads=[r_fx[0]], writes=[r_fx[1]])
            P.op("act", lambda e: e.activation(out=fx[1][:], in_=fx[1][:], func=AF.Exp),
                 reads=[r_fx[1]], writes=[r_fx[1]])
            P.op("act", lambda e: e.activation(out=fx[1][:], in_=fx[1][:], func=AF.Ln, bias=1.0),
                 reads=[r_fx[1]], writes=[r_fx[1]])
            P.op("dve", lambda e: e.tensor_scalar_min(out=fx[2][:], in0=fx[0][:], scalar1=0.0),
                 reads=[r_fx[0]], writes=[r_fx[2]])
            P.op("dve", lambda e: e.tensor_sub(out=fx[3][:], in0=fx[2][:], in1=fx[1][:]),
                 reads=[r_fx[1], r_fx[2]], writes=[r_fx[3]])
            init = 0.0 if t == 0 else Fc[1 - b][:, 511:512]
            P.op("dve", lambda e, b=b, init=init: e.tensor_tensor_scan(out=Fc[b][:], data0=onesf[:], data1=fx[3][:],
                                                                      initial=init, op0=ALU.mult, op1=ALU.add),
                 reads=[r_onesf, r_fx[3], r_Fc[1 - b]], writes=[r_Fc[b]])
            P.dma("sp", K.fd[0:4, tsl], Fc[b][:], ds[4], reads=[r_Fc[b]], writes=[K.r_fd])
            for bl in range(4):
                blk = t * 4 + bl
                c0 = blk * 128
                P.dma("sp", Ftok[:, blk, :], K.fd[0:4, c0:c0 + 128].rearrange("h s -> s h"), ds[12 + bl],
                      reads=[K.r_fd], writes=[r_Ftok[blk]], slow=True)
                P.dma("sp", Rq[:, blk, :], K.fd[0:4, c0 + 64:c0 + 65].rearrange("h o -> o h").broadcast_to([128, 4]),
                      ds[16 + bl], reads=[K.r_fd], writes=[r_Rq[blk]], slow=True)
            for h in range(4):
                for which in range(2):
                    c0 = which * 512 + h * 128
                    for kc in range(KC):
                        _mm(P, pb[:], w[:, kc, c0:c0 + 128], uT[:, kc, :], kc == 0, kc == KC - 1, [r_w, r_uT], [rpb])
                    P.op("act", lambda e: e.activation(out=sqh[:], in_=pb[:], func=AF.Square),
                         reads=[rpb], writes=[r_sqh])
                    nb_, rnb = K.bank[7], K.rbank[7]
                    _mm(P, nb_[:], K.ones_bf[:], sqh[:], True, True, [K.r_const, r_sqh], [rnb])
                    act_rstd(P, rq[:], r_rq, nb_[:], rnb, 1.0 / 128, tmp[:], r_tmp)
                    if which == 0:
                        dst, rd = qn[:, h, :], [r_qn[h]]
                    else:
                        dst, rd = KT[:, h, tsl], [r_KT[h][t]]
                    P.op("dve", lambda e, dst=dst, which=which: e.scalar_tensor_tensor(
                        out=dst, in0=pb[:], scalar=fvec[:, which:which + 1], in1=rq[:], op0=ALU.mult, op1=ALU.mult),
                        reads=[rpb, r_fvec, r_rq], writes=rd)
            for bl in range(4):
                blk = t * 4 + bl
                for kc in range(KC):
                    _mm(P, pb[:], uT[:, kc, bl * 128:(bl + 1) * 128], w[:, kc, 1024:1536], kc == 0, kc == KC - 1,
                        [r_w, r_uT], [rpb])
                P.op("dve", lambda e, blk=blk: e.tensor_copy(out=VA[:, blk, :, 0:128],
                                                            in_=pb[:].rearrange("p (h v) -> p h v", h=4)),
                     reads=[rpb, r_VAone], writes=[r_VA[blk]])
            yb = t % 2
            t4 = t * 4
            if t > 0:
                P.dma("sp", Rt[:], K.fd[0:4, t * 512:t * 512 + 1].rearrange("h o -> o h").broadcast_to([128, 4]),
                      ds[20], reads=[K.r_fd], writes=[r_Rt], slow=True)
                P.op("dve", lambda e, t4=t4: e.tensor_tensor(
                    out=Boff[:, 0:t4, :], in0=Rt[:].unsqueeze(1).to_broadcast([128, t4, 4]), in1=Ftok[:, 0:t4, :],
                    op=ALU.subtract), reads=[r_Rt] + r_Ftok[0:t4], writes=[r_Boff])
                P.op("dve", lambda e, t4=t4: e.tensor_tensor(
                    out=cfac[:], in0=Rq[:, t4:t4 + 4, :], in1=Rt[:].unsqueeze(1).to_broadcast([128, 4, 4]),
                    op=ALU.subtract), reads=[r_Rt] + r_Rq[t4:t4 + 4], writes=[r_cfac])
                P.op("act", lambda e: e.activation(out=cfac[:], in_=cfac[:], func=AF.Exp), reads=[r_cfac], writes=[r_cfac])
            P.op("dve", lambda e, t4=t4: e.tensor_tensor(
                out=Bin[:], in0=Rq[:, t4:t4 + 4, :].unsqueeze(1).to_broadcast([128, 4, 4, 4]),
                in1=Ftok[:, t4:t4 + 4, :].unsqueeze(2).to_broadcast([128, 4, 4, 4]), op=ALU.subtract),
                reads=r_Rq[t4:t4 + 4] + r_Ftok[t4:t4 + 4], writes=[r_Bin])

            units = []
            for h in range(4):
                for kb in range(t4):
                    units.append(("off", h, kb))
                for kl in range(4):
                    units.append(("in", h, kl))
            started = {}

            def oreg(h, r):
                bi = (2 if h % 2 == 0 else 5) + r // 3
                c0 = (r % 3) * 129
                return bi, K.bank[bi][:, c0:c0 + 129], K.rbank[bi]

            def pv(h, r, lhsT, rhs, reads):
                bi, reg, rb = oreg(h, r)
                first = not started.get((h, bi), False)
                started[(h, bi)] = True
                _mm(P, reg, lhsT, rhs, first, False, reads, [rb], skip_group_check=True)

            def emit_scores(u, slot):
                kind, h, kk = u
                sbk, rsb = K.bank[slot % 2], K.rbank[slot % 2]
                if kind == "off":
                    _mm(P, sbk[:], KT[:, h, kk * 128:(kk + 1) * 128], qn[:, h, :], True, True,
                        [r_KT[h][kk // 4], r_qn[h]], [rsb])
                else:
                    kb = t4 + kk
                    n = (4 - kk) * 128
                    _mm(P, sbk[:, 0:n], KT[:, h, kb * 128:(kb + 1) * 128], qn[:, h, kk * 128:512], True, True,
                        [r_KT[h][t], r_qn[h]], [rsb])

            def emit_exp(u, slot):
                kind, h, kk = u
                sbk, rsb = K.bank[slot % 2], K.rbank[slot % 2]
                pt, rpt = PT[slot % 3], r_PT[slot % 3]
                if kind == "off":
                    P.op("act", lambda e: e.activation(out=pt[:], in_=sbk[:], func=AF.Exp, bias=Boff[:, kk, h:h + 1],
                                                       scale=SCALE), reads=[rsb, r_Boff], writes=[rpt])
                else:
                    for ql in range(kk, 4):
                        i = ql - kk
                        P.op("act", lambda e, i=i, ql=ql: e.activation(
                            out=pt[:, i * 128:(i + 1) * 128], in_=sbk[:, i * 128:(i + 1) * 128], func=AF.Exp,
                            bias=Bin[:, kk, ql, h:h + 1], scale=SCALE), reads=[rsb, r_Bin], writes=[rpt])
                    P.op("pool", lambda e: e.tensor_tensor(out=pt[:, 0:128], in0=pt[:, 0:128], in1=K.cmask_bf[:], op=ALU.mult),
                         reads=[rpt, K.r_const], writes=[rpt])

            def emit_pv(u, slot):
                kind, h, kk = u
                pt, rpt = PT[slot % 3], r_PT[slot % 3]
                if kind == "off":
                    for ql in range(4):
                        pv(h, ql, pt[:, ql * 128:(ql + 1) * 128], VA[:, kk, h, :], [rpt, r_VA[kk], r_VAone])
                else:
                    kb = t4 + kk
                    for ql in range(kk, 4):
                        i = ql - kk
                        pv(h, 4 + ql, pt[:, i * 128:(i + 1) * 128], VA[:, kb, h, :], [rpt, r_VA[kb], r_VAone])
                    if kk == 3:
                        finish_head(h)

            def finish_head(h):
                for ql in range(4):
                    _, oin, rin = oreg(h, 4 + ql)
                    if t > 0:
                        _, oof, rof = oreg(h, ql)
                        P.op("dve", lambda e, oof=oof, ql=ql: e.tensor_scalar(
                            out=otmp[:, 0:129], in0=oof, scalar1=cfac[:, ql, h:h + 1], scalar2=None, op0=ALU.mult),
                            reads=[rof, r_cfac], writes=[r_otmp])
                        P.op("dve", lambda e, oin=oin: e.tensor_tensor(out=otmp[:, 0:129], in0=otmp[:, 0:129], in1=oin, op=ALU.add),
                             reads=[rin, r_otmp], writes=[r_otmp])
                        src, rsrc = otmp[:, 0:129], [r_otmp]
                    else:
                        src, rsrc = oin, [rin]
                    P.op("dve", lambda e, src=src: e.reciprocal(out=rec[:, h:h + 1], in_=src[:, 128:129]),
                         reads=rsrc, writes=[r_rec[h]])
                    P.op("dve", lambda e, src=src, ql=ql: e.tensor_scalar(
                        out=ytok[:, ql, h * 128:(h + 1) * 128], in0=src[:, 0:128], scalar1=rec[:, h:h + 1], scalar2=None,
                        op0=ALU.mult), reads=rsrc + [r_rec[h]], writes=[r_ytok[ql]])

            prev = None
            for u in units:
                slot = nslot[0]
                nslot[0] += 1
                emit_scores(u, slot)
                if prev is not None:
                    emit_pv(*prev)
                emit_exp(u, slot)
                prev = (u, slot)
            emit_pv(*prev)
            tb = pb[:].bitcast(BF16)
            for ql in range(4):
                for h in range(4):
                    P.op("pe", lambda e, h=h, ql=ql: e.transpose(out=tb[:, h * 128:(h + 1) * 128],
                                                                in_=ytok[:, ql, h * 128:(h + 1) * 128], identity=K.ident_bf[:]),
                         reads=[r_ytok[ql], K.r_const], writes=[rpb])
                P.op("dve", lambda e, ql=ql, yb=yb: e.tensor_copy(
                    out=yTt[yb][:, :, ql * 128:(ql + 1) * 128], in_=tb[:, 0:512].rearrange("p (h s) -> p h s", h=4)),
                    reads=[rpb], writes=[r_yTt[yb]])
            dst = K.ybuf.rearrange("(c p) t -> p c t", p=128)[:, 4:8, tsl]
            P.dma("sp", dst, yTt[yb][:], ds[7 + yb], reads=[r_yTt[yb]], writes=[K.r_y[c][t] for c in range(4, 8)])
        P.emit_block(final=final)


def hg_phase(K, layer, hin, kind, final=False):
    nc, P, T, NT, dr = K.nc, K.P, K.T, K.NT, K.dr
    j = layer // 2
    ds = K.dsem_pool
    ret = kind == "ret"
    NCOL = 2048
    wname = "w_ret" if ret else "w_hg"
    VOFF, GOFF = (512, 1024) if ret else (1024, 1536)
    NSET = 3 if ret else 2
    with ExitStack() as es:
        def sb(name, shape, dt):
            return es.enter_context(nc.sbuf_tensor(f"g{layer}_{name}", shape, dt))
        w = sb("w", [128, KC, NCOL], BF16); r_w = Res()
        gain = sb("gain", [128, KC], F32); r_gain = Res()
        vec = sb("vec", [128, 12], F32); r_vec = Res()
        ht = [sb(f"ht{i}", [128, KC, 512], F32) for i in range(2)]; r_ht = RL(2)
        uT = [sb(f"uT{i}", [128, KC, 512], BF16) for i in range(2)]; r_uT = RL(2)
        sq = sb("sq", [128, KC, 512], BF16); r_sq = Res()
        rstd = sb("rstd", [128, 512], F32); r_rstd = Res()
        tmp = sb("tmp", [128, 512], F32); r_tmp = Res()
        rmask = sb("rmask", [128, 512], F32); r_rmask = Res()
        m2 = sb("m2", [128, 64], F32); r_m2 = Res()
        qfl = [sb(f"qf{i}", [128, 512], F32) for i in range(NSET)]; r_qfl = RL(NSET)
        kfl = [sb(f"kf{i}", [128, 512], F32) for i in range(NSET)]; r_kfl = RL(NSET)
        lfl = [sb(f"lf{i}", [128, 512], F32) for i in range(2)]; r_lfl = RL(2)
        cum = sb("cum", [128, 512], F32); r_cum = Res()
        e1 = sb("e1", [128, 512], F32); r_e1 = Res()
        ex = [sb(f"ex{i}", [128, 512], F32) for i in range(2)]; r_ex = RL(2)
        qh = [sb(f"qh{i}", [128, 512], BF16) for i in range(2)]; r_qh = RL(2)
        kh = [sb(f"kh{i}", [128, 512], BF16) for i in range(2)]; r_kh = RL(2)
        qi = [sb(f"qi{i}", [128, 512], BF16) for i in range(2)]; r_qi = RL(2)
        ko = [sb(f"ko{i}", [128, 512], BF16) for i in range(2)]; r_ko = RL(2)
        alast = [sb(f"alast{i}", [128, 8], F32) for i in range(2)]; r_alast = RL(2)
        kotok = [sb(f"kotok{i}", [128, 4, 128], BF16) for i in range(2)]; r_kotok = RL(2)
        vtok = [sb(f"vtok{i}", [128, 4, 512], BF16) for i in range(2)]; r_vtok = RL(2)
        sgt = [sb(f"sgt{i}", [128, 4, 512], BF16) for i in range(2)]; r_sgt = RL(2)
        PT = sb("PT", [128, 4, 64], BF16); r_PT = Res()
        S_f = sb("S_f", [128, 4, 2, 128], F32); r_Sf = [RL(2) for _ in range(4)]
        S_b = sb("S_b", [128, 4, 8, 128], BF16); r_Sb = [RL(8) for _ in range(4)]
        sqo = sb("sqo", [128, 512], BF16); r_sqo = Res()
        cen = sb("cen", [128, 512], F32); r_cen = Res()
        cn = sb("cn", [128, 512], F32); r_cn = Res()
        rstd2 = sb("rstd2", [128, 512], F32); r_rstd2 = Res()
        tmp2 = sb("tmp2", [128, 512], F32); r_tmp2 = Res()
        yTt = [sb(f"yTt{i}", [128, 4, 512], BF16) for i in range(2)]; r_yTt = RL(2)
        if ret:
            rot = [sb(f"rot{i}", [128, 4, 512], F32) for i in range(2)]; r_rot = RL(2)
            rtab = sb("rtab", [128, 4, 4, 64], F32); r_rtab = Res()
            rtmp = sb("rtmp", [128, 3, 64], F32); r_rtmp = Res()
            ral = sb("ral", [128, 4, 8], F32); r_ral = Res()
            meanb = sb("meanb", [128, 128], BF16); r_meanb = Res()

        P.dma("pool", w[:, :, 0:1024], dr[wname][j][:, :, 0:1024], K.dsem_q[0], writes=[r_w])
        P.dma("pool", w[:, :, 1024:2048], dr[wname][j][:, :, 1024:2048], K.dsem_q[0], writes=[r_w])
        P.dma("sp", gain[:], dr["norm_mix"][layer], ds[1], writes=[r_gain])
        if ret:
            P.dma("sp", vec[:], dr["ret_vec"][j], ds[30], writes=[r_vec])
        P.dma("sp", rmask[:], dr["rmask"], ds[31], writes=[r_rmask])
        P.op("dve", lambda e: e.tensor_copy(out=m2[0:64, :], in_=K.cmask_f[0:64, 0:64]), reads=[K.r_const], writes=[r_m2])
        P.op("dve", lambda e: e.tensor_copy(out=m2[64:128, :], in_=K.cmask_f[64:128, 64:128]), reads=[K.r_const, r_m2],
             writes=[r_m2])
        if not ret:
            raw = sb("raw", [128, 12], F32); r_raw = Res()
            P.dma("sp", raw[:], dr["hg_vec"][j], ds[32], writes=[r_raw])
            if j == 0:
                P.op("dve", lambda e: e.tensor_scalar(out=vec[:, 0:4], in0=raw[:, 0:4], scalar1=0.0, scalar2=None,
                                                      op0=ALU.mult), reads=[r_raw, r_vec], writes=[r_vec])
            else:
                P.op("dve", lambda e: e.tensor_tensor(out=vec[:, 0:4], in0=raw[:, 4:8], in1=raw[:, 0:4], op=ALU.subtract),
                     reads=[r_raw, r_vec], writes=[r_vec])
                P.op("act", lambda e: e.activation(out=vec[:, 0:4], in_=vec[:, 0:4], func=AF.Sigmoid),
                     reads=[r_vec], writes=[r_vec])
            P.op("dve", lambda e: e.tensor_scalar(out=vec[:, 4:8], in0=vec[:, 0:4], scalar1=-1.0, scalar2=1.0,
                                                  op0=ALU.mult, op1=ALU.add), reads=[r_vec], writes=[r_vec])
            P.op("dve", lambda e: e.tensor_copy(out=vec[:, 8:9], in_=raw[:, 8:9]), reads=[r_raw, r_vec], writes=[r_vec])
        P.op("pool", lambda e: e.memset(S_f[:], 0.0), writes=[x for l in r_Sf for x in l])
        P.op("pool", lambda e: e.memset(S_b[:], 0.0), writes=[x for l in r_Sb for x in l])

        def decay_factors(cum3, n, r_c, outs, al_out, r_al):
            n64 = n * 64
            e13 = e1[:, 0:n64].rearrange("p (c s) -> p c s", s=64)
            P.op("dve", lambda e: e.tensor_tensor(out=e13, in0=cum3, in1=cum3[:, :, 31:32].to_broadcast([128, n, 64]),
                                                  op=ALU.subtract), reads=[r_c], writes=[r_e1])
            P.op("act", lambda e: e.activation(out=ex[0][:, 0:n64], in_=e1[:, 0:n64], func=AF.Exp), reads=[r_e1], writes=[r_ex[0]])
            outs[0](ex[0][:, 0:n64], r_ex[0])
            P.op("act", lambda e: e.activation(out=ex[1][:, 0:n64], in_=e1[:, 0:n64], func=AF.Exp, scale=-1.0),
                 reads=[r_e1], writes=[r_ex[1]])
            outs[1](ex[1][:, 0:n64], r_ex[1])
            yield
            P.op("act", lambda e: e.activation(out=ex[0][:, 0:n64].rearrange("p (c s) -> p c s", s=64), in_=cum3, func=AF.Exp),
                 reads=[r_c], writes=[r_ex[0]])
            outs[2](ex[0][:, 0:n64], r_ex[0])
            P.op("dve", lambda e: e.tensor_tensor(out=e13, in0=cum3, in1=cum3[:, :, 63:64].to_broadcast([128, n, 64]),
                                                  op=ALU.subtract), reads=[r_c], writes=[r_e1])
            P.op("act", lambda e: e.activation(out=ex[1][:, 0:n64], in_=e1[:, 0:n64], func=AF.Exp, scale=-1.0),
                 reads=[r_e1], writes=[r_ex[1]])
            outs[3](ex[1][:, 0:n64], r_ex[1])
            P.op("act", lambda e: e.activation(out=al_out, in_=cum3[:, :, 63:64], func=AF.Exp), reads=[r_c], writes=[r_al])
            yield

        if ret:
            P.op("pool", lambda e: e.memset(meanb[:], 1.0 / 128), writes=[r_meanb])
            for h in range(4):
                P.op("dve", lambda e, h=h: e.tensor_scalar(out=rtmp[:, 0, :], in0=rmask[:, 0:64], scalar1=0.0,
                                                          scalar2=vec[:, h:h + 1], op0=ALU.mult, op1=ALU.add),
                     reads=[r_rmask, r_vec, r_rtmp], writes=[r_rtmp])
                P.op("dve", lambda e: e.tensor_tensor_scan(out=cum[:, 0:64], data0=rmask[:, 0:64], data1=rtmp[:, 0, :],
                                                           initial=0.0, op0=ALU.mult, op1=ALU.add),
                     reads=[r_rmask, r_rtmp], writes=[r_cum])

                def mk(i, h=h):
                    def f(ap, r):
                        P.op("dve", lambda e: e.tensor_copy(out=rtab[:, h, i, :], in_=ap), reads=[r, r_rtab], writes=[r_rtab])
                    return f
                for _ in decay_factors(cum[:, 0:64].rearrange("p (c s) -> p c s", s=64), 1, r_cum, [mk(0), mk(1), mk(2), mk(3)],
                                       ral[:, h, 0:1].rearrange("p (c o) -> p c o", o=1), r_ral):
                    pass
                P.op("dve", lambda e, h=h: e.tensor_copy(out=ral[:, h, 1:8], in_=ral[:, h, 0:1].to_broadcast([128, 7])),
                     reads=[r_ral], writes=[r_ral])

        pb, rpb = K.bank[6], K.rbank[6]
        tb = pb[:].bitcast(BF16)
        gbk, rgb = K.bank[1], K.rbank[1]
        sbk, rsb = K.bank[0], K.rbank[0]

        def proj_fm(bank, rbank, b, c0):
            for kc in range(KC):
                _mm(P, bank[:], w[:, kc, c0:c0 + 128], uT[b][:, kc, :], kc == 0, kc == KC - 1, [r_w, r_uT[b]], [rbank])

        def tile_prologue(t):
            b = t % 2
            tsl = slice(t * 512, (t + 1) * 512)
            if t == 0:
                P.dma("sp", ht[0][:], hview(hin, 0), ds[2], reads=[K.r_h[c][0] for c in range(KC)], writes=[r_ht[0]])
            if t + 1 < NT:
                P.dma("sp", ht[1 - b][:], hview(hin, t + 1), ds[2 + (1 - b)],
                      reads=[K.r_h[c][t + 1] for c in range(KC)], writes=[r_ht[1 - b]])
            if ret:
                P.dma("sp", rot[b][:], dr["rot_tab"][:, :, tsl], ds[4 + b], writes=[r_rot[b]])
            rmsnorm_tile2(K, ht[b], r_ht[b], gain, r_gain, uT[b], r_uT[b], sq, r_sq, rstd, r_rstd, tmp, r_tmp, 7)
            yield
            pbanks = [(K.bank[1], K.rbank[1]), (K.bank[7], K.rbank[7])]
            for bl in range(4):
                bk, rbk = pbanks[bl % 2]
                for kc in range(KC):
                    _mm(P, bk[:], uT[b][:, kc, bl * 128:(bl + 1) * 128], w[:, kc, VOFF:VOFF + 512], kc == 0, kc == KC - 1,
                        [r_w, r_uT[b]], [rbk])
                P.op("act", lambda e, bl=bl, b=b, bk=bk: e.copy(out=vtok[b][:, bl, :], in_=bk[:]), reads=[rbk],
                     writes=[r_vtok[b]])
                yield
            for h in range(4):
                bk, rbk = pbanks[h % 2]
                proj_fm(bk, rbk, b, GOFF + h * 128)
                P.op("act", lambda e, h=h, b=b, bk=bk: e.activation(out=sgt[b][:, h, :], in_=bk[:], func=AF.Silu),
                     reads=[rbk], writes=[r_sgt[b]])
                yield

        def stageA1(t, h, i):
            b = t % 2
            a = i % NSET
            qf, kf, lf = qfl[a], kfl[a], lfl[i % 2]
            r_qf, r_kf, r_lf = r_qfl[a], r_kfl[a], r_lfl[i % 2]
            if ret:
                if h % 2 == 1:
                    return
                a1 = (i + 1) % NSET
                pair = h // 2
                for which, d0, rd0, d1, rd1 in ((0, qf, r_qf, qfl[a1], r_qfl[a1]), (1, kf, r_kf, kfl[a1], r_kfl[a1])):
                    proj_fm(pb, rpb, b, which * 256 + pair * 128)
                    P.op("dve", lambda e, b=b: e.tensor_tensor(out=e1[:], in0=pb[:], in1=rot[b][:, 0, :], op=ALU.mult),
                         reads=[rpb, r_rot[b]], writes=[r_e1])
                    P.op("dve", lambda e, b=b: e.tensor_tensor(out=cum[:], in0=pb[:], in1=rot[b][:, 2, :], op=ALU.mult),
                         reads=[rpb, r_rot[b]], writes=[r_cum])
                    yield
                    proj_fm(pb, rpb, b, 1536 + which * 256 + pair * 128)
                    P.op("dve", lambda e, b=b, d0=d0: e.tensor_tensor(out=d0[:], in0=pb[:], in1=rot[b][:, 1, :], op=ALU.mult),
                         reads=[rpb, r_rot[b]], writes=[rd0])
                    P.op("dve", lambda e, b=b, d1=d1: e.tensor_tensor(out=d1[:], in0=pb[:], in1=rot[b][:, 3, :], op=ALU.mult),
                         reads=[rpb, r_rot[b]], writes=[rd1])
                    P.op("pool", lambda e, d0=d0: e.tensor_tensor(out=d0[:], in0=d0[:], in1=e1[:], op=ALU.add),
                         reads=[r_e1, rd0], writes=[rd0])
                    P.op("pool", lambda e, d1=d1: e.tensor_tensor(out=d1[:], in0=d1[:], in1=cum[:], op=ALU.add),
                         reads=[r_cum, rd1], writes=[rd1])
                    yield
            else:
                proj_fm(pb, rpb, b, h * 128)
                P.op("act", lambda e: e.copy(out=qf[:], in_=pb[:]), reads=[rpb], writes=[r_qf])
                proj_fm(pb, rpb, b, 512 + h * 128)
                P.op("act", lambda e: e.activation(out=lf[:], in_=pb[:], func=AF.Exp, scale=-1.0), reads=[rpb], writes=[r_lf])
                yield
                P.op("act", lambda e: e.activation(out=lf[:], in_=lf[:], func=AF.Ln, bias=1.0), reads=[r_lf], writes=[r_lf])
                yield
                P.op("act", lambda e: e.activation(out=lf[:], in_=lf[:], func=AF.Exp, scale=-1.0), reads=[r_lf], writes=[r_lf])
                yield
                P.op("dve", lambda e: e.tensor_scalar(out=lf[:], in0=lf[:], scalar1=vec[:, 4 + h:5 + h],
                                                      scalar2=vec[:, h:h + 1], op0=ALU.mult, op1=ALU.add),
                     reads=[r_lf, r_vec], writes=[r_lf])
                yield
                P.op("pool", lambda e: e.tensor_scalar(out=kf[:], in0=lf[:], scalar1=-1.0, scalar2=1.0,
                                                       op0=ALU.mult, op1=ALU.add), reads=[r_lf], writes=[r_kf])
                P.op("act", lambda e: e.activation(out=lf[:], in_=lf[:], func=AF.Ln), reads=[r_lf], writes=[r_lf])
                yield

        def stageA2(t, h, i):
            b = t % 2
            a = i % NSET
            s = i % 2
            qf, kf, lf = qfl[a], kfl[a], lfl[i % 2]
            r_qf, r_kf, r_lf = r_qfl[a], r_kfl[a], r_lfl[i % 2]
            if ret:
                def tabv(k):
                    return rtab[:, h, k:k + 1, :].to_broadcast([128, 8, 64])

                def v3(x):
                    return x.rearrange("p (c s) -> p c s", s=64)
                P.op("pool", lambda e: e.tensor_tensor(out=v3(qh[s][:]), in0=v3(qf[:]), in1=tabv(0), op=ALU.mult),
                     reads=[r_qf, r_rtab], writes=[r_qh[s]])
                P.op("dve", lambda e: e.scalar_tensor_tensor(out=v3(kh[s][:]), in0=v3(kf[:]), scalar=0.125, in1=tabv(1),
                                                             op0=ALU.mult, op1=ALU.mult), reads=[r_kf, r_rtab], writes=[r_kh[s]])
                yield
                P.op("pool", lambda e: e.tensor_tensor(out=v3(qi[s][:]), in0=v3(qf[:]), in1=tabv(2), op=ALU.mult),
                     reads=[r_qf, r_rtab], writes=[r_qi[s]])
                P.op("dve", lambda e: e.scalar_tensor_tensor(out=v3(ko[s][:]), in0=v3(kf[:]), scalar=0.125, in1=tabv(3),
                                                             op0=ALU.mult, op1=ALU.mult), reads=[r_kf, r_rtab], writes=[r_ko[s]])
                al, r_al = ral[:, h, :], r_ral
                yield
            else:
                P.op("dve", lambda e: e.tensor_tensor_scan(out=cum[:], data0=rmask[:], data1=lf[:], initial=0.0,
                                                           op0=ALU.mult, op1=ALU.add),
                     reads=[r_rmask, r_lf], writes=[r_cum])
                yield

                def o_qh(ap, r):
                    P.op("pool", lambda e: e.tensor_tensor(out=qh[s][:], in0=qf[:], in1=ap, op=ALU.mult),
                         reads=[r_qf, r], writes=[r_qh[s]])

                def o_kh(ap, r):
                    P.op("dve", lambda e: e.tensor_tensor(out=kh[s][:], in0=kf[:], in1=ap, op=ALU.mult),
                         reads=[r_kf, r], writes=[r_kh[s]])

                def o_qi(ap, r):
                    P.op("pool", lambda e: e.tensor_tensor(out=qi[s][:], in0=qf[:], in1=ap, op=ALU.mult),
                         reads=[r_qf, r], writes=[r_qi[s]])

                def o_ko(ap, r):
                    P.op("dve", lambda e: e.tensor_tensor(out=ko[s][:], in0=kf[:], in1=ap, op=ALU.mult),
                         reads=[r_kf, r], writes=[r_ko[s]])
                yield from decay_factors(cum[:].rearrange("p (c s) -> p c s", s=64), 8, r_cum, [o_qh, o_kh, o_qi, o_ko],
                                         alast[s][:].rearrange("p (c o) -> p c o", o=1), r_alast[s])
                al, r_al = alast[s][:], r_alast[s]
            K._al[(t, h)] = (al, r_al)
            for bl in range(4):
                P.op("pe", lambda e, bl=bl: e.transpose(out=tb[:, bl * 128:(bl + 1) * 128],
                                                        in_=ko[s][:, bl * 128:(bl + 1) * 128], identity=K.ident_bf[:]),
                     reads=[r_ko[s], K.r_const], writes=[rpb])
            P.op("act", lambda e: e.copy(out=kotok[s][:].rearrange("p b d -> p (b d)"), in_=tb[:, 0:512]),
                 reads=[rpb], writes=[r_kotok[s]])
            yield

        def stageBC(t, h, s):
            b = t % 2
            yb = t % 2
            hs = slice(h * 128, (h + 1) * 128)
            al, r_al = K._al[(t, h)]
            ob, rob = K.bank[2 + (h % 2)], K.rbank[2 + (h % 2)]
            for c in range(8):
                bl, p0 = c // 2, (c % 2) * 64
                dbk, rdb = K.bank[4 + (c % 2)], K.rbank[4 + (c % 2)]
                _mm(P, dbk[:, (c // 2) * 128:(c // 2 + 1) * 128], kotok[s][p0:p0 + 64, bl, :], vtok[b][p0:p0 + 64, bl, hs],
                    True, True, [r_kotok[s], r_vtok[b]], [rdb])
            yield
            for c in range(8):
                p0 = (c % 2) * 64
                csl = slice(c * 64, (c + 1) * 64)
                _mm(P, sbk[p0:p0 + 64, (c // 2) * 64:(c // 2 + 1) * 64], kh[s][:, csl], qh[s][:, csl], True, True,
                    [r_kh[s], r_qh[s]], [rsb])
            P.op("dve", lambda e: e.tensor_tensor(out=PT[:], in0=sbk[:, 0:256].rearrange("p (c s) -> p c s", s=64),
                                                  in1=m2[:].unsqueeze(1).to_broadcast([128, 4, 64]), op=ALU.mult),
                 reads=[rsb, r_m2], writes=[r_PT])
            yield
            for c in range(8):
                dbk, rdb = K.bank[4 + (c % 2)], K.rbank[4 + (c % 2)]
                src, dst = (c + 1) % 2, c % 2
                P.op("dve", lambda e, c=c, src=src, dst=dst, dbk=dbk: e.scalar_tensor_tensor(
                    out=S_f[:, h, dst, :], in0=S_f[:, h, src, :], scalar=al[:, c:c + 1],
                    in1=dbk[:, (c // 2) * 128:(c // 2 + 1) * 128], op0=ALU.mult, op1=ALU.add),
                    reads=[r_Sf[h][src], r_al, rdb], writes=[r_Sf[h][dst]])
                if c < 7:
                    P.op("pool", lambda e, c=c, dst=dst: e.tensor_copy(out=S_b[:, h, c + 1, :], in_=S_f[:, h, dst, :]),
                         reads=[r_Sf[h][dst]], writes=[r_Sb[h][c + 1]])
                if c % 2 == 1:
                    yield
            for c in range(8):
                bl, p0 = c // 2, (c % 2) * 64
                csl = slice(c * 64, (c + 1) * 64)
                _mm(P, ob[:, csl], vtok[b][p0:p0 + 64, bl, hs], PT[p0:p0 + 64, c // 2, :], True, False,
                    [r_vtok[b], r_PT], [rob])
                _mm(P, ob[:, csl], S_b[:, h, c, :], qi[s][:, csl], False, True, [r_Sb[h][c], r_qi[s]], [rob])
                if c % 2 == 1:
                    yield
            P.op("pool", lambda e: e.tensor_copy(out=S_b[:, h, 0, :], in_=S_f[:, h, 1, :]),
                 reads=[r_Sf[h][1]], writes=[r_Sb[h][0]])
            nb_, rnb = K.bank[7], K.rbank[7]
            if ret:
                P.op("act", lambda e: e.copy(out=sqo[:], in_=ob[:]), reads=[rob], writes=[r_sqo])
                _mm(P, nb_[:], meanb[:], sqo[:], True, True, [r_meanb, r_sqo], [rnb])
                P.op("act", lambda e: e.copy(out=tmp2[:], in_=nb_[:]), reads=[rnb], writes=[r_tmp2])
                yield
                P.op("dve", lambda e: e.tensor_tensor(out=cen[:], in0=ob[:], in1=tmp2[:], op=ALU.subtract),
                     reads=[rob, r_tmp2], writes=[r_cen])
                P.op("act", lambda e: e.activation(out=sqo[:], in_=cen[:], func=AF.Square), reads=[r_cen], writes=[r_sqo])
                osrc, r_osrc, nscale = cen[:], r_cen, 1.0
                _mm(P, nb_[:], meanb[:], sqo[:], True, True, [r_meanb, r_sqo], [rnb])
                nwcol = vec[:, 4 + h:5 + h]
            else:
                P.op("act", lambda e: e.activation(out=sqo[:], in_=ob[:], func=AF.Square), reads=[rob], writes=[r_sqo])
                osrc, r_osrc, nscale = ob[:], rob, 1.0 / 128
                _mm(P, nb_[:], K.ones_bf[:], sqo[:], True, True, [K.r_const, r_sqo], [rnb])
                nwcol = vec[:, 8:9]
            yield
            act_rstd(P, rstd2[:], r_rstd2, nb_[:], rnb, nscale, tmp2[:], r_tmp2)
            P.op("dve", lambda e: e.scalar_tensor_tensor(out=cn[:], in0=osrc, scalar=nwcol, in1=rstd2[:], op0=ALU.mult,
                                                         op1=ALU.mult), reads=[r_osrc, r_vec, r_rstd2], writes=[r_cn])
            P.op("pool", lambda e: e.tensor_tensor(out=yTt[yb][:, h, :], in0=cn[:], in1=sgt[b][:, h, :], op=ALU.mult),
                 reads=[r_cn, r_sgt[b]], writes=[r_yTt[yb]])
            yield
            if h == 3:
                tsl = slice(t * 512, (t + 1) * 512)
                dst = K.ybuf.rearrange("(c p) t -> p c t", p=128)[:, 0:4, tsl]
                P.dma("sp", dst, yTt[yb][:], ds[7 + yb], reads=[r_yTt[yb]], writes=[K.r_y[c][t] for c in range(4)])

        K._al = {}
        items = [(t, h) for t in range(NT) for h in range(4)]
        n_items = len(items)

        def run_all(g):
            for _ in g:
                pass
        run_all(tile_prologue(0))
        run_all(stageA1(items[0][0], items[0][1], 0))
        run_all(stageA2(items[0][0], items[0][1], 0))
        if n_items > 1:
            run_all(stageA1(items[1][0], items[1][1], 1))
        bg = None
        for i, (t, h) in enumerate(items):
            if h == 0 and t + 1 < NT:
                bg = tile_prologue(t + 1)
            if h == 2 and bg is not None:
                run_all(bg)
                bg = None
            gens = [stageBC(t, h, i % 2)]
            if i + 1 < n_items:
                gens.append(stageA2(items[i + 1][0], items[i + 1][1], i + 1))
            if i + 2 < n_items:
                gens.append(stageA1(items[i + 2][0], items[i + 2][1], i + 2))
            while gens:
                for g in list(gens):
                    try:
                        next(g)
                    except StopIteration:
                        gens.remove(g)
                if bg is not None:
                    try:
                        next(bg)
                    except StopIteration:
                        bg = None
        P.emit_block(final=final)


def ssd_phase(K, layer, hin, final=False):
    nc, P, T, NT, dr = K.nc, K.P, K.T, K.NT, K.dr
    j = layer // 2
    ds = K.dsem_pool
    with ExitStack() as es:
        def sb(name, shape, dt):
            return es.enter_context(nc.sbuf_tensor(f"s{layer}_{name}", shape, dt))
        w = sb("w", [128, KC, 1288], BF16); r_w = Res()
        gain = sb("gain", [128, KC], F32); r_gain = Res()
        rows = sb("rows", [128, 536], F32); r_rows = Res()
        cw = sb("cw", [128, 6, 6], F32); r_cw = Res()
        smask = sb("smask", [128, 128], F32); r_smask = Res()
        onesF = sb("onesF", [128, 128], F32); r_onesF = Res()
        negA = sb("negA", [128, 8], F32); r_negA = Res()
        dg = sb("dg", [128, 6, 4, 128], BF16); r_dg = Res()
        dsk = sb("dsk", [128, 4, 128], BF16); r_dsk = Res()
        ht = [sb(f"ht{i}", [128, KC, 512], F32) for i in range(2)]; r_ht = RL(2)
        uT = [sb(f"uT{i}", [128, KC, 512], BF16) for i in range(2)]; r_uT = RL(2)
        sq = sb("sq", [128, KC, 512], BF16); r_sq = Res()
        rstd = sb("rstd", [128, 512], F32); r_rstd = Res()
        tmp = sb("tmp", [128, 512], F32); r_tmp = Res()
        xpad = sb("xpad", [128, 6, 516], BF16); r_xpad = RL(6)
        xc = [sb(f"xc{i}", [128, 6, 512], BF16) for i in range(2)]; r_xc = [RL(6) for _ in range(2)]
        vtok = sb("vtok", [128, 512], BF16); r_vtok = Res()
        btok = [sb(f"btok{i}", [128, 128], BF16) for i in range(2)]; r_btok = RL(2)
        sz = [sb(f"sz{i}", [128, 512], F32) for i in range(2)]; r_sz = RL(2)
        sm = [sb(f"sm{i}", [128, 8], F32) for i in range(7)]; r_sm = RL(7)
        ecum = [sb(f"ecum{i}", [128, 8], F32) for i in range(2)]; r_ecum = RL(2)
        cumt = [sb(f"cumt{i}", [128, 8], F32) for i in range(2)]; r_cumt = RL(2)
        sm16 = sb("sm16", [128, 16], F32); r_sm16 = Res()
        vp = sb("vp", [128, 512], BF16); r_vp = Res()
        vpp = [sb(f"vpp{i}", [128, 512], BF16) for i in range(2)]; r_vpp = RL(2)
        LM = sb("LM", [128, 8, 128], F32); r_LM = Res()
        E = sb("E", [128, 8, 128], F32); r_E = RL(2)
        GM = sb("GM", [128, 2, 128], F32); r_GM = Res()
        PT = sb("PT", [128, 8, 128], BF16); r_PT = RL(2)
        o1 = sb("o1", [128, 512], F32); r_o1 = Res()
        o2 = sb("o2", [128, 512], F32); r_o2 = Res()
        o3 = sb("o3", [128, 512], F32); r_o3 = Res()
        ss = sb("ss", [128, 2], F32); r_ss = Res()
        ss2 = sb("ss2", [128, 2], F32); r_ss2 = Res()
        S_f = sb("S_f", [128, 512], F32); r_Sf = Res()
        S_b = sb("S_b", [128, 512], BF16); r_Sb = Res()
        ytok = sb("ytok", [128, 512], BF16); r_ytok = Res()
        yTt = [sb(f"yTt{i}", [128, 4, 512], BF16) for i in range(2)]; r_yTt = RL(2)

        P.dma("pool", w[:, :, 512:1288], dr["w_ssd"][j][:, :, 512:1288], K.dsem_q[0], writes=[r_w])
        P.dma("pool", w[:, :, 0:512], dr["w_ssd"][j][:, :, 0:512], K.dsem_q[0], writes=[r_w])
        P.dma("sp", gain[:], dr["norm_mix"][layer], ds[1], writes=[r_gain])
        P.dma("sp", rows[:], dr["ssd_rows"][j], ds[30], writes=[r_rows])
        P.dma("sp", cw[:], dr["ssd_conv"][j], ds[31], writes=[r_cw])
        P.dma("sp", smask[:], dr["smask"], ds[32], writes=[r_smask])
        P.op("pool", lambda e: e.memset(onesF[:], 1.0), writes=[r_onesF])
        P.op("pool", lambda e: e.memset(S_f[:], 0.0), writes=[r_Sf])
        P.op("pool", lambda e: e.memset(S_b[:], 0.0), writes=[r_Sb])
        P.op("pool", lambda e: e.memset(xpad[:], 0.0), writes=r_xpad)
        P.op("act", lambda e: e.activation(out=negA[:], in_=rows[:, 8:16], func=AF.Exp), reads=[r_rows], writes=[r_negA])
        P.op("dve", lambda e: e.tensor_scalar(out=negA[:], in0=negA[:], scalar1=-1.0, scalar2=None, op0=ALU.mult),
             reads=[r_negA], writes=[r_negA])
        for cc in range(6):
            for k in range(4):
                P.op("dve", lambda e, cc=cc, k=k: e.tensor_scalar(out=dg[:, cc, k, :], in0=K.ident_f[:], scalar1=cw[:, cc, k:k + 1],
                                                                 scalar2=None, op0=ALU.mult),
                     reads=[K.r_const, r_cw, r_dg], writes=[r_dg])
        for cc in range(4):
            P.op("dve", lambda e, cc=cc: e.tensor_scalar(out=dsk[:, cc, :], in0=K.ident_f[:], scalar1=cw[:, cc, 5:6],
                                                        scalar2=None, op0=ALU.mult),
                 reads=[K.r_const, r_cw, r_dsk], writes=[r_dsk])

        pb, rpb = K.bank[6], K.rbank[6]
        tb = pb[:].bitcast(BF16)
        xb, rxb = K.bank[7], K.rbank[7]
        r_b2lo, r_b2hi = Res(), Res()
        b2 = K.bank[2]
        qb5, rqb5 = K.bank[5], K.rbank[5]
        tb5 = qb5[:].bitcast(BF16)

        def stageX(t):
            b = t % 2
            if t == 0:
                P.dma("sp", ht[0][:], hview(hin, 0), ds[2], reads=[K.r_h[c][0] for c in range(KC)], writes=[r_ht[0]])
            if t + 1 < NT:
                P.dma("sp", ht[1 - b][:], hview(hin, t + 1), ds[2 + (1 - b)],
                      reads=[K.r_h[c][t + 1] for c in range(KC)], writes=[r_ht[1 - b]])
            rmsnorm_tile2(K, ht[b], r_ht[b], gain, r_gain, uT[b], r_uT[b], sq, r_sq, rstd, r_rstd, tmp, r_tmp, 7)
            yield
            for cc in range(6):
                c0 = 512 + cc * 128
                for kc in range(KC):
                    _mm(P, xb[:], w[:, kc, c0:c0 + 128], uT[b][:, kc, :], kc == 0, kc == KC - 1, [r_w, r_uT[b]], [rxb])
                P.op("act", lambda e, cc=cc: e.copy(out=xpad[:, cc, 3:515], in_=xb[:]), reads=[rxb], writes=[r_xpad[cc]])
                yield
                for k in range(4):
                    _mm(P, xb[:], dg[:, cc, k, :], xpad[:, cc, k:k + 512], k == 0, k == 3, [r_dg, r_xpad[cc]], [rxb])
                P.op("act", lambda e, cc=cc, b=b: e.activation(out=xc[b][:, cc, :], in_=xb[:], func=AF.Silu, bias=cw[:, cc, 4:5]),
                     reads=[rxb, r_cw], writes=[r_xc[b][cc]])
                P.op("pool", lambda e, cc=cc: e.tensor_copy(out=xpad[:, cc, 0:3], in_=xpad[:, cc, 512:515]),
                     reads=[r_xpad[cc]], writes=[r_xpad[cc]])
                yield

        def stageP(t, bl):
            b = t % 2
            g_ = t * 4 + bl
            p = g_ % 2
            bsl = slice(bl * 128, (bl + 1) * 128)
            xcb, rxc = xc[b], r_xc[b]
            for cc in range(4):
                P.op("pe", lambda e, cc=cc: e.transpose(out=tb[:, cc * 128:(cc + 1) * 128], in_=xcb[:, cc, bsl],
                                                        identity=K.ident_bf[:]),
                     reads=[rxc[cc], K.r_const], writes=[rpb])
            P.op("act", lambda e: e.copy(out=vtok[:], in_=tb[:, 0:512]), reads=[rpb], writes=[r_vtok])
            P.op("pe", lambda e: e.transpose(out=tb[:, 0:128], in_=xcb[:, 4, bsl], identity=K.ident_bf[:]),
                 reads=[rxc[4], K.r_const], writes=[rpb])
            P.op("act", lambda e: e.copy(out=btok[p][:], in_=tb[:, 0:128]), reads=[rpb], writes=[r_btok[p]])
            yield
            for kc in range(KC):
                _mm(P, pb[:, 0:8], uT[b][:, kc, bsl], w[:, kc, 1280:1288], kc == 0, kc == KC - 1, [r_w, r_uT[b]], [rpb])
            x_, ax, dt_, l_, wdec, _u1, _u2 = sm
            P.op("dve", lambda e: e.tensor_tensor(out=x_[:], in0=pb[:, 0:8], in1=rows[:, 0:8], op=ALU.add),
                 reads=[rpb, r_rows], writes=[r_sm[0]])
            P.op("dve", lambda e: e.scalar_tensor_tensor(out=ax[:], in0=x_[:], scalar=-1.0, in1=x_[:], op0=ALU.mult,
                                                         op1=ALU.min), reads=[r_sm[0]], writes=[r_sm[1]])
            P.op("act", lambda e: e.activation(out=ax[:], in_=ax[:], func=AF.Exp), reads=[r_sm[1]], writes=[r_sm[1]])
            P.op("act", lambda e: e.activation(out=ax[:], in_=ax[:], func=AF.Ln, bias=1.0), reads=[r_sm[1]], writes=[r_sm[1]])
            yield
            P.op("dve", lambda e: e.scalar_tensor_tensor(out=dt_[:], in0=x_[:], scalar=0.0, in1=ax[:], op0=ALU.max,
                                                         op1=ALU.add), reads=[r_sm[0], r_sm[1]], writes=[r_sm[2]])
            P.op("dve", lambda e: e.tensor_tensor(out=l_[:], in0=dt_[:], in1=negA[:], op=ALU.mult),
                 reads=[r_sm[2], r_negA], writes=[r_sm[3]])
            _mm(P, pb[:, 16:24], K.cmask_f[:], l_[:], True, True, [K.r_const, r_sm[3]], [rpb])
            _mm(P, pb[:, 24:32], onesF[:], l_[:], True, True, [r_onesF, r_sm[3]], [rpb])
            P.op("act", lambda e: e.copy(out=sm16[:], in_=pb[:, 16:32]), reads=[rpb], writes=[r_sm16])
            P.op("pool", lambda e: e.tensor_tensor(out=LM[:], in0=smask[:].unsqueeze(1).to_broadcast([128, 8, 128]),
                                                   in1=l_[:].unsqueeze(2).to_broadcast([128, 8, 128]), op=ALU.mult),
                 reads=[r_smask, r_sm[3], r_LM], writes=[r_LM])
            yield
            for kc in range(KC):
                _mm(P, pb[:], uT[b][:, kc, bsl], w[:, kc, 0:512], kc == 0, kc == KC - 1, [r_w, r_uT[b]], [rpb])
            P.op("act", lambda e: e.activation(out=sz[p][:], in_=pb[:], func=AF.Silu), reads=[rpb], writes=[r_sz[p]])
            yield
            P.op("act", lambda e: e.activation(out=ecum[p][:], in_=sm16[:, 0:8], func=AF.Exp), reads=[r_sm16], writes=[r_ecum[p]])
            P.op("dve", lambda e: e.tensor_tensor(out=wdec[:], in0=sm16[:, 8:16], in1=sm16[:, 0:8], op=ALU.subtract),
                 reads=[r_sm16], writes=[r_sm[4]])
            P.op("act", lambda e: e.activation(out=wdec[:], in_=wdec[:], func=AF.Exp), reads=[r_sm[4]], writes=[r_sm[4]])
            P.op("act", lambda e: e.activation(out=cumt[p][:], in_=sm16[:, 8:16], func=AF.Exp), reads=[r_sm16], writes=[r_cumt[p]])
            P.op("dve", lambda e: e.tensor_tensor(out=vp[:].rearrange("p (h d) -> p h d", h=8),
                                                  in0=vtok[:].rearrange("p (h d) -> p h d", h=8),
                                                  in1=dt_[:].unsqueeze(2).to_broadcast([128, 8, 64]), op=ALU.mult),
                 reads=[r_vtok, r_sm[2]], writes=[r_vp])
            P.op("pool", lambda e: e.tensor_tensor(out=vpp[p][:].rearrange("p (h d) -> p h d", h=8),
                                                   in0=vp[:].rearrange("p (h d) -> p h d", h=8),
                                                   in1=wdec[:].unsqueeze(2).to_broadcast([128, 8, 64]), op=ALU.mult),
                 reads=[r_vp, r_sm[4]], writes=[r_vpp[p]])
            yield
            _mm(P, b2[:, 0:128], xcb[0:64, 4, bsl], xcb[0:64, 5, bsl], True, True, [rxc[4], rxc[5]], [r_b2lo])
            for hh in range(2):
                db, rdb = K.bank[hh], K.rbank[hh]
                for h4 in range(4):
                    h = hh * 4 + h4
                    _mm(P, db[:, h4 * 128:(h4 + 1) * 128], LM[:, h, :], K.cmask_f[:], True, True, [r_LM, K.r_const], [rdb])
                if hh == 0:
                    _mm(P, b2[:, 128:256], xcb[64:128, 4, bsl], xcb[64:128, 5, bsl], True, True, [rxc[4], rxc[5]], [r_b2lo])
                P.op("act", lambda e, hh=hh, db=db: e.activation(out=E[:, hh * 4:(hh + 1) * 4, :].rearrange("p h i -> p (h i)"),
                                                               in_=db[:], func=AF.Exp), reads=[rdb], writes=[r_E[hh]])
                yield
            P.op("dve", lambda e: e.tensor_tensor(out=GM[:], in0=b2[:, 0:256].rearrange("p (g i) -> p g i", g=2),
                                                  in1=K.cmask_f[:].unsqueeze(1).to_broadcast([128, 2, 128]), op=ALU.mult),
                 reads=[r_b2lo, K.r_const], writes=[r_GM])
            for g in range(2):
                P.op("dve" if g == 0 else "pool", lambda e, g=g: e.tensor_tensor(
                    out=PT[:, g * 4:(g + 1) * 4, :], in0=E[:, g * 4:(g + 1) * 4, :],
                    in1=GM[:, g:g + 1, :].to_broadcast([128, 4, 128]), op=ALU.mult),
                    reads=[r_E[g], r_GM], writes=[r_PT[g]])
            yield
            ab, rab = K.bank[3 + p], K.rbank[3 + p]
            for cc in range(4):
                _mm(P, ab[:, cc * 128:(cc + 1) * 128], xcb[:, cc, bsl], dsk[:, cc, :], cc == 0, False, [rxc[cc], r_dsk], [rab],
                    skip_group_check=True)
            for h in range(8):
                _mm(P, ab[:, h * 64:(h + 1) * 64], PT[:, h, :], vp[:, h * 64:(h + 1) * 64], False, h == 7,
                    [r_PT[h // 4], r_vp], [rab], skip_group_check=True)
            yield

        def stageQ(t, bl):
            b = t % 2
            yb = t % 2
            g_ = t * 4 + bl
            p = g_ % 2
            bsl = slice(bl * 128, (bl + 1) * 128)
            xcb, rxc = xc[b], r_xc[b]
            ab, rab = K.bank[3 + p], K.rbank[3 + p]
            _mm(P, qb5[:], xcb[:, 5, bsl], S_b[:], True, True, [rxc[5], r_Sb], [rqb5])
            P.op("dve", lambda e: e.tensor_tensor(
                out=o1[:].rearrange("p (h d) -> p h d", h=8), in0=qb5[:].rearrange("p (h d) -> p h d", h=8),
                in1=ecum[p][:].unsqueeze(2).to_broadcast([128, 8, 64]), op=ALU.mult),
                reads=[rqb5, r_ecum[p]], writes=[r_o1])
            yield
            P.op("dve", lambda e: e.tensor_tensor(out=o1[:], in0=o1[:], in1=ab[:], op=ALU.add),
                 reads=[r_o1, rab], writes=[r_o1])
            _mm(P, qb5[:], btok[p][:], vpp[p][:], True, True, [r_btok[p], r_vpp[p]], [rqb5])
            for g in range(2):
                gs = slice(g * 64, (g + 1) * 64)
                sv = S_f[gs, g * 256:(g + 1) * 256].rearrange("p (h d) -> p h d", h=4)
                P.op("dve", lambda e, g=g, gs=gs, sv=sv: e.tensor_tensor(
                    out=sv, in0=sv, in1=cumt[p][gs, g * 4:(g + 1) * 4].unsqueeze(2).to_broadcast([64, 4, 64]), op=ALU.mult),
                    reads=[r_Sf, r_cumt[p]], writes=[r_Sf])
                P.op("dve", lambda e, g=g, gs=gs: e.tensor_tensor(
                    out=S_f[gs, g * 256:(g + 1) * 256], in0=S_f[gs, g * 256:(g + 1) * 256],
                    in1=qb5[gs, g * 256:(g + 1) * 256], op=ALU.add), reads=[r_Sf, rqb5], writes=[r_Sf])
            P.op("pool", lambda e: e.tensor_copy(out=S_b[:], in_=S_f[:]), reads=[r_Sf], writes=[r_Sb])
            yield
            P.op("dve", lambda e: e.tensor_tensor(out=o2[:], in0=o1[:], in1=sz[p][:], op=ALU.mult),
                 reads=[r_o1, r_sz[p]], writes=[r_o2])
            P.op("pool", lambda e: e.tensor_tensor(out=o3[:], in0=o2[:], in1=o2[:], op=ALU.mult),
                 reads=[r_o2], writes=[r_o3])
            yield
            P.op("dve", lambda e: e.tensor_reduce(out=ss[:], in_=o3[:].rearrange("p (g d) -> p g d", g=2), axis=AX.X,
                                                  op=ALU.add), reads=[r_o3], writes=[r_ss])
            act_rstd(P, ss[:], r_ss, ss[:], r_ss, 1.0 / 256, ss2[:], r_ss2)
            yield
            P.op("dve", lambda e: e.tensor_tensor(out=o2[:].rearrange("p (g d) -> p g d", g=2),
                                                  in0=o2[:].rearrange("p (g d) -> p g d", g=2),
                                                  in1=ss[:].unsqueeze(2).to_broadcast([128, 2, 256]), op=ALU.mult),
                 reads=[r_o2, r_ss], writes=[r_o2])
            P.op("pool", lambda e: e.tensor_tensor(out=ytok[:], in0=o2[:], in1=rows[:, 24:536], op=ALU.mult),
                 reads=[r_o2, r_rows], writes=[r_ytok])
            yield
            for cc in range(4):
                P.op("pe", lambda e, cc=cc: e.transpose(out=tb5[:, cc * 128:(cc + 1) * 128],
                                                        in_=ytok[:, cc * 128:(cc + 1) * 128], identity=K.ident_bf[:]),
                     reads=[r_ytok, K.r_const], writes=[rqb5])
            P.op("act", lambda e: e.copy(out=yTt[yb][:, :, bsl], in_=tb5[:, 0:512].rearrange("p (c s) -> p c s", c=4)),
                 reads=[rqb5], writes=[r_yTt[yb]])
            yield
            if bl == 3:
                tsl = slice(t * 512, (t + 1) * 512)
                dst = K.ybuf.rearrange("(c p) t -> p c t", p=128)[:, 4:8, tsl]
                P.dma("sp", dst, yTt[yb][:], ds[7 + yb], reads=[r_yTt[yb]], writes=[K.r_y[c][t] for c in range(4, 8)])

        def run_all(g):
            for _ in g:
                pass
        items = [(t, bl) for t in range(NT) for bl in range(4)]
        n_items = len(items)
        run_all(stageX(0))
        run_all(stageP(*items[0]))
        bg = None
        for i, (t, bl) in enumerate(items):
            if bl == 0 and t + 1 < NT:
                bg = stageX(t + 1)
            if bl == 3 and bg is not None:
                run_all(bg)
                bg = None
            gens = [stageQ(t, bl)]
            if i + 1 < n_items:
                gens.append(stageP(*items[i + 1]))
            while gens:
                for g in list(gens):
                    try:
                        next(g)
                    except StopIteration:
                        gens.remove(g)
                if bg is not None:
                    try:
                        next(bg)
                    except StopIteration:
                        bg = None
        P.emit_block(final=final)


ALL_PHASES = ("ret", "ssd", "hg", "fox", "ffn")


def _wl(W):
    return np.ascontiguousarray(W.reshape(W.shape[0], KC, 128, W.shape[2]).transpose(0, 2, 1, 3))


def _vec(v):
    return np.ascontiguousarray(v.reshape(v.shape[0], KC, 128).transpose(0, 2, 1))


def prepare_inputs(T, norm_mix, norm_ffn, ffn_w_gate, ffn_w_up, ffn_w_down, ab_w_in, ab_w_out, ret_gn_w, ssd_conv_w,
                   ssd_conv_b, ssd_dt_bias, ssd_a_log, ssd_d, ssd_norm_w, cd_w_in, cd_w_out, hg_lb_logits, hg_norm_w,
                   fox_f_bias, fox_q_norm_w, fox_k_norm_w):
    f32 = np.float32
    A = lambda a: np.asarray(a, dtype=f32)
    norm_mix, norm_ffn = A(norm_mix), A(norm_ffn)
    wg, wu, wd = A(ffn_w_gate), A(ffn_w_up), A(ffn_w_down)
    ab_in, ab_out, cd_in, cd_out = A(ab_w_in), A(ab_w_out), A(cd_w_in), A(cd_w_out)
    sh = {}
    sh["norm_mix"], sh["norm_ffn"] = _vec(norm_mix), _vec(norm_ffn)
    sh["wg"] = np.ascontiguousarray(wg.reshape(4, KC, 128, NJ, 128).transpose(0, 3, 2, 1, 4))
    sh["wu"] = np.ascontiguousarray(wu.reshape(4, KC, 128, NJ, 128).transpose(0, 3, 2, 1, 4))
    sh["wd"] = np.ascontiguousarray(wd.reshape(4, NJ, 128, KC, 128).transpose(0, 3, 2, 1, 4))
    w_out = np.stack([ab_out[0], cd_out[0], ab_out[1], cd_out[1]])
    sh["w_out"] = _wl(w_out)
    sh["w_fox"] = _wl(cd_in[:, :, 2048:3588])
    fv = np.zeros((2, 128, 3), f32)
    fv[:, :, 0] = A(fox_q_norm_w); fv[:, :, 1] = A(fox_k_norm_w); fv[:, 0:4, 2] = A(fox_f_bias)
    sh["fox_vec"] = fv
    sh["cmask"] = np.triu(np.ones((128, 128), f32))
    sh["smask"] = np.tril(np.ones((128, 128), f32), -1)
    sh["ident"] = np.eye(128, dtype=f32)
    sh["rmask"] = np.tile((np.arange(512) % 64 != 0).astype(f32), (128, 1))
    sh["w_hg"] = _wl(cd_in[:, :, 0:2048])
    hv = np.zeros((2, 128, 12), f32)
    lbl = A(hg_lb_logits)
    for jj in range(2):
        hv[jj, :, 0:4] = lbl[0].reshape(4, 128).T
        hv[jj, :, 4:8] = lbl[1].reshape(4, 128).T
        hv[jj, :, 8] = A(hg_norm_w)[jj]
    sh["hg_vec"] = hv

    def swp(Wx):
        return np.ascontiguousarray(Wx.reshape(D, 4, 32, 2)[:, :, :, ::-1].reshape(D, 256))
    wr = []
    for jj in range(2):
        Wq, Wk = ab_in[jj][:, 0:256], ab_in[jj][:, 256:512]
        wr.append(np.concatenate([Wq, Wk, ab_in[jj][:, 512:1024], ab_in[jj][:, 1024:1536], swp(Wq), swp(Wk)], 1))
    sh["w_ret"] = _wl(np.stack(wr))
    rv = np.zeros((2, 128, 12), f32)
    lg = np.log1p(-np.exp2(-5.0 - np.arange(4, dtype=np.float64)))
    rv[:, :, 0:4] = lg[None, None, :].astype(f32)
    rv[:, :, 4:8] = A(ret_gn_w).transpose(0, 2, 1)
    sh["ret_vec"] = rv
    freqs = (np.float32(10000.0) ** (-np.linspace(0.0, 1.0, 32, dtype=f32))).astype(f32)
    ang = (np.arange(T, dtype=f32)[:, None] * freqs[None, :]).astype(f32).astype(np.float64)
    cs = np.repeat(np.cos(ang), 2, axis=1).T
    sn = np.repeat(np.sin(ang), 2, axis=1)
    sn[:, 0::2] *= -1
    sn = sn.T
    rt = np.zeros((128, 4, T), f32)
    rt[0:64, 0, :] = cs
    rt[0:64, 1, :] = sn
    rt[64:128, 2, :] = cs
    rt[64:128, 3, :] = sn
    sh["rot_tab"] = rt
    sh["w_ssd"] = _wl(ab_in[:, :, 1536:2824])
    rows = np.zeros((2, 128, 536), f32)
    rows[:, :, 0:8] = A(ssd_dt_bias)[:, None, :]
    rows[:, :, 8:16] = A(ssd_a_log)[:, None, :]
    rows[:, :, 16:24] = A(ssd_d)[:, None, :]
    rows[:, :, 24:536] = A(ssd_norm_w)[:, None, :]
    sh["ssd_rows"] = rows
    cv = np.zeros((2, 128, 6, 6), f32)
    cwv, cbv, dsv = A(ssd_conv_w), A(ssd_conv_b), A(ssd_d)
    for jj in range(2):
        cv[jj, :, :, 0:4] = cwv[jj].T.reshape(6, 128, 4).transpose(1, 0, 2)
        cv[jj, :, :, 4] = cbv[jj].reshape(6, 128).T
        cv[jj, :, 0:4, 5] = np.repeat(dsv[jj], 64).reshape(4, 128).T
    sh["ssd_conv"] = cv
    return sh


def kernel(x, **params):
    x = np.asarray(x, dtype=np.float32)
    B, T, _ = x.shape
    shared = prepare_inputs(T, **params)
    nc = build_program(T, [0, 1, 2, 3], phases=ALL_PHASES)
    in_maps = []
    for b in range(B):
        m = dict(shared)
        m["xT"] = np.ascontiguousarray(x[b].T)
        in_maps.append(m)
    res = run_bass_kernel_spmd(nc, in_maps, core_ids=list(range(B)))
    out = np.stack([np.ascontiguousarray(res.results[b]["out"].T) for b in range(B)], axis=0)
    return out.astype(np.float32)
```

```python
import math
import numpy as np
from contextlib import ExitStack
import concourse.bass as bass
import concourse.mybir as mybir
from concourse.bass_utils import run_bass_kernel_spmd

F32 = mybir.dt.float32
BF16 = mybir.dt.bfloat16
AF = mybir.ActivationFunctionType
ALU = mybir.AluOpType
AX = mybir.AxisListType

D = 1024
KC = 8
FF = 2816
NJ = 22
EPS = 1e-6
import os
NOSYNC_ENGINES = tuple(x for x in os.environ.get("KNOSYNC", "").split(",") if x)
ENGS = ["pe", "act", "dve", "pool", "sp"]
CENG = ["pe", "act", "dve", "pool"]


class Res:
    __slots__ = ("w", "rs")

    def __init__(self):
        self.w = None
        self.rs = []


def RL(n):
    return [Res() for _ in range(n)]


class Ev:
    __slots__ = ("sem", "val", "op", "blk")

    def __init__(self, blk, sem=None, val=None, op=None):
        self.blk, self.sem, self.val, self.op = blk, sem, val, op


class Op:
    __slots__ = ("eng", "fn", "waits", "marked", "ev", "incs")

    def __init__(self, eng, fn):
        self.eng, self.fn = eng, fn
        self.waits = []
        self.marked = False
        self.ev = None
        self.incs = None


class DSem:
    def __init__(self, sem):
        self.sem = sem
        self.count = 0


class Prog:
    def __init__(self, nc, es, same_engine_sync=True):
        self.nc = nc
        self.es = es
        self.ops = {e: [] for e in ENGS}
        self.csem = {e: es.enter_context(nc.semaphore("c_" + e)) for e in CENG}
        self.ccount = {e: 0 for e in CENG}
        self.bar = es.enter_context(nc.semaphore("bar"))
        self.nbar = 0
        self.same = same_engine_sync
        self.nosync = set(NOSYNC_ENGINES)
        self.dsems = []
        self.blk = 0
        self.blk_dsems = set()

    def dsem(self):
        d = DSem(self.es.enter_context(self.nc.semaphore(f"d{len(self.dsems)}")))
        self.dsems.append(d)
        return d

    def _deps(self, op, reads, writes):
        evs = []
        for r in reads:
            if r.w is not None:
                evs.append(r.w)
        for r in writes:
            if r.w is not None:
                evs.append(r.w)
            evs.extend(r.rs)
        for ev in evs:
            if ev.blk != self.blk:
                continue
            if ev.op is not None:
                p = ev.op
                if p.eng == op.eng and (op.eng == "pe" or op.eng in self.nosync):
                    continue
                p.marked = True
            op.waits.append(ev)

    def _commit(self, ev, reads, writes):
        for r in reads:
            r.rs.append(ev)
        for r in writes:
            r.w = ev
            r.rs = []

    def op(self, eng, fn, reads=(), writes=()):
        o = Op(eng, fn)
        self._deps(o, reads, writes)
        o.ev = Ev(self.blk, op=o)
        self.ops[eng].append(o)
        self._commit(o.ev, reads, writes)
        return o

    def dma(self, q, out, in_, ds, reads=(), writes=(), slow=False):
        if slow:
            o = Op(q, lambda e: e.dma_start(out=out, in_=in_, allow_slow_non_contiguous=True))
        else:
            o = Op(q, lambda e: e.dma_start(out=out, in_=in_))
        self._deps(o, reads, writes)
        ds.count += 16
        self.blk_dsems.add(ds)
        o.incs = (ds.sem, 16)
        o.ev = Ev(self.blk, sem=ds.sem, val=ds.count)
        self.ops[q].append(o)
        self._commit(o.ev, reads, writes)
        return o

    def emit_block(self, final=False):
        nc = self.nc
        finals = []
        for e in CENG:
            ops = [o for o in self.ops[e] if o.fn is not None]
            if ops:
                ops[-1].marked = True
            c = self.ccount[e]
            for o in self.ops[e]:
                if o.incs is None and o.marked:
                    c += 1
                    o.ev.sem, o.ev.val = self.csem[e], c
            self.ccount[e] = c
            if ops:
                finals.append((self.csem[e], c))
        for ds in self.blk_dsems:
            finals.append((ds.sem, ds.count))
        self.nbar += 1
        nbar = self.nbar
        engobj = {"pe": "tensor", "act": "scalar", "dve": "vector", "pool": "gpsimd", "sp": "sync"}
        with nc.Block() as block:
            for e in ENGS:
                ops = self.ops[e]

                def body(eng, ops=ops, e=e):
                    waited = {}
                    for o in ops:
                        for ev in o.waits:
                            k = id(ev.sem)
                            if waited.get(k, 0) < ev.val:
                                eng.wait_ge(ev.sem, ev.val)
                                waited[k] = ev.val
                        ins = o.fn(eng)
                        if o.incs is not None:
                            ins.then_inc(o.incs[0], o.incs[1])
                        elif o.marked:
                            ins.then_inc(self.csem[e], 1)
                    if e == "sp":
                        for (s, v) in finals:
                            eng.wait_ge(s, v)
                        eng.sem_inc(self.bar, 1)
                    if not (final and e != "sp"):
                        eng.wait_ge(self.bar, nbar)

                getattr(block, engobj[e])(body)
        self.ops = {e: [] for e in ENGS}
        self.blk += 1
        self.blk_dsems = set()


class KB:
    pass


def _mm(P, out, lhsT, rhs, start, stop, reads, writes, **kw):
    return P.op("pe", lambda e: e.matmul(out, lhsT, rhs, start=start, stop=stop, **kw), reads=reads, writes=writes)


def build_program(T, layers, phases=("mix", "ffn"), test_ybuf=False):
    nc = bass.Bass("TRN2", target_bir_lowering=False)
    NT = T // 512
    K = KB()
    K.nc, K.T, K.NT = nc, T, NT
    dr = {}

    def din(name, shape, dt=F32):
        dr[name] = nc.dram_tensor(name, list(shape), dt, kind="ExternalInput").ap()
        return dr[name]

    din("xT", [D, T])
    din("norm_mix", [4, 128, KC])
    din("norm_ffn", [4, 128, KC])
    din("wg", [4, NJ, 128, KC, 128])
    din("wu", [4, NJ, 128, KC, 128])
    din("wd", [4, KC, 128, NJ, 128])
    din("w_out", [4, 128, KC, D])
    din("w_fox", [2, 128, KC, 1540])
    din("fox_vec", [2, 128, 3])
    din("cmask", [128, 128])
    din("w_hg", [2, 128, KC, 2048])
    din("hg_vec", [2, 128, 12])
    din("w_ret", [2, 128, KC, 2048])
    din("ret_vec", [2, 128, 12])
    din("rmask", [128, 512])
    din("w_ssd", [2, 128, KC, 1288])
    din("ssd_rows", [2, 128, 536])
    din("ssd_conv", [2, 128, 6, 6])
    din("smask", [128, 128])
    din("rot_tab", [128, 4, T])
    din("ident", [128, 128])
    out = nc.dram_tensor("out", [D, T], F32, kind="ExternalOutput").ap()
    if test_ybuf == "out":
        ybuf = nc.dram_tensor("ybuf", [D, T], BF16, kind="ExternalOutput").ap()
    elif test_ybuf:
        ybuf = nc.dram_tensor("ybuf", [D, T], BF16, kind="ExternalInput").ap()
    else:
        ybuf = nc.dram_tensor("ybuf", [D, T], BF16).ap()
    K.dr, K.out, K.ybuf = dr, out, ybuf

    with ExitStack() as es:
        P = Prog(nc, es)
        K.P = P
        K.bank = [es.enter_context(nc.psum_tensor(f"bank{i}", [128, 512], F32)) for i in range(8)]
        K.rbank = RL(8)
        K.ones_bf = es.enter_context(nc.sbuf_tensor("ones_bf", [128, 128], BF16))
        K.r_const = Res()
        P.op("pool", lambda e: e.memset(K.ones_bf[:], 1.0), writes=[K.r_const])
        K.r_h = [RL(NT) for _ in range(KC)]
        K.r_y = [RL(NT) for _ in range(KC)]
        K.dsem_pool = [P.dsem() for _ in range(40)]
        K.dsem_q = [P.dsem() for _ in range(8)]
        K.ident_bf = es.enter_context(nc.sbuf_tensor("ident_bf", [128, 128], BF16))
        K.ident_f = es.enter_context(nc.sbuf_tensor("ident_f", [128, 128], F32))
        K.cmask_bf = es.enter_context(nc.sbuf_tensor("cmask_bf", [128, 128], BF16))
        K.cmask_f = es.enter_context(nc.sbuf_tensor("cmask_f", [128, 128], F32))
        P.dma("pool", K.ident_bf[:], dr["ident"], K.dsem_q[7], writes=[K.r_const])
        P.dma("sp", K.ident_f[:], dr["ident"], K.dsem_pool[37], writes=[K.r_const])
        P.dma("pool", K.cmask_bf[:], dr["cmask"], K.dsem_q[7], writes=[K.r_const])
        P.dma("sp", K.cmask_f[:], dr["cmask"], K.dsem_pool[39], writes=[K.r_const])
        K.fd = nc.dram_tensor("fd", [8, T], F32).ap()
        K.r_fd = Res()

        plan = []
        for li, layer in enumerate(layers):
            if layer % 2 == 0:
                for ph in ("ret", "ssd"):
                    if ph in phases:
                        plan.append((ph, li, layer))
            else:
                for ph in ("hg", "fox"):
                    if ph in phases:
                        plan.append((ph, li, layer))
            if "ffn" in phases:
                plan.append(("ffn", li, layer))
        for pi, (ph, li, layer) in enumerate(plan):
            hin = dr["xT"] if li == 0 else out
            fin = pi == len(plan) - 1
            if ph == "ret":
                hg_phase(K, layer, hin, "ret", final=fin)
            elif ph == "hg":
                hg_phase(K, layer, hin, "hg", final=fin)
            elif ph == "ssd":
                ssd_phase(K, layer, hin, final=fin)
            elif ph == "fox":
                fox_phase(K, layer, hin, final=fin)
            else:
                ffn_phase(K, layer, hin, final=fin)
    return nc


def rmsnorm_tile(K, es_bufs, h_t, r_h_t, gain, r_gain, u_out, r_u, sq, r_sq, rstd, r_rstd, bank_i):
    P = K.P
    bank, rb = K.bank[bank_i], K.rbank[bank_i]
    for c in range(KC):
        P.op("act", lambda e, c=c: e.activation(out=sq[:, c, :], in_=h_t[:, c, :], func=AF.Square),
             reads=[r_h_t], writes=[r_sq])
    for c in range(KC):
        _mm(P, bank[:], K.ones_bf[:], sq[:, c, :], c == 0, c == KC - 1, [K.r_const, r_sq], [rb])
    P.op("act", lambda e: e.activation(out=rstd[:], in_=bank[:], func=AF.Sqrt, bias=EPS, scale=1.0 / D),
         reads=[rb], writes=[r_rstd])
    P.op("dve", lambda e: e.reciprocal(out=rstd[:], in_=rstd[:]), reads=[r_rstd], writes=[r_rstd])
    for c in range(KC):
        P.op("dve", lambda e, c=c: e.scalar_tensor_tensor(out=u_out(c), in0=h_t[:, c, :], scalar=gain[:, c:c + 1],
                                                         in1=rstd[:], op0=ALU.mult, op1=ALU.mult),
             reads=[r_h_t, r_gain, r_rstd], writes=[r_u])


def ffn_phase(K, layer, hin, final=False):
    nc, P, T, NT, dr, out = K.nc, K.P, K.T, K.NT, K.dr, K.out
    TG = min(1024, T)
    NG = T // TG
    TPG = TG // 512
    ds = K.dsem_pool
    with ExitStack() as es:
        def sb(name, shape, dt):
            return es.enter_context(nc.sbuf_tensor(f"f{layer}_{name}", shape, dt))
        wout = sb("wout", [128, KC, D], BF16); r_wout = Res()
        gain = sb("gain", [128, KC], F32); r_gain = Res()
        uT = [sb(f"uT{i}", [128, KC, TG], BF16) for i in range(2)]; r_uT = [RL(TPG) for _ in range(2)]
        aT = sb("aT", [128, NJ, TG], BF16); r_aT = [RL(TPG) for _ in range(NJ)]
        ht = [sb(f"ht{i}", [128, KC, 512], F32) for i in range(2)]; r_ht = RL(2)
        yt = [sb(f"yt{i}", [128, KC, 512], BF16) for i in range(2)]; r_yt = RL(2)
        sq = sb("sq", [128, KC, 512], BF16); r_sq = Res()
        rstd = sb("rstd", [128, 512], F32); r_rstd = Res()
        wgc = [sb(f"wgc{i}", [128, KC, 128], BF16) for i in range(2)]; r_wgc = RL(2)
        wuc = [sb(f"wuc{i}", [128, KC, 128], BF16) for i in range(2)]; r_wuc = RL(2)
        wdc = [sb(f"wdc{i}", [128, NJ, 128], BF16) for i in range(2)]; r_wdc = RL(2)
        sg = [sb(f"sg{i}", [128, 512], F32) for i in range(2)]; r_sg = RL(2)
        hs = [sb(f"hs{i}", [128, 512], F32) for i in range(4)]; r_hs = RL(4)

        P.dma("pool", wout[:], dr["w_out"][layer], K.dsem_q[0], writes=[r_wout])
        P.dma("sp", gain[:], dr["norm_ffn"][layer], ds[1], writes=[r_gain])

        def hview(ap, t):
            return ap.rearrange("(c p) t -> p c t", p=128)[:, :, t * 512:(t + 1) * 512]

        def load_tile(tg):
            b = tg % 2
            rh = [K.r_h[c][tg] for c in range(KC)]
            ry = [K.r_y[c][tg] for c in range(KC)]
            P.dma("sp", ht[b][:], hview(hin, tg), ds[2 + b], reads=rh, writes=[r_ht[b]])
            P.dma("sp", yt[b][:], hview(K.ybuf, tg), ds[4 + b], reads=ry, writes=[r_yt[b]])

        wcount = [0]
        tmpn = sb("tmpn", [128, 512], F32); r_tmpn = Res()

        def stage1(g):
            gp = g % 2
            load_tile(g * TPG)
            for tl in range(TPG):
                tg = g * TPG + tl
                b = tg % 2
                if tl + 1 < TPG:
                    load_tile(tg + 1)
                for dc in range(KC):
                    bi = dc % 2
                    for kc in range(KC):
                        _mm(P, K.bank[bi][:], wout[:, kc, dc * 128:(dc + 1) * 128], yt[b][:, kc, :], kc == 0, kc == KC - 1,
                            [r_wout, r_yt[b]], [K.rbank[bi]])
                    P.op("dve", lambda e, dc=dc, bi=bi, b=b: e.tensor_tensor(out=ht[b][:, dc, :], in0=ht[b][:, dc, :],
                                                                           in1=K.bank[bi][:], op=ALU.add),
                         reads=[K.rbank[bi], r_ht[b]], writes=[r_ht[b]])
                    yield
                P.dma("sp", hview(out, tg), ht[b][:], ds[6 + b], reads=[r_ht[b]],
                      writes=[K.r_h[c][tg] for c in range(KC)])
                rmsnorm_tile2(K, ht[b], r_ht[b], gain, r_gain, uT[gp][:, :, tl * 512:(tl + 1) * 512], r_uT[gp][tl],
                              sq, r_sq, rstd, r_rstd, tmpn, r_tmpn, 2)
                yield

        def run_all(gen):
            for _ in gen:
                pass

        run_all(stage1(0))
        for g in range(NG):
            gp = g % 2
            uTg, r_uTg = uT[gp], r_uT[gp]
            def load_w2(j):
                b = wcount[0] % 2
                wcount[0] += 1
                P.dma("pool", wgc[b][:], dr["wg"][layer, j], K.dsem_q[1 + b], writes=[r_wgc[b]])
                P.dma("pool", wuc[b][:], dr["wu"][layer, j], K.dsem_q[3 + b], writes=[r_wuc[b]])
                return b
            nb = load_w2(0)
            for j in range(NJ):
                b = nb
                if j + 1 < NJ:
                    nb = load_w2(j + 1)
                for tl in range(TPG):
                    pg, pu = 3 + 2 * (tl % 2), 4 + 2 * (tl % 2)
                    sl = slice(tl * 512, (tl + 1) * 512)
                    for kc in range(KC):
                        _mm(P, K.bank[pg][:], wgc[b][:, kc, :], uTg[:, kc, sl], kc == 0, kc == KC - 1,
                            [r_wgc[b], r_uTg[tl]], [K.rbank[pg]])
                    for kc in range(KC):
                        _mm(P, K.bank[pu][:], wuc[b][:, kc, :], uTg[:, kc, sl], kc == 0, kc == KC - 1,
                            [r_wuc[b], r_uTg[tl]], [K.rbank[pu]])
                    s = tl % 2
                    P.op("act", lambda e, s=s, pg=pg: e.activation(out=sg[s][:], in_=K.bank[pg][:], func=AF.Silu),
                         reads=[K.rbank[pg]], writes=[r_sg[s]])
                    P.op("dve", lambda e, s=s, pu=pu, j=j, sl=sl: e.tensor_tensor(out=aT[:, j, sl], in0=sg[s][:],
                                                                                 in1=K.bank[pu][:], op=ALU.mult),
                         reads=[r_sg[s], K.rbank[pu]], writes=[r_aT[j][tl]])
            bg = stage1(g + 1) if g + 1 < NG else None
            P.dma("pool", wdc[0][:], dr["wd"][layer, 0], K.dsem_q[5], writes=[r_wdc[0]])
            cnt = 0
            for dc in range(KC):
                b = dc % 2
                if dc + 1 < KC:
                    P.dma("pool", wdc[1 - b][:], dr["wd"][layer, dc + 1], K.dsem_q[5 + (1 - b)], writes=[r_wdc[1 - b]])
                for tl in range(TPG):
                    tg = g * TPG + tl
                    bi = 3 + cnt % 2
                    hb = cnt % 4
                    cnt += 1
                    src = out[dc * 128:(dc + 1) * 128, tg * 512:(tg + 1) * 512]
                    P.dma("sp", hs[hb][:], src, ds[14 + hb], reads=[K.r_h[dc][tg]], writes=[r_hs[hb]])
                    for j in range(NJ):
                        _mm(P, K.bank[bi][:], wdc[b][:, j, :], aT[:, j, tl * 512:(tl + 1) * 512], j == 0, j == NJ - 1,
                            [r_wdc[b], r_aT[j][tl]], [K.rbank[bi]])
                    P.op("dve", lambda e, hb=hb, bi=bi: e.tensor_tensor(out=hs[hb][:], in0=hs[hb][:], in1=K.bank[bi][:],
                                                                       op=ALU.add),
                         reads=[K.rbank[bi], r_hs[hb]], writes=[r_hs[hb]])
                    P.dma("sp", src, hs[hb][:], ds[18 + hb], reads=[r_hs[hb]], writes=[K.r_h[dc][tg]])
                    if bg is not None:
                        try:
                            next(bg)
                            next(bg)
                        except StopIteration:
                            bg = None
            if bg is not None:
                run_all(bg)
        P.emit_block(final=final)


def act_rstd(P, out, r_out, in_, r_in, scale, tmp, r_tmp):
    P.op("act", lambda e: e.activation(out=tmp, in_=in_, func=AF.Ln, bias=EPS, scale=scale), reads=[r_in], writes=[r_tmp])
    P.op("act", lambda e: e.activation(out=out, in_=tmp, func=AF.Exp, scale=-0.5), reads=[r_tmp], writes=[r_out])


def rmsnorm_tile2(K, h_t, r_h_t, gain, r_gain, uT, r_u, sq, r_sq, rstd, r_rstd, tmp, r_tmp, bank_i):
    P = K.P
    bank, rb = K.bank[bank_i], K.rbank[bank_i]
    if not hasattr(K, "_sqres"):
        K._sqres = {}
    rs = K._sqres.setdefault(id(r_sq), RL(KC))
    eng_of = ["act", "act", "act", "dve", "act", "dve", "pool", "act"]
    for c in range(KC):
        if eng_of[c] == "act":
            P.op("act", lambda e, c=c: e.activation(out=sq[:, c, :], in_=h_t[:, c, :], func=AF.Square),
                 reads=[r_h_t], writes=[rs[c]])
        else:
            P.op(eng_of[c], lambda e, c=c: e.tensor_tensor(out=sq[:, c, :], in0=h_t[:, c, :], in1=h_t[:, c, :], op=ALU.mult),
                 reads=[r_h_t], writes=[rs[c]])
    for c in range(KC):
        _mm(P, bank[:], K.ones_bf[:], sq[:, c, :], c == 0, c == KC - 1, [K.r_const, rs[c]], [rb])
    act_rstd(P, rstd[:], r_rstd, bank[:], rb, 1.0 / D, tmp[:], r_tmp)
    for c in range(KC):
        P.op("dve", lambda e, c=c: e.scalar_tensor_tensor(out=uT[:, c, :], in0=h_t[:, c, :], scalar=gain[:, c:c + 1],
                                                         in1=rstd[:], op0=ALU.mult, op1=ALU.add if False else ALU.mult),
             reads=[r_h_t, r_gain, r_rstd], writes=[r_u])


def hview(ap, t):
    return ap.rearrange("(c p) t -> p c t", p=128)[:, :, t * 512:(t + 1) * 512]


def fox_phase(K, layer, hin, final=False):
    nc, P, T, NT, dr = K.nc, K.P, K.T, K.NT, K.dr
    j = layer // 2
    NB = T // 128
    ds = K.dsem_pool
    SCALE = 128 ** -0.5
    with ExitStack() as es:
        def sb(name, shape, dt):
            return es.enter_context(nc.sbuf_tensor(f"x{layer}_{name}", shape, dt))
        w = sb("w", [128, KC, 1540], BF16); r_w = Res()
        gain = sb("gain", [128, KC], F32); r_gain = Res()
        fvec = sb("fvec", [128, 3], F32); r_fvec = Res()
        KT = sb("KT", [128, 4, T], BF16); r_KT = [RL(NT) for _ in range(4)]
        VA = sb("VA", [128, NB, 4, 129], BF16); r_VA = RL(NB); r_VAone = Res()
        Ftok = sb("Ftok", [128, NB, 4], F32); r_Ftok = RL(NB)
        Rq = sb("Rq", [128, NB, 4], F32); r_Rq = RL(NB)
        Bq = sb("Bq", [128, NB, 4], F32); r_Bq = Res()
        ht = [sb(f"ht{i}", [128, KC, 512], F32) for i in range(2)]; r_ht = RL(2)
        uT = sb("uT", [128, KC, 512], BF16); r_uT = Res()
        sq = sb("sq", [128, KC, 512], BF16); r_sq = Res()
        rstd = sb("rstd", [128, 512], F32); r_rstd = Res()
        tmp = sb("tmp", [128, 512], F32); r_tmp = Res()
        qn = sb("qn", [128, 4, 512], BF16); r_qn = RL(4)
        sqh2 = [sb(f"sqh{i}", [128, 512], BF16) for i in range(2)]; r_sqh2 = RL(2)
        rq2 = [sb(f"rq{i}", [128, 512], F32) for i in range(2)]; r_rq2 = RL(2)
        tmp2 = [sb(f"tmpq{i}", [128, 512], F32) for i in range(2)]; r_tmp2 = RL(2)
        fx = [sb(f"fx{i}", [4, 512], F32) for i in range(4)]; r_fx = RL(4)
        Fc = [sb(f"Fc{i}", [4, 512], F32) for i in range(2)]; r_Fc = RL(2)
        onesf = sb("onesf", [4, 512], F32); r_onesf = Res()
        PT = [sb(f"PT{i}", [128, 512], BF16) for i in range(3)]; r_PT = RL(3)
        ytok = sb("ytok", [128, 4, 512], BF16); r_ytok = RL(4)
        Rt = sb("Rt", [128, 4], F32); r_Rt = Res()
        Boff = sb("Boff", [128, NB, 4], F32); r_Boff = Res()
        Bin = sb("Bin", [128, 4, 4, 4], F32); r_Bin = Res()
        cfac = sb("cfac", [128, 4, 4], F32); r_cfac = Res()
        otmp = sb("otmp", [128, 132], F32); r_otmp = Res()
        nslot = [0]
        rec = sb("rec", [128, 4], F32); r_rec = RL(4)
        yTt = [sb(f"yTt{i}", [128, 4, 512], BF16) for i in range(2)]; r_yTt = RL(2)

        P.dma("pool", w[:, :, 0:768], dr["w_fox"][j][:, :, 0:768], K.dsem_q[0], writes=[r_w])
        P.dma("pool", w[:, :, 768:1540], dr["w_fox"][j][:, :, 768:1540], K.dsem_q[0], writes=[r_w])
        P.dma("sp", gain[:], dr["norm_mix"][layer], ds[1], writes=[r_gain])
        P.dma("sp", fvec[:], dr["fox_vec"][j], ds[30], writes=[r_fvec])
        P.op("pool", lambda e: e.memset(onesf[:], 1.0), writes=[r_onesf])
        P.op("pool", lambda e: e.memset(VA[:, :, :, 128:129], 1.0), writes=[r_VAone])

        P.dma("sp", ht[0][:], hview(hin, 0), ds[2], reads=[K.r_h[c][0] for c in range(KC)], writes=[r_ht[0]])
        npt = 0
        for t in range(NT):
            b = t % 2
            if t + 1 < NT:
                P.dma("sp", ht[1 - b][:], hview(hin, t + 1), ds[2 + (1 - b)],
                      reads=[K.r_h[c][t + 1] for c in range(KC)], writes=[r_ht[1 - b]])
            rmsnorm_tile2(K, ht[b], r_ht[b], gain, r_gain, uT, r_uT, sq, r_sq, rstd, r_rstd, tmp, r_tmp, 7)
            tsl = slice(t * 512, (t + 1) * 512)
            pb, rpb = K.bank[6], K.rbank[6]
            for kc in range(KC):
                _mm(P, pb[0:4, :], w[:, kc, 1536:1540], uT[:, kc, :], kc == 0, kc == KC - 1, [r_w, r_uT], [rpb])
            P.op("dve", lambda e: e.tensor_scalar(out=fx[0][:], in0=pb[0:4, :], scalar1=fvec[0:4, 2:3], scalar2=None,
                                                  op0=ALU.add), reads=[rpb, r_fvec], writes=[r_fx[0]])
            P.op("dve", lambda e: e.scalar_tensor_tensor(out=fx[1][:], in0=fx[0][:], scalar=-1.0, in1=fx[0][:],
                                                         op0=ALU.mult, op1=ALU.min),
                 reads=[r_fx[0]], writes=[r_fx[1]])
            P.op("act", lambda e: e.activation(out=fx[1][:], in_=fx[1][:], func=AF.Exp),
                 reads=[r_fx[1]], writes=[r_fx[1]])
            P.op("act", lambda e: e.activation(out=fx[1][:], in_=fx[1][:], func=AF.Ln, bias=1.0),
                 reads=[r_fx[1]], writes=[r_fx[1]])
            P.op("dve", lambda e: e.tensor_scalar_min(out=fx[2][:], in0=fx[0][:], scalar1=0.0),
                 reads=[r_fx[0]], writes=[r_fx[2]])
            P.op("dve", lambda e: e.tensor_sub(out=fx[3][:], in0=fx[2][:], in1=fx[1][:]),
                 reads=[r_fx[1], r_fx[2]], writes=[r_fx[3]])
            init = 0.0 if t == 0 else Fc[1 - b][:, 511:512]
            P.op("dve", lambda e, b=b, init=init: e.tensor_tensor_scan(out=Fc[b][:], data0=onesf[:], data1=fx[3][:],
                                                                      initial=init, op0=ALU.mult, op1=ALU.add),
                 reads=[r_onesf, r_fx[3], r_Fc[1 - b]], writes=[r_Fc[b]])
            P.dma("sp", K.fd[0:4, tsl], Fc[b][:], ds[4], reads=[r_Fc[b]], writes=[K.r_fd])
            for bl in range(4):
                blk = t * 4 + bl
                c0 = blk * 128
                P.dma("sp", Ftok[:, blk, :], K.fd[0:4, c0:c0 + 128].rearrange("h s -> s h"), ds[12 + bl],
                      reads=[K.r_fd], writes=[r_Ftok[blk]], slow=True)
                P.dma("sp", Rq[:, blk, :], K.fd[0:4, c0 + 64:c0 + 65].rearrange("h o -> o h").broadcast_to([128, 4]),
                      ds[16 + bl], reads=[K.r_fd], writes=[r_Rq[blk]], slow=True)
            pbanks = [6, 0, 1, 2]
            nbanks = [7, 3, 4, 5]
            gi = 0
            for h in range(4):
                for which in range(2):
                    c0 = which * 512 + h * 128
                    pq, rpq = K.bank[pbanks[gi % 4]], K.rbank[pbanks[gi % 4]]
                    nb_, rnb = K.bank[nbanks[gi % 4]], K.rbank[nbanks[gi % 4]]
                    sqq, r_sqq = sqh2[gi % 2], r_sqh2[gi % 2]
                    rqq, r_rqq = rq2[gi % 2], r_rq2[gi % 2]
                    tq, r_tq = tmp2[gi % 2], r_tmp2[gi % 2]
                    gi += 1
                    for kc in range(KC):
                        _mm(P, pq[:], w[:, kc, c0:c0 + 128], uT[:, kc, :], kc == 0, kc == KC - 1, [r_w, r_uT], [rpq])
                    P.op("act", lambda e, pq=pq, sqq=sqq: e.activation(out=sqq[:], in_=pq[:], func=AF.Square),
                         reads=[rpq], writes=[r_sqq])
                    _mm(P, nb_[:], K.ones_bf[:], sqq[:], True, True, [K.r_const, r_sqq], [rnb])
                    act_rstd(P, rqq[:], r_rqq, nb_[:], rnb, 1.0 / 128, tq[:], r_tq)
                    if which == 0:
                        dst, rd = qn[:, h, :], [r_qn[h]]
                    else:
                        dst, rd = KT[:, h, tsl], [r_KT[h][t]]
                    P.op("dve", lambda e, dst=dst, which=which, pq=pq, rqq=rqq: e.scalar_tensor_tensor(
                        out=dst, in0=pq[:], scalar=fvec[:, which:which + 1], in1=rqq[:], op0=ALU.mult, op1=ALU.mult),
                        reads=[rpq, r_fvec, r_rqq], writes=rd)
            for bl in range(4):
                blk = t * 4 + bl
                pq, rpq = K.bank[pbanks[bl % 4]], K.rbank[pbanks[bl % 4]]
                for kc in range(KC):
                    _mm(P, pq[:], uT[:, kc, bl * 128:(bl + 1) * 128], w[:, kc, 1024:1536], kc == 0, kc == KC - 1,
                        [r_w, r_uT], [rpq])
                P.op("dve", lambda e, blk=blk, pq=pq: e.tensor_copy(out=VA[:, blk, :, 0:128],
                                                                   in_=pq[:].rearrange("p (h v) -> p h v", h=4)),
                     reads=[rpq, r_VAone], writes=[r_VA[blk]])
            yb = t % 2
            t4 = t * 4
            if t > 0:
                P.dma("sp", Rt[:], K.fd[0:4, t * 512:t * 512 + 1].rearrange("h o -> o h").broadcast_to([128, 4]),
                      ds[20], reads=[K.r_fd], writes=[r_Rt], slow=True)
                P.op("dve", lambda e, t4=t4: e.tensor_tensor(
                    out=Boff[:, 0:t4, :], in0=Rt[:].unsqueeze(1).to_broadcast([128, t4, 4]), in1=Ftok[:, 0:t4, :],
                    op=ALU.subtract), reads=[r_Rt] + r_Ftok[0:t4], writes=[r_Boff])
                P.op("dve", lambda e, t4=t4: e.tensor_tensor(
                    out=cfac[:], in0=Rq[:, t4:t4 + 4, :], in1=Rt[:].unsqueeze(1).to_broadcast([128, 4, 4]),
                    op=ALU.subtract), reads=[r_Rt] + r_Rq[t4:t4 + 4], writes=[r_cfac])
                P.op("act", lambda e: e.activation(out=cfac[:], in_=cfac[:], func=AF.Exp), reads=[r_cfac], writes=[r_cfac])
            P.op("dve", lambda e, t4=t4: e.tensor_tensor(
                out=Bin[:], in0=Rq[:, t4:t4 + 4, :].unsqueeze(1).to_broadcast([128, 4, 4, 4]),
                in1=Ftok[:, t4:t4 + 4, :].unsqueeze(2).to_broadcast([128, 4, 4, 4]), op=ALU.subtract),
                reads=r_Rq[t4:t4 + 4] + r_Ftok[t4:t4 + 4], writes=[r_Bin])

            units = []
            for h in range(4):
                for kb in range(t4):
                    units.append(("off", h, kb))
                for kl in range(4):
                    units.append(("in", h, kl))
            started = {}

            def oreg(h, r):
                bi = (2 if h % 2 == 0 else 5) + r // 3
                c0 = (r % 3) * 129
                return bi, K.bank[bi][:, c0:c0 + 129], K.rbank[bi]

            def pv(h, r, lhsT, rhs, reads):
                bi, reg, rb = oreg(h, r)
                first = not started.get((h, bi), False)
                started[(h, bi)] = True
                _mm(P, reg, lhsT, rhs, first, False, reads, [rb], skip_group_check=True)

            def emit_scores(u, slot):
                kind, h, kk = u
                sbk, rsb = K.bank[slot % 2], K.rbank[slot % 2]
                if kind == "off":
                    _mm(P, sbk[:], KT[:, h, kk * 128:(kk + 1) * 128], qn[:, h, :], True, True,
                        [r_KT[h][kk // 4], r_qn[h]], [rsb])
                else:
                    kb = t4 + kk
                    n = (4 - kk) * 128
                    _mm(P, sbk[:, 0:n], KT[:, h, kb * 128:(kb + 1) * 128], qn[:, h, kk * 128:512], True, True,
                        [r_KT[h][t], r_qn[h]], [rsb])

            def emit_exp(u, slot):
                kind, h, kk = u
                sbk, rsb = K.bank[slot % 2], K.rbank[slot % 2]
                pt, rpt = PT[slot % 3], r_PT[slot % 3]
                if kind == "off":
                    P.op("act", lambda e: e.activation(out=pt[:], in_=sbk[:], func=AF.Exp, bias=Boff[:, kk, h:h + 1],
                                                       scale=SCALE), reads=[rsb, r_Boff], writes=[rpt])
                else:
                    for ql in range(kk, 4):
                        i = ql - kk
                        P.op("act", lambda e, i=i, ql=ql: e.activation(
                            out=pt[:, i * 128:(i + 1) * 128], in_=sbk[:, i * 128:(i + 1) * 128], func=AF.Exp,
                            bias=Bin[:, kk, ql, h:h + 1], scale=SCALE), reads=[rsb, r_Bin], writes=[rpt])
                    P.op("pool", lambda e: e.tensor_tensor(out=pt[:, 0:128], in0=pt[:, 0:128], in1=K.cmask_bf[:], op=ALU.mult),
                         reads=[rpt, K.r_const], writes=[rpt])

            def emit_pv(u, slot):
                kind, h, kk = u
                pt, rpt = PT[slot % 3], r_PT[slot % 3]
                if kind == "off":
                    for ql in range(4):
                        pv(h, ql, pt[:, ql * 128:(ql + 1) * 128], VA[:, kk, h, :], [rpt, r_VA[kk], r_VAone])
                else:
                    kb = t4 + kk
                    for ql in range(kk, 4):
                        i = ql - kk
                        pv(h, 4 + ql, pt[:, i * 128:(i + 1) * 128], VA[:, kb, h, :], [rpt, r_VA[kb], r_VAone])
                    if kk == 3:
                        finish_head(h)

            def finish_head(h):
                for ql in range(4):
                    _, oin, rin = oreg(h, 4 + ql)
                    if t > 0:
                        _, oof, rof = oreg(h, ql)
                        P.op("dve", lambda e, oof=oof, ql=ql: e.tensor_scalar(
                            out=otmp[:, 0:129], in0=oof, scalar1=cfac[:, ql, h:h + 1], scalar2=None, op0=ALU.mult),
                            reads=[rof, r_cfac], writes=[r_otmp])
                        P.op("dve", lambda e, oin=oin: e.tensor_tensor(out=otmp[:, 0:129], in0=otmp[:, 0:129], in1=oin, op=ALU.add),
                             reads=[rin, r_otmp], writes=[r_otmp])
                        src, rsrc = otmp[:, 0:129], [r_otmp]
                    else:
                        src, rsrc = oin, [rin]
                    P.op("dve", lambda e, src=src: e.reciprocal(out=rec[:, h:h + 1], in_=src[:, 128:129]),
                         reads=rsrc, writes=[r_rec[h]])
                    P.op("dve", lambda e, src=src, ql=ql: e.tensor_scalar(
                        out=ytok[:, ql, h * 128:(h + 1) * 128], in0=src[:, 0:128], scalar1=rec[:, h:h + 1], scalar2=None,
                        op0=ALU.mult), reads=rsrc + [r_rec[h]], writes=[r_ytok[ql]])

            prev = None
            for u in units:
                slot = nslot[0]
                nslot[0] += 1
                emit_scores(u, slot)
                if prev is not None:
                    emit_pv(*prev)
                emit_exp(u, slot)
                prev = (u, slot)
            emit_pv(*prev)
            tb = pb[:].bitcast(BF16)
            for ql in range(4):
                for h in range(4):
                    P.op("pe", lambda e, h=h, ql=ql: e.transpose(out=tb[:, h * 128:(h + 1) * 128],
                                                                in_=ytok[:, ql, h * 128:(h + 1) * 128], identity=K.ident_bf[:]),
                         reads=[r_ytok[ql], K.r_const], writes=[rpb])
                P.op("dve", lambda e, ql=ql, yb=yb: e.tensor_copy(
                    out=yTt[yb][:, :, ql * 128:(ql + 1) * 128], in_=tb[:, 0:512].rearrange("p (h s) -> p h s", h=4)),
                    reads=[rpb], writes=[r_yTt[yb]])
            dst = K.ybuf.rearrange("(c p) t -> p c t", p=128)[:, 4:8, tsl]
            P.dma("sp", dst, yTt[yb][:], ds[7 + yb], reads=[r_yTt[yb]], writes=[K.r_y[c][t] for c in range(4, 8)])
        P.emit_block(final=final)


def hg_phase(K, layer, hin, kind, final=False):
    nc, P, T, NT, dr = K.nc, K.P, K.T, K.NT, K.dr
    j = layer // 2
    ds = K.dsem_pool
    ret = kind == "ret"
    NCOL = 2048
    wname = "w_ret" if ret else "w_hg"
    VOFF, GOFF = (512, 1024) if ret else (1024, 1536)
    NSET = 3 if ret else 2
    with ExitStack() as es:
        def sb(name, shape, dt):
            return es.enter_context(nc.sbuf_tensor(f"g{layer}_{name}", shape, dt))
        w = sb("w", [128, KC, NCOL], BF16); r_w = Res()
        gain = sb("gain", [128, KC], F32); r_gain = Res()
        vec = sb("vec", [128, 12], F32); r_vec = Res()
        ht = [sb(f"ht{i}", [128, KC, 512], F32) for i in range(2)]; r_ht = RL(2)
        uT = [sb(f"uT{i}", [128, KC, 512], BF16) for i in range(2)]; r_uT = RL(2)
        sq = sb("sq", [128, KC, 512], BF16); r_sq = Res()
        rstd = sb("rstd", [128, 512], F32); r_rstd = Res()
        tmp = sb("tmp", [128, 512], F32); r_tmp = Res()
        rmask = sb("rmask", [128, 512], F32); r_rmask = Res()
        m2 = sb("m2", [128, 64], F32); r_m2 = Res()
        qfl = [sb(f"qf{i}", [128, 512], F32) for i in range(NSET)]; r_qfl = RL(NSET)
        kfl = [sb(f"kf{i}", [128, 512], F32) for i in range(NSET)]; r_kfl = RL(NSET)
        lfl = [sb(f"lf{i}", [128, 512], F32) for i in range(2)]; r_lfl = RL(2)
        cum = sb("cum", [128, 512], F32); r_cum = Res()
        e1 = sb("e1", [128, 512], F32); r_e1 = Res()
        ex = [sb(f"ex{i}", [128, 512], F32) for i in range(2)]; r_ex = RL(2)
        qh = [sb(f"qh{i}", [128, 512], BF16) for i in range(2)]; r_qh = RL(2)
        kh = [sb(f"kh{i}", [128, 512], BF16) for i in range(2)]; r_kh = RL(2)
        qi = [sb(f"qi{i}", [128, 512], BF16) for i in range(2)]; r_qi = RL(2)
        ko = [sb(f"ko{i}", [128, 512], BF16) for i in range(2)]; r_ko = RL(2)
        alast = [sb(f"alast{i}", [128, 8], F32) for i in range(2)]; r_alast = RL(2)
        kotok = [sb(f"kotok{i}", [128, 4, 128], BF16) for i in range(2)]; r_kotok = RL(2)
        vtok = [sb(f"vtok{i}", [128, 4, 512], BF16) for i in range(2)]; r_vtok = RL(2)
        sgt = [sb(f"sgt{i}", [128, 4, 512], BF16) for i in range(2)]; r_sgt = RL(2)
        PT = sb("PT", [128, 4, 64], BF16); r_PT = Res()
        S_f = sb("S_f", [128, 4, 2, 128], F32); r_Sf = [RL(2) for _ in range(4)]
        S_b = sb("S_b", [128, 4, 8, 128], BF16); r_Sb = [RL(8) for _ in range(4)]
        sqo = sb("sqo", [128, 512], BF16); r_sqo = Res()
        cen = sb("cen", [128, 512], F32); r_cen = Res()
        cn = sb("cn", [128, 512], F32); r_cn = Res()
        rstd2 = sb("rstd2", [128, 512], F32); r_rstd2 = Res()
        tmp2 = sb("tmp2", [128, 512], F32); r_tmp2 = Res()
        yTt = [sb(f"yTt{i}", [128, 4, 512], BF16) for i in range(2)]; r_yTt = RL(2)
        if ret:
            rot = [sb(f"rot{i}", [128, 4, 512], F32) for i in range(2)]; r_rot = RL(2)
            rtab = sb("rtab", [128, 4, 4, 64], F32); r_rtab = Res()
            rtmp = sb("rtmp", [128, 3, 64], F32); r_rtmp = Res()
            ral = sb("ral", [128, 4, 8], F32); r_ral = Res()
            meanb = sb("meanb", [128, 128], BF16); r_meanb = Res()

        P.dma("pool", w[:, :, 0:1024], dr[wname][j][:, :, 0:1024], K.dsem_q[0], writes=[r_w])
        P.dma("pool", w[:, :, 1024:2048], dr[wname][j][:, :, 1024:2048], K.dsem_q[0], writes=[r_w])
        P.dma("sp", gain[:], dr["norm_mix"][layer], ds[1], writes=[r_gain])
        if ret:
            P.dma("sp", vec[:], dr["ret_vec"][j], ds[30], writes=[r_vec])
        P.dma("sp", rmask[:], dr["rmask"], ds[31], writes=[r_rmask])
        P.op("dve", lambda e: e.tensor_copy(out=m2[0:64, :], in_=K.cmask_f[0:64, 0:64]), reads=[K.r_const], writes=[r_m2])
        P.op("dve", lambda e: e.tensor_copy(out=m2[64:128, :], in_=K.cmask_f[64:128, 64:128]), reads=[K.r_const, r_m2],
             writes=[r_m2])
        if not ret:
            raw = sb("raw", [128, 12], F32); r_raw = Res()
            P.dma("sp", raw[:], dr["hg_vec"][j], ds[32], writes=[r_raw])
            if j == 0:
                P.op("dve", lambda e: e.tensor_scalar(out=vec[:, 0:4], in0=raw[:, 0:4], scalar1=0.0, scalar2=None,
                                                      op0=ALU.mult), reads=[r_raw, r_vec], writes=[r_vec])
            else:
                P.op("dve", lambda e: e.tensor_tensor(out=vec[:, 0:4], in0=raw[:, 4:8], in1=raw[:, 0:4], op=ALU.subtract),
                     reads=[r_raw, r_vec], writes=[r_vec])
                P.op("act", lambda e: e.activation(out=vec[:, 0:4], in_=vec[:, 0:4], func=AF.Sigmoid),
                     reads=[r_vec], writes=[r_vec])
            P.op("dve", lambda e: e.tensor_scalar(out=vec[:, 4:8], in0=vec[:, 0:4], scalar1=-1.0, scalar2=1.0,
                                                  op0=ALU.mult, op1=ALU.add), reads=[r_vec], writes=[r_vec])
            P.op("dve", lambda e: e.tensor_copy(out=vec[:, 8:9], in_=raw[:, 8:9]), reads=[r_raw, r_vec], writes=[r_vec])
        P.op("pool", lambda e: e.memset(S_f[:], 0.0), writes=[x for l in r_Sf for x in l])
        P.op("pool", lambda e: e.memset(S_b[:], 0.0), writes=[x for l in r_Sb for x in l])

        def decay_factors(cum3, n, r_c, outs, al_out, r_al):
            n64 = n * 64
            e13 = e1[:, 0:n64].rearrange("p (c s) -> p c s", s=64)
            P.op("dve", lambda e: e.tensor_tensor(out=e13, in0=cum3, in1=cum3[:, :, 31:32].to_broadcast([128, n, 64]),
                                                  op=ALU.subtract), reads=[r_c], writes=[r_e1])
            P.op("act", lambda e: e.activation(out=ex[0][:, 0:n64], in_=e1[:, 0:n64], func=AF.Exp), reads=[r_e1], writes=[r_ex[0]])
            outs[0](ex[0][:, 0:n64], r_ex[0])
            P.op("act", lambda e: e.activation(out=ex[1][:, 0:n64], in_=e1[:, 0:n64], func=AF.Exp, scale=-1.0),
                 reads=[r_e1], writes=[r_ex[1]])
            outs[1](ex[1][:, 0:n64], r_ex[1])
            yield
            P.op("act", lambda e: e.activation(out=ex[0][:, 0:n64].rearrange("p (c s) -> p c s", s=64), in_=cum3, func=AF.Exp),
                 reads=[r_c], writes=[r_ex[0]])
            outs[2](ex[0][:, 0:n64], r_ex[0])
            P.op("dve", lambda e: e.tensor_tensor(out=e13, in0=cum3, in1=cum3[:, :, 63:64].to_broadcast([128, n, 64]),
                                                  op=ALU.subtract), reads=[r_c], writes=[r_e1])
            P.op("act", lambda e: e.activation(out=ex[1][:, 0:n64], in_=e1[:, 0:n64], func=AF.Exp, scale=-1.0),
                 reads=[r_e1], writes=[r_ex[1]])
            outs[3](ex[1][:, 0:n64], r_ex[1])
            P.op("act", lambda e: e.activation(out=al_out, in_=cum3[:, :, 63:64], func=AF.Exp), reads=[r_c], writes=[r_al])
            yield

        if ret:
            P.op("pool", lambda e: e.memset(meanb[:], 1.0 / 128), writes=[r_meanb])
            for h in range(4):
                P.op("dve", lambda e, h=h: e.tensor_scalar(out=rtmp[:, 0, :], in0=rmask[:, 0:64], scalar1=0.0,
                                                          scalar2=vec[:, h:h + 1], op0=ALU.mult, op1=ALU.add),
                     reads=[r_rmask, r_vec, r_rtmp], writes=[r_rtmp])
                P.op("dve", lambda e: e.tensor_tensor_scan(out=cum[:, 0:64], data0=rmask[:, 0:64], data1=rtmp[:, 0, :],
                                                           initial=0.0, op0=ALU.mult, op1=ALU.add),
                     reads=[r_rmask, r_rtmp], writes=[r_cum])

                def mk(i, h=h):
                    def f(ap, r):
                        P.op("dve", lambda e: e.tensor_copy(out=rtab[:, h, i, :], in_=ap), reads=[r, r_rtab], writes=[r_rtab])
                    return f
                for _ in decay_factors(cum[:, 0:64].rearrange("p (c s) -> p c s", s=64), 1, r_cum, [mk(0), mk(1), mk(2), mk(3)],
                                       ral[:, h, 0:1].rearrange("p (c o) -> p c o", o=1), r_ral):
                    pass
                P.op("dve", lambda e, h=h: e.tensor_copy(out=ral[:, h, 1:8], in_=ral[:, h, 0:1].to_broadcast([128, 7])),
                     reads=[r_ral], writes=[r_ral])

        pb, rpb = K.bank[6], K.rbank[6]
        tb = pb[:].bitcast(BF16)
        gbk, rgb = K.bank[1], K.rbank[1]
        sbk, rsb = K.bank[0], K.rbank[0]

        def proj_fm(bank, rbank, b, c0):
            for kc in range(KC):
                _mm(P, bank[:], w[:, kc, c0:c0 + 128], uT[b][:, kc, :], kc == 0, kc == KC - 1, [r_w, r_uT[b]], [rbank])

        def tile_prologue(t):
            b = t % 2
            tsl = slice(t * 512, (t + 1) * 512)
            if t == 0:
                P.dma("sp", ht[0][:], hview(hin, 0), ds[2], reads=[K.r_h[c][0] for c in range(KC)], writes=[r_ht[0]])
            if t + 1 < NT:
                P.dma("sp", ht[1 - b][:], hview(hin, t + 1), ds[2 + (1 - b)],
                      reads=[K.r_h[c][t + 1] for c in range(KC)], writes=[r_ht[1 - b]])
            if ret:
                P.dma("sp", rot[b][:], dr["rot_tab"][:, :, tsl], ds[4 + b], writes=[r_rot[b]])
            rmsnorm_tile2(K, ht[b], r_ht[b], gain, r_gain, uT[b], r_uT[b], sq, r_sq, rstd, r_rstd, tmp, r_tmp, 7)
            yield
            pbanks = [(K.bank[1], K.rbank[1]), (K.bank[7], K.rbank[7])]
            for bl in range(4):
                bk, rbk = pbanks[bl % 2]
                for kc in range(KC):
                    _mm(P, bk[:], uT[b][:, kc, bl * 128:(bl + 1) * 128], w[:, kc, VOFF:VOFF + 512], kc == 0, kc == KC - 1,
                        [r_w, r_uT[b]], [rbk])
                P.op("act", lambda e, bl=bl, b=b, bk=bk: e.copy(out=vtok[b][:, bl, :], in_=bk[:]), reads=[rbk],
                     writes=[r_vtok[b]])
                yield
            for h in range(4):
                bk, rbk = pbanks[h % 2]
                proj_fm(bk, rbk, b, GOFF + h * 128)
                P.op("act", lambda e, h=h, b=b, bk=bk: e.activation(out=sgt[b][:, h, :], in_=bk[:], func=AF.Silu),
                     reads=[rbk], writes=[r_sgt[b]])
                yield

        def stageA1(t, h, i):
            b = t % 2
            a = i % NSET
            qf, kf, lf = qfl[a], kfl[a], lfl[i % 2]
            r_qf, r_kf, r_lf = r_qfl[a], r_kfl[a], r_lfl[i % 2]
            if ret:
                if h % 2 == 1:
                    return
                a1 = (i + 1) % NSET
                pair = h // 2
                for which, d0, rd0, d1, rd1 in ((0, qf, r_qf, qfl[a1], r_qfl[a1]), (1, kf, r_kf, kfl[a1], r_kfl[a1])):
                    proj_fm(pb, rpb, b, which * 256 + pair * 128)
                    P.op("dve", lambda e, b=b: e.tensor_tensor(out=e1[:], in0=pb[:], in1=rot[b][:, 0, :], op=ALU.mult),
                         reads=[rpb, r_rot[b]], writes=[r_e1])
                    P.op("dve", lambda e, b=b: e.tensor_tensor(out=cum[:], in0=pb[:], in1=rot[b][:, 2, :], op=ALU.mult),
                         reads=[rpb, r_rot[b]], writes=[r_cum])
                    yield
                    proj_fm(pb, rpb, b, 1536 + which * 256 + pair * 128)
                    P.op("dve", lambda e, b=b, d0=d0: e.tensor_tensor(out=d0[:], in0=pb[:], in1=rot[b][:, 1, :], op=ALU.mult),
                         reads=[rpb, r_rot[b]], writes=[rd0])
                    P.op("dve", lambda e, b=b, d1=d1: e.tensor_tensor(out=d1[:], in0=pb[:], in1=rot[b][:, 3, :], op=ALU.mult),
                         reads=[rpb, r_rot[b]], writes=[rd1])
                    P.op("pool", lambda e, d0=d0: e.tensor_tensor(out=d0[:], in0=d0[:], in1=e1[:], op=ALU.add),
                         reads=[r_e1, rd0], writes=[rd0])
                    P.op("pool", lambda e, d1=d1: e.tensor_tensor(out=d1[:], in0=d1[:], in1=cum[:], op=ALU.add),
                         reads=[r_cum, rd1], writes=[rd1])
                    yield
            else:
                proj_fm(pb, rpb, b, h * 128)
                P.op("act", lambda e: e.copy(out=qf[:], in_=pb[:]), reads=[rpb], writes=[r_qf])
                proj_fm(pb, rpb, b, 512 + h * 128)
                P.op("act", lambda e: e.activation(out=lf[:], in_=pb[:], func=AF.Exp, scale=-1.0), reads=[rpb], writes=[r_lf])
                yield
                P.op("act", lambda e: e.activation(out=lf[:], in_=lf[:], func=AF.Ln, bias=1.0), reads=[r_lf], writes=[r_lf])
                yield
                P.op("act", lambda e: e.activation(out=lf[:], in_=lf[:], func=AF.Exp, scale=-1.0), reads=[r_lf], writes=[r_lf])
                yield
                P.op("dve", lambda e: e.tensor_scalar(out=lf[:], in0=lf[:], scalar1=vec[:, 4 + h:5 + h],
                                                      scalar2=vec[:, h:h + 1], op0=ALU.mult, op1=ALU.add),
                     reads=[r_lf, r_vec], writes=[r_lf])
                yield
                P.op("pool", lambda e: e.tensor_scalar(out=kf[:], in0=lf[:], scalar1=-1.0, scalar2=1.0,
                                                       op0=ALU.mult, op1=ALU.add), reads=[r_lf], writes=[r_kf])
                P.op("act", lambda e: e.activation(out=lf[:], in_=lf[:], func=AF.Ln), reads=[r_lf], writes=[r_lf])
                yield

        def stageA2(t, h, i):
            b = t % 2
            a = i % NSET
            s = i % 2
            qf, kf, lf = qfl[a], kfl[a], lfl[i % 2]
            r_qf, r_kf, r_lf = r_qfl[a], r_kfl[a], r_lfl[i % 2]
            if ret:
                def tabv(k):
                    return rtab[:, h, k:k + 1, :].to_broadcast([128, 8, 64])

                def v3(x):
                    return x.rearrange("p (c s) -> p c s", s=64)
                P.op("pool", lambda e: e.tensor_tensor(out=v3(qh[s][:]), in0=v3(qf[:]), in1=tabv(0), op=ALU.mult),
                     reads=[r_qf, r_rtab], writes=[r_qh[s]])
                P.op("dve", lambda e: e.scalar_tensor_tensor(out=v3(kh[s][:]), in0=v3(kf[:]), scalar=0.125, in1=tabv(1),
                                                             op0=ALU.mult, op1=ALU.mult), reads=[r_kf, r_rtab], writes=[r_kh[s]])
                yield
                P.op("pool", lambda e: e.tensor_tensor(out=v3(qi[s][:]), in0=v3(qf[:]), in1=tabv(2), op=ALU.mult),
                     reads=[r_qf, r_rtab], writes=[r_qi[s]])
                P.op("dve", lambda e: e.scalar_tensor_tensor(out=v3(ko[s][:]), in0=v3(kf[:]), scalar=0.125, in1=tabv(3),
                                                             op0=ALU.mult, op1=ALU.mult), reads=[r_kf, r_rtab], writes=[r_ko[s]])
                al, r_al = ral[:, h, :], r_ral
                yield
            else:
                P.op("dve", lambda e: e.tensor_tensor_scan(out=cum[:], data0=rmask[:], data1=lf[:], initial=0.0,
                                                           op0=ALU.mult, op1=ALU.add),
                     reads=[r_rmask, r_lf], writes=[r_cum])
                yield

                def o_qh(ap, r):
                    P.op("pool", lambda e: e.tensor_tensor(out=qh[s][:], in0=qf[:], in1=ap, op=ALU.mult),
                         reads=[r_qf, r], writes=[r_qh[s]])

                def o_kh(ap, r):
                    P.op("dve", lambda e: e.tensor_tensor(out=kh[s][:], in0=kf[:], in1=ap, op=ALU.mult),
                         reads=[r_kf, r], writes=[r_kh[s]])

                def o_qi(ap, r):
                    P.op("pool", lambda e: e.tensor_tensor(out=qi[s][:], in0=qf[:], in1=ap, op=ALU.mult),
                         reads=[r_qf, r], writes=[r_qi[s]])

                def o_ko(ap, r):
                    P.op("dve", lambda e: e.tensor_tensor(out=ko[s][:], in0=kf[:], in1=ap, op=ALU.mult),
                         reads=[r_kf, r], writes=[r_ko[s]])
                yield from decay_factors(cum[:].rearrange("p (c s) -> p c s", s=64), 8, r_cum, [o_qh, o_kh, o_qi, o_ko],
                                         alast[s][:].rearrange("p (c o) -> p c o", o=1), r_alast[s])
                al, r_al = alast[s][:], r_alast[s]
            K._al[(t, h)] = (al, r_al)
            for bl in range(4):
                P.op("pe", lambda e, bl=bl: e.transpose(out=tb[:, bl * 128:(bl + 1) * 128],
                                                        in_=ko[s][:, bl * 128:(bl + 1) * 128], identity=K.ident_bf[:]),
                     reads=[r_ko[s], K.r_const], writes=[rpb])
            P.op("act", lambda e: e.copy(out=kotok[s][:].rearrange("p b d -> p (b d)"), in_=tb[:, 0:512]),
                 reads=[rpb], writes=[r_kotok[s]])
            yield

        def stageBC(t, h, s):
            b = t % 2
            yb = t % 2
            hs = slice(h * 128, (h + 1) * 128)
            al, r_al = K._al[(t, h)]
            ob, rob = K.bank[2 + (h % 2)], K.rbank[2 + (h % 2)]
            for c in range(8):
                bl, p0 = c // 2, (c % 2) * 64
                dbk, rdb = K.bank[4 + (c % 2)], K.rbank[4 + (c % 2)]
                _mm(P, dbk[:, (c // 2) * 128:(c // 2 + 1) * 128], kotok[s][p0:p0 + 64, bl, :], vtok[b][p0:p0 + 64, bl, hs],
                    True, True, [r_kotok[s], r_vtok[b]], [rdb])
            yield
            for c in range(8):
                p0 = (c % 2) * 64
                csl = slice(c * 64, (c + 1) * 64)
                _mm(P, sbk[p0:p0 + 64, (c // 2) * 64:(c // 2 + 1) * 64], kh[s][:, csl], qh[s][:, csl], True, True,
                    [r_kh[s], r_qh[s]], [rsb])
            P.op("dve", lambda e: e.tensor_tensor(out=PT[:], in0=sbk[:, 0:256].rearrange("p (c s) -> p c s", s=64),
                                                  in1=m2[:].unsqueeze(1).to_broadcast([128, 4, 64]), op=ALU.mult),
                 reads=[rsb, r_m2], writes=[r_PT])
            yield
            for c in range(8):
                dbk, rdb = K.bank[4 + (c % 2)], K.rbank[4 + (c % 2)]
                src, dst = (c + 1) % 2, c % 2
                P.op("dve", lambda e, c=c, src=src, dst=dst, dbk=dbk: e.scalar_tensor_tensor(
                    out=S_f[:, h, dst, :], in0=S_f[:, h, src, :], scalar=al[:, c:c + 1],
                    in1=dbk[:, (c // 2) * 128:(c // 2 + 1) * 128], op0=ALU.mult, op1=ALU.add),
                    reads=[r_Sf[h][src], r_al, rdb], writes=[r_Sf[h][dst]])
                if c < 7:
                    P.op("pool", lambda e, c=c, dst=dst: e.tensor_copy(out=S_b[:, h, c + 1, :], in_=S_f[:, h, dst, :]),
                         reads=[r_Sf[h][dst]], writes=[r_Sb[h][c + 1]])
                if c % 2 == 1:
                    yield
            for c in range(8):
                bl, p0 = c // 2, (c % 2) * 64
                csl = slice(c * 64, (c + 1) * 64)
                _mm(P, ob[:, csl], vtok[b][p0:p0 + 64, bl, hs], PT[p0:p0 + 64, c // 2, :], True, False,
                    [r_vtok[b], r_PT], [rob])
                _mm(P, ob[:, csl], S_b[:, h, c, :], qi[s][:, csl], False, True, [r_Sb[h][c], r_qi[s]], [rob])
                if c % 2 == 1:
                    yield
            P.op("pool", lambda e: e.tensor_copy(out=S_b[:, h, 0, :], in_=S_f[:, h, 1, :]),
                 reads=[r_Sf[h][1]], writes=[r_Sb[h][0]])
            nb_, rnb = K.bank[7], K.rbank[7]
            if ret:
                P.op("act", lambda e: e.copy(out=sqo[:], in_=ob[:]), reads=[rob], writes=[r_sqo])
                _mm(P, nb_[:], meanb[:], sqo[:], True, True, [r_meanb, r_sqo], [rnb])
                P.op("act", lambda e: e.copy(out=tmp2[:], in_=nb_[:]), reads=[rnb], writes=[r_tmp2])
                yield
                P.op("dve", lambda e: e.tensor_tensor(out=cen[:], in0=ob[:], in1=tmp2[:], op=ALU.subtract),
                     reads=[rob, r_tmp2], writes=[r_cen])
                P.op("act", lambda e: e.activation(out=sqo[:], in_=cen[:], func=AF.Square), reads=[r_cen], writes=[r_sqo])
                osrc, r_osrc, nscale = cen[:], r_cen, 1.0
                _mm(P, nb_[:], meanb[:], sqo[:], True, True, [r_meanb, r_sqo], [rnb])
                nwcol = vec[:, 4 + h:5 + h]
            else:
                P.op("act", lambda e: e.activation(out=sqo[:], in_=ob[:], func=AF.Square), reads=[rob], writes=[r_sqo])
                osrc, r_osrc, nscale = ob[:], rob, 1.0 / 128
                _mm(P, nb_[:], K.ones_bf[:], sqo[:], True, True, [K.r_const, r_sqo], [rnb])
                nwcol = vec[:, 8:9]
            yield
            act_rstd(P, rstd2[:], r_rstd2, nb_[:], rnb, nscale, tmp2[:], r_tmp2)
            P.op("dve", lambda e: e.scalar_tensor_tensor(out=cn[:], in0=osrc, scalar=nwcol, in1=rstd2[:], op0=ALU.mult,
                                                         op1=ALU.mult), reads=[r_osrc, r_vec, r_rstd2], writes=[r_cn])
            P.op("pool", lambda e: e.tensor_tensor(out=yTt[yb][:, h, :], in0=cn[:], in1=sgt[b][:, h, :], op=ALU.mult),
                 reads=[r_cn, r_sgt[b]], writes=[r_yTt[yb]])
            yield
            if h == 3:
                tsl = slice(t * 512, (t + 1) * 512)
                dst = K.ybuf.rearrange("(c p) t -> p c t", p=128)[:, 0:4, tsl]
                P.dma("sp", dst, yTt[yb][:], ds[7 + yb], reads=[r_yTt[yb]], writes=[K.r_y[c][t] for c in range(4)])

        K._al = {}
        items = [(t, h) for t in range(NT) for h in range(4)]
        n_items = len(items)

        def run_all(g):
            for _ in g:
                pass
        run_all(tile_prologue(0))
        run_all(stageA1(items[0][0], items[0][1], 0))
        run_all(stageA2(items[0][0], items[0][1], 0))
        if n_items > 1:
            run_all(stageA1(items[1][0], items[1][1], 1))
        bg = None
        for i, (t, h) in enumerate(items):
            if h == 0 and t + 1 < NT:
                bg = tile_prologue(t + 1)
            if h == 2 and bg is not None:
                run_all(bg)
                bg = None
            gens = [stageBC(t, h, i % 2)]
            if i + 1 < n_items:
                gens.append(stageA2(items[i + 1][0], items[i + 1][1], i + 1))
            if i + 2 < n_items:
                gens.append(stageA1(items[i + 2][0], items[i + 2][1], i + 2))
            while gens:
                for g in list(gens):
                    try:
                        next(g)
                    except StopIteration:
                        gens.remove(g)
                if bg is not None:
                    try:
                        next(bg)
                    except StopIteration:
                        bg = None
        P.emit_block(final=final)


def ssd_phase(K, layer, hin, final=False):
    nc, P, T, NT, dr = K.nc, K.P, K.T, K.NT, K.dr
    j = layer // 2
    ds = K.dsem_pool
    with ExitStack() as es:
        def sb(name, shape, dt):
            return es.enter_context(nc.sbuf_tensor(f"s{layer}_{name}", shape, dt))
        w = sb("w", [128, KC, 1288], BF16); r_w = Res()
        gain = sb("gain", [128, KC], F32); r_gain = Res()
        rows = sb("rows", [128, 536], F32); r_rows = Res()
        cw = sb("cw", [128, 6, 6], F32); r_cw = Res()
        smask = sb("smask", [128, 128], F32); r_smask = Res()
        onesF = sb("onesF", [128, 128], F32); r_onesF = Res()
        negA = sb("negA", [128, 8], F32); r_negA = Res()
        dg = sb("dg", [128, 6, 4, 128], BF16); r_dg = Res()
        dsk = sb("dsk", [128, 4, 128], BF16); r_dsk = Res()
        ht = [sb(f"ht{i}", [128, KC, 512], F32) for i in range(2)]; r_ht = RL(2)
        uT = [sb(f"uT{i}", [128, KC, 512], BF16) for i in range(2)]; r_uT = RL(2)
        sq = sb("sq", [128, KC, 512], BF16); r_sq = Res()
        rstd = sb("rstd", [128, 512], F32); r_rstd = Res()
        tmp = sb("tmp", [128, 512], F32); r_tmp = Res()
        xpad = sb("xpad", [128, 6, 516], BF16); r_xpad = RL(6)
        xc = [sb(f"xc{i}", [128, 6, 512], BF16) for i in range(2)]; r_xc = [RL(6) for _ in range(2)]
        vtok = sb("vtok", [128, 512], BF16); r_vtok = Res()
        btok = [sb(f"btok{i}", [128, 128], BF16) for i in range(2)]; r_btok = RL(2)
        sz = [sb(f"sz{i}", [128, 512], F32) for i in range(2)]; r_sz = RL(2)
        sm = [sb(f"sm{i}", [128, 8], F32) for i in range(7)]; r_sm = RL(7)
        ecum = [sb(f"ecum{i}", [128, 8], F32) for i in range(2)]; r_ecum = RL(2)
        cumt = [sb(f"cumt{i}", [128, 8], F32) for i in range(2)]; r_cumt = RL(2)
        sm16 = sb("sm16", [128, 16], F32); r_sm16 = Res()
        vp = sb("vp", [128, 512], BF16); r_vp = Res()
        vpp = [sb(f"vpp{i}", [128, 512], BF16) for i in range(2)]; r_vpp = RL(2)
        LM = sb("LM", [128, 8, 128], F32); r_LM = Res()
        E = sb("E", [128, 8, 128], F32); r_E = RL(2)
        GM = sb("GM", [128, 2, 128], F32); r_GM = Res()
        PT = sb("PT", [128, 8, 128], BF16); r_PT = RL(2)
        o1 = sb("o1", [128, 512], F32); r_o1 = Res()
        o2 = sb("o2", [128, 512], F32); r_o2 = Res()
        o3 = sb("o3", [128, 512], F32); r_o3 = Res()
        ss = sb("ss", [128, 2], F32); r_ss = Res()
        ss2 = sb("ss2", [128, 2], F32); r_ss2 = Res()
        S_f = sb("S_f", [128, 512], F32); r_Sf = Res()
        S_b = sb("S_b", [128, 512], BF16); r_Sb = Res()
        ytok = sb("ytok", [128, 512], BF16); r_ytok = Res()
        yTt = [sb(f"yTt{i}", [128, 4, 512], BF16) for i in range(2)]; r_yTt = RL(2)

        P.dma("pool", w[:, :, 512:1288], dr["w_ssd"][j][:, :, 512:1288], K.dsem_q[0], writes=[r_w])
        P.dma("pool", w[:, :, 0:512], dr["w_ssd"][j][:, :, 0:512], K.dsem_q[0], writes=[r_w])
        P.dma("sp", gain[:], dr["norm_mix"][layer], ds[1], writes=[r_gain])
        P.dma("sp", rows[:], dr["ssd_rows"][j], ds[30], writes=[r_rows])
        P.dma("sp", cw[:], dr["ssd_conv"][j], ds[31], writes=[r_cw])
        P.dma("sp", smask[:], dr["smask"], ds[32], writes=[r_smask])
        P.op("pool", lambda e: e.memset(onesF[:], 1.0), writes=[r_onesF])
        P.op("pool", lambda e: e.memset(S_f[:], 0.0), writes=[r_Sf])
        P.op("pool", lambda e: e.memset(S_b[:], 0.0), writes=[r_Sb])
        P.op("pool", lambda e: e.memset(xpad[:], 0.0), writes=r_xpad)
        P.op("act", lambda e: e.activation(out=negA[:], in_=rows[:, 8:16], func=AF.Exp), reads=[r_rows], writes=[r_negA])
        P.op("dve", lambda e: e.tensor_scalar(out=negA[:], in0=negA[:], scalar1=-1.0, scalar2=None, op0=ALU.mult),
             reads=[r_negA], writes=[r_negA])
        for cc in range(6):
            for k in range(4):
                P.op("dve", lambda e, cc=cc, k=k: e.tensor_scalar(out=dg[:, cc, k, :], in0=K.ident_f[:], scalar1=cw[:, cc, k:k + 1],
                                                                 scalar2=None, op0=ALU.mult),
                     reads=[K.r_const, r_cw, r_dg], writes=[r_dg])
        for cc in range(4):
            P.op("dve", lambda e, cc=cc: e.tensor_scalar(out=dsk[:, cc, :], in0=K.ident_f[:], scalar1=cw[:, cc, 5:6],
                                                        scalar2=None, op0=ALU.mult),
                 reads=[K.r_const, r_cw, r_dsk], writes=[r_dsk])

        pb, rpb = K.bank[6], K.rbank[6]
        tb = pb[:].bitcast(BF16)
        xb, rxb = K.bank[7], K.rbank[7]
        r_b2lo, r_b2hi = Res(), Res()
        b2 = K.bank[2]
        qb5, rqb5 = K.bank[5], K.rbank[5]
        tb5 = qb5[:].bitcast(BF16)

        def stageX(t):
            b = t % 2
            if t == 0:
                P.dma("sp", ht[0][:], hview(hin, 0), ds[2], reads=[K.r_h[c][0] for c in range(KC)], writes=[r_ht[0]])
            if t + 1 < NT:
                P.dma("sp", ht[1 - b][:], hview(hin, t + 1), ds[2 + (1 - b)],
                      reads=[K.r_h[c][t + 1] for c in range(KC)], writes=[r_ht[1 - b]])
            rmsnorm_tile2(K, ht[b], r_ht[b], gain, r_gain, uT[b], r_uT[b], sq, r_sq, rstd, r_rstd, tmp, r_tmp, 7)
            yield
            for cc in range(6):
                c0 = 512 + cc * 128
                for kc in range(KC):
                    _mm(P, xb[:], w[:, kc, c0:c0 + 128], uT[b][:, kc, :], kc == 0, kc == KC - 1, [r_w, r_uT[b]], [rxb])
                P.op("act", lambda e, cc=cc: e.copy(out=xpad[:, cc, 3:515], in_=xb[:]), reads=[rxb], writes=[r_xpad[cc]])
                yield
                for k in range(4):
                    _mm(P, xb[:], dg[:, cc, k, :], xpad[:, cc, k:k + 512], k == 0, k == 3, [r_dg, r_xpad[cc]], [rxb])
                P.op("act", lambda e, cc=cc, b=b: e.activation(out=xc[b][:, cc, :], in_=xb[:], func=AF.Silu, bias=cw[:, cc, 4:5]),
                     reads=[rxb, r_cw], writes=[r_xc[b][cc]])
                P.op("pool", lambda e, cc=cc: e.tensor_copy(out=xpad[:, cc, 0:3], in_=xpad[:, cc, 512:515]),
                     reads=[r_xpad[cc]], writes=[r_xpad[cc]])
                yield

        def stageP(t, bl):
            b = t % 2
            g_ = t * 4 + bl
            p = g_ % 2
            bsl = slice(bl * 128, (bl + 1) * 128)
            xcb, rxc = xc[b], r_xc[b]
            for cc in range(4):
                P.op("pe", lambda e, cc=cc: e.transpose(out=tb[:, cc * 128:(cc + 1) * 128], in_=xcb[:, cc, bsl],
                                                        identity=K.ident_bf[:]),
                     reads=[rxc[cc], K.r_const], writes=[rpb])
            P.op("act", lambda e: e.copy(out=vtok[:], in_=tb[:, 0:512]), reads=[rpb], writes=[r_vtok])
            P.op("pe", lambda e: e.transpose(out=tb[:, 0:128], in_=xcb[:, 4, bsl], identity=K.ident_bf[:]),
                 reads=[rxc[4], K.r_const], writes=[rpb])
            P.op("act", lambda e: e.copy(out=btok[p][:], in_=tb[:, 0:128]), reads=[rpb], writes=[r_btok[p]])
            yield
            for kc in range(KC):
                _mm(P, pb[:, 0:8], uT[b][:, kc, bsl], w[:, kc, 1280:1288], kc == 0, kc == KC - 1, [r_w, r_uT[b]], [rpb])
            x_, ax, dt_, l_, wdec, _u1, _u2 = sm
            P.op("dve", lambda e: e.tensor_tensor(out=x_[:], in0=pb[:, 0:8], in1=rows[:, 0:8], op=ALU.add),
                 reads=[rpb, r_rows], writes=[r_sm[0]])
            P.op("dve", lambda e: e.scalar_tensor_tensor(out=ax[:], in0=x_[:], scalar=-1.0, in1=x_[:], op0=ALU.mult,
                                                         op1=ALU.min), reads=[r_sm[0]], writes=[r_sm[1]])
            P.op("act", lambda e: e.activation(out=ax[:], in_=ax[:], func=AF.Exp), reads=[r_sm[1]], writes=[r_sm[1]])
            P.op("act", lambda e: e.activation(out=ax[:], in_=ax[:], func=AF.Ln, bias=1.0), reads=[r_sm[1]], writes=[r_sm[1]])
            yield
            P.op("dve", lambda e: e.scalar_tensor_tensor(out=dt_[:], in0=x_[:], scalar=0.0, in1=ax[:], op0=ALU.max,
                                                         op1=ALU.add), reads=[r_sm[0], r_sm[1]], writes=[r_sm[2]])
            P.op("dve", lambda e: e.tensor_tensor(out=l_[:], in0=dt_[:], in1=negA[:], op=ALU.mult),
                 reads=[r_sm[2], r_negA], writes=[r_sm[3]])
            _mm(P, pb[:, 16:24], K.cmask_f[:], l_[:], True, True, [K.r_const, r_sm[3]], [rpb])
            _mm(P, pb[:, 24:32], onesF[:], l_[:], True, True, [r_onesF, r_sm[3]], [rpb])
            P.op("act", lambda e: e.copy(out=sm16[:], in_=pb[:, 16:32]), reads=[rpb], writes=[r_sm16])
            P.op("pool", lambda e: e.tensor_tensor(out=LM[:], in0=smask[:].unsqueeze(1).to_broadcast([128, 8, 128]),
                                                   in1=l_[:].unsqueeze(2).to_broadcast([128, 8, 128]), op=ALU.mult),
                 reads=[r_smask, r_sm[3], r_LM], writes=[r_LM])
            yield
            for kc in range(KC):
                _mm(P, pb[:], uT[b][:, kc, bsl], w[:, kc, 0:512], kc == 0, kc == KC - 1, [r_w, r_uT[b]], [rpb])
            P.op("act", lambda e: e.activation(out=sz[p][:], in_=pb[:], func=AF.Silu), reads=[rpb], writes=[r_sz[p]])
            yield
            P.op("act", lambda e: e.activation(out=ecum[p][:], in_=sm16[:, 0:8], func=AF.Exp), reads=[r_sm16], writes=[r_ecum[p]])
            P.op("dve", lambda e: e.tensor_tensor(out=wdec[:], in0=sm16[:, 8:16], in1=sm16[:, 0:8], op=ALU.subtract),
                 reads=[r_sm16], writes=[r_sm[4]])
            P.op("act", lambda e: e.activation(out=wdec[:], in_=wdec[:], func=AF.Exp), reads=[r_sm[4]], writes=[r_sm[4]])
            P.op("act", lambda e: e.activation(out=cumt[p][:], in_=sm16[:, 8:16], func=AF.Exp), reads=[r_sm16], writes=[r_cumt[p]])
            P.op("dve", lambda e: e.tensor_tensor(out=vp[:].rearrange("p (h d) -> p h d", h=8),
                                                  in0=vtok[:].rearrange("p (h d) -> p h d", h=8),
                                                  in1=dt_[:].unsqueeze(2).to_broadcast([128, 8, 64]), op=ALU.mult),
                 reads=[r_vtok, r_sm[2]], writes=[r_vp])
            P.op("pool", lambda e: e.tensor_tensor(out=vpp[p][:].rearrange("p (h d) -> p h d", h=8),
                                                   in0=vp[:].rearrange("p (h d) -> p h d", h=8),
                                                   in1=wdec[:].unsqueeze(2).to_broadcast([128, 8, 64]), op=ALU.mult),
                 reads=[r_vp, r_sm[4]], writes=[r_vpp[p]])
            yield
            _mm(P, b2[:, 0:128], xcb[0:64, 4, bsl], xcb[0:64, 5, bsl], True, True, [rxc[4], rxc[5]], [r_b2lo])
            for hh in range(2):
                db, rdb = K.bank[hh], K.rbank[hh]
                for h4 in range(4):
                    h = hh * 4 + h4
                    _mm(P, db[:, h4 * 128:(h4 + 1) * 128], LM[:, h, :], K.cmask_f[:], True, True, [r_LM, K.r_const], [rdb])
                if hh == 0:
                    _mm(P, b2[:, 128:256], xcb[64:128, 4, bsl], xcb[64:128, 5, bsl], True, True, [rxc[4], rxc[5]], [r_b2lo])
                P.op("act", lambda e, hh=hh, db=db: e.activation(out=E[:, hh * 4:(hh + 1) * 4, :].rearrange("p h i -> p (h i)"),
                                                               in_=db[:], func=AF.Exp), reads=[rdb], writes=[r_E[hh]])
                yield
            P.op("dve", lambda e: e.tensor_tensor(out=GM[:], in0=b2[:, 0:256].rearrange("p (g i) -> p g i", g=2),
                                                  in1=K.cmask_f[:].unsqueeze(1).to_broadcast([128, 2, 128]), op=ALU.mult),
                 reads=[r_b2lo, K.r_const], writes=[r_GM])
            for g in range(2):
                P.op("dve" if g == 0 else "pool", lambda e, g=g: e.tensor_tensor(
                    out=PT[:, g * 4:(g + 1) * 4, :], in0=E[:, g * 4:(g + 1) * 4, :],
                    in1=GM[:, g:g + 1, :].to_broadcast([128, 4, 128]), op=ALU.mult),
                    reads=[r_E[g], r_GM], writes=[r_PT[g]])
            yield
            ab, rab = K.bank[3 + p], K.rbank[3 + p]
            for cc in range(4):
                _mm(P, ab[:, cc * 128:(cc + 1) * 128], xcb[:, cc, bsl], dsk[:, cc, :], cc == 0, False, [rxc[cc], r_dsk], [rab],
                    skip_group_check=True)
            for h in range(8):
                _mm(P, ab[:, h * 64:(h + 1) * 64], PT[:, h, :], vp[:, h * 64:(h + 1) * 64], False, h == 7,
                    [r_PT[h // 4], r_vp], [rab], skip_group_check=True)
            yield

        def stageQ(t, bl):
            b = t % 2
            yb = t % 2
            g_ = t * 4 + bl
            p = g_ % 2
            bsl = slice(bl * 128, (bl + 1) * 128)
            xcb, rxc = xc[b], r_xc[b]
            ab, rab = K.bank[3 + p], K.rbank[3 + p]
            _mm(P, qb5[:], xcb[:, 5, bsl], S_b[:], True, True, [rxc[5], r_Sb], [rqb5])
            P.op("dve", lambda e: e.tensor_tensor(
                out=o1[:].rearrange("p (h d) -> p h d", h=8), in0=qb5[:].rearrange("p (h d) -> p h d", h=8),
                in1=ecum[p][:].unsqueeze(2).to_broadcast([128, 8, 64]), op=ALU.mult),
                reads=[rqb5, r_ecum[p]], writes=[r_o1])
            yield
            P.op("dve", lambda e: e.tensor_tensor(out=o1[:], in0=o1[:], in1=ab[:], op=ALU.add),
                 reads=[r_o1, rab], writes=[r_o1])
            _mm(P, qb5[:], btok[p][:], vpp[p][:], True, True, [r_btok[p], r_vpp[p]], [rqb5])
            for g in range(2):
                gs = slice(g * 64, (g + 1) * 64)
                sv = S_f[gs, g * 256:(g + 1) * 256].rearrange("p (h d) -> p h d", h=4)
                P.op("dve", lambda e, g=g, gs=gs, sv=sv: e.tensor_tensor(
                    out=sv, in0=sv, in1=cumt[p][gs, g * 4:(g + 1) * 4].unsqueeze(2).to_broadcast([64, 4, 64]), op=ALU.mult),
                    reads=[r_Sf, r_cumt[p]], writes=[r_Sf])
                P.op("dve", lambda e, g=g, gs=gs: e.tensor_tensor(
                    out=S_f[gs, g * 256:(g + 1) * 256], in0=S_f[gs, g * 256:(g + 1) * 256],
                    in1=qb5[gs, g * 256:(g + 1) * 256], op=ALU.add), reads=[r_Sf, rqb5], writes=[r_Sf])
            P.op("pool", lambda e: e.tensor_copy(out=S_b[:], in_=S_f[:]), reads=[r_Sf], writes=[r_Sb])
            yield
            P.op("dve", lambda e: e.tensor_tensor(out=o2[:], in0=o1[:], in1=sz[p][:], op=ALU.mult),
                 reads=[r_o1, r_sz[p]], writes=[r_o2])
            P.op("pool", lambda e: e.tensor_tensor(out=o3[:], in0=o2[:], in1=o2[:], op=ALU.mult),
                 reads=[r_o2], writes=[r_o3])
            yield
            P.op("dve", lambda e: e.tensor_reduce(out=ss[:], in_=o3[:].rearrange("p (g d) -> p g d", g=2), axis=AX.X,
                                                  op=ALU.add), reads=[r_o3], writes=[r_ss])
            act_rstd(P, ss[:], r_ss, ss[:], r_ss, 1.0 / 256, ss2[:], r_ss2)
            yield
            P.op("dve", lambda e: e.tensor_tensor(out=o2[:].rearrange("p (g d) -> p g d", g=2),
                                                  in0=o2[:].rearrange("p (g d) -> p g d", g=2),
                                                  in1=ss[:].unsqueeze(2).to_broadcast([128, 2, 256]), op=ALU.mult),
                 reads=[r_o2, r_ss], writes=[r_o2])
            P.op("pool", lambda e: e.tensor_tensor(out=ytok[:], in0=o2[:], in1=rows[:, 24:536], op=ALU.mult),
                 reads=[r_o2, r_rows], writes=[r_ytok])
            yield
            for cc in range(4):
                P.op("pe", lambda e, cc=cc: e.transpose(out=tb5[:, cc * 128:(cc + 1) * 128],
                                                        in_=ytok[:, cc * 128:(cc + 1) * 128], identity=K.ident_bf[:]),
                     reads=[r_ytok, K.r_const], writes=[rqb5])
            P.op("act", lambda e: e.copy(out=yTt[yb][:, :, bsl], in_=tb5[:, 0:512].rearrange("p (c s) -> p c s", c=4)),
                 reads=[rqb5], writes=[r_yTt[yb]])
            yield
            if bl == 3:
                tsl = slice(t * 512, (t + 1) * 512)
                dst = K.ybuf.rearrange("(c p) t -> p c t", p=128)[:, 4:8, tsl]
                P.dma("sp", dst, yTt[yb][:], ds[7 + yb], reads=[r_yTt[yb]], writes=[K.r_y[c][t] for c in range(4, 8)])

        def run_all(g):
            for _ in g:
                pass
        items = [(t, bl) for t in range(NT) for bl in range(4)]
        n_items = len(items)
        run_all(stageX(0))
        run_all(stageP(*items[0]))
        bg = None
        for i, (t, bl) in enumerate(items):
            if bl == 0 and t + 1 < NT:
                bg = stageX(t + 1)
            if bl == 3 and bg is not None:
                run_all(bg)
                bg = None
            gens = [stageQ(t, bl)]
            if i + 1 < n_items:
                gens.append(stageP(*items[i + 1]))
            while gens:
                for g in list(gens):
                    try:
                        next(g)
                    except StopIteration:
                        gens.remove(g)
                if bg is not None:
                    try:
                        next(bg)
                    except StopIteration:
                        bg = None
        P.emit_block(final=final)


ALL_PHASES = ("ret", "ssd", "hg", "fox", "ffn")


def _wl(W):
    return np.ascontiguousarray(W.reshape(W.shape[0], KC, 128, W.shape[2]).transpose(0, 2, 1, 3))


def _vec(v):
    return np.ascontiguousarray(v.reshape(v.shape[0], KC, 128).transpose(0, 2, 1))


def prepare_inputs(T, norm_mix, norm_ffn, ffn_w_gate, ffn_w_up, ffn_w_down, ab_w_in, ab_w_out, ret_gn_w, ssd_conv_w,
                   ssd_conv_b, ssd_dt_bias, ssd_a_log, ssd_d, ssd_norm_w, cd_w_in, cd_w_out, hg_lb_logits, hg_norm_w,
                   fox_f_bias, fox_q_norm_w, fox_k_norm_w):
    f32 = np.float32
    A = lambda a: np.asarray(a, dtype=f32)
    norm_mix, norm_ffn = A(norm_mix), A(norm_ffn)
    wg, wu, wd = A(ffn_w_gate), A(ffn_w_up), A(ffn_w_down)
    ab_in, ab_out, cd_in, cd_out = A(ab_w_in), A(ab_w_out), A(cd_w_in), A(cd_w_out)
    sh = {}
    sh["norm_mix"], sh["norm_ffn"] = _vec(norm_mix), _vec(norm_ffn)
    sh["wg"] = np.ascontiguousarray(wg.reshape(4, KC, 128, NJ, 128).transpose(0, 3, 2, 1, 4))
    sh["wu"] = np.ascontiguousarray(wu.reshape(4, KC, 128, NJ, 128).transpose(0, 3, 2, 1, 4))
    sh["wd"] = np.ascontiguousarray(wd.reshape(4, NJ, 128, KC, 128).transpose(0, 3, 2, 1, 4))
    w_out = np.stack([ab_out[0], cd_out[0], ab_out[1], cd_out[1]])
    sh["w_out"] = _wl(w_out)
    sh["w_fox"] = _wl(cd_in[:, :, 2048:3588])
    fv = np.zeros((2, 128, 3), f32)
    fv[:, :, 0] = A(fox_q_norm_w); fv[:, :, 1] = A(fox_k_norm_w); fv[:, 0:4, 2] = A(fox_f_bias)
    sh["fox_vec"] = fv
    sh["cmask"] = np.triu(np.ones((128, 128), f32))
    sh["smask"] = np.tril(np.ones((128, 128), f32), -1)
    sh["ident"] = np.eye(128, dtype=f32)
    sh["rmask"] = np.tile((np.arange(512) % 64 != 0).astype(f32), (128, 1))
    sh["w_hg"] = _wl(cd_in[:, :, 0:2048])
    hv = np.zeros((2, 128, 12), f32)
    lbl = A(hg_lb_logits)
    for jj in range(2):
        hv[jj, :, 0:4] = lbl[0].reshape(4, 128).T
        hv[jj, :, 4:8] = lbl[1].reshape(4, 128).T
        hv[jj, :, 8] = A(hg_norm_w)[jj]
    sh["hg_vec"] = hv

    def swp(Wx):
        return np.ascontiguousarray(Wx.reshape(D, 4, 32, 2)[:, :, :, ::-1].reshape(D, 256))
    wr = []
    for jj in range(2):
        Wq, Wk = ab_in[jj][:, 0:256], ab_in[jj][:, 256:512]
        wr.append(np.concatenate([Wq, Wk, ab_in[jj][:, 512:1024], ab_in[jj][:, 1024:1536], swp(Wq), swp(Wk)], 1))
    sh["w_ret"] = _wl(np.stack(wr))
    rv = np.zeros((2, 128, 12), f32)
    lg = np.log1p(-np.exp2(-5.0 - np.arange(4, dtype=np.float64)))
    rv[:, :, 0:4] = lg[None, None, :].astype(f32)
    rv[:, :, 4:8] = A(ret_gn_w).transpose(0, 2, 1)
    sh["ret_vec"] = rv
    freqs = (np.float32(10000.0) ** (-np.linspace(0.0, 1.0, 32, dtype=f32))).astype(f32)
    ang = (np.arange(T, dtype=f32)[:, None] * freqs[None, :]).astype(f32).astype(np.float64)
    cs = np.repeat(np.cos(ang), 2, axis=1).T
    sn = np.repeat(np.sin(ang), 2, axis=1)
    sn[:, 0::2] *= -1
    sn = sn.T
    rt = np.zeros((128, 4, T), f32)
    rt[0:64, 0, :] = cs
    rt[0:64, 1, :] = sn
    rt[64:128, 2, :] = cs
    rt[64:128, 3, :] = sn
    sh["rot_tab"] = rt
    sh["w_ssd"] = _wl(ab_in[:, :, 1536:2824])
    rows = np.zeros((2, 128, 536), f32)
    rows[:, :, 0:8] = A(ssd_dt_bias)[:, None, :]
    rows[:, :, 8:16] = A(ssd_a_log)[:, None, :]
    rows[:, :, 16:24] = A(ssd_d)[:, None, :]
    rows[:, :, 24:536] = A(ssd_norm_w)[:, None, :]
    sh["ssd_rows"] = rows
    cv = np.zeros((2, 128, 6, 6), f32)
    cwv, cbv, dsv = A(ssd_conv_w), A(ssd_conv_b), A(ssd_d)
    for jj in range(2):
        cv[jj, :, :, 0:4] = cwv[jj].T.reshape(6, 128, 4).transpose(1, 0, 2)
        cv[jj, :, :, 4] = cbv[jj].reshape(6, 128).T
        cv[jj, :, 0:4, 5] = np.repeat(dsv[jj], 64).reshape(4, 128).T
    sh["ssd_conv"] = cv
    return sh


def kernel(x, **params):
    x = np.asarray(x, dtype=np.float32)
    B, T, _ = x.shape
    shared = prepare_inputs(T, **params)
    nc = build_program(T, [0, 1, 2, 3], phases=ALL_PHASES)
    in_maps = []
    for b in range(B):
        m = dict(shared)
        m["xT"] = np.ascontiguousarray(x[b].T)
        in_maps.append(m)
    res = run_bass_kernel_spmd(nc, in_maps, core_ids=list(range(B)))
    out = np.stack([np.ascontiguousarray(res.results[b]["out"].T) for b in range(B)], axis=0)
    return out.astype(np.float32)
```
